# Optimizing a Trainium2 kernel written in Bass

```python
import math
import jax, jax.numpy as jnp
from jax import lax
import numpy as np

D_MODEL = 1024
BATCH = 2
SEQ = 16384
DEPTH = 4
DEC_BATCH = 32
DEC_SEQ = 2048
PAST_LEN = 128

CONV_WIDTH = D_MODEL // 4
CONV_K = 31
DIFF_HEADS = 4
DIFF_HEAD_DIM = 64
DIFF_V_DIM = 2 * DIFF_HEAD_DIM
GQA_Q_HEADS = 4
GQA_KV_HEADS = 2
GQA_HEAD_DIM = 64
MIX_WIDTH = CONV_WIDTH + DIFF_HEADS * DIFF_V_DIM + GQA_Q_HEADS * GQA_HEAD_DIM
FFN_HIDDEN = 2816
FFN_CONV_K = 3
ROPE_THETA = 500000.0
PARTIAL_ROT_DIM = DIFF_HEAD_DIM // 4
AXIAL_THETA = 10000.0
AXIAL_HALF = GQA_HEAD_DIM // 2
GRID_W = 64
Q_BLOCK = 128
EPS = 1e-6

A_VAL_W = CONV_WIDTH
A_GATE_W = CONV_WIDTH
B_Q_W = DIFF_HEADS * 2 * DIFF_HEAD_DIM
B_K_W = DIFF_HEADS * 2 * DIFF_HEAD_DIM
B_V_W = DIFF_HEADS * DIFF_V_DIM
C_Q_W = GQA_Q_HEADS * GQA_HEAD_DIM
C_K_W = GQA_KV_HEADS * GQA_HEAD_DIM
C_V_W = GQA_KV_HEADS * GQA_HEAD_DIM
IN_WIDTH = A_VAL_W + A_GATE_W + B_Q_W + B_K_W + B_V_W + C_Q_W + C_K_W + C_V_W
SPLITS = (
    A_VAL_W,
    A_VAL_W + A_GATE_W,
    A_VAL_W + A_GATE_W + B_Q_W,
    A_VAL_W + A_GATE_W + B_Q_W + B_K_W,
    A_VAL_W + A_GATE_W + B_Q_W + B_K_W + B_V_W,
    A_VAL_W + A_GATE_W + B_Q_W + B_K_W + B_V_W + C_Q_W,
    A_VAL_W + A_GATE_W + B_Q_W + B_K_W + B_V_W + C_Q_W + C_K_W,
)

kernel_name = "hymba_style_conv_diffattn_axialgqa_encoder"


def rms_norm(x, g):
    xf = x.astype(jnp.float32)
    y = xf * lax.rsqrt(jnp.mean(xf * xf, axis=-1, keepdims=True) + EPS)
    return (y * g.astype(jnp.float32)).astype(x.dtype)


def layer_norm(x, g, b):
    xf = x.astype(jnp.float32)
    mu = jnp.mean(xf, axis=-1, keepdims=True)
    xc = xf - mu
    var = jnp.mean(xc * xc, axis=-1, keepdims=True)
    y = xc * lax.rsqrt(var + EPS) * g.astype(jnp.float32) + b.astype(jnp.float32)
    return y.astype(x.dtype)


def depthwise_conv(x, w, b):
    k = w.shape[0]
    pad = k // 2
    y = lax.conv_general_dilated(
        x, w[:, None, :].astype(x.dtype), window_strides=(1,), padding=[(pad, pad)],
        dimension_numbers=("NWC", "WIO", "NWC"), feature_group_count=x.shape[-1])
    return y + b.astype(x.dtype)


def rope_angles(pos, dim, theta):
    inv = theta ** (-jnp.arange(0, dim, 2, dtype=jnp.float32) / dim)
    ang = pos.astype(jnp.float32)[:, None] * inv[None, :]
    return jnp.cos(ang), jnp.sin(ang)


def rotate(x, cos, sin):
    half = x.shape[-1] // 2
    shape = (1, cos.shape[0]) + (1,) * (x.ndim - 3) + (cos.shape[1],)
    c = cos.reshape(shape).astype(x.dtype)
    s = sin.reshape(shape).astype(x.dtype)
    x1, x2 = x[..., :half], x[..., half:]
    return jnp.concatenate([x1 * c - x2 * s, x2 * c + x1 * s], axis=-1)


def to_blocks(t):
    b, s = t.shape[:2]
    t = t.reshape((b, s // Q_BLOCK, Q_BLOCK) + t.shape[2:])
    return jnp.moveaxis(t, 1, 0)


def from_blocks(t):
    t = jnp.moveaxis(t, 0, 1)
    return t.reshape((t.shape[0], t.shape[1] * t.shape[2]) + t.shape[3:])


def diff_attention(q1, q2, k1, k2, v, lam):
    scale = DIFF_HEAD_DIM ** -0.5
    q1 = q1 * scale
    q2 = q2 * scale

    def one_block(qb):
        qb1, qb2 = qb
        s1 = jnp.einsum("bqhd,bkhd->bhqk", qb1, k1).astype(jnp.float32)
        s2 = jnp.einsum("bqhd,bkhd->bhqk", qb2, k2).astype(jnp.float32)
        w = jax.nn.softmax(s1, axis=-1) - lam * jax.nn.softmax(s2, axis=-1)
        return jnp.einsum("bhqk,bkhe->bqhe", w.astype(v.dtype), v)

    return from_blocks(lax.map(one_block, (to_blocks(q1), to_blocks(q2))))


def gqa_attention(q, k, v):
    b, s = q.shape[:2]
    rep = GQA_Q_HEADS // GQA_KV_HEADS
    qg = q.reshape(b, s, GQA_KV_HEADS, rep, GQA_HEAD_DIM) * (GQA_HEAD_DIM ** -0.5)

    def one_block(qb):
        sc = jnp.einsum("bqgrd,bkgd->bgrqk", qb, k).astype(jnp.float32)
        p = jax.nn.softmax(sc, axis=-1)
        return jnp.einsum("bgrqk,bkgd->bqgrd", p.astype(v.dtype), v)

    out = from_blocks(lax.map(one_block, to_blocks(qg)))
    return out.reshape(b, s, GQA_Q_HEADS * GQA_HEAD_DIM)


def encoder_layer(x, layer_idx, norm1_g, w_in, conv_a_w, conv_a_b, ln_a_g, ln_a_b,
                  qn_b_g, kn_b_g, lam_q1, lam_k1, lam_q2, lam_k2, subln_b_g,
                  qn_c_g, kn_c_g, w_out, norm2_g, w_up, conv_f_w, conv_f_b, w_down):
    b, s, _ = x.shape
    rows = s // GRID_W
    pos = jnp.arange(s, dtype=jnp.int32)
    row_ids = jnp.repeat(jnp.arange(rows, dtype=jnp.int32), GRID_W)
    col_ids = jnp.tile(jnp.arange(GRID_W, dtype=jnp.int32), rows)

    h = rms_norm(x, norm1_g)
    proj = h @ w_in
    a_val, a_gate, bq, bk, bv, cq, ck, cv = jnp.split(proj, SPLITS, axis=-1)

    a = a_val * jax.nn.sigmoid(a_gate)
    a = depthwise_conv(a, conv_a_w, conv_a_b)
    a = jax.nn.silu(layer_norm(a, ln_a_g, ln_a_b))

    bq = rms_norm(bq.reshape(b, s, DIFF_HEADS, 2, DIFF_HEAD_DIM), qn_b_g)
    bk = rms_norm(bk.reshape(b, s, DIFF_HEADS, 2, DIFF_HEAD_DIM), kn_b_g)
    cos_p, sin_p = rope_angles(pos, PARTIAL_ROT_DIM, ROPE_THETA)
    bq = jnp.concatenate([rotate(bq[..., :PARTIAL_ROT_DIM], cos_p, sin_p), bq[..., PARTIAL_ROT_DIM:]], axis=-1)
    bk = jnp.concatenate([rotate(bk[..., :PARTIAL_ROT_DIM], cos_p, sin_p), bk[..., PARTIAL_ROT_DIM:]], axis=-1)
    lam_init = 0.8 - 0.6 * math.exp(-0.3 * layer_idx)
    lam = (jnp.exp(jnp.sum(lam_q1.astype(jnp.float32) * lam_k1.astype(jnp.float32)))
           - jnp.exp(jnp.sum(lam_q2.astype(jnp.float32) * lam_k2.astype(jnp.float32)))
           + lam_init)
    ob = diff_attention(bq[..., 0, :], bq[..., 1, :], bk[..., 0, :], bk[..., 1, :],
                        bv.reshape(b, s, DIFF_HEADS, DIFF_V_DIM), lam)
    ob = (rms_norm(ob, subln_b_g) * (1.0 - lam_init)).reshape(b, s, DIFF_HEADS * DIFF_V_DIM)

    cq = rms_norm(cq.reshape(b, s, GQA_Q_HEADS, GQA_HEAD_DIM), qn_c_g)
    ck = rms_norm(ck.reshape(b, s, GQA_KV_HEADS, GQA_HEAD_DIM), kn_c_g)
    cos_r, sin_r = rope_angles(row_ids, AXIAL_HALF, AXIAL_THETA)
    cos_c, sin_c = rope_angles(col_ids, AXIAL_HALF, AXIAL_THETA)

    def axial(t):
        return jnp.concatenate([rotate(t[..., :AXIAL_HALF], cos_r, sin_r),
                                rotate(t[..., AXIAL_HALF:], cos_c, sin_c)], axis=-1)

    oc = gqa_attention(axial(cq), axial(ck), cv.reshape(b, s, GQA_KV_HEADS, GQA_HEAD_DIM))

    x = x + jnp.concatenate([a, ob, oc], axis=-1) @ w_out

    u = depthwise_conv(rms_norm(x, norm2_g) @ w_up, conv_f_w, conv_f_b)
    u_val, u_gate = jnp.split(u, 2, axis=-1)
    return x + (u_val * jax.nn.silu(u_gate)) @ w_down


def setup_inputs(seed: int = 0) -> dict:
    key = jax.random.key(seed)
    ks = jax.random.split(key, 24)
    f32 = jnp.float32

    def nrm(k, shape, scale):
        return jax.random.normal(k, shape, f32) * scale

    def gain(k, shape):
        return 1.0 + 0.02 * jax.random.normal(k, shape, f32)

    return {
        "x_prompt": nrm(ks[0], (BATCH, SEQ, D_MODEL), 1.0),
        "x_sample": nrm(ks[1], (DEC_BATCH, DEC_SEQ, D_MODEL), 1.0),
        "norm1_g": gain(ks[2], (DEPTH, D_MODEL)),
        "w_in": nrm(ks[3], (DEPTH, D_MODEL, IN_WIDTH), D_MODEL ** -0.5),
        "conv_a_w": nrm(ks[4], (DEPTH, CONV_K, CONV_WIDTH), CONV_K ** -0.5),
        "conv_a_b": nrm(ks[5], (DEPTH, CONV_WIDTH), 0.01),
        "ln_a_g": gain(ks[6], (DEPTH, CONV_WIDTH)),
        "ln_a_b": nrm(ks[7], (DEPTH, CONV_WIDTH), 0.01),
        "qn_b_g": gain(ks[8], (DEPTH, DIFF_HEAD_DIM)),
        "kn_b_g": gain(ks[9], (DEPTH, DIFF_HEAD_DIM)),
        "lam_q1": nrm(ks[10], (DEPTH, DIFF_HEAD_DIM), 0.1),
        "lam_k1": nrm(ks[11], (DEPTH, DIFF_HEAD_DIM), 0.1),
        "lam_q2": nrm(ks[12], (DEPTH, DIFF_HEAD_DIM), 0.1),
        "lam_k2": nrm(ks[13], (DEPTH, DIFF_HEAD_DIM), 0.1),
        "subln_b_g": gain(ks[14], (DEPTH, DIFF_V_DIM)),
        "qn_c_g": gain(ks[15], (DEPTH, GQA_HEAD_DIM)),
        "kn_c_g": gain(ks[16], (DEPTH, GQA_HEAD_DIM)),
        "w_out": nrm(ks[17], (DEPTH, MIX_WIDTH, D_MODEL), MIX_WIDTH ** -0.5),
        "norm2_g": gain(ks[18], (DEPTH, D_MODEL)),
        "w_up": nrm(ks[19], (DEPTH, D_MODEL, 2 * FFN_HIDDEN), D_MODEL ** -0.5),
        "conv_f_w": nrm(ks[20], (DEPTH, FFN_CONV_K, 2 * FFN_HIDDEN), FFN_CONV_K ** -0.5),
        "conv_f_b": nrm(ks[21], (DEPTH, 2 * FFN_HIDDEN), 0.01),
        "w_down": nrm(ks[22], (DEPTH, FFN_HIDDEN, D_MODEL), FFN_HIDDEN ** -0.5),
    }


def reference(x_prompt, x_sample, norm1_g, w_in, conv_a_w, conv_a_b, ln_a_g, ln_a_b,
              qn_b_g, kn_b_g, lam_q1, lam_k1, lam_q2, lam_k2, subln_b_g,
              qn_c_g, kn_c_g, w_out, norm2_g, w_up, conv_f_w, conv_f_b, w_down):
    y_prompt = x_prompt
    y_sample = x_sample
    for l in range(DEPTH):
        layer_params = (norm1_g[l], w_in[l], conv_a_w[l], conv_a_b[l], ln_a_g[l], ln_a_b[l],
                        qn_b_g[l], kn_b_g[l], lam_q1[l], lam_k1[l], lam_q2[l], lam_k2[l],
                        subln_b_g[l], qn_c_g[l], kn_c_g[l], w_out[l], norm2_g[l], w_up[l],
                        conv_f_w[l], conv_f_b[l], w_down[l])
        y_prompt = encoder_layer(y_prompt, l, *layer_params)
        y_sample = encoder_layer(y_sample, l, *layer_params)
    return (y_prompt, y_sample)
```

```python
import math
from contextlib import ExitStack

import numpy as np
import ml_dtypes

import concourse.bass as bass
import concourse.mybir as mybir
from concourse.bass_utils import run_bass_kernel_spmd

F32 = mybir.dt.float32
BF16 = mybir.dt.bfloat16
AF = mybir.ActivationFunctionType
ALU = mybir.AluOpType
AX = mybir.AxisListType

D = 1024
INW = 2560
FF = 2816
EPS = 1e-6
NCORES = 8
RANKS = 4


class Cfg:
    def __init__(self, depth=4, seg=2048, np_=2, ns=4, use_cc=True, stop=99):
        self.stop = stop
        self.depth = depth
        self.seg = seg
        self.np = np_
        self.ns = ns
        self.ptok = np_ * seg
        self.T = (np_ + ns) * seg
        self.sp = RANKS * self.ptok
        self.use_cc = use_cc


EPOCH = 8000


class Counter:
    def __init__(self, kk, name, step):
        self.kk = kk
        self.name = name
        self.step = step
        self.count = 0
        self.sems = []
        self.per = EPOCH // step

    def sem_for(self, c):
        idx = (c - 1) // self.per
        while len(self.sems) <= idx:
            self.sems.append(self.kk.es.enter_context(self.kk.nc.semaphore(f"s_{self.name}{len(self.sems)}")))
        return self.sems[idx], (c - idx * self.per) * self.step


class Issuer:
    def __init__(self, name, h, cnt):
        self.name = name
        self.h = h
        self.cnt = cnt
        self.waited = {}


class Buf:
    def __init__(self, t):
        self.t = t
        self.w = None
        self.r = {}

    def __getitem__(self, idx):
        return self.t[idx]


class K:
    def __init__(self, nc):
        self.nc = nc
        self.es = ExitStack()
        self.counters = []
        mk = lambda n, s: self._mkc(n, s)
        self.PE = Issuer("pe", nc.tensor, mk("pe", 1))
        self.ACT = Issuer("act", nc.scalar, mk("act", 1))
        self.DVE = Issuer("dve", nc.vector, mk("dve", 1))
        self.POOL = Issuer("pool", nc.gpsimd, mk("pool", 1))
        self.SP = Issuer("sp", nc.sync, mk("spc", 1))
        self.issuers = [self.PE, self.ACT, self.DVE, self.POOL, self.SP]
        NL = 16
        self.q_sp = [mk(f"qsp{i}_", 16) for i in range(NL)]
        self.q_pool = [mk(f"qpool{i}_", 16) for i in range(NL)]
        self.q_cc = [mk(f"qcc{i}_", 1) for i in range(4)]
        self.rr = {"sp": 0, "pool": 0, "cc": 0}
        self.ninst = 0

    def lane(self, which):
        lst = {"sp": self.q_sp, "pool": self.q_pool, "cc": self.q_cc}[which]
        c = lst[self.rr[which] % len(lst)]
        self.rr[which] += 1
        return c

    def _mkc(self, n, s):
        c = Counter(self, n, s)
        self.counters.append(c)
        return c

    def _wait(self, iss, c, n):
        if n <= 0:
            return
        if iss.waited.get(c, 0) >= n:
            return
        if c is self.PE.cnt and iss is self.PE:
            return
        sem, val = c.sem_for(n)
        iss.h.wait_ge(sem, val)
        iss.waited[c] = n

    def emit(self, iss, fn, reads=(), writes=(), counter=None):
        c = counter if counter is not None else iss.cnt
        deps = {}

        def add(cn):
            cc, n = cn
            if deps.get(cc, 0) < n:
                deps[cc] = n

        for b in reads:
            if b.w is not None:
                add(b.w)
        for b in writes:
            if b.w is not None:
                add(b.w)
            for cc, n in b.r.items():
                add((cc, n))
        for cc, n in deps.items():
            self._wait(iss, cc, n)
        inst = fn()
        c.count += 1
        n = c.count
        sem, val = c.sem_for(n)
        inst.then_inc(sem, c.step)
        for b in reads:
            if b.r.get(c, 0) < n:
                b.r[c] = n
        for b in writes:
            b.w = (c, n)
            b.r = {}
        self.ninst += 1
        return inst

    def pe(self, fn, r=(), w=()):
        return self.emit(self.PE, fn, r, w)

    def act(self, fn, r=(), w=()):
        return self.emit(self.ACT, fn, r, w)

    def dve(self, fn, r=(), w=()):
        return self.emit(self.DVE, fn, r, w)

    def pool(self, fn, r=(), w=()):
        return self.emit(self.POOL, fn, r, w)

    def dma_sp(self, out, in_, r=(), w=(), **kw):
        return self.emit(self.SP, lambda: self.nc.sync.dma_start(out=out, in_=in_, **kw), r, w, counter=self.lane("sp"))

    def dma_pool(self, out, in_, r=(), w=(), **kw):
        return self.emit(self.POOL, lambda: self.nc.gpsimd.dma_start(out=out, in_=in_, **kw), r, w, counter=self.lane("pool"))

    def barrier(self):
        snap = [(c, c.count) for c in self.counters]
        for iss in self.issuers:
            for c, n in snap:
                self._wait(iss, c, n)

    def final_wait(self):
        snap = [(c, c.count) for c in self.counters]
        for c, n in snap:
            self._wait(self.SP, c, n)


PV_G1 = 0
PV_G2 = 8
PV_CAW = 16
PV_CAB = 78
PV_LNG = 80
PV_LNB = 82
PV_SUB = 84
PV_FW = 85
PV_FB = 217
NPV = 261


def build_program(cfg):
    nc = bass.Bass("TRN2", target_bir_lowering=False)
    kk = K(nc)
    es = kk.es
    L = cfg.depth
    T, SEG, PTOK = cfg.T, cfg.seg, cfg.ptok
    NTB = T // 128

    def din(name, shape, dt=F32):
        return nc.dram_tensor(name, list(shape), dt, kind="ExternalInput").ap()

    def dscr(name, shape, dt):
        return nc.dram_tensor(name, list(shape), dt, kind="Internal").ap()

    def dcc(name, shape, dt):
        return nc.dram_tensor(name, list(shape), dt).ap()

    x_in = din("x_in", [T, D])
    rot_in = din("rot", [T, 80])
    ident_in = din("ident", [128, 128])
    pv_in = din("pv", [128, L * NPV])
    gt_in = din("gt", [L, 1408])
    lam_in = din("lamv", [128, L * 4 * 64])
    sela_in = din("sela", [120, 30])
    selh_in = din("selh", [8, 2])
    w_in_d = din("w_in", [L, D, INW])
    w_out_d = din("w_out", [L, D, D])
    w_up_d = din("w_up", [L, D, 2 * FF])
    w_dn_d = din("w_down", [L, FF, D])
    y_out = nc.dram_tensor("y_out", [T, D], F32, kind="ExternalOutput").ap()

    wb_in = dscr("wb_in", [L, D, INW], BF16)
    wb_out = dscr("wb_out", [L, D, D], BF16)
    wb_up = dscr("wb_up", [L, D, 2 * FF], BF16)
    wb_dn = dscr("wb_dn", [L, FF, D], BF16)
    xm_d = dscr("xm", [T, D], F32)
    xa_d = dscr("xa", [T, D], F32)
    xb_d = dscr("xb", [T, D], F32)
    at_d = dscr("at", [2, 128, T], F32)
    qkt_d = dscr("qkt", [11, 128, T], BF16)
    v_d = dscr("v", [T, 640], BF16)
    cata_d = dscr("cata", [2, 128, T], BF16)
    h2t_d = dscr("h2t", [8, 128, T], BF16)
    kts_l = [dcc(f"kts{u}", [128, PTOK], BF16) for u in range(5)]
    vs_l = [dcc(f"vs{u}", [PTOK, 128], BF16) for u in range(5)]
    ahs_d = dcc("ahs", [30, 256], F32)
    hhs_d = dcc("hhs", [2, D], BF16)
    kta_l = [dcc(f"kta{u}", [RANKS * 128, PTOK], BF16) for u in range(5)]
    va_l = [dcc(f"va{u}", [RANKS * PTOK, 128], BF16) for u in range(5)]
    aha_d = dcc("aha", [RANKS * 30, 256], F32)
    hha_d = dcc("hha", [RANKS * 2, D], BF16)
    RG = [[0, 1, 2, 3], [4, 5, 6, 7]]

    uid = [0]

    def sb(stack, name, shape, dt):
        uid[0] += 1
        return Buf(stack.enter_context(nc.sbuf_tensor(f"sb{uid[0]}_{name}", list(shape), dt)))

    def ps(stack, name, shape, dt):
        uid[0] += 1
        return Buf(stack.enter_context(nc.psum_tensor(f"ps{uid[0]}_{name}", list(shape), dt)))

    ident_f = sb(es, "ident_f", [128, 128], F32)
    ident_b = sb(es, "ident_b", [128, 128], BF16)
    ones_b = sb(es, "ones_b", [128, 128], BF16)
    ones3 = sb(es, "ones3", [128, 192], BF16)
    onesA = sb(es, "onesA", [128, 128], F32)
    onesS = sb(es, "onesS", [128, 128], F32)
    onesF = sb(es, "onesF", [128, 128], F32)
    swapF = sb(es, "swapF", [128, 128], F32)
    epst = sb(es, "epst", [128, 1], F32)
    pv = sb(es, "pv", [128, L * NPV], F32)
    neglam = sb(es, "neglam", [128, L], F32)
    sela = sb(es, "sela", [120, 30], F32)
    selh = sb(es, "selh", [8, 2], BF16)

    def pvc(l, off, n=1):
        return pv[:, l * NPV + off: l * NPV + off + n]

    kk.dma_sp(ident_f[:], ident_in[:, :], w=[ident_f])
    kk.dma_sp(pv[:], pv_in[:, :], w=[pv])
    kk.dma_sp(sela[:], sela_in[:, :], w=[sela])
    kk.dve(lambda: nc.vector.tensor_copy(out=ident_b[:], in_=ident_f[:]), r=[ident_f], w=[ident_b])
    kk.dve(lambda: nc.vector.memset(ones_b[:], 1.0), w=[ones_b])
    kk.dve(lambda: nc.vector.memset(ones3[:], 0.0), w=[ones3])
    kk.dve(lambda: nc.vector.memset(ones3[:, 64:128], 1.0), w=[ones3])
    kk.dve(lambda: nc.vector.memset(onesA[:], 1.0 / 256), w=[onesA])
    kk.dve(lambda: nc.vector.memset(onesS[:], 1.0 / 128), w=[onesS])
    kk.dve(lambda: nc.vector.memset(epst[:], EPS), w=[epst])
    kk.dve(lambda: nc.vector.memset(onesF[:], 1.0), w=[onesF])
    kk.dve(lambda: nc.vector.tensor_copy(out=swapF[:, 0:64], in_=ident_f[:, 64:128]), r=[ident_f], w=[swapF])
    kk.dve(lambda: nc.vector.tensor_copy(out=swapF[:, 64:128], in_=ident_f[:, 0:64]), r=[ident_f], w=[swapF])
    with ExitStack() as st:
        lamv = sb(st, "lamv", [128, L * 4 * 64], F32)
        selh_f = sb(st, "selh_f", [8, 2], F32)
        lp = sb(st, "lp", [128, L * 2 * 64], F32)
        lsum = sb(st, "lsum", [128, L * 2], F32)
        kk.dma_sp(lamv[:], lam_in[:, :], w=[lamv])
        kk.dma_sp(selh_f[:], selh_in[:, :], w=[selh_f])
        kk.dve(lambda: nc.vector.tensor_copy(out=selh[:], in_=selh_f[:]), r=[selh_f], w=[selh])
        lv = lamv[:].rearrange("p (l f d) -> p l f d", l=L, f=4)
        lpv = lp[:].rearrange("p (l f d) -> p l f d", l=L, f=2)
        kk.dve(lambda: nc.vector.tensor_tensor(out=lpv[:, :, 0, :], in0=lv[:, :, 0, :], in1=lv[:, :, 1, :], op=ALU.mult), r=[lamv], w=[lp])
        kk.dve(lambda: nc.vector.tensor_tensor(out=lpv[:, :, 1, :], in0=lv[:, :, 2, :], in1=lv[:, :, 3, :], op=ALU.mult), r=[lamv], w=[lp])
        kk.dve(lambda: nc.vector.tensor_reduce(out=lsum[:], in_=lp[:].rearrange("p (g d) -> p g d", d=64), axis=AX.X, op=ALU.add), r=[lp], w=[lsum])
        kk.act(lambda: nc.scalar.activation(out=lsum[:], in_=lsum[:], func=AF.Exp), r=[lsum], w=[lsum])
        for l in range(L):
            lam_init = 0.8 - 0.6 * math.exp(-0.3 * l)
            kk.dve(lambda l=l, li=lam_init: nc.vector.scalar_tensor_tensor(
                out=neglam[:, l:l + 1], in0=lsum[:, 2 * l + 1:2 * l + 2], scalar=-li, in1=lsum[:, 2 * l:2 * l + 1],
                op0=ALU.add, op1=ALU.subtract), r=[lsum], w=[neglam])
        kk.barrier()

    with ExitStack() as st:
        CW = 2816
        stg = [sb(st, f"wstg{i}", [128, CW], F32) for i in range(3)]
        stb = [sb(st, f"wstb{i}", [128, CW], BF16) for i in range(3)]
        items = []
        for l in range(L):
            for kc in range(8):
                items.append((w_in_d[l, kc * 128:(kc + 1) * 128, :], wb_in[l, kc * 128:(kc + 1) * 128, :], INW, pvc(l, PV_G1 + kc)))
            for kc in range(8):
                items.append((w_out_d[l, kc * 128:(kc + 1) * 128, :], wb_out[l, kc * 128:(kc + 1) * 128, :], D, None))
            for kc in range(8):
                for hh in range(2):
                    items.append((w_up_d[l, kc * 128:(kc + 1) * 128, hh * FF:(hh + 1) * FF],
                                  wb_up[l, kc * 128:(kc + 1) * 128, hh * FF:(hh + 1) * FF], FF, pvc(l, PV_G2 + kc)))
            for kc in range(22):
                items.append((w_dn_d[l, kc * 128:(kc + 1) * 128, :], wb_dn[l, kc * 128:(kc + 1) * 128, :], D, None))
        for i, (src, dst, n, g) in enumerate(items):
            a, b = stg[i % 3], stb[i % 3]
            kk.dma_sp(a[:, 0:n], src, w=[a])
            if i % 2 == 0:
                if g is None:
                    kk.dve(lambda a=a, b=b, n=n: nc.vector.tensor_copy(out=b[:, 0:n], in_=a[:, 0:n]), r=[a], w=[b])
                else:
                    kk.dve(lambda a=a, b=b, n=n, g=g: nc.vector.tensor_scalar(out=b[:, 0:n], in0=a[:, 0:n], scalar1=g, scalar2=None, op0=ALU.mult), r=[a, pv], w=[b])
            else:
                if g is None:
                    kk.act(lambda a=a, b=b, n=n: nc.scalar.copy(out=b[:, 0:n], in_=a[:, 0:n]), r=[a], w=[b])
                else:
                    kk.act(lambda a=a, b=b, n=n, g=g: nc.scalar.activation(out=b[:, 0:n], in_=a[:, 0:n], func=AF.Copy, scale=g), r=[a, pv], w=[b])
            kk.dma_pool(dst, b[:, 0:n], r=[b])
        kk.barrier()

    def rmsnorm_to_T(st_bufs, xsrc, l_unused, hb, hT, psT, junk, ss, lnv):
        kk.dve(lambda: nc.vector.scalar_tensor_tensor(out=junk[:], in0=xsrc[:], scalar=1.0, in1=xsrc[:], op0=ALU.mult, op1=ALU.mult,
                                                      accum_out=ss[:]), r=[xsrc], w=[junk, ss])
        kk.act(lambda: nc.scalar.activation(out=lnv[:], in_=ss[:], func=AF.Ln, bias=epst[:], scale=1.0 / D), r=[ss, epst], w=[lnv])
        kk.act(lambda: nc.scalar.activation(out=lnv[:], in_=lnv[:], func=AF.Exp, scale=-0.5), r=[lnv], w=[lnv])
        kk.act(lambda: nc.scalar.activation(out=hb[:], in_=xsrc[:], func=AF.Copy, scale=lnv[:]), r=[xsrc, lnv], w=[hb])
        for kc in range(8):
            kk.pe(lambda kc=kc: nc.tensor.transpose(out=psT[:, kc * 128:(kc + 1) * 128], in_=hb[:, kc * 128:(kc + 1) * 128], identity=ident_b[:]),
                  r=[hb, ident_b], w=[psT])
        kk.dve(lambda: nc.vector.tensor_copy(out=hT[:], in_=psT[:, 0:1024]), r=[psT], w=[hT])

    prompt_blocks = PTOK // 128

    for l in range(L):
        lam_init = 0.8 - 0.6 * math.exp(-0.3 * l)
        x_src = x_in if l == 0 else (xa_d if l % 2 == 1 else xb_d)
        x_dst = y_out if l == L - 1 else (xa_d if l % 2 == 0 else xb_d)

        with ExitStack() as st:
          if cfg.stop >= 1:
            WIN = sb(st, "WIN", [128, 8 * INW], BF16)
            G = sb(st, "G", [128, 1408], F32)
            xt = [sb(st, f"xt{i}", [128, D], F32) for i in range(3)]
            rot = [sb(st, f"rot{i}", [128, 80], F32) for i in range(3)]
            qk2 = [sb(st, f"qk2{i}", [128, 1408], F32) for i in range(2)]
            vb2 = [sb(st, f"vb2{i}", [128, 640], BF16) for i in range(2)]
            atm2 = [sb(st, f"atm2{i}", [128, 256], F32) for i in range(2)]
            junk = sb(st, "junk", [128, D], BF16)
            ss = sb(st, "ss", [128, 1], F32)
            lnv = sb(st, "lnv", [128, 1], F32)
            hb = sb(st, "hb", [128, D], BF16)
            hT = sb(st, "hT", [128, D], BF16)
            eg = sb(st, "eg", [128, 256], F32)
            a_tm = sb(st, "a_tm", [128, 256], F32)
            aT = sb(st, "aT", [128, 256], F32)
            qk = sb(st, "qk", [128, 1408], F32)
            sq = sb(st, "sq", [128, 1408], F32)
            ssg = sb(st, "ssg", [128, 22], F32)
            rt = [sb(st, f"rt{i}", [128, 192], F32) for i in range(4)]
            qkb = sb(st, "qkb", [128, 1408], BF16)
            qkT = sb(st, "qkT", [128, 1408], BF16)
            vb = sb(st, "vb", [128, 640], BF16)
            psP = [ps(st, f"psP{j}", [128, 512], F32) for j in range(5)]
            psA = ps(st, "psA", [128, 256], F32)
            psQ = ps(st, "psQ", [128, 2048], BF16)
            psT = psQ

            for kc in range(8):
                kk.dma_sp(WIN[:, kc * INW:(kc + 1) * INW], wb_in[l, kc * 128:(kc + 1) * 128, :], w=[WIN])
            kk.dma_sp(G[:], gt_in[l:l + 1, :].to_broadcast([128, 1408]), w=[G])
            kk.dve(lambda: nc.vector.tensor_scalar(out=G[:, 0:512], in0=G[:, 0:512], scalar1=0.125, scalar2=None, op0=ALU.mult), r=[G], w=[G])
            kk.dve(lambda: nc.vector.tensor_scalar(out=G[:, 1024:1280], in0=G[:, 1024:1280], scalar1=0.125, scalar2=None, op0=ALU.mult), r=[G], w=[G])

            def load_blk(tb):
                b = tb % 3
                kk.dma_sp(xt[b][:], x_src[tb * 128:(tb + 1) * 128, :], w=[xt[b]])
                kk.dma_sp(rot[b][:], rot_in[tb * 128:(tb + 1) * 128, :], w=[rot[b]])

            def front_head(tb):
                X = xt[tb % 3]
                rmsnorm_to_T(None, X, l, hb, hT, psT, junk, ss, lnv)
                for j in range(5):
                    for kc in range(8):
                        kk.pe(lambda j=j, kc=kc: nc.tensor.matmul(psP[j][:], lhsT=hT[:, kc * 128:(kc + 1) * 128],
                                                                  rhs=WIN[:, kc * INW + j * 512: kc * INW + (j + 1) * 512],
                                                                  start=(kc == 0), stop=(kc == 7)), r=[hT, WIN], w=[psP[j]])

            def front_tail(tb):
                b = tb % 2
                QK, VBb, ATM = qk2[b], vb2[b], atm2[b]
                kk.act(lambda: nc.scalar.activation(out=eg[:], in_=psP[0][:, 256:512], func=AF.Exp, scale=-1.0), r=[psP[0]], w=[eg])
                kk.act(lambda: nc.scalar.copy(out=QK[:, 0:512], in_=psP[1][:]), r=[psP[1]], w=[QK])
                kk.act(lambda: nc.scalar.copy(out=QK[:, 512:1024], in_=psP[2][:]), r=[psP[2]], w=[QK])
                kk.act(lambda: nc.scalar.copy(out=QK[:, 1024:1408], in_=psP[3][:, 0:384]), r=[psP[3]], w=[QK])
                kk.act(lambda: nc.scalar.copy(out=VBb[:, 0:512], in_=psP[4][:]), r=[psP[4]], w=[VBb])
                kk.act(lambda: nc.scalar.copy(out=VBb[:, 512:640], in_=psP[3][:, 384:512]), r=[psP[3]], w=[VBb])
                kk.dve(lambda: nc.vector.tensor_scalar(out=eg[:], in0=eg[:], scalar1=1.0, scalar2=None, op0=ALU.add), r=[eg], w=[eg])
                kk.dve(lambda: nc.vector.reciprocal(out=eg[:], in_=eg[:]), r=[eg], w=[eg])
                kk.dve(lambda: nc.vector.tensor_tensor(out=ATM[:], in0=psP[0][:, 0:256], in1=eg[:], op=ALU.mult), r=[psP[0], eg], w=[ATM])

            def back(tb):
                b = tb % 2
                t0 = tb * 128
                R = rot[tb % 3]
                qk, vb, a_tm = qk2[b], vb2[b], atm2[b]
                for c in range(2):
                    kk.pe(lambda c=c: nc.tensor.transpose(out=psA[:, c * 128:(c + 1) * 128], in_=a_tm[:, c * 128:(c + 1) * 128], identity=ident_f[:]),
                          r=[a_tm, ident_f], w=[psA])
                kk.act(lambda: nc.scalar.copy(out=aT[:], in_=psA[:]), r=[psA], w=[aT])
                kk.dma_pool(at_d[:, :, t0:t0 + 128].rearrange("c p t -> p c t"), aT[:].rearrange("p (c t) -> p c t", c=2), r=[aT])
                if tb == 0:
                    kk.dma_pool(ahs_d[0:15, :], a_tm[0:15, :], r=[a_tm])
                if tb == prompt_blocks - 1:
                    kk.dma_pool(ahs_d[15:30, :], a_tm[113:128, :], r=[a_tm])

            def back_qk(tb):
                b = tb % 2
                t0 = tb * 128
                R = rot[tb % 3]
                qk, vb, a_tm = qk2[b], vb2[b], atm2[b]
                kk.pool(lambda: nc.gpsimd.tensor_tensor(out=sq[:], in0=qk[:], in1=qk[:], op=ALU.mult), r=[qk], w=[sq])
                kk.dve(lambda: nc.vector.tensor_reduce(out=ssg[:], in_=sq[:].rearrange("p (g d) -> p g d", d=64), axis=AX.X, op=ALU.add), r=[sq], w=[ssg])
                kk.act(lambda: nc.scalar.activation(out=ssg[:], in_=ssg[:], func=AF.Ln, bias=epst[:], scale=1.0 / 64), r=[ssg, epst], w=[ssg])
                kk.act(lambda: nc.scalar.activation(out=ssg[:], in_=ssg[:], func=AF.Exp, scale=-0.5), r=[ssg], w=[ssg])
                qk3 = qk[:].rearrange("p (g d) -> p g d", d=64)
                kk.dve(lambda: nc.vector.tensor_tensor(out=qk3, in0=qk3, in1=ssg[:].unsqueeze(2).to_broadcast([128, 22, 64]), op=ALU.mult), r=[qk, ssg], w=[qk])
                kk.dve(lambda: nc.vector.tensor_tensor(out=qk[:], in0=qk[:], in1=G[:], op=ALU.mult), r=[qk, G], w=[qk])
                qB = qk[:, 0:1024].rearrange("p (g d) -> p g d", d=64)
                x1, x2 = qB[:, :, 0:8], qB[:, :, 8:16]
                cB = R[:, 0:8].unsqueeze(1).to_broadcast([128, 16, 8])
                sB = R[:, 8:16].unsqueeze(1).to_broadcast([128, 16, 8])
                tv = [rt[i][:, 0:128].rearrange("p (g d) -> p g d", d=8) for i in range(4)]
                kk.dve(lambda: nc.vector.tensor_tensor(out=tv[0], in0=x1, in1=cB, op=ALU.mult), r=[qk, R], w=[rt[0]])
                kk.dve(lambda: nc.vector.tensor_tensor(out=tv[1], in0=x2, in1=sB, op=ALU.mult), r=[qk, R], w=[rt[1]])
                kk.dve(lambda: nc.vector.tensor_tensor(out=tv[2], in0=x2, in1=cB, op=ALU.mult), r=[qk, R], w=[rt[2]])
                kk.dve(lambda: nc.vector.tensor_tensor(out=tv[3], in0=x1, in1=sB, op=ALU.mult), r=[qk, R], w=[rt[3]])
                kk.dve(lambda: nc.vector.tensor_tensor(out=x1, in0=tv[0], in1=tv[1], op=ALU.subtract), r=[rt[0], rt[1]], w=[qk])
                kk.dve(lambda: nc.vector.tensor_tensor(out=x2, in0=tv[2], in1=tv[3], op=ALU.add), r=[rt[2], rt[3]], w=[qk])
                qC = qk[:, 1024:1408].rearrange("p (g h x d) -> p g h x d", g=6, h=2, x=2)
                y1, y2 = qC[:, :, :, 0, :], qC[:, :, :, 1, :]
                RC = R[:, 16:80].rearrange("p (h x d) -> p h x d", h=2, x=2)
                cC = RC[:, :, 0, :].unsqueeze(1).to_broadcast([128, 6, 2, 16])
                sC = RC[:, :, 1, :].unsqueeze(1).to_broadcast([128, 6, 2, 16])
                tw = [rt[i][:, 0:192].rearrange("p (g h d) -> p g h d", g=6, h=2) for i in range(4)]
                kk.dve(lambda: nc.vector.tensor_tensor(out=tw[0], in0=y1, in1=cC, op=ALU.mult), r=[qk, R], w=[rt[0]])
                kk.dve(lambda: nc.vector.tensor_tensor(out=tw[1], in0=y2, in1=sC, op=ALU.mult), r=[qk, R], w=[rt[1]])
                kk.dve(lambda: nc.vector.tensor_tensor(out=tw[2], in0=y2, in1=cC, op=ALU.mult), r=[qk, R], w=[rt[2]])
                kk.dve(lambda: nc.vector.tensor_tensor(out=tw[3], in0=y1, in1=sC, op=ALU.mult), r=[qk, R], w=[rt[3]])
                kk.dve(lambda: nc.vector.tensor_tensor(out=y1, in0=tw[0], in1=tw[1], op=ALU.subtract), r=[rt[0], rt[1]], w=[qk])
                kk.dve(lambda: nc.vector.tensor_tensor(out=y2, in0=tw[2], in1=tw[3], op=ALU.add), r=[rt[2], rt[3]], w=[qk])
                kk.act(lambda: nc.scalar.copy(out=qkb[:], in_=qk[:]), r=[qk], w=[qkb])
                for c in range(11):
                    kk.pe(lambda c=c: nc.tensor.transpose(out=psQ[:, c * 128:(c + 1) * 128], in_=qkb[:, c * 128:(c + 1) * 128], identity=ident_b[:]),
                          r=[qkb, ident_b], w=[psQ])
                kk.dve(lambda: nc.vector.tensor_copy(out=qkT[:], in_=psQ[:, 0:1408]), r=[psQ], w=[qkT])
                kk.dma_pool(qkt_d[:, :, t0:t0 + 128].rearrange("c p t -> p c t"), qkT[:].rearrange("p (c t) -> p c t", c=11), r=[qkT])
                kk.dma_pool(v_d[t0:t0 + 128, :], vb[:], r=[vb])
                if tb < prompt_blocks:
                    for u in range(5):
                        kc0 = 512 + u * 128 if u < 4 else 1280
                        kk.dma_pool(kts_l[u][:, t0:t0 + 128], qkT[:, kc0:kc0 + 128], r=[qkT])
                        kk.dma_pool(vs_l[u][t0:t0 + 128, :], vb[:, u * 128:(u + 1) * 128], r=[vb])

            load_blk(0)
            if NTB > 1:
                load_blk(1)
            front_head(0)
            front_tail(0)
            for tb in range(NTB):
                if tb + 2 < NTB:
                    load_blk(tb + 2)
                back(tb)
                if tb + 1 < NTB:
                    front_head(tb + 1)
                back_qk(tb)
                if tb + 1 < NTB:
                    front_tail(tb + 1)
            kk.barrier()

        if cfg.use_cc and cfg.stop >= 2:
            for src, dst in [(kts_l[u], kta_l[u]) for u in range(5)] + [(vs_l[u], va_l[u]) for u in range(5)] + [(ahs_d, aha_d)]:
                kk.emit(kk.POOL, lambda src=src, dst=dst: nc.gpsimd.collective_compute(
                    "AllGather", ALU.bypass, replica_groups=RG, ins=[src.opt()], outs=[dst.opt()]), counter=kk.lane("cc"))
            kk.barrier()

        with ExitStack() as st:
          if cfg.stop >= 3:
            abuf = [sb(st, f"abuf{i}", [128, SEG + 30], F32) for i in range(2)]
            convo = [sb(st, f"convo{i}", [128, SEG], F32) for i in range(2)]
            sqb = [sb(st, f"sqb{i}", [128, 512], F32) for i in range(2)]
            mean_sb = sb(st, "mean_sb", [128, 512], F32)
            m2 = sb(st, "m2", [128, 512], F32)
            rstd = sb(st, "rstd", [128, 512], F32)
            dd = [sb(st, f"dd{i}", [128, 512], F32) for i in range(2)]
            ee = [sb(st, f"ee{i}", [128, 512], F32) for i in range(2)]
            ob = [sb(st, f"ob{i}", [128, 512], BF16) for i in range(2)]
            ahr = sb(st, "ahr", [120, 256], F32)
            ps_mean = ps(st, "ps_mean", [128, 512], F32)
            ps_msq = ps(st, "ps_msq", [128, 512], F32)
            ps_halo = ps(st, "ps_halo", [128, 64], F32)
            if cfg.use_cc:
                kk.dma_sp(ahr[:], aha_d[:, :], w=[ahr])
            nseg = cfg.np + cfg.ns
            for s in range(nseg):
                t0 = s * SEG
                is_p = s < cfg.np
                for c in range(2):
                    A = abuf[c]
                    left_local = is_p and s > 0
                    right_local = is_p and s < cfg.np - 1
                    lo = t0 - 15 if left_local else t0
                    hi = t0 + SEG + 15 if right_local else t0 + SEG
                    kk.dma_sp(A[:, 15 + (lo - t0): 15 + (hi - t0)], at_d[c, :, lo:hi], w=[A])
                    if not left_local:
                        kk.pool(lambda A=A: nc.gpsimd.memset(A[:, 0:15], 0.0), w=[A])
                    if not right_local:
                        kk.pool(lambda A=A: nc.gpsimd.memset(A[:, SEG + 15:SEG + 30], 0.0), w=[A])
                    if cfg.use_cc and is_p and (s == 0 or s == cfg.np - 1):
                        kk.pe(lambda c=c: nc.tensor.matmul(ps_halo[:, 0:30], lhsT=ahr[:, c * 128:(c + 1) * 128], rhs=sela[:], start=True, stop=True),
                              r=[ahr, sela], w=[ps_halo])
                        if s == 0:
                            kk.act(lambda A=A: nc.scalar.copy(out=A[:, 0:15], in_=ps_halo[:, 0:15]), r=[ps_halo], w=[A])
                        if s == cfg.np - 1:
                            kk.act(lambda A=A: nc.scalar.copy(out=A[:, SEG + 15:SEG + 30], in_=ps_halo[:, 15:30]), r=[ps_halo], w=[A])
                    CO = convo[c]
                    kk.dve(lambda A=A, CO=CO, c=c: nc.vector.tensor_scalar(out=CO[:], in0=A[:, 0:SEG], scalar1=pvc(l, PV_CAW + c * 31),
                                                                          scalar2=pvc(l, PV_CAB + c), op0=ALU.mult, op1=ALU.add), r=[A, pv], w=[CO])
                    for j in range(1, 31):
                        kk.dve(lambda A=A, CO=CO, c=c, j=j: nc.vector.scalar_tensor_tensor(out=CO[:], in0=A[:, j:j + SEG], scalar=pvc(l, PV_CAW + c * 31 + j),
                                                                                         in1=CO[:], op0=ALU.mult, op1=ALU.add), r=[A, pv, CO], w=[CO])
                for ti in range(SEG // 512):
                    c0 = ti * 512
                    for c in range(2):
                        kk.pool(lambda c=c: nc.gpsimd.tensor_tensor(out=sqb[c][:], in0=convo[c][:, c0:c0 + 512], in1=convo[c][:, c0:c0 + 512], op=ALU.mult),
                                r=[convo[c]], w=[sqb[c]])
                    for c in range(2):
                        kk.pe(lambda c=c: nc.tensor.matmul(ps_mean[:], lhsT=onesA[:], rhs=convo[c][:, c0:c0 + 512], start=(c == 0), stop=(c == 1)),
                              r=[onesA, convo[c]], w=[ps_mean])
                    for c in range(2):
                        kk.pe(lambda c=c: nc.tensor.matmul(ps_msq[:], lhsT=onesA[:], rhs=sqb[c][:], start=(c == 0), stop=(c == 1)),
                              r=[onesA, sqb[c]], w=[ps_msq])
                    kk.act(lambda: nc.scalar.copy(out=mean_sb[:], in_=ps_mean[:]), r=[ps_mean], w=[mean_sb])
                    kk.dve(lambda: nc.vector.tensor_tensor(out=m2[:], in0=mean_sb[:], in1=mean_sb[:], op=ALU.mult), r=[mean_sb], w=[m2])
                    kk.dve(lambda: nc.vector.tensor_tensor(out=m2[:], in0=ps_msq[:], in1=m2[:], op=ALU.subtract), r=[ps_msq, m2], w=[m2])
                    kk.dve(lambda: nc.vector.tensor_scalar(out=m2[:], in0=m2[:], scalar1=0.0, scalar2=None, op0=ALU.max), r=[m2], w=[m2])
                    kk.act(lambda: nc.scalar.activation(out=rstd[:], in_=m2[:], func=AF.Ln, bias=epst[:], scale=1.0), r=[m2, epst], w=[rstd])
                    kk.act(lambda: nc.scalar.activation(out=rstd[:], in_=rstd[:], func=AF.Exp, scale=-0.5), r=[rstd], w=[rstd])
                    for c in range(2):
                        Dd, Ee, Ob = dd[c], ee[c], ob[c]
                        kk.dve(lambda c=c, Dd=Dd: nc.vector.tensor_tensor(out=Dd[:], in0=convo[c][:, c0:c0 + 512], in1=mean_sb[:], op=ALU.subtract),
                               r=[convo[c], mean_sb], w=[Dd])
                        kk.dve(lambda Dd=Dd: nc.vector.tensor_tensor(out=Dd[:], in0=Dd[:], in1=rstd[:], op=ALU.mult), r=[Dd, rstd], w=[Dd])
                        kk.dve(lambda c=c, Dd=Dd: nc.vector.tensor_scalar(out=Dd[:], in0=Dd[:], scalar1=pvc(l, PV_LNG + c), scalar2=pvc(l, PV_LNB + c),
                                                                         op0=ALU.mult, op1=ALU.add), r=[Dd, pv], w=[Dd])
                        kk.act(lambda Dd=Dd, Ee=Ee: nc.scalar.activation(out=Ee[:], in_=Dd[:], func=AF.Exp, scale=-1.0), r=[Dd], w=[Ee])
                        kk.pool(lambda Ee=Ee: nc.gpsimd.tensor_scalar(out=Ee[:], in0=Ee[:], scalar1=1.0, scalar2=1.0, op0=ALU.add, op1=ALU.mult), r=[Ee], w=[Ee])
                        kk.dve(lambda Ee=Ee: nc.vector.reciprocal(out=Ee[:], in_=Ee[:]), r=[Ee], w=[Ee])
                        kk.pool(lambda Dd=Dd, Ee=Ee, Ob=Ob: nc.gpsimd.tensor_tensor(out=Ob[:], in0=Dd[:], in1=Ee[:], op=ALU.mult), r=[Dd, Ee], w=[Ob])
                        kk.dma_pool(cata_d[c, :, t0 + c0:t0 + c0 + 512], Ob[:], r=[Ob])
            kk.barrier()

        with ExitStack() as st:
          if cfg.stop >= 4:
            LKMAX = cfg.sp if cfg.use_cc else max(PTOK, SEG)
            NCKMAX = LKMAX // 128
            GTOK = max(PTOK, SEG)
            KT = sb(st, "KT", [128, LKMAX], BF16)
            VB = sb(st, "VB", [128, NCKMAX * 192], BF16)
            catT = sb(st, "catT", [128, 8 * GTOK], BF16)
            WOUT = sb(st, "WOUT", [128, 8 * D], BF16)
            Qa = [sb(st, f"Qa{i}", [128, 512], BF16) for i in range(2)]
            Qb = [sb(st, f"Qb{i}", [128, 512], BF16) for i in range(2)]
            NPT = 4
            PT = [sb(st, f"PT{i}", [128, 1024], BF16) for i in range(NPT)]
            acc = sb(st, "acc", [128, 1024], F32)
            accp = sb(st, "accp", [128, 1024], F32)
            fo = [sb(st, f"fo{i}", [128, 512], F32) for i in range(2)]
            fl = [sb(st, f"fl{i}", [128, 512], F32) for i in range(2)]
            fd = sb(st, "fd", [128, 512], F32)
            fsq = fl[1]
            frs = fl[0]
            xt = [accp, accp]
            xm = acc
            ss = sb(st, "css", [128, 1], F32)
            lnv = sb(st, "clnv", [128, 1], F32)
            hb = sb(st, "chb", [128, D], BF16)
            hT = sb(st, "chT", [128, D], BF16)
            junk = hb
            sc = [ps(st, f"sc{i}", [128, 1024], F32) for i in range(2)]
            po = [ps(st, f"po{m}", [128, 512], F32) for m in range(2)]
            pf = [ps(st, f"pf{m}", [128, 512], F32) for m in range(2)]

            for kc in range(8):
                kk.dma_sp(WOUT[:, kc * D:(kc + 1) * D], wb_out[l, kc * 128:(kc + 1) * 128, :], w=[WOUT])

            groups = [("p", 0, PTOK)] + [("s", PTOK + i * SEG, SEG) for i in range(cfg.ns)]
            scslot = [0]
            ptslot = [0]
            pend = {}

            def flush_pending():
                if pend.get("sums"):
                    pend["sums"]()
                    pend["sums"] = None
                if pend.get("st23"):
                    pend["st23"]()
                    pend["st23"] = None
                if pend.get("st3"):
                    pend["st3"]()
                    pend["st3"] = None
            for (gk, g0, gn) in groups:
                use_all = (gk == "p" and cfg.use_cc)
                Lk = cfg.sp if use_all else gn
                nck = Lk // 128
                VB3 = VB[:, 0:nck * 192].rearrange("p (c e) -> p c e", e=192)
                for c in range(2):
                    kk.dma_sp(catT[:, c * GTOK: c * GTOK + gn], cata_d[c, :, g0:g0 + gn], w=[catT])
                for u in range(6):
                    isB = u < 4
                    kchunk = u if isB else 4
                    if use_all:
                        for r in range(RANKS):
                            kk.dma_sp(KT[:, r * PTOK:(r + 1) * PTOK], kta_l[kchunk][r * 128:(r + 1) * 128, :], w=[KT])
                    else:
                        kk.dma_sp(KT[:, 0:Lk], qkt_d[4 + u if isB else 10, :, g0:g0 + gn], w=[KT])
                    if isB:
                        vcols = slice(u * 128, (u + 1) * 128)
                        vdst = lambda c0, c1: VB3[:, c0:c1, 0:128]
                    else:
                        g = u - 4
                        vcols = slice(512 + g * 64, 512 + (g + 1) * 64)
                        vdst = lambda c0, c1: VB3[:, c0:c1, 64:128]
                        kk.pool(lambda: nc.gpsimd.memset(VB3[:, :, 0:64], 1.0), w=[VB])
                        kk.pool(lambda: nc.gpsimd.memset(VB3[:, :, 128:192], 1.0), w=[VB])
                    if use_all:
                        vsrc = va_l[kchunk]
                        voff = 0
                        vcols = slice(0, 128) if isB else slice(g * 64, (g + 1) * 64)
                    else:
                        vsrc = v_d
                        voff = g0
                    for c0 in range(0, nck, 16):
                        c1 = min(nck, c0 + 16)
                        kk.dma_sp(vdst(c0, c1), vsrc[voff + c0 * 128: voff + c1 * 128, vcols].rearrange("(c p) e -> p c e", p=128), w=[VB])
                    if isB:
                        qrows = [slice(0, 64), slice(64, 128)]
                    else:
                        qrows = [slice(g * 64, (g + 1) * 64)] * 2
                    if u == 0 or u >= 4:
                        for qi_ in range(2):
                            for m_, QQ in enumerate((Qa[qi_], Qb[qi_])):
                                zr = slice(64, 128) if qrows[m_].start == 0 else slice(0, 64)
                                kk.pool(lambda QQ=QQ, zr=zr: nc.gpsimd.memset(QQ[zr, :], 0.0), w=[QQ])
                    if isB:
                        lhs_v = [lambda ck: VB3[:, ck, 0:128], lambda ck: VB3[:, ck, 0:128]]
                        rows = [slice(0, 64), slice(64, 128)]
                    else:
                        lhs_v = [lambda ck: VB3[:, ck, 64:192], lambda ck: VB3[:, ck, 0:128]]
                        rows = [slice(g * 64, (g + 1) * 64)] * 2
                    for qt in range(gn // 512):
                        q0 = g0 + qt * 512
                        qi = qt % 2
                        if isB:
                            kk.dma_sp(Qa[qi][0:64, :], qkt_d[u, 0:64, q0:q0 + 512], w=[Qa[qi]])
                            kk.dma_sp(Qb[qi][64:128, :], qkt_d[u, 64:128, q0:q0 + 512], w=[Qb[qi]])
                        else:
                            kk.dma_sp(Qa[qi][qrows[0], :], qkt_d[8, qrows[0], q0:q0 + 512], w=[Qa[qi]])
                            kk.dma_sp(Qb[qi][qrows[1], :], qkt_d[9, qrows[1], q0:q0 + 512], w=[Qb[qi]])
                        qbufs = [Qa[qi], Qb[qi]]
                        slots = {}

                        def scores(ck):
                            sl = scslot[0] % 2
                            scslot[0] += 1
                            slots[ck] = sl
                            for m in range(2):
                                kk.pe(lambda m=m, sl=sl, ck=ck: nc.tensor.matmul(sc[sl][:, m * 512:(m + 1) * 512], lhsT=KT[:, ck * 128:(ck + 1) * 128],
                                                                                 rhs=qbufs[m][:, :], start=True, stop=True),
                                      r=[KT, qbufs[m]], w=[sc[sl]])

                        scores(0)
                        for ck in range(nck):
                            if ck + 1 < nck:
                                scores(ck + 1)
                            if ck == min(2, nck - 1) and pend.get("sums"):
                                pend["sums"]()
                                pend["sums"] = None
                            if ck == min(6, nck - 1):
                                if pend.get("sums"):
                                    pend["sums"]()
                                    pend["sums"] = None
                                if pend.get("st23"):
                                    pend["st23"]()
                                    pend["st23"] = None
                            if ck == min(11, nck - 1) and pend.get("st3"):
                                if pend.get("st23"):
                                    pend["st23"]()
                                    pend["st23"] = None
                                pend["st3"]()
                                pend["st3"] = None
                            ssl = slots.pop(ck)
                            sl = ptslot[0] % NPT
                            ptslot[0] += 1
                            kk.act(lambda sl=sl, ssl=ssl: nc.scalar.activation(out=PT[sl][:], in_=sc[ssl][:], func=AF.Exp), r=[sc[ssl]], w=[PT[sl]])
                            for m in range(2):
                                kk.pe(lambda m=m, sl=sl, ck=ck: nc.tensor.matmul(
                                    po[m][:], lhsT=lhs_v[m](ck), rhs=PT[sl][:, m * 512:(m + 1) * 512], start=(ck == 0), stop=(ck == nck - 1)),
                                    r=[VB, PT[sl]], w=[po[m]])
                            if isB:
                                tb16 = accp[:].bitcast(BF16)
                                t01, t23 = tb16[:, 0:1024], tb16[:, 1024:2048]
                                if ck % 4 == 0:
                                    prev_sl = sl
                                elif ck % 4 == 1:
                                    kk.dve(lambda a_=prev_sl, b_=sl: nc.vector.tensor_tensor(out=t01, in0=PT[a_][:], in1=PT[b_][:], op=ALU.add),
                                           r=[PT[prev_sl], PT[sl]], w=[accp])
                                elif ck % 4 == 2:
                                    prev_sl = sl
                                else:
                                    kk.dve(lambda a_=prev_sl, b_=sl: nc.vector.tensor_tensor(out=t23, in0=PT[a_][:], in1=PT[b_][:], op=ALU.add),
                                           r=[PT[prev_sl], PT[sl]], w=[accp])
                                    kk.dve(lambda: nc.vector.tensor_tensor(out=t01, in0=t01, in1=t23, op=ALU.add), r=[accp], w=[accp])
                                    if ck == 3:
                                        kk.dve(lambda: nc.vector.tensor_copy(out=acc[:], in_=t01), r=[accp], w=[acc])
                                    else:
                                        kk.dve(lambda: nc.vector.tensor_tensor(out=acc[:], in0=acc[:], in1=t01, op=ALU.add), r=[accp, acc], w=[acc])
                        cchunk = 2 + u if isB else 6 + (u - 4)
                        dst = catT[:, cchunk * GTOK + qt * 512: cchunk * GTOK + (qt + 1) * 512]
                        kk.act(lambda: nc.scalar.copy(out=fo[0][:], in_=po[0][:]), r=[po[0]], w=[fo[0]])
                        kk.dve(lambda: nc.vector.tensor_copy(out=fo[1][:], in_=po[1][:]), r=[po[1]], w=[fo[1]])
                        if isB:
                            def st_sums():
                                for m in range(2):
                                    kk.pe(lambda m=m: nc.tensor.matmul(pf[m][:], lhsT=onesF[:], rhs=acc[:, m * 512:(m + 1) * 512], start=True, stop=True),
                                          r=[onesF, acc], w=[pf[m]])

                            def st23(dst=dst):
                                for m in range(2):
                                    kk.act(lambda m=m: nc.scalar.activation(out=fl[m][:], in_=pf[m][:], func=AF.Ln), r=[pf[m]], w=[fl[m]])
                                    kk.act(lambda m=m: nc.scalar.activation(out=fl[m][:], in_=fl[m][:], func=AF.Exp, scale=-1.0), r=[fl[m]], w=[fl[m]])
                                    kk.dve(lambda m=m: nc.vector.tensor_tensor(out=fo[m][:], in0=fo[m][:], in1=fl[m][:], op=ALU.mult), r=[fo[m], fl[m]], w=[fo[m]])
                                kk.dve(lambda: nc.vector.scalar_tensor_tensor(out=fd[:], in0=fo[1][:], scalar=neglam[:, l:l + 1], in1=fo[0][:],
                                                                              op0=ALU.mult, op1=ALU.add), r=[fo[0], fo[1], neglam], w=[fd])
                                kk.dve(lambda: nc.vector.tensor_tensor(out=fsq[:], in0=fd[:], in1=fd[:], op=ALU.mult), r=[fd], w=[fsq])

                            def st3(dst=dst):
                                kk.pe(lambda: nc.tensor.matmul(pf[0][:], lhsT=onesS[:], rhs=fsq[:], start=True, stop=True), r=[onesS, fsq], w=[pf[0]])
                                kk.act(lambda: nc.scalar.activation(out=frs[:], in_=pf[0][:], func=AF.Ln, bias=epst[:], scale=1.0), r=[pf[0], epst], w=[frs])
                                kk.act(lambda: nc.scalar.activation(out=frs[:], in_=frs[:], func=AF.Exp, scale=-0.5), r=[frs], w=[frs])
                                kk.dve(lambda: nc.vector.tensor_tensor(out=fd[:], in0=fd[:], in1=frs[:], op=ALU.mult), r=[fd, frs], w=[fd])
                                kk.dve(lambda: nc.vector.tensor_scalar(out=dst, in0=fd[:], scalar1=pvc(l, PV_SUB), scalar2=1.0 - lam_init,
                                                                      op0=ALU.mult, op1=ALU.mult), r=[fd, pv], w=[catT])
                        else:
                            st_sums = None

                            def st23(dst=dst):
                                for m in range(2):
                                    lr = slice(64, 128) if m == 0 else slice(0, 64)
                                    orr = slice(0, 64) if m == 0 else slice(64, 128)
                                    kk.act(lambda m=m, lr=lr: nc.scalar.activation(out=fl[m][lr, :], in_=fo[m][lr, :], func=AF.Ln), r=[fo[m]], w=[fl[m]])
                                    kk.act(lambda m=m, lr=lr: nc.scalar.activation(out=fl[m][lr, :], in_=fl[m][lr, :], func=AF.Exp, scale=-1.0), r=[fl[m]], w=[fl[m]])
                                    kk.pool(lambda m=m, orr=orr: nc.gpsimd.memset(fl[m][orr, :], 0.0), w=[fl[m]])

                            def st3(dst=dst):
                                for m in range(2):
                                    kk.pe(lambda m=m: nc.tensor.matmul(pf[m][:], lhsT=swapF[:], rhs=fl[m][:], start=True, stop=True), r=[swapF, fl[m]], w=[pf[m]])
                                kk.dve(lambda: nc.vector.tensor_tensor(out=dst[0:64, :], in0=fo[0][0:64, :], in1=pf[0][0:64, :], op=ALU.mult),
                                       r=[fo[0], pf[0]], w=[catT])
                                kk.dve(lambda: nc.vector.tensor_tensor(out=dst[64:128, :], in0=fo[1][64:128, :], in1=pf[1][64:128, :], op=ALU.mult),
                                       r=[fo[1], pf[1]], w=[catT])
                        pend["sums"] = st_sums
                        pend["st23"] = st23
                        pend["st3"] = st3
                flush_pending()
                nblk = gn // 128
                for bi in range(nblk):
                    t0 = g0 + bi * 128
                    X = xt[0]
                    kk.dma_sp(X[:], x_src[t0:t0 + 128, :], w=[X])
                    for jn in range(2):
                        for kc in range(8):
                            kk.pe(lambda jn=jn, kc=kc: nc.tensor.matmul(po[jn][:], lhsT=catT[:, kc * GTOK + bi * 128: kc * GTOK + (bi + 1) * 128],
                                                                        rhs=WOUT[:, kc * D + jn * 512: kc * D + (jn + 1) * 512],
                                                                        start=(kc == 0), stop=(kc == 7)), r=[catT, WOUT], w=[po[jn]])
                    for jn in range(2):
                        kk.dve(lambda jn=jn, X=X: nc.vector.tensor_tensor(out=xm[:, jn * 512:(jn + 1) * 512], in0=po[jn][:], in1=X[:, jn * 512:(jn + 1) * 512],
                                                                         op=ALU.add), r=[po[jn], X], w=[xm])
                    kk.dma_pool(xm_d[t0:t0 + 128, :], xm[:], r=[xm])
                    psT = sc[0]
                    psT_bf = psT[:, 0:512].bitcast(BF16)
                    rmsnorm_to_T_c1(kk, nc, xm, hb, hT, psT, psT_bf, junk, ss, lnv, epst, ident_b)
                    kk.dma_pool(h2t_d[:, :, t0:t0 + 128].rearrange("c p t -> p c t"), hT[:].rearrange("p (c t) -> p c t", c=8), r=[hT])
                    if gk == "p" and bi == 0:
                        kk.dma_pool(hhs_d[0:1, :], hb[0:1, :], r=[hb])
                    if gk == "p" and bi == nblk - 1:
                        kk.dma_pool(hhs_d[1:2, :], hb[127:128, :], r=[hb])
            kk.barrier()

        if cfg.use_cc and cfg.stop >= 5:
            kk.emit(kk.POOL, lambda: nc.gpsimd.collective_compute(
                "AllGather", ALU.bypass, replica_groups=RG, ins=[hhs_d.opt()], outs=[hha_d.opt()]), counter=kk.lane("cc"))
            kk.barrier()

        with ExitStack() as st:
          if cfg.stop >= 6:
            WUP = sb(st, "WUP", [128, 8 * 2 * FF], BF16)
            WDN = sb(st, "WDN", [128, 22 * D], BF16)
            h2t = [sb(st, f"h2t{i}", [128, 8 * 512], BF16) for i in range(2)]
            gT = sb(st, "gT", [128, 22 * 512], BF16)
            tvb = [sb(st, f"tv{i}", [128, 512], F32) for i in range(2)]
            tgb = [sb(st, f"tg{i}", [128, 512], F32) for i in range(2)]
            sgb = [sb(st, f"sg{i}", [128, 512], F32) for i in range(2)]
            xmb = [sb(st, f"fxm{i}", [128, D], F32) for i in range(2)]
            xo = [sb(st, f"fxo{i}", [128, D], F32) for i in range(2)]
            hh = sb(st, "hh", [8, D], BF16)
            halo_h = sb(st, "halo_h", [128, 16], BF16)
            psv = [ps(st, f"psv{i}", [128, 512], F32) for i in range(2)]
            psg = [ps(st, f"psg{i}", [128, 512], F32) for i in range(2)]
            psy = [ps(st, f"psy{i}", [128, 512], F32) for i in range(4)]
            for kc in range(8):
                kk.dma_sp(WUP[:, kc * 2 * FF:(kc + 1) * 2 * FF], wb_up[l, kc * 128:(kc + 1) * 128, :], w=[WUP])
            for kc in range(22):
                kk.dma_sp(WDN[:, kc * D:(kc + 1) * D], wb_dn[l, kc * 128:(kc + 1) * 128, :], w=[WDN])
            if cfg.use_cc:
                kk.dma_sp(hh[:], hha_d[:, :], w=[hh])
                for kc in range(8):
                    kk.pe(lambda kc=kc: nc.tensor.matmul(psy[0][:, kc * 2:kc * 2 + 2], lhsT=hh[:, kc * 128:(kc + 1) * 128], rhs=selh[:], start=True, stop=True),
                          r=[hh, selh], w=[psy[0]])
                kk.dve(lambda: nc.vector.tensor_copy(out=halo_h[:], in_=psy[0][:, 0:16]), r=[psy[0]], w=[halo_h])
            else:
                kk.dve(lambda: nc.vector.memset(halo_h[:], 0.0), w=[halo_h])
            halo3 = halo_h[:].rearrange("p (c x) -> p c x", x=2)

            fsegs = [("p", 0, PTOK)] + [("s", PTOK + i * SEG, SEG) for i in range(cfg.ns)]
            tiles = []
            for (gk, g0, gn) in fsegs:
                s0 = 0
                while s0 < gn:
                    n = min(510, gn - s0)
                    tiles.append((gk, g0, gn, s0, n))
                    s0 += n
            pslot = [0]
            yslot = [0]

            def load_tile(i):
                gk, g0, gn, s0, n = tiles[i]
                H = h2t[i % 2]
                H3 = H[:].rearrange("p (c t) -> p c t", c=8)
                lo = s0 - 1
                hi = s0 + n + 1
                clo, chi = max(lo, 0), min(hi, gn)
                kk.dma_sp(H3[:, :, clo - lo: chi - lo], h2t_d[:, :, g0 + clo: g0 + chi].rearrange("c p t -> p c t"), w=[H])
                if lo < 0:
                    if gk == "p":
                        kk.pool(lambda: nc.gpsimd.tensor_copy(out=H3[:, :, 0:1], in_=halo3[:, :, 0:1]), r=[halo_h], w=[H])
                    else:
                        kk.pool(lambda: nc.gpsimd.memset(H3[:, :, 0:1], 0.0), w=[H])
                if hi > gn:
                    if gk == "p":
                        kk.pool(lambda: nc.gpsimd.tensor_copy(out=H3[:, :, n + 1:n + 2], in_=halo3[:, :, 1:2]), r=[halo_h], w=[H])
                    else:
                        kk.pool(lambda: nc.gpsimd.memset(H3[:, :, n + 1:n + 2], 0.0), w=[H])

            load_tile(0)
            for i, (gk, g0, gn, s0, n) in enumerate(tiles):
                if i + 1 < len(tiles):
                    load_tile(i + 1)
                H = h2t[i % 2]
                N = n + 2
                for j in range(22):
                    sl = pslot[0] % 2
                    pslot[0] += 1
                    PV_, PG_ = psv[sl], psg[sl]
                    for (P_, ch) in ((PV_, j), (PG_, 22 + j)):
                        for kc in range(8):
                            kk.pe(lambda P_=P_, ch=ch, kc=kc: nc.tensor.matmul(P_[:, 0:N], lhsT=WUP[:, kc * 2 * FF + ch * 128: kc * 2 * FF + (ch + 1) * 128],
                                                                               rhs=H[:, kc * 512: kc * 512 + N], start=(kc == 0), stop=(kc == 7)),
                                  r=[WUP, H], w=[P_])
                    TV, TG, SG = tvb[sl], tgb[sl], sgb[sl]
                    for (P_, TT, ch) in ((PV_, TV, j), (PG_, TG, 22 + j)):
                        kk.act(lambda P_=P_, TT=TT, ch=ch: nc.scalar.activation(out=TT[:, 0:n], in_=P_[:, 0:n], func=AF.Identity,
                                                                                 scale=pvc(l, PV_FW + ch * 3), bias=pvc(l, PV_FB + ch)), r=[P_, pv], w=[TT])
                        for jj in (1, 2):
                            kk.dve(lambda P_=P_, TT=TT, ch=ch, jj=jj: nc.vector.scalar_tensor_tensor(out=TT[:, 0:n], in0=P_[:, jj:jj + n],
                                                                                                      scalar=pvc(l, PV_FW + ch * 3 + jj), in1=TT[:, 0:n],
                                                                                                      op0=ALU.mult, op1=ALU.add), r=[P_, pv, TT], w=[TT])
                    kk.act(lambda TG=TG, SG=SG: nc.scalar.activation(out=SG[:, 0:n], in_=TG[:, 0:n], func=AF.Silu), r=[TG], w=[SG])
                    kk.pool(lambda TV=TV, SG=SG, j=j: nc.gpsimd.tensor_tensor(out=gT[:, j * 512: j * 512 + n], in0=TV[:, 0:n], in1=SG[:, 0:n], op=ALU.mult),
                            r=[TV, SG], w=[gT])
                b0 = 0
                bi = 0
                while b0 < n:
                    m = min(128, n - b0)
                    tk = g0 + s0 + b0
                    XM, XO = xmb[bi % 2], xo[bi % 2]
                    kk.dma_sp(XM[0:m, :], xm_d[tk:tk + m, :], w=[XM])
                    for jn in range(2):
                        Y = psy[yslot[0] % 4]
                        yslot[0] += 1
                        for j in range(22):
                            kk.pe(lambda Y=Y, j=j, jn=jn, b0=b0, m=m: nc.tensor.matmul(Y[0:m, :], lhsT=gT[:, j * 512 + b0: j * 512 + b0 + m],
                                                                                       rhs=WDN[:, j * D + jn * 512: j * D + (jn + 1) * 512],
                                                                                       start=(j == 0), stop=(j == 21)), r=[gT, WDN], w=[Y])
                        kk.dve(lambda Y=Y, jn=jn, m=m, XM=XM, XO=XO: nc.vector.tensor_tensor(out=XO[0:m, jn * 512:(jn + 1) * 512], in0=Y[0:m, :],
                                                                                            in1=XM[0:m, jn * 512:(jn + 1) * 512], op=ALU.add),
                               r=[Y, XM], w=[XO])
                    kk.dma_pool(x_dst[tk:tk + m, :], XO[0:m, :], r=[XO])
                    b0 += m
                    bi += 1
            kk.barrier()

    kk.final_wait()
    return nc, kk


def rmsnorm_to_T_c1(kk, nc, xsrc, hb, hT, psT, psT_bf, junk, ss, lnv, epst, ident_b):
    kk.dve(lambda: nc.vector.scalar_tensor_tensor(out=junk[:], in0=xsrc[:], scalar=1.0, in1=xsrc[:], op0=ALU.mult, op1=ALU.mult,
                                                  accum_out=ss[:]), r=[xsrc], w=[junk, ss])
    kk.act(lambda: nc.scalar.activation(out=lnv[:], in_=ss[:], func=AF.Ln, bias=epst[:], scale=1.0 / D), r=[ss, epst], w=[lnv])
    kk.act(lambda: nc.scalar.activation(out=lnv[:], in_=lnv[:], func=AF.Exp, scale=-0.5), r=[lnv], w=[lnv])
    kk.act(lambda: nc.scalar.activation(out=hb[:], in_=xsrc[:], func=AF.Copy, scale=lnv[:]), r=[xsrc, lnv], w=[hb])
    for kc in range(8):
        kk.pe(lambda kc=kc: nc.tensor.transpose(out=psT_bf[:, kc * 128:(kc + 1) * 128], in_=hb[:, kc * 128:(kc + 1) * 128], identity=ident_b[:]),
              r=[hb, ident_b], w=[psT])
    kk.dve(lambda: nc.vector.tensor_copy(out=hT[:], in_=psT_bf[:, 0:1024]), r=[psT], w=[hT])


def _perm_w_in():
    a = np.arange(0, 512)
    bq = np.arange(512, 1024)
    bk = np.arange(1024, 1536)
    bv = np.arange(1536, 2048)
    cq = np.arange(2048, 2304).reshape(4, 64)[[0, 2, 1, 3]].reshape(-1)
    ck = np.arange(2304, 2432)
    cv = np.arange(2432, 2560)
    return np.concatenate([a, bq, bk, cq, ck, cv, bv])


def _rot_table(pos):
    pos = pos.astype(np.float32)
    invB = (np.float32(500000.0) ** (-np.arange(0, 16, 2, dtype=np.float32) / np.float32(16))).astype(np.float32)
    invC = (np.float32(10000.0) ** (-np.arange(0, 32, 2, dtype=np.float32) / np.float32(32))).astype(np.float32)
    angB = pos[:, None] * invB[None, :]
    p_i = pos.astype(np.int64)
    row = (p_i // 64).astype(np.float32)
    col = (p_i % 64).astype(np.float32)
    angR = row[:, None] * invC[None, :]
    angC = col[:, None] * invC[None, :]
    out = np.concatenate([np.cos(angB), np.sin(angB), np.cos(angR), np.sin(angR), np.cos(angC), np.sin(angC)], axis=1)
    return np.ascontiguousarray(out.astype(np.float32))


def make_in_maps(cfg, inputs):
    L = cfg.depth
    f = lambda k: np.asarray(inputs[k], dtype=np.float32)
    xp, xs = f("x_prompt"), f("x_sample")
    perm = _perm_w_in()
    w_in = np.ascontiguousarray(f("w_in")[:L][:, :, perm])
    w_out = np.ascontiguousarray(f("w_out")[:L])
    w_up = np.ascontiguousarray(f("w_up")[:L])
    w_dn = np.ascontiguousarray(f("w_down")[:L])
    pv = np.zeros((128, L, NPV), np.float32)
    for l in range(L):
        pv[:, l, PV_G1:PV_G1 + 8] = f("norm1_g")[l].reshape(8, 128).T
        pv[:, l, PV_G2:PV_G2 + 8] = f("norm2_g")[l].reshape(8, 128).T
        caw = f("conv_a_w")[l]
        pv[:, l, PV_CAW:PV_CAW + 62] = caw.reshape(31, 2, 128).transpose(2, 1, 0).reshape(128, 62)
        pv[:, l, PV_CAB:PV_CAB + 2] = f("conv_a_b")[l].reshape(2, 128).T
        pv[:, l, PV_LNG:PV_LNG + 2] = f("ln_a_g")[l].reshape(2, 128).T
        pv[:, l, PV_LNB:PV_LNB + 2] = f("ln_a_b")[l].reshape(2, 128).T
        pv[:, l, PV_SUB] = f("subln_b_g")[l]
        fw = f("conv_f_w")[l]
        pv[:, l, PV_FW:PV_FW + 132] = fw.reshape(3, 44, 128).transpose(2, 1, 0).reshape(128, 132)
        pv[:, l, PV_FB:PV_FB + 44] = f("conv_f_b")[l].reshape(44, 128).T
    pv = np.ascontiguousarray(pv.reshape(128, L * NPV))
    gt = np.zeros((L, 1408), np.float32)
    for l in range(L):
        gt[l] = np.concatenate([np.tile(f("qn_b_g")[l], 8), np.tile(f("kn_b_g")[l], 8), np.tile(f("qn_c_g")[l], 4), np.tile(f("kn_c_g")[l], 2)])
    lam = np.stack([f("lam_q1")[:L], f("lam_k1")[:L], f("lam_q2")[:L], f("lam_k2")[:L]], axis=1)
    lamv = np.ascontiguousarray(np.broadcast_to(lam.reshape(1, L * 4 * 64), (128, L * 4 * 64))).astype(np.float32)
    ident = np.eye(128, dtype=np.float32)
    in_maps = []
    for c in range(NCORES):
        p, r = c // RANKS, c % RANKS
        xpc = xp[p, r * cfg.ptok:(r + 1) * cfg.ptok]
        xsc = xs[c * cfg.ns:(c + 1) * cfg.ns].reshape(cfg.ns * cfg.seg, D)
        x_in = np.ascontiguousarray(np.concatenate([xpc, xsc], axis=0))
        pos = np.concatenate([np.arange(r * cfg.ptok, (r + 1) * cfg.ptok)] + [np.arange(cfg.seg)] * cfg.ns)
        rot = _rot_table(pos)
        sela = np.zeros((120, 30), np.float32)
        selh = np.zeros((8, 2), np.float32)
        if r > 0:
            for j in range(15):
                sela[(r - 1) * 30 + 15 + j, j] = 1.0
            selh[(r - 1) * 2 + 1, 0] = 1.0
        if r < RANKS - 1:
            for j in range(15):
                sela[(r + 1) * 30 + j, 15 + j] = 1.0
            selh[(r + 1) * 2 + 0, 1] = 1.0
        in_maps.append(dict(x_in=x_in, rot=rot, ident=ident, pv=pv, gt=gt, lamv=lamv, sela=sela, selh=selh,
                            w_in=w_in, w_out=w_out, w_up=w_up, w_down=w_dn))
    return in_maps


def assemble(cfg, results, nb_prompt, nb_sample):
    yp = np.zeros((nb_prompt, cfg.sp, D), np.float32)
    ys = np.zeros((nb_sample, cfg.seg, D), np.float32)
    for c in range(NCORES):
        y = np.asarray(results[c]["y_out"], dtype=np.float32).reshape(cfg.T, D)
        p, r = c // RANKS, c % RANKS
        yp[p, r * cfg.ptok:(r + 1) * cfg.ptok] = y[:cfg.ptok]
        ys[c * cfg.ns:(c + 1) * cfg.ns] = y[cfg.ptok:].reshape(cfg.ns, cfg.seg, D)
    return yp, ys


def run(cfg, inputs):
    nc, kk = build_program(cfg)
    in_maps = make_in_maps(cfg, inputs)
    res = run_bass_kernel_spmd(nc, in_maps, core_ids=list(range(NCORES)))
    return assemble(cfg, res.results, 2, NCORES * cfg.ns)


def kernel(**inputs):
    cfg = Cfg()
    return run(cfg, inputs)
```

```python
import math
from contextlib import ExitStack

import numpy as np
import ml_dtypes

import concourse.bass as bass
import concourse.mybir as mybir
from concourse.bass_utils import run_bass_kernel_spmd

F32 = mybir.dt.float32
BF16 = mybir.dt.bfloat16
AF = mybir.ActivationFunctionType
ALU = mybir.AluOpType
AX = mybir.AxisListType

D = 1024
INW = 2560
FF = 2816
EPS = 1e-6
NCORES = 8
RANKS = 4


class Cfg:
    def __init__(self, depth=4, seg=2048, np_=2, ns=4, use_cc=True, stop=99):
        self.stop = stop
        self.depth = depth
        self.seg = seg
        self.np = np_
        self.ns = ns
        self.ptok = np_ * seg
        self.T = (np_ + ns) * seg
        self.sp = RANKS * self.ptok
        self.use_cc = use_cc


EPOCH = 8000


class Counter:
    def __init__(self, kk, name, step):
        self.kk = kk
        self.name = name
        self.step = step
        self.count = 0
        self.sems = []
        self.per = EPOCH // step

    def sem_for(self, c):
        idx = (c - 1) // self.per
        while len(self.sems) <= idx:
            self.sems.append(self.kk.es.enter_context(self.kk.nc.semaphore(f"s_{self.name}{len(self.sems)}")))
        return self.sems[idx], (c - idx * self.per) * self.step


class Issuer:
    def __init__(self, name, h, cnt):
        self.name = name
        self.h = h
        self.cnt = cnt
        self.waited = {}


class Buf:
    def __init__(self, t):
        self.t = t
        self.w = None
        self.r = {}

    def __getitem__(self, idx):
        return self.t[idx]


class K:
    def __init__(self, nc):
        self.nc = nc
        self.es = ExitStack()
        self.counters = []
        mk = lambda n, s: self._mkc(n, s)
        self.PE = Issuer("pe", nc.tensor, mk("pe", 1))
        self.ACT = Issuer("act", nc.scalar, mk("act", 1))
        self.DVE = Issuer("dve", nc.vector, mk("dve", 1))
        self.POOL = Issuer("pool", nc.gpsimd, mk("pool", 1))
        self.SP = Issuer("sp", nc.sync, mk("spc", 1))
        self.issuers = [self.PE, self.ACT, self.DVE, self.POOL, self.SP]
        NL = 16
        self.q_sp = [mk(f"qsp{i}_", 16) for i in range(NL)]
        self.q_pool = [mk(f"qpool{i}_", 16) for i in range(NL)]
        self.q_cc = [mk(f"qcc{i}_", 1) for i in range(4)]
        self.rr = {"sp": 0, "pool": 0, "cc": 0}
        self.ninst = 0

    def lane(self, which):
        lst = {"sp": self.q_sp, "pool": self.q_pool, "cc": self.q_cc}[which]
        c = lst[self.rr[which] % len(lst)]
        self.rr[which] += 1
        return c

    def _mkc(self, n, s):
        c = Counter(self, n, s)
        self.counters.append(c)
        return c

    def _wait(self, iss, c, n):
        if n <= 0:
            return
        if iss.waited.get(c, 0) >= n:
            return
        if c is self.PE.cnt and iss is self.PE:
            return
        sem, val = c.sem_for(n)
        iss.h.wait_ge(sem, val)
        iss.waited[c] = n

    def emit(self, iss, fn, reads=(), writes=(), counter=None):
        c = counter if counter is not None else iss.cnt
        deps = {}

        def add(cn):
            cc, n = cn
            if deps.get(cc, 0) < n:
                deps[cc] = n

        for b in reads:
            if b.w is not None:
                add(b.w)
        for b in writes:
            if b.w is not None:
                add(b.w)
            for cc, n in b.r.items():
                add((cc, n))
        for cc, n in deps.items():
            self._wait(iss, cc, n)
        inst = fn()
        c.count += 1
        n = c.count
        sem, val = c.sem_for(n)
        inst.then_inc(sem, c.step)
        for b in reads:
            if b.r.get(c, 0) < n:
                b.r[c] = n
        for b in writes:
            b.w = (c, n)
            b.r = {}
        self.ninst += 1
        return inst

    def pe(self, fn, r=(), w=()):
        return self.emit(self.PE, fn, r, w)

    def act(self, fn, r=(), w=()):
        return self.emit(self.ACT, fn, r, w)

    def dve(self, fn, r=(), w=()):
        return self.emit(self.DVE, fn, r, w)

    def pool(self, fn, r=(), w=()):
        return self.emit(self.POOL, fn, r, w)

    def dma_sp(self, out, in_, r=(), w=(), **kw):
        return self.emit(self.SP, lambda: self.nc.sync.dma_start(out=out, in_=in_, **kw), r, w, counter=self.lane("sp"))

    def dma_pool(self, out, in_, r=(), w=(), **kw):
        return self.emit(self.POOL, lambda: self.nc.gpsimd.dma_start(out=out, in_=in_, **kw), r, w, counter=self.lane("pool"))

    def barrier(self):
        snap = [(c, c.count) for c in self.counters]
        for iss in self.issuers:
            for c, n in snap:
                self._wait(iss, c, n)

    def final_wait(self):
        snap = [(c, c.count) for c in self.counters]
        for c, n in snap:
            self._wait(self.SP, c, n)


PV_G1 = 0
PV_G2 = 8
PV_CAW = 16
PV_CAB = 78
PV_LNG = 80
PV_LNB = 82
PV_SUB = 84
PV_FW = 85
PV_FB = 217
NPV = 261


def build_program(cfg):
    nc = bass.Bass("TRN2", target_bir_lowering=False)
    kk = K(nc)
    es = kk.es
    L = cfg.depth
    T, SEG, PTOK = cfg.T, cfg.seg, cfg.ptok
    NTB = T // 128

    def din(name, shape, dt=F32):
        return nc.dram_tensor(name, list(shape), dt, kind="ExternalInput").ap()

    def dscr(name, shape, dt):
        return nc.dram_tensor(name, list(shape), dt, kind="Internal").ap()

    def dcc(name, shape, dt):
        return nc.dram_tensor(name, list(shape), dt).ap()

    x_in = din("x_in", [T, D])
    rot_in = din("rot", [T, 80])
    ident_in = din("ident", [128, 128])
    pv_in = din("pv", [128, L * NPV])
    gt_in = din("gt", [L, 1408])
    lam_in = din("lamv", [128, L * 4 * 64])
    sela_in = din("sela", [120, 30])
    selh_in = din("selh", [8, 2])
    w_in_d = din("w_in", [L, D, INW])
    w_out_d = din("w_out", [L, D, D])
    w_up_d = din("w_up", [L, D, 2 * FF])
    w_dn_d = din("w_down", [L, FF, D])
    y_out = nc.dram_tensor("y_out", [T, D], F32, kind="ExternalOutput").ap()

    wb_in = dscr("wb_in", [L, D, INW], BF16)
    wb_out = dscr("wb_out", [L, D, D], BF16)
    wb_up = dscr("wb_up", [L, D, 2 * FF], BF16)
    wb_dn = dscr("wb_dn", [L, FF, D], BF16)
    xm_d = dscr("xm", [T, D], F32)
    xa_d = dscr("xa", [T, D], F32)
    xb_d = dscr("xb", [T, D], F32)
    at_d = dscr("at", [2, 128, T], F32)
    qkt_d = dscr("qkt", [11, 128, T], BF16)
    v_d = dscr("v", [T, 640], BF16)
    cata_d = dscr("cata", [2, 128, T], BF16)
    h2t_d = dscr("h2t", [8, 128, T], BF16)
    kts_l = [dcc(f"kts{u}", [128, PTOK], BF16) for u in range(5)]
    vs_l = [dcc(f"vs{u}", [PTOK, 128], BF16) for u in range(5)]
    ahs_d = dcc("ahs", [30, 256], F32)
    hhs_d = dcc("hhs", [2, D], BF16)
    kta_l = [dcc(f"kta{u}", [RANKS * 128, PTOK], BF16) for u in range(5)]
    va_l = [dcc(f"va{u}", [RANKS * PTOK, 128], BF16) for u in range(5)]
    aha_d = dcc("aha", [RANKS * 30, 256], F32)
    hha_d = dcc("hha", [RANKS * 2, D], BF16)
    RG = [[0, 1, 2, 3], [4, 5, 6, 7]]

    uid = [0]

    def sb(stack, name, shape, dt):
        uid[0] += 1
        return Buf(stack.enter_context(nc.sbuf_tensor(f"sb{uid[0]}_{name}", list(shape), dt)))

    def ps(stack, name, shape, dt):
        uid[0] += 1
        return Buf(stack.enter_context(nc.psum_tensor(f"ps{uid[0]}_{name}", list(shape), dt)))

    ident_f = sb(es, "ident_f", [128, 128], F32)
    ident_b = sb(es, "ident_b", [128, 128], BF16)
    ones_b = sb(es, "ones_b", [128, 128], BF16)
    ones3 = sb(es, "ones3", [128, 192], BF16)
    onesA = sb(es, "onesA", [128, 128], F32)
    onesS = sb(es, "onesS", [128, 128], F32)
    onesF = sb(es, "onesF", [128, 128], F32)
    swapF = sb(es, "swapF", [128, 128], F32)
    epst = sb(es, "epst", [128, 1], F32)
    pv = sb(es, "pv", [128, L * NPV], F32)
    neglam = sb(es, "neglam", [128, L], F32)
    sela = sb(es, "sela", [120, 30], F32)
    selh = sb(es, "selh", [8, 2], BF16)

    def pvc(l, off, n=1):
        return pv[:, l * NPV + off: l * NPV + off + n]

    kk.dma_sp(ident_f[:], ident_in[:, :], w=[ident_f])
    kk.dma_sp(pv[:], pv_in[:, :], w=[pv])
    kk.dma_sp(sela[:], sela_in[:, :], w=[sela])
    kk.dve(lambda: nc.vector.tensor_copy(out=ident_b[:], in_=ident_f[:]), r=[ident_f], w=[ident_b])
    kk.dve(lambda: nc.vector.memset(ones_b[:], 1.0), w=[ones_b])
    kk.dve(lambda: nc.vector.memset(ones3[:], 0.0), w=[ones3])
    kk.dve(lambda: nc.vector.memset(ones3[:, 64:128], 1.0), w=[ones3])
    kk.dve(lambda: nc.vector.memset(onesA[:], 1.0 / 256), w=[onesA])
    kk.dve(lambda: nc.vector.memset(onesS[:], 1.0 / 128), w=[onesS])
    kk.dve(lambda: nc.vector.memset(epst[:], EPS), w=[epst])
    kk.dve(lambda: nc.vector.memset(onesF[:], 1.0), w=[onesF])
    kk.dve(lambda: nc.vector.tensor_copy(out=swapF[:, 0:64], in_=ident_f[:, 64:128]), r=[ident_f], w=[swapF])
    kk.dve(lambda: nc.vector.tensor_copy(out=swapF[:, 64:128], in_=ident_f[:, 0:64]), r=[ident_f], w=[swapF])
    with ExitStack() as st:
        lamv = sb(st, "lamv", [128, L * 4 * 64], F32)
        selh_f = sb(st, "selh_f", [8, 2], F32)
        lp = sb(st, "lp", [128, L * 2 * 64], F32)
        lsum = sb(st, "lsum", [128, L * 2], F32)
        kk.dma_sp(lamv[:], lam_in[:, :], w=[lamv])
        kk.dma_sp(selh_f[:], selh_in[:, :], w=[selh_f])
        kk.dve(lambda: nc.vector.tensor_copy(out=selh[:], in_=selh_f[:]), r=[selh_f], w=[selh])
        lv = lamv[:].rearrange("p (l f d) -> p l f d", l=L, f=4)
        lpv = lp[:].rearrange("p (l f d) -> p l f d", l=L, f=2)
        kk.dve(lambda: nc.vector.tensor_tensor(out=lpv[:, :, 0, :], in0=lv[:, :, 0, :], in1=lv[:, :, 1, :], op=ALU.mult), r=[lamv], w=[lp])
        kk.dve(lambda: nc.vector.tensor_tensor(out=lpv[:, :, 1, :], in0=lv[:, :, 2, :], in1=lv[:, :, 3, :], op=ALU.mult), r=[lamv], w=[lp])
        kk.dve(lambda: nc.vector.tensor_reduce(out=lsum[:], in_=lp[:].rearrange("p (g d) -> p g d", d=64), axis=AX.X, op=ALU.add), r=[lp], w=[lsum])
        kk.act(lambda: nc.scalar.activation(out=lsum[:], in_=lsum[:], func=AF.Exp), r=[lsum], w=[lsum])
        for l in range(L):
            lam_init = 0.8 - 0.6 * math.exp(-0.3 * l)
            kk.dve(lambda l=l, li=lam_init: nc.vector.scalar_tensor_tensor(
                out=neglam[:, l:l + 1], in0=lsum[:, 2 * l + 1:2 * l + 2], scalar=-li, in1=lsum[:, 2 * l:2 * l + 1],
                op0=ALU.add, op1=ALU.subtract), r=[lsum], w=[neglam])
        kk.barrier()

    with ExitStack() as st:
        CW = 2816
        stg = [sb(st, f"wstg{i}", [128, CW], F32) for i in range(3)]
        stb = [sb(st, f"wstb{i}", [128, CW], BF16) for i in range(3)]
        items = []
        for l in range(L):
            for kc in range(8):
                items.append((w_in_d[l, kc * 128:(kc + 1) * 128, :], wb_in[l, kc * 128:(kc + 1) * 128, :], INW, pvc(l, PV_G1 + kc)))
            for kc in range(8):
                items.append((w_out_d[l, kc * 128:(kc + 1) * 128, :], wb_out[l, kc * 128:(kc + 1) * 128, :], D, None))
            for kc in range(8):
                for hh in range(2):
                    items.append((w_up_d[l, kc * 128:(kc + 1) * 128, hh * FF:(hh + 1) * FF],
                                  wb_up[l, kc * 128:(kc + 1) * 128, hh * FF:(hh + 1) * FF], FF, pvc(l, PV_G2 + kc)))
            for kc in range(22):
                items.append((w_dn_d[l, kc * 128:(kc + 1) * 128, :], wb_dn[l, kc * 128:(kc + 1) * 128, :], D, None))
        for i, (src, dst, n, g) in enumerate(items):
            a, b = stg[i % 3], stb[i % 3]
            kk.dma_sp(a[:, 0:n], src, w=[a])
            if i % 2 == 0:
                if g is None:
                    kk.dve(lambda a=a, b=b, n=n: nc.vector.tensor_copy(out=b[:, 0:n], in_=a[:, 0:n]), r=[a], w=[b])
                else:
                    kk.dve(lambda a=a, b=b, n=n, g=g: nc.vector.tensor_scalar(out=b[:, 0:n], in0=a[:, 0:n], scalar1=g, scalar2=None, op0=ALU.mult), r=[a, pv], w=[b])
            else:
                if g is None:
                    kk.act(lambda a=a, b=b, n=n: nc.scalar.copy(out=b[:, 0:n], in_=a[:, 0:n]), r=[a], w=[b])
                else:
                    kk.act(lambda a=a, b=b, n=n, g=g: nc.scalar.activation(out=b[:, 0:n], in_=a[:, 0:n], func=AF.Copy, scale=g), r=[a, pv], w=[b])
            kk.dma_pool(dst, b[:, 0:n], r=[b])
        kk.barrier()

    def rmsnorm_to_T(st_bufs, xsrc, l_unused, hb, hT, psT, junk, ss, lnv):
        kk.dve(lambda: nc.vector.scalar_tensor_tensor(out=junk[:], in0=xsrc[:], scalar=1.0, in1=xsrc[:], op0=ALU.mult, op1=ALU.mult,
                                                      accum_out=ss[:]), r=[xsrc], w=[junk, ss])
        kk.act(lambda: nc.scalar.activation(out=lnv[:], in_=ss[:], func=AF.Ln, bias=epst[:], scale=1.0 / D), r=[ss, epst], w=[lnv])
        kk.act(lambda: nc.scalar.activation(out=lnv[:], in_=lnv[:], func=AF.Exp, scale=-0.5), r=[lnv], w=[lnv])
        kk.act(lambda: nc.scalar.activation(out=hb[:], in_=xsrc[:], func=AF.Copy, scale=lnv[:]), r=[xsrc, lnv], w=[hb])
        for kc in range(8):
            kk.pe(lambda kc=kc: nc.tensor.transpose(out=psT[:, kc * 128:(kc + 1) * 128], in_=hb[:, kc * 128:(kc + 1) * 128], identity=ident_b[:]),
                  r=[hb, ident_b], w=[psT])
        kk.dve(lambda: nc.vector.tensor_copy(out=hT[:], in_=psT[:, 0:1024]), r=[psT], w=[hT])

    prompt_blocks = PTOK // 128

    for l in range(L):
        lam_init = 0.8 - 0.6 * math.exp(-0.3 * l)
        x_src = x_in if l == 0 else (xa_d if l % 2 == 1 else xb_d)
        x_dst = y_out if l == L - 1 else (xa_d if l % 2 == 0 else xb_d)

        with ExitStack() as st:
          if cfg.stop >= 1:
            WIN = sb(st, "WIN", [128, 8 * INW], BF16)
            G = sb(st, "G", [128, 1408], F32)
            xt = [sb(st, f"xt{i}", [128, D], F32) for i in range(3)]
            rot = [sb(st, f"rot{i}", [128, 80], F32) for i in range(3)]
            qk2 = [sb(st, f"qk2{i}", [128, 1408], F32) for i in range(2)]
            vb2 = [sb(st, f"vb2{i}", [128, 640], BF16) for i in range(2)]
            atm2 = [sb(st, f"atm2{i}", [128, 256], F32) for i in range(2)]
            junk = sb(st, "junk", [128, D], BF16)
            ss = sb(st, "ss", [128, 1], F32)
            lnv = sb(st, "lnv", [128, 1], F32)
            hb = sb(st, "hb", [128, D], BF16)
            hT = sb(st, "hT", [128, D], BF16)
            eg = sb(st, "eg", [128, 256], F32)
            a_tm = sb(st, "a_tm", [128, 256], F32)
            aT = sb(st, "aT", [128, 256], F32)
            qk = sb(st, "qk", [128, 1408], F32)
            sq = sb(st, "sq", [128, 1408], F32)
            ssg = sb(st, "ssg", [128, 22], F32)
            rt = [sb(st, f"rt{i}", [128, 192], F32) for i in range(4)]
            qkb = sb(st, "qkb", [128, 1408], BF16)
            qkT = sb(st, "qkT", [128, 1408], BF16)
            vb = sb(st, "vb", [128, 640], BF16)
            psP = [ps(st, f"psP{j}", [128, 512], F32) for j in range(5)]
            psA = ps(st, "psA", [128, 256], F32)
            psQ = ps(st, "psQ", [128, 2048], BF16)
            psT = psQ

            for kc in range(8):
                kk.dma_sp(WIN[:, kc * INW:(kc + 1) * INW], wb_in[l, kc * 128:(kc + 1) * 128, :], w=[WIN])
            kk.dma_sp(G[:], gt_in[l:l + 1, :].to_broadcast([128, 1408]), w=[G])
            kk.dve(lambda: nc.vector.tensor_scalar(out=G[:, 0:512], in0=G[:, 0:512], scalar1=0.125, scalar2=None, op0=ALU.mult), r=[G], w=[G])
            kk.dve(lambda: nc.vector.tensor_scalar(out=G[:, 1024:1280], in0=G[:, 1024:1280], scalar1=0.125, scalar2=None, op0=ALU.mult), r=[G], w=[G])

            def load_blk(tb):
                b = tb % 3
                kk.dma_sp(xt[b][:], x_src[tb * 128:(tb + 1) * 128, :], w=[xt[b]])
                kk.dma_sp(rot[b][:], rot_in[tb * 128:(tb + 1) * 128, :], w=[rot[b]])

            def front_head(tb):
                X = xt[tb % 3]
                rmsnorm_to_T(None, X, l, hb, hT, psT, junk, ss, lnv)
                for j in range(5):
                    for kc in range(8):
                        kk.pe(lambda j=j, kc=kc: nc.tensor.matmul(psP[j][:], lhsT=hT[:, kc * 128:(kc + 1) * 128],
                                                                  rhs=WIN[:, kc * INW + j * 512: kc * INW + (j + 1) * 512],
                                                                  start=(kc == 0), stop=(kc == 7)), r=[hT, WIN], w=[psP[j]])

            def front_tail(tb):
                b = tb % 2
                QK, VBb, ATM = qk2[b], vb2[b], atm2[b]
                kk.act(lambda: nc.scalar.activation(out=eg[:], in_=psP[0][:, 256:512], func=AF.Exp, scale=-1.0), r=[psP[0]], w=[eg])
                kk.act(lambda: nc.scalar.copy(out=QK[:, 0:512], in_=psP[1][:]), r=[psP[1]], w=[QK])
                kk.act(lambda: nc.scalar.copy(out=QK[:, 512:1024], in_=psP[2][:]), r=[psP[2]], w=[QK])
                kk.act(lambda: nc.scalar.copy(out=QK[:, 1024:1408], in_=psP[3][:, 0:384]), r=[psP[3]], w=[QK])
                kk.act(lambda: nc.scalar.copy(out=VBb[:, 0:512], in_=psP[4][:]), r=[psP[4]], w=[VBb])
                kk.act(lambda: nc.scalar.copy(out=VBb[:, 512:640], in_=psP[3][:, 384:512]), r=[psP[3]], w=[VBb])
                kk.dve(lambda: nc.vector.tensor_scalar(out=eg[:], in0=eg[:], scalar1=1.0, scalar2=None, op0=ALU.add), r=[eg], w=[eg])
                kk.dve(lambda: nc.vector.reciprocal(out=eg[:], in_=eg[:]), r=[eg], w=[eg])
                kk.dve(lambda: nc.vector.tensor_tensor(out=ATM[:], in0=psP[0][:, 0:256], in1=eg[:], op=ALU.mult), r=[psP[0], eg], w=[ATM])

            def back(tb):
                b = tb % 2
                t0 = tb * 128
                R = rot[tb % 3]
                qk, vb, a_tm = qk2[b], vb2[b], atm2[b]
                for c in range(2):
                    kk.pe(lambda c=c: nc.tensor.transpose(out=psA[:, c * 128:(c + 1) * 128], in_=a_tm[:, c * 128:(c + 1) * 128], identity=ident_f[:]),
                          r=[a_tm, ident_f], w=[psA])
                kk.act(lambda: nc.scalar.copy(out=aT[:], in_=psA[:]), r=[psA], w=[aT])
                kk.dma_pool(at_d[:, :, t0:t0 + 128].rearrange("c p t -> p c t"), aT[:].rearrange("p (c t) -> p c t", c=2), r=[aT])
                if tb == 0:
                    kk.dma_pool(ahs_d[0:15, :], a_tm[0:15, :], r=[a_tm])
                if tb == prompt_blocks - 1:
                    kk.dma_pool(ahs_d[15:30, :], a_tm[113:128, :], r=[a_tm])

            def back_qk(tb):
                b = tb % 2
                t0 = tb * 128
                R = rot[tb % 3]
                qk, vb, a_tm = qk2[b], vb2[b], atm2[b]
                kk.pool(lambda: nc.gpsimd.tensor_tensor(out=sq[:], in0=qk[:], in1=qk[:], op=ALU.mult), r=[qk], w=[sq])
                kk.dve(lambda: nc.vector.tensor_reduce(out=ssg[:], in_=sq[:].rearrange("p (g d) -> p g d", d=64), axis=AX.X, op=ALU.add), r=[sq], w=[ssg])
                kk.act(lambda: nc.scalar.activation(out=ssg[:], in_=ssg[:], func=AF.Ln, bias=epst[:], scale=1.0 / 64), r=[ssg, epst], w=[ssg])
                kk.act(lambda: nc.scalar.activation(out=ssg[:], in_=ssg[:], func=AF.Exp, scale=-0.5), r=[ssg], w=[ssg])
                qk3 = qk[:].rearrange("p (g d) -> p g d", d=64)
                kk.dve(lambda: nc.vector.tensor_tensor(out=qk3, in0=qk3, in1=ssg[:].unsqueeze(2).to_broadcast([128, 22, 64]), op=ALU.mult), r=[qk, ssg], w=[qk])
                kk.dve(lambda: nc.vector.tensor_tensor(out=qk[:], in0=qk[:], in1=G[:], op=ALU.mult), r=[qk, G], w=[qk])
                qB = qk[:, 0:1024].rearrange("p (g d) -> p g d", d=64)
                x1, x2 = qB[:, :, 0:8], qB[:, :, 8:16]
                cB = R[:, 0:8].unsqueeze(1).to_broadcast([128, 16, 8])
                sB = R[:, 8:16].unsqueeze(1).to_broadcast([128, 16, 8])
                tv = [rt[i][:, 0:128].rearrange("p (g d) -> p g d", d=8) for i in range(4)]
                kk.dve(lambda: nc.vector.tensor_tensor(out=tv[0], in0=x1, in1=cB, op=ALU.mult), r=[qk, R], w=[rt[0]])
                kk.dve(lambda: nc.vector.tensor_tensor(out=tv[1], in0=x2, in1=sB, op=ALU.mult), r=[qk, R], w=[rt[1]])
                kk.dve(lambda: nc.vector.tensor_tensor(out=tv[2], in0=x2, in1=cB, op=ALU.mult), r=[qk, R], w=[rt[2]])
                kk.dve(lambda: nc.vector.tensor_tensor(out=tv[3], in0=x1, in1=sB, op=ALU.mult), r=[qk, R], w=[rt[3]])
                kk.dve(lambda: nc.vector.tensor_tensor(out=x1, in0=tv[0], in1=tv[1], op=ALU.subtract), r=[rt[0], rt[1]], w=[qk])
                kk.dve(lambda: nc.vector.tensor_tensor(out=x2, in0=tv[2], in1=tv[3], op=ALU.add), r=[rt[2], rt[3]], w=[qk])
                qC = qk[:, 1024:1408].rearrange("p (g h x d) -> p g h x d", g=6, h=2, x=2)
                y1, y2 = qC[:, :, :, 0, :], qC[:, :, :, 1, :]
                RC = R[:, 16:80].rearrange("p (h x d) -> p h x d", h=2, x=2)
                cC = RC[:, :, 0, :].unsqueeze(1).to_broadcast([128, 6, 2, 16])
                sC = RC[:, :, 1, :].unsqueeze(1).to_broadcast([128, 6, 2, 16])
                tw = [rt[i][:, 0:192].rearrange("p (g h d) -> p g h d", g=6, h=2) for i in range(4)]
                kk.dve(lambda: nc.vector.tensor_tensor(out=tw[0], in0=y1, in1=cC, op=ALU.mult), r=[qk, R], w=[rt[0]])
                kk.dve(lambda: nc.vector.tensor_tensor(out=tw[1], in0=y2, in1=sC, op=ALU.mult), r=[qk, R], w=[rt[1]])
                kk.dve(lambda: nc.vector.tensor_tensor(out=tw[2], in0=y2, in1=cC, op=ALU.mult), r=[qk, R], w=[rt[2]])
                kk.dve(lambda: nc.vector.tensor_tensor(out=tw[3], in0=y1, in1=sC, op=ALU.mult), r=[qk, R], w=[rt[3]])
                kk.dve(lambda: nc.vector.tensor_tensor(out=y1, in0=tw[0], in1=tw[1], op=ALU.subtract), r=[rt[0], rt[1]], w=[qk])
                kk.dve(lambda: nc.vector.tensor_tensor(out=y2, in0=tw[2], in1=tw[3], op=ALU.add), r=[rt[2], rt[3]], w=[qk])
                kk.act(lambda: nc.scalar.copy(out=qkb[:], in_=qk[:]), r=[qk], w=[qkb])
                for c in range(11):
                    kk.pe(lambda c=c: nc.tensor.transpose(out=psQ[:, c * 128:(c + 1) * 128], in_=qkb[:, c * 128:(c + 1) * 128], identity=ident_b[:]),
                          r=[qkb, ident_b], w=[psQ])
                kk.dve(lambda: nc.vector.tensor_copy(out=qkT[:], in_=psQ[:, 0:1408]), r=[psQ], w=[qkT])
                kk.dma_pool(qkt_d[:, :, t0:t0 + 128].rearrange("c p t -> p c t"), qkT[:].rearrange("p (c t) -> p c t", c=11), r=[qkT])
                kk.dma_pool(v_d[t0:t0 + 128, :], vb[:], r=[vb])
                if tb < prompt_blocks:
                    for u in range(5):
                        kc0 = 512 + u * 128 if u < 4 else 1280
                        kk.dma_pool(kts_l[u][:, t0:t0 + 128], qkT[:, kc0:kc0 + 128], r=[qkT])
                        kk.dma_pool(vs_l[u][t0:t0 + 128, :], vb[:, u * 128:(u + 1) * 128], r=[vb])

            load_blk(0)
            if NTB > 1:
                load_blk(1)
            front_head(0)
            front_tail(0)
            for tb in range(NTB):
                if tb + 2 < NTB:
                    load_blk(tb + 2)
                back(tb)
                if tb + 1 < NTB:
                    front_head(tb + 1)
                back_qk(tb)
                if tb + 1 < NTB:
                    front_tail(tb + 1)
            kk.barrier()

        if cfg.use_cc and cfg.stop >= 2:
            for src, dst in [(ahs_d, aha_d)] + [(kts_l[u], kta_l[u]) for u in range(5)] + [(vs_l[u], va_l[u]) for u in range(5)]:
                kk.emit(kk.POOL, lambda src=src, dst=dst: nc.gpsimd.collective_compute(
                    "AllGather", ALU.bypass, replica_groups=RG, ins=[src.opt()], outs=[dst.opt()]), counter=kk.lane("cc"))

        with ExitStack() as st:
          if cfg.stop >= 3:
            abuf = [sb(st, f"abuf{i}", [128, SEG + 30], F32) for i in range(2)]
            convo = [sb(st, f"convo{i}", [128, SEG], F32) for i in range(2)]
            sqb = [sb(st, f"sqb{i}", [128, 512], F32) for i in range(2)]
            mean_sb = sb(st, "mean_sb", [128, 512], F32)
            m2 = sb(st, "m2", [128, 512], F32)
            rstd = sb(st, "rstd", [128, 512], F32)
            dd = [sb(st, f"dd{i}", [128, 512], F32) for i in range(2)]
            ee = [sb(st, f"ee{i}", [128, 512], F32) for i in range(2)]
            ob = [sb(st, f"ob{i}", [128, 512], BF16) for i in range(2)]
            ahr = sb(st, "ahr", [120, 256], F32)
            ps_mean = ps(st, "ps_mean", [128, 512], F32)
            ps_msq = ps(st, "ps_msq", [128, 512], F32)
            ps_halo = ps(st, "ps_halo", [128, 64], F32)
            nseg = cfg.np + cfg.ns
            for s in list(range(cfg.np, nseg)) + list(range(cfg.np)):
                if s == 0:
                    kk.barrier()
                    if cfg.use_cc:
                        kk.dma_sp(ahr[:], aha_d[:, :], w=[ahr])
                t0 = s * SEG
                is_p = s < cfg.np
                for c in range(2):
                    A = abuf[c]
                    left_local = is_p and s > 0
                    right_local = is_p and s < cfg.np - 1
                    lo = t0 - 15 if left_local else t0
                    hi = t0 + SEG + 15 if right_local else t0 + SEG
                    kk.dma_sp(A[:, 15 + (lo - t0): 15 + (hi - t0)], at_d[c, :, lo:hi], w=[A])
                    if not left_local:
                        kk.pool(lambda A=A: nc.gpsimd.memset(A[:, 0:15], 0.0), w=[A])
                    if not right_local:
                        kk.pool(lambda A=A: nc.gpsimd.memset(A[:, SEG + 15:SEG + 30], 0.0), w=[A])
                    if cfg.use_cc and is_p and (s == 0 or s == cfg.np - 1):
                        kk.pe(lambda c=c: nc.tensor.matmul(ps_halo[:, 0:30], lhsT=ahr[:, c * 128:(c + 1) * 128], rhs=sela[:], start=True, stop=True),
                              r=[ahr, sela], w=[ps_halo])
                        if s == 0:
                            kk.act(lambda A=A: nc.scalar.copy(out=A[:, 0:15], in_=ps_halo[:, 0:15]), r=[ps_halo], w=[A])
                        if s == cfg.np - 1:
                            kk.act(lambda A=A: nc.scalar.copy(out=A[:, SEG + 15:SEG + 30], in_=ps_halo[:, 15:30]), r=[ps_halo], w=[A])
                    CO = convo[c]
                    kk.dve(lambda A=A, CO=CO, c=c: nc.vector.tensor_scalar(out=CO[:], in0=A[:, 0:SEG], scalar1=pvc(l, PV_CAW + c * 31),
                                                                          scalar2=pvc(l, PV_CAB + c), op0=ALU.mult, op1=ALU.add), r=[A, pv], w=[CO])
                    for j in range(1, 31):
                        kk.dve(lambda A=A, CO=CO, c=c, j=j: nc.vector.scalar_tensor_tensor(out=CO[:], in0=A[:, j:j + SEG], scalar=pvc(l, PV_CAW + c * 31 + j),
                                                                                         in1=CO[:], op0=ALU.mult, op1=ALU.add), r=[A, pv, CO], w=[CO])
                for ti in range(SEG // 512):
                    c0 = ti * 512
                    for c in range(2):
                        kk.pool(lambda c=c: nc.gpsimd.tensor_tensor(out=sqb[c][:], in0=convo[c][:, c0:c0 + 512], in1=convo[c][:, c0:c0 + 512], op=ALU.mult),
                                r=[convo[c]], w=[sqb[c]])
                    for c in range(2):
                        kk.pe(lambda c=c: nc.tensor.matmul(ps_mean[:], lhsT=onesA[:], rhs=convo[c][:, c0:c0 + 512], start=(c == 0), stop=(c == 1)),
                              r=[onesA, convo[c]], w=[ps_mean])
                    for c in range(2):
                        kk.pe(lambda c=c: nc.tensor.matmul(ps_msq[:], lhsT=onesA[:], rhs=sqb[c][:], start=(c == 0), stop=(c == 1)),
                              r=[onesA, sqb[c]], w=[ps_msq])
                    kk.act(lambda: nc.scalar.copy(out=mean_sb[:], in_=ps_mean[:]), r=[ps_mean], w=[mean_sb])
                    kk.dve(lambda: nc.vector.tensor_tensor(out=m2[:], in0=mean_sb[:], in1=mean_sb[:], op=ALU.mult), r=[mean_sb], w=[m2])
                    kk.dve(lambda: nc.vector.tensor_tensor(out=m2[:], in0=ps_msq[:], in1=m2[:], op=ALU.subtract), r=[ps_msq, m2], w=[m2])
                    kk.dve(lambda: nc.vector.tensor_scalar(out=m2[:], in0=m2[:], scalar1=0.0, scalar2=None, op0=ALU.max), r=[m2], w=[m2])
                    kk.act(lambda: nc.scalar.activation(out=rstd[:], in_=m2[:], func=AF.Ln, bias=epst[:], scale=1.0), r=[m2, epst], w=[rstd])
                    kk.act(lambda: nc.scalar.activation(out=rstd[:], in_=rstd[:], func=AF.Exp, scale=-0.5), r=[rstd], w=[rstd])
                    for c in range(2):
                        Dd, Ee, Ob = dd[c], ee[c], ob[c]
                        kk.dve(lambda c=c, Dd=Dd: nc.vector.tensor_tensor(out=Dd[:], in0=convo[c][:, c0:c0 + 512], in1=mean_sb[:], op=ALU.subtract),
                               r=[convo[c], mean_sb], w=[Dd])
                        kk.dve(lambda Dd=Dd: nc.vector.tensor_tensor(out=Dd[:], in0=Dd[:], in1=rstd[:], op=ALU.mult), r=[Dd, rstd], w=[Dd])
                        kk.dve(lambda c=c, Dd=Dd: nc.vector.tensor_scalar(out=Dd[:], in0=Dd[:], scalar1=pvc(l, PV_LNG + c), scalar2=pvc(l, PV_LNB + c),
                                                                         op0=ALU.mult, op1=ALU.add), r=[Dd, pv], w=[Dd])
                        kk.act(lambda Dd=Dd, Ee=Ee: nc.scalar.activation(out=Ee[:], in_=Dd[:], func=AF.Exp, scale=-1.0), r=[Dd], w=[Ee])
                        kk.pool(lambda Ee=Ee: nc.gpsimd.tensor_scalar(out=Ee[:], in0=Ee[:], scalar1=1.0, scalar2=1.0, op0=ALU.add, op1=ALU.mult), r=[Ee], w=[Ee])
                        kk.dve(lambda Ee=Ee: nc.vector.reciprocal(out=Ee[:], in_=Ee[:]), r=[Ee], w=[Ee])
                        kk.pool(lambda Dd=Dd, Ee=Ee, Ob=Ob: nc.gpsimd.tensor_tensor(out=Ob[:], in0=Dd[:], in1=Ee[:], op=ALU.mult), r=[Dd, Ee], w=[Ob])
                        kk.dma_pool(cata_d[c, :, t0 + c0:t0 + c0 + 512], Ob[:], r=[Ob])
            kk.barrier()

        with ExitStack() as st:
          if cfg.stop >= 4:
            LKMAX = cfg.sp if cfg.use_cc else max(PTOK, SEG)
            NCKMAX = LKMAX // 128
            GTOK = max(PTOK, SEG)
            KT = sb(st, "KT", [128, LKMAX], BF16)
            VB = sb(st, "VB", [128, NCKMAX * 192], BF16)
            catT = sb(st, "catT", [128, 8 * GTOK], BF16)
            WOUT = sb(st, "WOUT", [128, 8 * D], BF16)
            Qa = [sb(st, f"Qa{i}", [128, 512], BF16) for i in range(2)]
            Qb = [sb(st, f"Qb{i}", [128, 512], BF16) for i in range(2)]
            NPT = 4
            PT = [sb(st, f"PT{i}", [128, 1024], BF16) for i in range(NPT)]
            acc = sb(st, "acc", [128, 1024], F32)
            accp = sb(st, "accp", [128, 1024], F32)
            fo = [sb(st, f"fo{i}", [128, 512], F32) for i in range(2)]
            fl = [sb(st, f"fl{i}", [128, 512], F32) for i in range(2)]
            fd = sb(st, "fd", [128, 512], F32)
            fsq = fl[1]
            frs = fl[0]
            xt = [accp, accp]
            xm = acc
            ss = sb(st, "css", [128, 1], F32)
            lnv = sb(st, "clnv", [128, 1], F32)
            hb = sb(st, "chb", [128, D], BF16)
            hT = sb(st, "chT", [128, D], BF16)
            junk = hb
            sc = [ps(st, f"sc{i}", [128, 1024], F32) for i in range(2)]
            po = [ps(st, f"po{m}", [128, 512], F32) for m in range(2)]
            pf = [ps(st, f"pf{m}", [128, 512], F32) for m in range(2)]

            for kc in range(8):
                kk.dma_sp(WOUT[:, kc * D:(kc + 1) * D], wb_out[l, kc * 128:(kc + 1) * 128, :], w=[WOUT])

            groups = [("p", 0, PTOK)] + [("s", PTOK + i * SEG, SEG) for i in range(cfg.ns)]
            scslot = [0]
            ptslot = [0]
            pend = {}

            def flush_pending():
                if pend.get("sums"):
                    pend["sums"]()
                    pend["sums"] = None
                if pend.get("st23"):
                    pend["st23"]()
                    pend["st23"] = None
                if pend.get("st3"):
                    pend["st3"]()
                    pend["st3"] = None
            for (gk, g0, gn) in groups:
                use_all = (gk == "p" and cfg.use_cc)
                Lk = cfg.sp if use_all else gn
                nck = Lk // 128
                VB3 = VB[:, 0:nck * 192].rearrange("p (c e) -> p c e", e=192)
                for c in range(2):
                    kk.dma_sp(catT[:, c * GTOK: c * GTOK + gn], cata_d[c, :, g0:g0 + gn], w=[catT])
                for u in range(6):
                    isB = u < 4
                    kchunk = u if isB else 4
                    if use_all:
                        for r in range(RANKS):
                            kk.dma_sp(KT[:, r * PTOK:(r + 1) * PTOK], kta_l[kchunk][r * 128:(r + 1) * 128, :], w=[KT])
                    else:
                        kk.dma_sp(KT[:, 0:Lk], qkt_d[4 + u if isB else 10, :, g0:g0 + gn], w=[KT])
                    if isB:
                        vcols = slice(u * 128, (u + 1) * 128)
                        vdst = lambda c0, c1: VB3[:, c0:c1, 0:128]
                    else:
                        g = u - 4
                        vcols = slice(512 + g * 64, 512 + (g + 1) * 64)
                        vdst = lambda c0, c1: VB3[:, c0:c1, 64:128]
                        kk.pool(lambda: nc.gpsimd.memset(VB3[:, :, 0:64], 1.0), w=[VB])
                        kk.pool(lambda: nc.gpsimd.memset(VB3[:, :, 128:192], 1.0), w=[VB])
                    if use_all:
                        vsrc = va_l[kchunk]
                        voff = 0
                        vcols = slice(0, 128) if isB else slice(g * 64, (g + 1) * 64)
                    else:
                        vsrc = v_d
                        voff = g0
                    for c0 in range(0, nck, 16):
                        c1 = min(nck, c0 + 16)
                        kk.dma_sp(vdst(c0, c1), vsrc[voff + c0 * 128: voff + c1 * 128, vcols].rearrange("(c p) e -> p c e", p=128), w=[VB])
                    if isB:
                        qrows = [slice(0, 64), slice(64, 128)]
                    else:
                        qrows = [slice(g * 64, (g + 1) * 64)] * 2
                    if u == 0 or u >= 4:
                        for qi_ in range(2):
                            for m_, QQ in enumerate((Qa[qi_], Qb[qi_])):
                                zr = slice(64, 128) if qrows[m_].start == 0 else slice(0, 64)
                                kk.pool(lambda QQ=QQ, zr=zr: nc.gpsimd.memset(QQ[zr, :], 0.0), w=[QQ])
                    if isB:
                        lhs_v = [lambda ck: VB3[:, ck, 0:128], lambda ck: VB3[:, ck, 0:128]]
                        rows = [slice(0, 64), slice(64, 128)]
                    else:
                        lhs_v = [lambda ck: VB3[:, ck, 64:192], lambda ck: VB3[:, ck, 0:128]]
                        rows = [slice(g * 64, (g + 1) * 64)] * 2
                    for qt in range(gn // 512):
                        q0 = g0 + qt * 512
                        qi = qt % 2
                        if isB:
                            kk.dma_sp(Qa[qi][0:64, :], qkt_d[u, 0:64, q0:q0 + 512], w=[Qa[qi]])
                            kk.dma_sp(Qb[qi][64:128, :], qkt_d[u, 64:128, q0:q0 + 512], w=[Qb[qi]])
                        else:
                            kk.dma_sp(Qa[qi][qrows[0], :], qkt_d[8, qrows[0], q0:q0 + 512], w=[Qa[qi]])
                            kk.dma_sp(Qb[qi][qrows[1], :], qkt_d[9, qrows[1], q0:q0 + 512], w=[Qb[qi]])
                        qbufs = [Qa[qi], Qb[qi]]
                        slots = {}

                        def scores(ck):
                            sl = scslot[0] % 2
                            scslot[0] += 1
                            slots[ck] = sl
                            for m in range(2):
                                kk.pe(lambda m=m, sl=sl, ck=ck: nc.tensor.matmul(sc[sl][:, m * 512:(m + 1) * 512], lhsT=KT[:, ck * 128:(ck + 1) * 128],
                                                                                 rhs=qbufs[m][:, :], start=True, stop=True),
                                      r=[KT, qbufs[m]], w=[sc[sl]])

                        scores(0)
                        for ck in range(nck):
                            if ck + 1 < nck:
                                scores(ck + 1)
                            if ck == min(2, nck - 1) and pend.get("sums"):
                                pend["sums"]()
                                pend["sums"] = None
                            if ck == min(6, nck - 1):
                                if pend.get("sums"):
                                    pend["sums"]()
                                    pend["sums"] = None
                                if pend.get("st23"):
                                    pend["st23"]()
                                    pend["st23"] = None
                            if ck == min(11, nck - 1) and pend.get("st3"):
                                if pend.get("st23"):
                                    pend["st23"]()
                                    pend["st23"] = None
                                pend["st3"]()
                                pend["st3"] = None
                            ssl = slots.pop(ck)
                            sl = ptslot[0] % NPT
                            ptslot[0] += 1
                            kk.act(lambda sl=sl, ssl=ssl: nc.scalar.activation(out=PT[sl][:], in_=sc[ssl][:], func=AF.Exp), r=[sc[ssl]], w=[PT[sl]])
                            for m in range(2):
                                kk.pe(lambda m=m, sl=sl, ck=ck: nc.tensor.matmul(
                                    po[m][:], lhsT=lhs_v[m](ck), rhs=PT[sl][:, m * 512:(m + 1) * 512], start=(ck == 0), stop=(ck == nck - 1)),
                                    r=[VB, PT[sl]], w=[po[m]])
                            if isB:
                                tb16 = accp[:].bitcast(BF16)
                                t01, t23 = tb16[:, 0:1024], tb16[:, 1024:2048]
                                if ck % 4 == 0:
                                    prev_sl = sl
                                elif ck % 4 == 1:
                                    kk.dve(lambda a_=prev_sl, b_=sl: nc.vector.tensor_tensor(out=t01, in0=PT[a_][:], in1=PT[b_][:], op=ALU.add),
                                           r=[PT[prev_sl], PT[sl]], w=[accp])
                                elif ck % 4 == 2:
                                    prev_sl = sl
                                else:
                                    kk.dve(lambda a_=prev_sl, b_=sl: nc.vector.tensor_tensor(out=t23, in0=PT[a_][:], in1=PT[b_][:], op=ALU.add),
                                           r=[PT[prev_sl], PT[sl]], w=[accp])
                                    kk.dve(lambda: nc.vector.tensor_tensor(out=t01, in0=t01, in1=t23, op=ALU.add), r=[accp], w=[accp])
                                    if ck == 3:
                                        kk.dve(lambda: nc.vector.tensor_copy(out=acc[:], in_=t01), r=[accp], w=[acc])
                                    else:
                                        kk.dve(lambda: nc.vector.tensor_tensor(out=acc[:], in0=acc[:], in1=t01, op=ALU.add), r=[accp, acc], w=[acc])
                        cchunk = 2 + u if isB else 6 + (u - 4)
                        dst = catT[:, cchunk * GTOK + qt * 512: cchunk * GTOK + (qt + 1) * 512]
                        kk.act(lambda: nc.scalar.copy(out=fo[0][:], in_=po[0][:]), r=[po[0]], w=[fo[0]])
                        kk.dve(lambda: nc.vector.tensor_copy(out=fo[1][:], in_=po[1][:]), r=[po[1]], w=[fo[1]])
                        if isB:
                            def st_sums():
                                for m in range(2):
                                    kk.pe(lambda m=m: nc.tensor.matmul(pf[m][:], lhsT=onesF[:], rhs=acc[:, m * 512:(m + 1) * 512], start=True, stop=True),
                                          r=[onesF, acc], w=[pf[m]])

                            def st23(dst=dst):
                                for m in range(2):
                                    kk.act(lambda m=m: nc.scalar.activation(out=fl[m][:], in_=pf[m][:], func=AF.Ln), r=[pf[m]], w=[fl[m]])
                                    kk.act(lambda m=m: nc.scalar.activation(out=fl[m][:], in_=fl[m][:], func=AF.Exp, scale=-1.0), r=[fl[m]], w=[fl[m]])
                                    kk.dve(lambda m=m: nc.vector.tensor_tensor(out=fo[m][:], in0=fo[m][:], in1=fl[m][:], op=ALU.mult), r=[fo[m], fl[m]], w=[fo[m]])
                                kk.dve(lambda: nc.vector.scalar_tensor_tensor(out=fd[:], in0=fo[1][:], scalar=neglam[:, l:l + 1], in1=fo[0][:],
                                                                              op0=ALU.mult, op1=ALU.add), r=[fo[0], fo[1], neglam], w=[fd])
                                kk.dve(lambda: nc.vector.tensor_tensor(out=fsq[:], in0=fd[:], in1=fd[:], op=ALU.mult), r=[fd], w=[fsq])

                            def st3(dst=dst):
                                kk.pe(lambda: nc.tensor.matmul(pf[0][:], lhsT=onesS[:], rhs=fsq[:], start=True, stop=True), r=[onesS, fsq], w=[pf[0]])
                                kk.act(lambda: nc.scalar.activation(out=frs[:], in_=pf[0][:], func=AF.Ln, bias=epst[:], scale=1.0), r=[pf[0], epst], w=[frs])
                                kk.act(lambda: nc.scalar.activation(out=frs[:], in_=frs[:], func=AF.Exp, scale=-0.5), r=[frs], w=[frs])
                                kk.dve(lambda: nc.vector.tensor_tensor(out=fd[:], in0=fd[:], in1=frs[:], op=ALU.mult), r=[fd, frs], w=[fd])
                                kk.dve(lambda: nc.vector.tensor_scalar(out=dst, in0=fd[:], scalar1=pvc(l, PV_SUB), scalar2=1.0 - lam_init,
                                                                      op0=ALU.mult, op1=ALU.mult), r=[fd, pv], w=[catT])
                        else:
                            st_sums = None

                            def st23(dst=dst):
                                for m in range(2):
                                    lr = slice(64, 128) if m == 0 else slice(0, 64)
                                    orr = slice(0, 64) if m == 0 else slice(64, 128)
                                    kk.act(lambda m=m, lr=lr: nc.scalar.activation(out=fl[m][lr, :], in_=fo[m][lr, :], func=AF.Ln), r=[fo[m]], w=[fl[m]])
                                    kk.act(lambda m=m, lr=lr: nc.scalar.activation(out=fl[m][lr, :], in_=fl[m][lr, :], func=AF.Exp, scale=-1.0), r=[fl[m]], w=[fl[m]])
                                    kk.pool(lambda m=m, orr=orr: nc.gpsimd.memset(fl[m][orr, :], 0.0), w=[fl[m]])

                            def st3(dst=dst):
                                for m in range(2):
                                    kk.pe(lambda m=m: nc.tensor.matmul(pf[m][:], lhsT=swapF[:], rhs=fl[m][:], start=True, stop=True), r=[swapF, fl[m]], w=[pf[m]])
                                kk.dve(lambda: nc.vector.tensor_tensor(out=dst[0:64, :], in0=fo[0][0:64, :], in1=pf[0][0:64, :], op=ALU.mult),
                                       r=[fo[0], pf[0]], w=[catT])
                                kk.dve(lambda: nc.vector.tensor_tensor(out=dst[64:128, :], in0=fo[1][64:128, :], in1=pf[1][64:128, :], op=ALU.mult),
                                       r=[fo[1], pf[1]], w=[catT])
                        pend["sums"] = st_sums
                        pend["st23"] = st23
                        pend["st3"] = st3
                flush_pending()
                nblk = gn // 128

                def outproj(bi, banks):
                    for jn in range(2):
                        for kc in range(8):
                            kk.pe(lambda jn=jn, kc=kc: nc.tensor.matmul(banks[jn][:], lhsT=catT[:, kc * GTOK + bi * 128: kc * GTOK + (bi + 1) * 128],
                                                                        rhs=WOUT[:, kc * D + jn * 512: kc * D + (jn + 1) * 512],
                                                                        start=(kc == 0), stop=(kc == 7)), r=[catT, WOUT], w=[banks[jn]])

                outproj(0, po)
                for bi in range(nblk):
                    t0 = g0 + bi * 128
                    X = xt[0]
                    banks = po if bi % 2 == 0 else pf
                    if bi + 1 < nblk:
                        outproj(bi + 1, pf if bi % 2 == 0 else po)
                    kk.dma_sp(X[:], x_src[t0:t0 + 128, :], w=[X])
                    for jn in range(2):
                        kk.dve(lambda jn=jn, X=X, banks=banks: nc.vector.tensor_tensor(out=xm[:, jn * 512:(jn + 1) * 512], in0=banks[jn][:],
                                                                                      in1=X[:, jn * 512:(jn + 1) * 512], op=ALU.add),
                               r=[banks[jn], X], w=[xm])
                    kk.dma_pool(xm_d[t0:t0 + 128, :], xm[:], r=[xm])
                    psT = sc[0]
                    psT_bf = psT[:, 0:512].bitcast(BF16)
                    rmsnorm_to_T_c1(kk, nc, xm, hb, hT, psT, psT_bf, junk, ss, lnv, epst, ident_b)
                    kk.dma_pool(h2t_d[:, :, t0:t0 + 128].rearrange("c p t -> p c t"), hT[:].rearrange("p (c t) -> p c t", c=8), r=[hT])
                    if gk == "p" and bi == 0:
                        kk.dma_pool(hhs_d[0:1, :], hb[0:1, :], r=[hb])
                    if gk == "p" and bi == nblk - 1:
                        kk.dma_pool(hhs_d[1:2, :], hb[127:128, :], r=[hb])
            kk.barrier()

        if cfg.use_cc and cfg.stop >= 5:
            kk.emit(kk.POOL, lambda: nc.gpsimd.collective_compute(
                "AllGather", ALU.bypass, replica_groups=RG, ins=[hhs_d.opt()], outs=[hha_d.opt()]), counter=kk.lane("cc"))
            kk.barrier()

        with ExitStack() as st:
          if cfg.stop >= 6:
            WUP = sb(st, "WUP", [128, 8 * 2 * FF], BF16)
            WDN = sb(st, "WDN", [128, 22 * D], BF16)
            h2t = [sb(st, f"h2t{i}", [128, 8 * 512], BF16) for i in range(2)]
            gT = sb(st, "gT", [128, 22 * 512], BF16)
            tvb = [sb(st, f"tv{i}", [128, 512], F32) for i in range(2)]
            tgb = [sb(st, f"tg{i}", [128, 512], F32) for i in range(2)]
            sgb = [sb(st, f"sg{i}", [128, 512], F32) for i in range(2)]
            xmb = [sb(st, f"fxm{i}", [128, D], F32) for i in range(2)]
            xo = [sb(st, f"fxo{i}", [128, D], F32) for i in range(2)]
            hh = sb(st, "hh", [8, D], BF16)
            halo_h = sb(st, "halo_h", [128, 16], BF16)
            psv = [ps(st, f"psv{i}", [128, 512], F32) for i in range(2)]
            psg = [ps(st, f"psg{i}", [128, 512], F32) for i in range(2)]
            psy = [ps(st, f"psy{i}", [128, 512], F32) for i in range(4)]
            for kc in range(8):
                kk.dma_sp(WUP[:, kc * 2 * FF:(kc + 1) * 2 * FF], wb_up[l, kc * 128:(kc + 1) * 128, :], w=[WUP])
            for kc in range(22):
                kk.dma_sp(WDN[:, kc * D:(kc + 1) * D], wb_dn[l, kc * 128:(kc + 1) * 128, :], w=[WDN])
            if cfg.use_cc:
                kk.dma_sp(hh[:], hha_d[:, :], w=[hh])
                for kc in range(8):
                    kk.pe(lambda kc=kc: nc.tensor.matmul(psy[0][:, kc * 2:kc * 2 + 2], lhsT=hh[:, kc * 128:(kc + 1) * 128], rhs=selh[:], start=True, stop=True),
                          r=[hh, selh], w=[psy[0]])
                kk.dve(lambda: nc.vector.tensor_copy(out=halo_h[:], in_=psy[0][:, 0:16]), r=[psy[0]], w=[halo_h])
            else:
                kk.dve(lambda: nc.vector.memset(halo_h[:], 0.0), w=[halo_h])
            halo3 = halo_h[:].rearrange("p (c x) -> p c x", x=2)

            fsegs = [("p", 0, PTOK)] + [("s", PTOK + i * SEG, SEG) for i in range(cfg.ns)]
            tiles = []
            for (gk, g0, gn) in fsegs:
                s0 = 0
                while s0 < gn:
                    n = min(510, gn - s0)
                    tiles.append((gk, g0, gn, s0, n))
                    s0 += n
            pslot = [0]
            yslot = [0]

            def load_tile(i):
                gk, g0, gn, s0, n = tiles[i]
                H = h2t[i % 2]
                H3 = H[:].rearrange("p (c t) -> p c t", c=8)
                lo = s0 - 1
                hi = s0 + n + 1
                clo, chi = max(lo, 0), min(hi, gn)
                kk.dma_sp(H3[:, :, clo - lo: chi - lo], h2t_d[:, :, g0 + clo: g0 + chi].rearrange("c p t -> p c t"), w=[H])
                if lo < 0:
                    if gk == "p":
                        kk.pool(lambda: nc.gpsimd.tensor_copy(out=H3[:, :, 0:1], in_=halo3[:, :, 0:1]), r=[halo_h], w=[H])
                    else:
                        kk.pool(lambda: nc.gpsimd.memset(H3[:, :, 0:1], 0.0), w=[H])
                if hi > gn:
                    if gk == "p":
                        kk.pool(lambda: nc.gpsimd.tensor_copy(out=H3[:, :, n + 1:n + 2], in_=halo3[:, :, 1:2]), r=[halo_h], w=[H])
                    else:
                        kk.pool(lambda: nc.gpsimd.memset(H3[:, :, n + 1:n + 2], 0.0), w=[H])

            load_tile(0)
            for i, (gk, g0, gn, s0, n) in enumerate(tiles):
                if i + 1 < len(tiles):
                    load_tile(i + 1)
                H = h2t[i % 2]
                N = n + 2
                for j in range(22):
                    sl = pslot[0] % 2
                    pslot[0] += 1
                    PV_, PG_ = psv[sl], psg[sl]
                    for (P_, ch) in ((PV_, j), (PG_, 22 + j)):
                        for kc in range(8):
                            kk.pe(lambda P_=P_, ch=ch, kc=kc: nc.tensor.matmul(P_[:, 0:N], lhsT=WUP[:, kc * 2 * FF + ch * 128: kc * 2 * FF + (ch + 1) * 128],
                                                                               rhs=H[:, kc * 512: kc * 512 + N], start=(kc == 0), stop=(kc == 7)),
                                  r=[WUP, H], w=[P_])
                    TV, TG, SG = tvb[sl], tgb[sl], sgb[sl]
                    for (P_, TT, ch) in ((PV_, TV, j), (PG_, TG, 22 + j)):
                        kk.act(lambda P_=P_, TT=TT, ch=ch: nc.scalar.activation(out=TT[:, 0:n], in_=P_[:, 0:n], func=AF.Identity,
                                                                                 scale=pvc(l, PV_FW + ch * 3), bias=pvc(l, PV_FB + ch)), r=[P_, pv], w=[TT])
                        for jj in (1, 2):
                            kk.dve(lambda P_=P_, TT=TT, ch=ch, jj=jj: nc.vector.scalar_tensor_tensor(out=TT[:, 0:n], in0=P_[:, jj:jj + n],
                                                                                                      scalar=pvc(l, PV_FW + ch * 3 + jj), in1=TT[:, 0:n],
                                                                                                      op0=ALU.mult, op1=ALU.add), r=[P_, pv, TT], w=[TT])
                    kk.act(lambda TG=TG, SG=SG: nc.scalar.activation(out=SG[:, 0:n], in_=TG[:, 0:n], func=AF.Silu), r=[TG], w=[SG])
                    kk.pool(lambda TV=TV, SG=SG, j=j: nc.gpsimd.tensor_tensor(out=gT[:, j * 512: j * 512 + n], in0=TV[:, 0:n], in1=SG[:, 0:n], op=ALU.mult),
                            r=[TV, SG], w=[gT])
                b0 = 0
                bi = 0
                while b0 < n:
                    m = min(128, n - b0)
                    tk = g0 + s0 + b0
                    XM, XO = xmb[bi % 2], xo[bi % 2]
                    kk.dma_sp(XM[0:m, :], xm_d[tk:tk + m, :], w=[XM])
                    for jn in range(2):
                        Y = psy[yslot[0] % 4]
                        yslot[0] += 1
                        for j in range(22):
                            kk.pe(lambda Y=Y, j=j, jn=jn, b0=b0, m=m: nc.tensor.matmul(Y[0:m, :], lhsT=gT[:, j * 512 + b0: j * 512 + b0 + m],
                                                                                       rhs=WDN[:, j * D + jn * 512: j * D + (jn + 1) * 512],
                                                                                       start=(j == 0), stop=(j == 21)), r=[gT, WDN], w=[Y])
                        kk.dve(lambda Y=Y, jn=jn, m=m, XM=XM, XO=XO: nc.vector.tensor_tensor(out=XO[0:m, jn * 512:(jn + 1) * 512], in0=Y[0:m, :],
                                                                                            in1=XM[0:m, jn * 512:(jn + 1) * 512], op=ALU.add),
                               r=[Y, XM], w=[XO])
                    kk.dma_pool(x_dst[tk:tk + m, :], XO[0:m, :], r=[XO])
                    b0 += m
                    bi += 1
            kk.barrier()

    kk.final_wait()
    return nc, kk


def rmsnorm_to_T_c1(kk, nc, xsrc, hb, hT, psT, psT_bf, junk, ss, lnv, epst, ident_b):
    kk.dve(lambda: nc.vector.scalar_tensor_tensor(out=junk[:], in0=xsrc[:], scalar=1.0, in1=xsrc[:], op0=ALU.mult, op1=ALU.mult,
                                                  accum_out=ss[:]), r=[xsrc], w=[junk, ss])
    kk.act(lambda: nc.scalar.activation(out=lnv[:], in_=ss[:], func=AF.Ln, bias=epst[:], scale=1.0 / D), r=[ss, epst], w=[lnv])
    kk.act(lambda: nc.scalar.activation(out=lnv[:], in_=lnv[:], func=AF.Exp, scale=-0.5), r=[lnv], w=[lnv])
    kk.act(lambda: nc.scalar.activation(out=hb[:], in_=xsrc[:], func=AF.Copy, scale=lnv[:]), r=[xsrc, lnv], w=[hb])
    for kc in range(8):
        kk.pe(lambda kc=kc: nc.tensor.transpose(out=psT_bf[:, kc * 128:(kc + 1) * 128], in_=hb[:, kc * 128:(kc + 1) * 128], identity=ident_b[:]),
              r=[hb, ident_b], w=[psT])
    kk.dve(lambda: nc.vector.tensor_copy(out=hT[:], in_=psT_bf[:, 0:1024]), r=[psT], w=[hT])


def _perm_w_in():
    a = np.arange(0, 512)
    bq = np.arange(512, 1024)
    bk = np.arange(1024, 1536)
    bv = np.arange(1536, 2048)
    cq = np.arange(2048, 2304).reshape(4, 64)[[0, 2, 1, 3]].reshape(-1)
    ck = np.arange(2304, 2432)
    cv = np.arange(2432, 2560)
    return np.concatenate([a, bq, bk, cq, ck, cv, bv])


def _rot_table(pos):
    pos = pos.astype(np.float32)
    invB = (np.float32(500000.0) ** (-np.arange(0, 16, 2, dtype=np.float32) / np.float32(16))).astype(np.float32)
    invC = (np.float32(10000.0) ** (-np.arange(0, 32, 2, dtype=np.float32) / np.float32(32))).astype(np.float32)
    angB = pos[:, None] * invB[None, :]
    p_i = pos.astype(np.int64)
    row = (p_i // 64).astype(np.float32)
    col = (p_i % 64).astype(np.float32)
    angR = row[:, None] * invC[None, :]
    angC = col[:, None] * invC[None, :]
    out = np.concatenate([np.cos(angB), np.sin(angB), np.cos(angR), np.sin(angR), np.cos(angC), np.sin(angC)], axis=1)
    return np.ascontiguousarray(out.astype(np.float32))


def make_in_maps(cfg, inputs):
    L = cfg.depth
    f = lambda k: np.asarray(inputs[k], dtype=np.float32)
    xp, xs = f("x_prompt"), f("x_sample")
    perm = _perm_w_in()
    w_in = np.ascontiguousarray(f("w_in")[:L][:, :, perm])
    w_out = np.ascontiguousarray(f("w_out")[:L])
    w_up = np.ascontiguousarray(f("w_up")[:L])
    w_dn = np.ascontiguousarray(f("w_down")[:L])
    pv = np.zeros((128, L, NPV), np.float32)
    for l in range(L):
        pv[:, l, PV_G1:PV_G1 + 8] = f("norm1_g")[l].reshape(8, 128).T
        pv[:, l, PV_G2:PV_G2 + 8] = f("norm2_g")[l].reshape(8, 128).T
        caw = f("conv_a_w")[l]
        pv[:, l, PV_CAW:PV_CAW + 62] = caw.reshape(31, 2, 128).transpose(2, 1, 0).reshape(128, 62)
        pv[:, l, PV_CAB:PV_CAB + 2] = f("conv_a_b")[l].reshape(2, 128).T
        pv[:, l, PV_LNG:PV_LNG + 2] = f("ln_a_g")[l].reshape(2, 128).T
        pv[:, l, PV_LNB:PV_LNB + 2] = f("ln_a_b")[l].reshape(2, 128).T
        pv[:, l, PV_SUB] = f("subln_b_g")[l]
        fw = f("conv_f_w")[l]
        pv[:, l, PV_FW:PV_FW + 132] = fw.reshape(3, 44, 128).transpose(2, 1, 0).reshape(128, 132)
        pv[:, l, PV_FB:PV_FB + 44] = f("conv_f_b")[l].reshape(44, 128).T
    pv = np.ascontiguousarray(pv.reshape(128, L * NPV))
    gt = np.zeros((L, 1408), np.float32)
    for l in range(L):
        gt[l] = np.concatenate([np.tile(f("qn_b_g")[l], 8), np.tile(f("kn_b_g")[l], 8), np.tile(f("qn_c_g")[l], 4), np.tile(f("kn_c_g")[l], 2)])
    lam = np.stack([f("lam_q1")[:L], f("lam_k1")[:L], f("lam_q2")[:L], f("lam_k2")[:L]], axis=1)
    lamv = np.ascontiguousarray(np.broadcast_to(lam.reshape(1, L * 4 * 64), (128, L * 4 * 64))).astype(np.float32)
    ident = np.eye(128, dtype=np.float32)
    in_maps = []
    for c in range(NCORES):
        p, r = c // RANKS, c % RANKS
        xpc = xp[p, r * cfg.ptok:(r + 1) * cfg.ptok]
        xsc = xs[c * cfg.ns:(c + 1) * cfg.ns].reshape(cfg.ns * cfg.seg, D)
        x_in = np.ascontiguousarray(np.concatenate([xpc, xsc], axis=0))
        pos = np.concatenate([np.arange(r * cfg.ptok, (r + 1) * cfg.ptok)] + [np.arange(cfg.seg)] * cfg.ns)
        rot = _rot_table(pos)
        sela = np.zeros((120, 30), np.float32)
        selh = np.zeros((8, 2), np.float32)
        if r > 0:
            for j in range(15):
                sela[(r - 1) * 30 + 15 + j, j] = 1.0
            selh[(r - 1) * 2 + 1, 0] = 1.0
        if r < RANKS - 1:
            for j in range(15):
                sela[(r + 1) * 30 + j, 15 + j] = 1.0
            selh[(r + 1) * 2 + 0, 1] = 1.0
        in_maps.append(dict(x_in=x_in, rot=rot, ident=ident, pv=pv, gt=gt, lamv=lamv, sela=sela, selh=selh,
                            w_in=w_in, w_out=w_out, w_up=w_up, w_down=w_dn))
    return in_maps


def assemble(cfg, results, nb_prompt, nb_sample):
    yp = np.zeros((nb_prompt, cfg.sp, D), np.float32)
    ys = np.zeros((nb_sample, cfg.seg, D), np.float32)
    for c in range(NCORES):
        y = np.asarray(results[c]["y_out"], dtype=np.float32).reshape(cfg.T, D)
        p, r = c // RANKS, c % RANKS
        yp[p, r * cfg.ptok:(r + 1) * cfg.ptok] = y[:cfg.ptok]
        ys[c * cfg.ns:(c + 1) * cfg.ns] = y[cfg.ptok:].reshape(cfg.ns, cfg.seg, D)
    return yp, ys


def run(cfg, inputs):
    nc, kk = build_program(cfg)
    in_maps = make_in_maps(cfg, inputs)
    res = run_bass_kernel_spmd(nc, in_maps, core_ids=list(range(NCORES)))
    return assemble(cfg, res.results, 2, NCORES * cfg.ns)


def kernel(**inputs):
    cfg = Cfg()
    return run(cfg, inputs)
```

```python
import math
from contextlib import ExitStack

import numpy as np
import ml_dtypes

import concourse.bass as bass
import concourse.mybir as mybir
from concourse.bass_utils import run_bass_kernel_spmd

F32 = mybir.dt.float32
BF16 = mybir.dt.bfloat16
AF = mybir.ActivationFunctionType
ALU = mybir.AluOpType
AX = mybir.AxisListType

D = 1024
INW = 2560
FF = 2816
EPS = 1e-6
NCORES = 8
RANKS = 4


class Cfg:
    def __init__(self, depth=4, seg=2048, np_=2, ns=4, use_cc=True, stop=99):
        self.stop = stop
        self.depth = depth
        self.seg = seg
        self.np = np_
        self.ns = ns
        self.ptok = np_ * seg
        self.T = (np_ + ns) * seg
        self.sp = RANKS * self.ptok
        self.use_cc = use_cc


EPOCH = 8000


class Counter:
    def __init__(self, kk, name, step):
        self.kk = kk
        self.name = name
        self.step = step
        self.count = 0
        self.sems = []
        self.per = EPOCH // step

    def sem_for(self, c):
        idx = (c - 1) // self.per
        while len(self.sems) <= idx:
            self.sems.append(self.kk.es.enter_context(self.kk.nc.semaphore(f"s_{self.name}{len(self.sems)}")))
        return self.sems[idx], (c - idx * self.per) * self.step


class Issuer:
    def __init__(self, name, h, cnt):
        self.name = name
        self.h = h
        self.cnt = cnt
        self.waited = {}


class Buf:
    def __init__(self, t):
        self.t = t
        self.w = None
        self.r = {}

    def __getitem__(self, idx):
        return self.t[idx]


class K:
    def __init__(self, nc):
        self.nc = nc
        self.es = ExitStack()
        self.counters = []
        mk = lambda n, s: self._mkc(n, s)
        self.PE = Issuer("pe", nc.tensor, mk("pe", 1))
        self.ACT = Issuer("act", nc.scalar, mk("act", 1))
        self.DVE = Issuer("dve", nc.vector, mk("dve", 1))
        self.POOL = Issuer("pool", nc.gpsimd, mk("pool", 1))
        self.SP = Issuer("sp", nc.sync, mk("spc", 1))
        self.issuers = [self.PE, self.ACT, self.DVE, self.POOL, self.SP]
        NL = 16
        self.q_sp = [mk(f"qsp{i}_", 16) for i in range(NL)]
        self.q_pool = [mk(f"qpool{i}_", 16) for i in range(NL)]
        self.q_cc = [mk(f"qcc{i}_", 1) for i in range(4)]
        self.rr = {"sp": 0, "pool": 0, "cc": 0}
        self.ninst = 0

    def lane(self, which):
        lst = {"sp": self.q_sp, "pool": self.q_pool, "cc": self.q_cc}[which]
        c = lst[self.rr[which] % len(lst)]
        self.rr[which] += 1
        return c

    def _mkc(self, n, s):
        c = Counter(self, n, s)
        self.counters.append(c)
        return c

    def _wait(self, iss, c, n):
        if n <= 0:
            return
        if iss.waited.get(c, 0) >= n:
            return
        if c is self.PE.cnt and iss is self.PE:
            return
        sem, val = c.sem_for(n)
        iss.h.wait_ge(sem, val)
        iss.waited[c] = n

    def emit(self, iss, fn, reads=(), writes=(), counter=None):
        c = counter if counter is not None else iss.cnt
        deps = {}

        def add(cn):
            cc, n = cn
            if deps.get(cc, 0) < n:
                deps[cc] = n

        for b in reads:
            if b.w is not None:
                add(b.w)
        for b in writes:
            if b.w is not None:
                add(b.w)
            for cc, n in b.r.items():
                add((cc, n))
        for cc, n in deps.items():
            self._wait(iss, cc, n)
        inst = fn()
        c.count += 1
        n = c.count
        sem, val = c.sem_for(n)
        inst.then_inc(sem, c.step)
        for b in reads:
            if b.r.get(c, 0) < n:
                b.r[c] = n
        for b in writes:
            b.w = (c, n)
            b.r = {}
        self.ninst += 1
        return inst

    def pe(self, fn, r=(), w=()):
        return self.emit(self.PE, fn, r, w)

    def act(self, fn, r=(), w=()):
        return self.emit(self.ACT, fn, r, w)

    def dve(self, fn, r=(), w=()):
        return self.emit(self.DVE, fn, r, w)

    def pool(self, fn, r=(), w=()):
        return self.emit(self.POOL, fn, r, w)

    def dma_sp(self, out, in_, r=(), w=(), **kw):
        return self.emit(self.SP, lambda: self.nc.sync.dma_start(out=out, in_=in_, **kw), r, w, counter=self.lane("sp"))

    def dma_pool(self, out, in_, r=(), w=(), **kw):
        return self.emit(self.POOL, lambda: self.nc.gpsimd.dma_start(out=out, in_=in_, **kw), r, w, counter=self.lane("pool"))

    def barrier(self):
        snap = [(c, c.count) for c in self.counters]
        for iss in self.issuers:
            for c, n in snap:
                self._wait(iss, c, n)

    def final_wait(self):
        snap = [(c, c.count) for c in self.counters]
        for c, n in snap:
            self._wait(self.SP, c, n)


PV_G1 = 0
PV_G2 = 8
PV_CAW = 16
PV_CAB = 78
PV_LNG = 80
PV_LNB = 82
PV_SUB = 84
PV_FW = 85
PV_FB = 217
NPV = 261


def build_program(cfg):
    nc = bass.Bass("TRN2", target_bir_lowering=False)
    kk = K(nc)
    es = kk.es
    L = cfg.depth
    T, SEG, PTOK = cfg.T, cfg.seg, cfg.ptok
    NTB = T // 128

    def din(name, shape, dt=F32):
        return nc.dram_tensor(name, list(shape), dt, kind="ExternalInput").ap()

    def dscr(name, shape, dt):
        return nc.dram_tensor(name, list(shape), dt, kind="Internal").ap()

    def dcc(name, shape, dt):
        return nc.dram_tensor(name, list(shape), dt).ap()

    x_in = din("x_in", [T, D])
    rot_in = din("rot", [T, 80])
    ident_in = din("ident", [128, 128])
    pv_in = din("pv", [128, L * NPV])
    gt_in = din("gt", [L, 1408])
    lam_in = din("lamv", [128, L * 4 * 64])
    sela_in = din("sela", [120, 30])
    selh_in = din("selh", [8, 2])
    w_in_d = din("w_in", [L, D, INW])
    w_out_d = din("w_out", [L, D, D])
    w_up_d = din("w_up", [L, D, 2 * FF])
    w_dn_d = din("w_down", [L, FF, D])
    y_out = nc.dram_tensor("y_out", [T, D], F32, kind="ExternalOutput").ap()

    wb_in = dscr("wb_in", [L, D, INW], BF16)
    wb_out = dscr("wb_out", [L, D, D], BF16)
    wb_up = dscr("wb_up", [L, D, 2 * FF], BF16)
    wb_dn = dscr("wb_dn", [L, FF, D], BF16)
    xm_d = dscr("xm", [T, D], F32)
    xa_d = dscr("xa", [T, D], F32)
    xb_d = dscr("xb", [T, D], F32)
    at_d = dscr("at", [2, 128, T], F32)
    qkt_d = dscr("qkt", [11, 128, T], BF16)
    v_d = dscr("v", [T, 640], BF16)
    cata_d = dscr("cata", [2, 128, T], BF16)
    h2t_d = dscr("h2t", [8, 128, T], BF16)
    kts_l = [dcc(f"kts{u}", [128, PTOK], BF16) for u in range(5)]
    vs_l = [dcc(f"vs{u}", [PTOK, 128], BF16) for u in range(5)]
    ahs_d = dcc("ahs", [30, 256], F32)
    hhs_d = dcc("hhs", [2, D], BF16)
    kta_l = [dcc(f"kta{u}", [RANKS * 128, PTOK], BF16) for u in range(5)]
    va_l = [dcc(f"va{u}", [RANKS * PTOK, 128], BF16) for u in range(5)]
    aha_d = dcc("aha", [RANKS * 30, 256], F32)
    hha_d = dcc("hha", [RANKS * 2, D], BF16)
    RG = [[0, 1, 2, 3], [4, 5, 6, 7]]

    uid = [0]

    def sb(stack, name, shape, dt):
        uid[0] += 1
        return Buf(stack.enter_context(nc.sbuf_tensor(f"sb{uid[0]}_{name}", list(shape), dt)))

    def ps(stack, name, shape, dt):
        uid[0] += 1
        return Buf(stack.enter_context(nc.psum_tensor(f"ps{uid[0]}_{name}", list(shape), dt)))

    ident_f = sb(es, "ident_f", [128, 128], F32)
    ident_b = sb(es, "ident_b", [128, 128], BF16)
    ones_b = sb(es, "ones_b", [128, 128], BF16)
    ones3 = sb(es, "ones3", [128, 192], BF16)
    onesA = sb(es, "onesA", [128, 128], F32)
    onesS = sb(es, "onesS", [128, 128], F32)
    onesF = sb(es, "onesF", [128, 128], F32)
    swapF = sb(es, "swapF", [128, 128], F32)
    epst = sb(es, "epst", [128, 1], F32)
    pv = sb(es, "pv", [128, L * NPV], F32)
    neglam = sb(es, "neglam", [128, L], F32)
    sela = sb(es, "sela", [120, 30], F32)
    selh = sb(es, "selh", [8, 2], BF16)

    def pvc(l, off, n=1):
        return pv[:, l * NPV + off: l * NPV + off + n]

    kk.dma_sp(ident_f[:], ident_in[:, :], w=[ident_f])
    kk.dma_sp(pv[:], pv_in[:, :], w=[pv])
    kk.dma_sp(sela[:], sela_in[:, :], w=[sela])
    kk.dve(lambda: nc.vector.tensor_copy(out=ident_b[:], in_=ident_f[:]), r=[ident_f], w=[ident_b])
    kk.dve(lambda: nc.vector.memset(ones_b[:], 1.0), w=[ones_b])
    kk.dve(lambda: nc.vector.memset(ones3[:], 0.0), w=[ones3])
    kk.dve(lambda: nc.vector.memset(ones3[:, 64:128], 1.0), w=[ones3])
    kk.dve(lambda: nc.vector.memset(onesA[:], 1.0 / 256), w=[onesA])
    kk.dve(lambda: nc.vector.memset(onesS[:], 1.0 / 128), w=[onesS])
    kk.dve(lambda: nc.vector.memset(epst[:], EPS), w=[epst])
    kk.dve(lambda: nc.vector.memset(onesF[:], 1.0), w=[onesF])
    kk.dve(lambda: nc.vector.tensor_copy(out=swapF[:, 0:64], in_=ident_f[:, 64:128]), r=[ident_f], w=[swapF])
    kk.dve(lambda: nc.vector.tensor_copy(out=swapF[:, 64:128], in_=ident_f[:, 0:64]), r=[ident_f], w=[swapF])
    with ExitStack() as st:
        lamv = sb(st, "lamv", [128, L * 4 * 64], F32)
        selh_f = sb(st, "selh_f", [8, 2], F32)
        lp = sb(st, "lp", [128, L * 2 * 64], F32)
        lsum = sb(st, "lsum", [128, L * 2], F32)
        kk.dma_sp(lamv[:], lam_in[:, :], w=[lamv])
        kk.dma_sp(selh_f[:], selh_in[:, :], w=[selh_f])
        kk.dve(lambda: nc.vector.tensor_copy(out=selh[:], in_=selh_f[:]), r=[selh_f], w=[selh])
        lv = lamv[:].rearrange("p (l f d) -> p l f d", l=L, f=4)
        lpv = lp[:].rearrange("p (l f d) -> p l f d", l=L, f=2)
        kk.dve(lambda: nc.vector.tensor_tensor(out=lpv[:, :, 0, :], in0=lv[:, :, 0, :], in1=lv[:, :, 1, :], op=ALU.mult), r=[lamv], w=[lp])
        kk.dve(lambda: nc.vector.tensor_tensor(out=lpv[:, :, 1, :], in0=lv[:, :, 2, :], in1=lv[:, :, 3, :], op=ALU.mult), r=[lamv], w=[lp])
        kk.dve(lambda: nc.vector.tensor_reduce(out=lsum[:], in_=lp[:].rearrange("p (g d) -> p g d", d=64), axis=AX.X, op=ALU.add), r=[lp], w=[lsum])
        kk.act(lambda: nc.scalar.activation(out=lsum[:], in_=lsum[:], func=AF.Exp), r=[lsum], w=[lsum])
        for l in range(L):
            lam_init = 0.8 - 0.6 * math.exp(-0.3 * l)
            kk.dve(lambda l=l, li=lam_init: nc.vector.scalar_tensor_tensor(
                out=neglam[:, l:l + 1], in0=lsum[:, 2 * l + 1:2 * l + 2], scalar=-li, in1=lsum[:, 2 * l:2 * l + 1],
                op0=ALU.add, op1=ALU.subtract), r=[lsum], w=[neglam])
        kk.barrier()

    with ExitStack() as st:
        CW = 2816
        stg = [sb(st, f"wstg{i}", [128, CW], F32) for i in range(3)]
        stb = [sb(st, f"wstb{i}", [128, CW], BF16) for i in range(3)]
        items = []
        for l in range(L):
            for kc in range(8):
                items.append((w_in_d[l, kc * 128:(kc + 1) * 128, :], wb_in[l, kc * 128:(kc + 1) * 128, :], INW, pvc(l, PV_G1 + kc)))
            for kc in range(8):
                items.append((w_out_d[l, kc * 128:(kc + 1) * 128, :], wb_out[l, kc * 128:(kc + 1) * 128, :], D, None))
            for kc in range(8):
                for hh in range(2):
                    items.append((w_up_d[l, kc * 128:(kc + 1) * 128, hh * FF:(hh + 1) * FF],
                                  wb_up[l, kc * 128:(kc + 1) * 128, hh * FF:(hh + 1) * FF], FF, pvc(l, PV_G2 + kc)))
            for kc in range(22):
                items.append((w_dn_d[l, kc * 128:(kc + 1) * 128, :], wb_dn[l, kc * 128:(kc + 1) * 128, :], D, None))
        for i, (src, dst, n, g) in enumerate(items):
            a, b = stg[i % 3], stb[i % 3]
            kk.dma_sp(a[:, 0:n], src, w=[a])
            if i % 2 == 0:
                if g is None:
                    kk.dve(lambda a=a, b=b, n=n: nc.vector.tensor_copy(out=b[:, 0:n], in_=a[:, 0:n]), r=[a], w=[b])
                else:
                    kk.dve(lambda a=a, b=b, n=n, g=g: nc.vector.tensor_scalar(out=b[:, 0:n], in0=a[:, 0:n], scalar1=g, scalar2=None, op0=ALU.mult), r=[a, pv], w=[b])
            else:
                if g is None:
                    kk.act(lambda a=a, b=b, n=n: nc.scalar.copy(out=b[:, 0:n], in_=a[:, 0:n]), r=[a], w=[b])
                else:
                    kk.act(lambda a=a, b=b, n=n, g=g: nc.scalar.activation(out=b[:, 0:n], in_=a[:, 0:n], func=AF.Copy, scale=g), r=[a, pv], w=[b])
            kk.dma_pool(dst, b[:, 0:n], r=[b])
        kk.barrier()

    def rmsnorm_to_T(st_bufs, xsrc, l_unused, hb, hT, psT, junk, ss, lnv):
        kk.dve(lambda: nc.vector.scalar_tensor_tensor(out=junk[:], in0=xsrc[:], scalar=1.0, in1=xsrc[:], op0=ALU.mult, op1=ALU.mult,
                                                      accum_out=ss[:]), r=[xsrc], w=[junk, ss])
        kk.act(lambda: nc.scalar.activation(out=lnv[:], in_=ss[:], func=AF.Ln, bias=epst[:], scale=1.0 / D), r=[ss, epst], w=[lnv])
        kk.act(lambda: nc.scalar.activation(out=lnv[:], in_=lnv[:], func=AF.Exp, scale=-0.5), r=[lnv], w=[lnv])
        kk.act(lambda: nc.scalar.activation(out=hb[:], in_=xsrc[:], func=AF.Copy, scale=lnv[:]), r=[xsrc, lnv], w=[hb])
        for kc in range(8):
            kk.pe(lambda kc=kc: nc.tensor.transpose(out=psT[:, kc * 128:(kc + 1) * 128], in_=hb[:, kc * 128:(kc + 1) * 128], identity=ident_b[:]),
                  r=[hb, ident_b], w=[psT])
        kk.dve(lambda: nc.vector.tensor_copy(out=hT[:], in_=psT[:, 0:1024]), r=[psT], w=[hT])

    prompt_blocks = PTOK // 128

    for l in range(L):
        lam_init = 0.8 - 0.6 * math.exp(-0.3 * l)
        x_src = x_in if l == 0 else (xa_d if l % 2 == 1 else xb_d)
        x_dst = y_out if l == L - 1 else (xa_d if l % 2 == 0 else xb_d)

        with ExitStack() as st:
          if cfg.stop >= 1:
            WIN = sb(st, "WIN", [128, 8 * INW], BF16)
            G = sb(st, "G", [128, 1408], F32)
            xt = [sb(st, f"xt{i}", [128, D], F32) for i in range(3)]
            rot = [sb(st, f"rot{i}", [128, 80], F32) for i in range(3)]
            qk2 = [sb(st, f"qk2{i}", [128, 1408], F32) for i in range(2)]
            vb2 = [sb(st, f"vb2{i}", [128, 640], BF16) for i in range(2)]
            atm2 = [sb(st, f"atm2{i}", [128, 256], F32) for i in range(2)]
            junk = sb(st, "junk", [128, D], BF16)
            ss = sb(st, "ss", [128, 1], F32)
            lnv = sb(st, "lnv", [128, 1], F32)
            hb = sb(st, "hb", [128, D], BF16)
            hT = sb(st, "hT", [128, D], BF16)
            eg = sb(st, "eg", [128, 256], F32)
            a_tm = sb(st, "a_tm", [128, 256], F32)
            aT = sb(st, "aT", [128, 256], F32)
            qk = sb(st, "qk", [128, 1408], F32)
            sq = sb(st, "sq", [128, 1408], F32)
            ssg = sb(st, "ssg", [128, 22], F32)
            rt = [sb(st, f"rt{i}", [128, 192], F32) for i in range(4)]
            qkb = sb(st, "qkb", [128, 1408], BF16)
            qkT = sb(st, "qkT", [128, 1408], BF16)
            vb = sb(st, "vb", [128, 640], BF16)
            psP = [ps(st, f"psP{j}", [128, 512], F32) for j in range(5)]
            psA = ps(st, "psA", [128, 256], F32)
            psQ = ps(st, "psQ", [128, 2048], BF16)
            psT = psQ

            for kc in range(8):
                kk.dma_sp(WIN[:, kc * INW:(kc + 1) * INW], wb_in[l, kc * 128:(kc + 1) * 128, :], w=[WIN])
            kk.dma_sp(G[:], gt_in[l:l + 1, :].to_broadcast([128, 1408]), w=[G])
            kk.dve(lambda: nc.vector.tensor_scalar(out=G[:, 0:512], in0=G[:, 0:512], scalar1=0.125, scalar2=None, op0=ALU.mult), r=[G], w=[G])
            kk.dve(lambda: nc.vector.tensor_scalar(out=G[:, 1024:1280], in0=G[:, 1024:1280], scalar1=0.125, scalar2=None, op0=ALU.mult), r=[G], w=[G])

            def load_blk(tb):
                b = tb % 3
                kk.dma_sp(xt[b][:], x_src[tb * 128:(tb + 1) * 128, :], w=[xt[b]])
                kk.dma_sp(rot[b][:], rot_in[tb * 128:(tb + 1) * 128, :], w=[rot[b]])

            def front_head(tb):
                X = xt[tb % 3]
                rmsnorm_to_T(None, X, l, hb, hT, psT, junk, ss, lnv)
                for j in range(5):
                    for kc in range(8):
                        kk.pe(lambda j=j, kc=kc: nc.tensor.matmul(psP[j][:], lhsT=hT[:, kc * 128:(kc + 1) * 128],
                                                                  rhs=WIN[:, kc * INW + j * 512: kc * INW + (j + 1) * 512],
                                                                  start=(kc == 0), stop=(kc == 7)), r=[hT, WIN], w=[psP[j]])

            def front_tail(tb):
                b = tb % 2
                QK, VBb, ATM = qk2[b], vb2[b], atm2[b]
                kk.act(lambda: nc.scalar.activation(out=eg[:], in_=psP[0][:, 256:512], func=AF.Exp, scale=-1.0), r=[psP[0]], w=[eg])
                kk.act(lambda: nc.scalar.copy(out=QK[:, 0:512], in_=psP[1][:]), r=[psP[1]], w=[QK])
                kk.act(lambda: nc.scalar.copy(out=QK[:, 512:1024], in_=psP[2][:]), r=[psP[2]], w=[QK])
                kk.act(lambda: nc.scalar.copy(out=QK[:, 1024:1408], in_=psP[3][:, 0:384]), r=[psP[3]], w=[QK])
                kk.act(lambda: nc.scalar.copy(out=VBb[:, 0:512], in_=psP[4][:]), r=[psP[4]], w=[VBb])
                kk.act(lambda: nc.scalar.copy(out=VBb[:, 512:640], in_=psP[3][:, 384:512]), r=[psP[3]], w=[VBb])
                kk.dve(lambda: nc.vector.tensor_scalar(out=eg[:], in0=eg[:], scalar1=1.0, scalar2=None, op0=ALU.add), r=[eg], w=[eg])
                kk.dve(lambda: nc.vector.reciprocal(out=eg[:], in_=eg[:]), r=[eg], w=[eg])
                kk.dve(lambda: nc.vector.tensor_tensor(out=ATM[:], in0=psP[0][:, 0:256], in1=eg[:], op=ALU.mult), r=[psP[0], eg], w=[ATM])

            def back(tb):
                b = tb % 2
                t0 = tb * 128
                R = rot[tb % 3]
                qk, vb, a_tm = qk2[b], vb2[b], atm2[b]
                for c in range(2):
                    kk.pe(lambda c=c: nc.tensor.transpose(out=psA[:, c * 128:(c + 1) * 128], in_=a_tm[:, c * 128:(c + 1) * 128], identity=ident_f[:]),
                          r=[a_tm, ident_f], w=[psA])
                kk.act(lambda: nc.scalar.copy(out=aT[:], in_=psA[:]), r=[psA], w=[aT])
                kk.dma_pool(at_d[:, :, t0:t0 + 128].rearrange("c p t -> p c t"), aT[:].rearrange("p (c t) -> p c t", c=2), r=[aT])
                if tb == 0:
                    kk.dma_pool(ahs_d[0:15, :], a_tm[0:15, :], r=[a_tm])
                if tb == prompt_blocks - 1:
                    kk.dma_pool(ahs_d[15:30, :], a_tm[113:128, :], r=[a_tm])

            def back_qk(tb):
                b = tb % 2
                t0 = tb * 128
                R = rot[tb % 3]
                qk, vb, a_tm = qk2[b], vb2[b], atm2[b]
                kk.pool(lambda: nc.gpsimd.tensor_tensor(out=sq[:], in0=qk[:], in1=qk[:], op=ALU.mult), r=[qk], w=[sq])
                kk.dve(lambda: nc.vector.tensor_reduce(out=ssg[:], in_=sq[:].rearrange("p (g d) -> p g d", d=64), axis=AX.X, op=ALU.add), r=[sq], w=[ssg])
                kk.act(lambda: nc.scalar.activation(out=ssg[:], in_=ssg[:], func=AF.Ln, bias=epst[:], scale=1.0 / 64), r=[ssg, epst], w=[ssg])
                kk.act(lambda: nc.scalar.activation(out=ssg[:], in_=ssg[:], func=AF.Exp, scale=-0.5), r=[ssg], w=[ssg])
                qk3 = qk[:].rearrange("p (g d) -> p g d", d=64)
                kk.dve(lambda: nc.vector.tensor_tensor(out=qk3, in0=qk3, in1=ssg[:].unsqueeze(2).to_broadcast([128, 22, 64]), op=ALU.mult), r=[qk, ssg], w=[qk])
                kk.dve(lambda: nc.vector.tensor_tensor(out=qk[:], in0=qk[:], in1=G[:], op=ALU.mult), r=[qk, G], w=[qk])
                qB = qk[:, 0:1024].rearrange("p (g d) -> p g d", d=64)
                x1, x2 = qB[:, :, 0:8], qB[:, :, 8:16]
                cB = R[:, 0:8].unsqueeze(1).to_broadcast([128, 16, 8])
                sB = R[:, 8:16].unsqueeze(1).to_broadcast([128, 16, 8])
                tv = [rt[i][:, 0:128].rearrange("p (g d) -> p g d", d=8) for i in range(4)]
                kk.dve(lambda: nc.vector.tensor_tensor(out=tv[0], in0=x1, in1=cB, op=ALU.mult), r=[qk, R], w=[rt[0]])
                kk.dve(lambda: nc.vector.tensor_tensor(out=tv[1], in0=x2, in1=sB, op=ALU.mult), r=[qk, R], w=[rt[1]])
                kk.dve(lambda: nc.vector.tensor_tensor(out=tv[2], in0=x2, in1=cB, op=ALU.mult), r=[qk, R], w=[rt[2]])
                kk.dve(lambda: nc.vector.tensor_tensor(out=tv[3], in0=x1, in1=sB, op=ALU.mult), r=[qk, R], w=[rt[3]])
                kk.dve(lambda: nc.vector.tensor_tensor(out=x1, in0=tv[0], in1=tv[1], op=ALU.subtract), r=[rt[0], rt[1]], w=[qk])
                kk.dve(lambda: nc.vector.tensor_tensor(out=x2, in0=tv[2], in1=tv[3], op=ALU.add), r=[rt[2], rt[3]], w=[qk])
                qC = qk[:, 1024:1408].rearrange("p (g h x d) -> p g h x d", g=6, h=2, x=2)
                y1, y2 = qC[:, :, :, 0, :], qC[:, :, :, 1, :]
                RC = R[:, 16:80].rearrange("p (h x d) -> p h x d", h=2, x=2)
                cC = RC[:, :, 0, :].unsqueeze(1).to_broadcast([128, 6, 2, 16])
                sC = RC[:, :, 1, :].unsqueeze(1).to_broadcast([128, 6, 2, 16])
                tw = [rt[i][:, 0:192].rearrange("p (g h d) -> p g h d", g=6, h=2) for i in range(4)]
                kk.dve(lambda: nc.vector.tensor_tensor(out=tw[0], in0=y1, in1=cC, op=ALU.mult), r=[qk, R], w=[rt[0]])
                kk.dve(lambda: nc.vector.tensor_tensor(out=tw[1], in0=y2, in1=sC, op=ALU.mult), r=[qk, R], w=[rt[1]])
                kk.dve(lambda: nc.vector.tensor_tensor(out=tw[2], in0=y2, in1=cC, op=ALU.mult), r=[qk, R], w=[rt[2]])
                kk.dve(lambda: nc.vector.tensor_tensor(out=tw[3], in0=y1, in1=sC, op=ALU.mult), r=[qk, R], w=[rt[3]])
                kk.dve(lambda: nc.vector.tensor_tensor(out=y1, in0=tw[0], in1=tw[1], op=ALU.subtract), r=[rt[0], rt[1]], w=[qk])
                kk.dve(lambda: nc.vector.tensor_tensor(out=y2, in0=tw[2], in1=tw[3], op=ALU.add), r=[rt[2], rt[3]], w=[qk])
                kk.act(lambda: nc.scalar.copy(out=qkb[:], in_=qk[:]), r=[qk], w=[qkb])
                for c in range(11):
                    kk.pe(lambda c=c: nc.tensor.transpose(out=psQ[:, c * 128:(c + 1) * 128], in_=qkb[:, c * 128:(c + 1) * 128], identity=ident_b[:]),
                          r=[qkb, ident_b], w=[psQ])
                kk.dve(lambda: nc.vector.tensor_copy(out=qkT[:], in_=psQ[:, 0:1408]), r=[psQ], w=[qkT])
                kk.dma_pool(qkt_d[:, :, t0:t0 + 128].rearrange("c p t -> p c t"), qkT[:].rearrange("p (c t) -> p c t", c=11), r=[qkT])
                kk.dma_pool(v_d[t0:t0 + 128, :], vb[:], r=[vb])
                if tb < prompt_blocks:
                    for u in range(5):
                        kc0 = 512 + u * 128 if u < 4 else 1280
                        kk.dma_pool(kts_l[u][:, t0:t0 + 128], qkT[:, kc0:kc0 + 128], r=[qkT])
                        kk.dma_pool(vs_l[u][t0:t0 + 128, :], vb[:, u * 128:(u + 1) * 128], r=[vb])

            load_blk(0)
            if NTB > 1:
                load_blk(1)
            front_head(0)
            front_tail(0)
            for tb in range(NTB):
                if tb + 2 < NTB:
                    load_blk(tb + 2)
                back(tb)
                if tb + 1 < NTB:
                    front_head(tb + 1)
                back_qk(tb)
                if tb + 1 < NTB:
                    front_tail(tb + 1)
            kk.barrier()

        if cfg.use_cc and cfg.stop >= 2:
            for src, dst in [(ahs_d, aha_d)] + [(kts_l[u], kta_l[u]) for u in range(5)] + [(vs_l[u], va_l[u]) for u in range(5)]:
                kk.emit(kk.POOL, lambda src=src, dst=dst: nc.gpsimd.collective_compute(
                    "AllGather", ALU.bypass, replica_groups=RG, ins=[src.opt()], outs=[dst.opt()]), counter=kk.lane("cc"))

        with ExitStack() as st:
          if cfg.stop >= 3:
            abuf = [sb(st, f"abuf{i}", [128, SEG + 30], F32) for i in range(2)]
            convo = [sb(st, f"convo{i}", [128, SEG], F32) for i in range(2)]
            sqb = [sb(st, f"sqb{i}", [128, 512], F32) for i in range(2)]
            mean_sb = sb(st, "mean_sb", [128, 512], F32)
            m2 = sb(st, "m2", [128, 512], F32)
            rstd = sb(st, "rstd", [128, 512], F32)
            dd = [sb(st, f"dd{i}", [128, 512], F32) for i in range(2)]
            ee = [sb(st, f"ee{i}", [128, 512], F32) for i in range(2)]
            ob = [sb(st, f"ob{i}", [128, 512], BF16) for i in range(2)]
            ahr = sb(st, "ahr", [120, 256], F32)
            ps_mean = ps(st, "ps_mean", [128, 512], F32)
            ps_msq = ps(st, "ps_msq", [128, 512], F32)
            ps_halo = ps(st, "ps_halo", [128, 64], F32)
            nseg = cfg.np + cfg.ns
            for s in list(range(cfg.np, nseg)) + list(range(cfg.np)):
                if s == 0:
                    kk.barrier()
                    if cfg.use_cc:
                        kk.dma_sp(ahr[:], aha_d[:, :], w=[ahr])
                t0 = s * SEG
                is_p = s < cfg.np
                for c in range(2):
                    A = abuf[c]
                    left_local = is_p and s > 0
                    right_local = is_p and s < cfg.np - 1
                    lo = t0 - 15 if left_local else t0
                    hi = t0 + SEG + 15 if right_local else t0 + SEG
                    kk.dma_sp(A[:, 15 + (lo - t0): 15 + (hi - t0)], at_d[c, :, lo:hi], w=[A])
                    if not left_local:
                        kk.pool(lambda A=A: nc.gpsimd.memset(A[:, 0:15], 0.0), w=[A])
                    if not right_local:
                        kk.pool(lambda A=A: nc.gpsimd.memset(A[:, SEG + 15:SEG + 30], 0.0), w=[A])
                    if cfg.use_cc and is_p and (s == 0 or s == cfg.np - 1):
                        kk.pe(lambda c=c: nc.tensor.matmul(ps_halo[:, 0:30], lhsT=ahr[:, c * 128:(c + 1) * 128], rhs=sela[:], start=True, stop=True),
                              r=[ahr, sela], w=[ps_halo])
                        if s == 0:
                            kk.act(lambda A=A: nc.scalar.copy(out=A[:, 0:15], in_=ps_halo[:, 0:15]), r=[ps_halo], w=[A])
                        if s == cfg.np - 1:
                            kk.act(lambda A=A: nc.scalar.copy(out=A[:, SEG + 15:SEG + 30], in_=ps_halo[:, 15:30]), r=[ps_halo], w=[A])
                    CO = convo[c]
                    kk.dve(lambda A=A, CO=CO, c=c: nc.vector.tensor_scalar(out=CO[:], in0=A[:, 0:SEG], scalar1=pvc(l, PV_CAW + c * 31),
                                                                          scalar2=pvc(l, PV_CAB + c), op0=ALU.mult, op1=ALU.add), r=[A, pv], w=[CO])
                    for j in range(1, 31):
                        kk.dve(lambda A=A, CO=CO, c=c, j=j: nc.vector.scalar_tensor_tensor(out=CO[:], in0=A[:, j:j + SEG], scalar=pvc(l, PV_CAW + c * 31 + j),
                                                                                         in1=CO[:], op0=ALU.mult, op1=ALU.add), r=[A, pv, CO], w=[CO])
                for ti in range(SEG // 512):
                    c0 = ti * 512
                    for c in range(2):
                        kk.pool(lambda c=c: nc.gpsimd.tensor_tensor(out=sqb[c][:], in0=convo[c][:, c0:c0 + 512], in1=convo[c][:, c0:c0 + 512], op=ALU.mult),
                                r=[convo[c]], w=[sqb[c]])
                    for c in range(2):
                        kk.pe(lambda c=c: nc.tensor.matmul(ps_mean[:], lhsT=onesA[:], rhs=convo[c][:, c0:c0 + 512], start=(c == 0), stop=(c == 1)),
                              r=[onesA, convo[c]], w=[ps_mean])
                    for c in range(2):
                        kk.pe(lambda c=c: nc.tensor.matmul(ps_msq[:], lhsT=onesA[:], rhs=sqb[c][:], start=(c == 0), stop=(c == 1)),
                              r=[onesA, sqb[c]], w=[ps_msq])
                    kk.act(lambda: nc.scalar.copy(out=mean_sb[:], in_=ps_mean[:]), r=[ps_mean], w=[mean_sb])
                    kk.dve(lambda: nc.vector.tensor_tensor(out=m2[:], in0=mean_sb[:], in1=mean_sb[:], op=ALU.mult), r=[mean_sb], w=[m2])
                    kk.dve(lambda: nc.vector.tensor_tensor(out=m2[:], in0=ps_msq[:], in1=m2[:], op=ALU.subtract), r=[ps_msq, m2], w=[m2])
                    kk.dve(lambda: nc.vector.tensor_scalar(out=m2[:], in0=m2[:], scalar1=0.0, scalar2=None, op0=ALU.max), r=[m2], w=[m2])
                    kk.act(lambda: nc.scalar.activation(out=rstd[:], in_=m2[:], func=AF.Ln, bias=epst[:], scale=1.0), r=[m2, epst], w=[rstd])
                    kk.act(lambda: nc.scalar.activation(out=rstd[:], in_=rstd[:], func=AF.Exp, scale=-0.5), r=[rstd], w=[rstd])
                    for c in range(2):
                        Dd, Ee, Ob = dd[c], ee[c], ob[c]
                        kk.dve(lambda c=c, Dd=Dd: nc.vector.tensor_tensor(out=Dd[:], in0=convo[c][:, c0:c0 + 512], in1=mean_sb[:], op=ALU.subtract),
                               r=[convo[c], mean_sb], w=[Dd])
                        kk.dve(lambda Dd=Dd: nc.vector.tensor_tensor(out=Dd[:], in0=Dd[:], in1=rstd[:], op=ALU.mult), r=[Dd, rstd], w=[Dd])
                        kk.dve(lambda c=c, Dd=Dd: nc.vector.tensor_scalar(out=Dd[:], in0=Dd[:], scalar1=pvc(l, PV_LNG + c), scalar2=pvc(l, PV_LNB + c),
                                                                         op0=ALU.mult, op1=ALU.add), r=[Dd, pv], w=[Dd])
                        kk.act(lambda Dd=Dd, Ee=Ee: nc.scalar.activation(out=Ee[:], in_=Dd[:], func=AF.Exp, scale=-1.0), r=[Dd], w=[Ee])
                        kk.pool(lambda Ee=Ee: nc.gpsimd.tensor_scalar(out=Ee[:], in0=Ee[:], scalar1=1.0, scalar2=1.0, op0=ALU.add, op1=ALU.mult), r=[Ee], w=[Ee])
                        kk.dve(lambda Ee=Ee: nc.vector.reciprocal(out=Ee[:], in_=Ee[:]), r=[Ee], w=[Ee])
                        kk.pool(lambda Dd=Dd, Ee=Ee, Ob=Ob: nc.gpsimd.tensor_tensor(out=Ob[:], in0=Dd[:], in1=Ee[:], op=ALU.mult), r=[Dd, Ee], w=[Ob])
                        kk.dma_pool(cata_d[c, :, t0 + c0:t0 + c0 + 512], Ob[:], r=[Ob])
            kk.barrier()

        with ExitStack() as st:
          if cfg.stop >= 4:
            LKMAX = cfg.sp if cfg.use_cc else max(PTOK, SEG)
            NCKMAX = LKMAX // 128
            GTOK = max(PTOK, SEG)
            KT = sb(st, "KT", [128, LKMAX], BF16)
            VB = sb(st, "VB", [128, NCKMAX * 192], BF16)
            catT = sb(st, "catT", [128, 8 * GTOK], BF16)
            WOUT = sb(st, "WOUT", [128, 8 * D], BF16)
            Qa = [sb(st, f"Qa{i}", [128, 512], BF16) for i in range(2)]
            Qb = [sb(st, f"Qb{i}", [128, 512], BF16) for i in range(2)]
            NPT = 4
            PT = [sb(st, f"PT{i}", [128, 1024], BF16) for i in range(NPT)]
            acc = sb(st, "acc", [128, 1024], F32)
            accp = sb(st, "accp", [128, 1024], F32)
            fo = [sb(st, f"fo{i}", [128, 512], F32) for i in range(2)]
            fl = [sb(st, f"fl{i}", [128, 512], F32) for i in range(2)]
            fd = sb(st, "fd", [128, 512], F32)
            fsq = fl[1]
            frs = fl[0]
            xt = [accp, accp]
            xm = acc
            ss = sb(st, "css", [128, 1], F32)
            lnv = sb(st, "clnv", [128, 1], F32)
            hb = sb(st, "chb", [128, D], BF16)
            hT = sb(st, "chT", [128, D], BF16)
            junk = hb
            sc = [ps(st, f"sc{i}", [128, 1024], F32) for i in range(2)]
            po = [ps(st, f"po{m}", [128, 512], F32) for m in range(2)]
            pf = [ps(st, f"pf{m}", [128, 512], F32) for m in range(2)]

            for kc in range(8):
                kk.dma_sp(WOUT[:, kc * D:(kc + 1) * D], wb_out[l, kc * 128:(kc + 1) * 128, :], w=[WOUT])

            groups = [("p", 0, PTOK)] + [("s", PTOK + i * SEG, SEG) for i in range(cfg.ns)]
            scslot = [0]
            ptslot = [0]
            pend = {}

            def flush_pending():
                if pend.get("sums"):
                    pend["sums"]()
                    pend["sums"] = None
                if pend.get("st23"):
                    pend["st23"]()
                    pend["st23"] = None
                if pend.get("st3"):
                    pend["st3"]()
                    pend["st3"] = None
            for (gk, g0, gn) in groups:
                use_all = (gk == "p" and cfg.use_cc)
                Lk = cfg.sp if use_all else gn
                nck = Lk // 128
                VB3 = VB[:, 0:nck * 192].rearrange("p (c e) -> p c e", e=192)
                for c in range(2):
                    kk.dma_sp(catT[:, c * GTOK: c * GTOK + gn], cata_d[c, :, g0:g0 + gn], w=[catT])
                for u in range(6):
                    isB = u < 4
                    kchunk = u if isB else 4
                    if use_all:
                        for r in range(RANKS):
                            kk.dma_sp(KT[:, r * PTOK:(r + 1) * PTOK], kta_l[kchunk][r * 128:(r + 1) * 128, :], w=[KT])
                    else:
                        kk.dma_sp(KT[:, 0:Lk], qkt_d[4 + u if isB else 10, :, g0:g0 + gn], w=[KT])
                    if isB:
                        vcols = slice(u * 128, (u + 1) * 128)
                        vdst = lambda c0, c1: VB3[:, c0:c1, 0:128]
                    else:
                        g = u - 4
                        vcols = slice(512 + g * 64, 512 + (g + 1) * 64)
                        vdst = lambda c0, c1: VB3[:, c0:c1, 64:128]
                        kk.pool(lambda: nc.gpsimd.memset(VB3[:, :, 0:64], 1.0), w=[VB])
                        kk.pool(lambda: nc.gpsimd.memset(VB3[:, :, 128:192], 1.0), w=[VB])
                    if use_all:
                        vsrc = va_l[kchunk]
                        voff = 0
                        vcols = slice(0, 128) if isB else slice(g * 64, (g + 1) * 64)
                    else:
                        vsrc = v_d
                        voff = g0
                    for c0 in range(0, nck, 16):
                        c1 = min(nck, c0 + 16)
                        kk.dma_sp(vdst(c0, c1), vsrc[voff + c0 * 128: voff + c1 * 128, vcols].rearrange("(c p) e -> p c e", p=128), w=[VB])
                    if isB:
                        qrows = [slice(0, 64), slice(64, 128)]
                    else:
                        qrows = [slice(g * 64, (g + 1) * 64)] * 2
                    if u == 0 or u >= 4:
                        for qi_ in range(2):
                            for m_, QQ in enumerate((Qa[qi_], Qb[qi_])):
                                zr = slice(64, 128) if qrows[m_].start == 0 else slice(0, 64)
                                kk.pool(lambda QQ=QQ, zr=zr: nc.gpsimd.memset(QQ[zr, :], 0.0), w=[QQ])
                    if isB:
                        lhs_v = [lambda ck: VB3[:, ck, 0:128], lambda ck: VB3[:, ck, 0:128]]
                        rows = [slice(0, 64), slice(64, 128)]
                    else:
                        lhs_v = [lambda ck: VB3[:, ck, 64:192], lambda ck: VB3[:, ck, 0:128]]
                        rows = [slice(g * 64, (g + 1) * 64)] * 2
                    for qt in range(gn // 512):
                        q0 = g0 + qt * 512
                        qi = qt % 2
                        if isB:
                            kk.dma_sp(Qa[qi][0:64, :], qkt_d[u, 0:64, q0:q0 + 512], w=[Qa[qi]])
                            kk.dma_sp(Qb[qi][64:128, :], qkt_d[u, 64:128, q0:q0 + 512], w=[Qb[qi]])
                        else:
                            kk.dma_sp(Qa[qi][qrows[0], :], qkt_d[8, qrows[0], q0:q0 + 512], w=[Qa[qi]])
                            kk.dma_sp(Qb[qi][qrows[1], :], qkt_d[9, qrows[1], q0:q0 + 512], w=[Qb[qi]])
                        qbufs = [Qa[qi], Qb[qi]]
                        slots = {}

                        def scores(ck):
                            sl = scslot[0] % 2
                            scslot[0] += 1
                            slots[ck] = sl
                            for m in range(2):
                                kk.pe(lambda m=m, sl=sl, ck=ck: nc.tensor.matmul(sc[sl][:, m * 512:(m + 1) * 512], lhsT=KT[:, ck * 128:(ck + 1) * 128],
                                                                                 rhs=qbufs[m][:, :], start=True, stop=True),
                                      r=[KT, qbufs[m]], w=[sc[sl]])

                        scores(0)
                        for ck in range(nck):
                            if ck + 1 < nck:
                                scores(ck + 1)
                            if ck == min(2, nck - 1) and pend.get("sums"):
                                pend["sums"]()
                                pend["sums"] = None
                            if ck == min(6, nck - 1):
                                if pend.get("sums"):
                                    pend["sums"]()
                                    pend["sums"] = None
                                if pend.get("st23"):
                                    pend["st23"]()
                                    pend["st23"] = None
                            if ck == min(11, nck - 1) and pend.get("st3"):
                                if pend.get("st23"):
                                    pend["st23"]()
                                    pend["st23"] = None
                                pend["st3"]()
                                pend["st3"] = None
                            ssl = slots.pop(ck)
                            sl = ptslot[0] % NPT
                            ptslot[0] += 1
                            kk.act(lambda sl=sl, ssl=ssl: nc.scalar.activation(out=PT[sl][:], in_=sc[ssl][:], func=AF.Exp), r=[sc[ssl]], w=[PT[sl]])
                            def emit_pv(ckk, sll):
                                for m in range(2):
                                    kk.pe(lambda m=m: nc.tensor.matmul(
                                        po[m][:], lhsT=lhs_v[m](ckk), rhs=PT[sll][:, m * 512:(m + 1) * 512], start=(ckk == 0), stop=(ckk == nck - 1)),
                                        r=[VB, PT[sll]], w=[po[m]])
                            if ck >= 1:
                                emit_pv(ck - 1, pv_prev_sl)
                            pv_prev_sl = sl
                            if ck == nck - 1:
                                emit_pv(ck, sl)
                            if isB:
                                tb16 = accp[:].bitcast(BF16)
                                t01, t23 = tb16[:, 0:1024], tb16[:, 1024:2048]
                                if ck % 4 == 0:
                                    prev_sl = sl
                                elif ck % 4 == 1:
                                    kk.dve(lambda a_=prev_sl, b_=sl: nc.vector.tensor_tensor(out=t01, in0=PT[a_][:], in1=PT[b_][:], op=ALU.add),
                                           r=[PT[prev_sl], PT[sl]], w=[accp])
                                elif ck % 4 == 2:
                                    prev_sl = sl
                                else:
                                    kk.dve(lambda a_=prev_sl, b_=sl: nc.vector.tensor_tensor(out=t23, in0=PT[a_][:], in1=PT[b_][:], op=ALU.add),
                                           r=[PT[prev_sl], PT[sl]], w=[accp])
                                    kk.dve(lambda: nc.vector.tensor_tensor(out=t01, in0=t01, in1=t23, op=ALU.add), r=[accp], w=[accp])
                                    if ck == 3:
                                        kk.dve(lambda: nc.vector.tensor_copy(out=acc[:], in_=t01), r=[accp], w=[acc])
                                    else:
                                        kk.dve(lambda: nc.vector.tensor_tensor(out=acc[:], in0=acc[:], in1=t01, op=ALU.add), r=[accp, acc], w=[acc])
                        cchunk = 2 + u if isB else 6 + (u - 4)
                        dst = catT[:, cchunk * GTOK + qt * 512: cchunk * GTOK + (qt + 1) * 512]
                        kk.act(lambda: nc.scalar.copy(out=fo[0][:], in_=po[0][:]), r=[po[0]], w=[fo[0]])
                        kk.dve(lambda: nc.vector.tensor_copy(out=fo[1][:], in_=po[1][:]), r=[po[1]], w=[fo[1]])
                        if isB:
                            def st_sums():
                                for m in range(2):
                                    kk.pe(lambda m=m: nc.tensor.matmul(pf[m][:], lhsT=onesF[:], rhs=acc[:, m * 512:(m + 1) * 512], start=True, stop=True),
                                          r=[onesF, acc], w=[pf[m]])

                            def st23(dst=dst):
                                for m in range(2):
                                    kk.act(lambda m=m: nc.scalar.activation(out=fl[m][:], in_=pf[m][:], func=AF.Ln), r=[pf[m]], w=[fl[m]])
                                    kk.act(lambda m=m: nc.scalar.activation(out=fl[m][:], in_=fl[m][:], func=AF.Exp, scale=-1.0), r=[fl[m]], w=[fl[m]])
                                    kk.dve(lambda m=m: nc.vector.tensor_tensor(out=fo[m][:], in0=fo[m][:], in1=fl[m][:], op=ALU.mult), r=[fo[m], fl[m]], w=[fo[m]])
                                kk.dve(lambda: nc.vector.scalar_tensor_tensor(out=fd[:], in0=fo[1][:], scalar=neglam[:, l:l + 1], in1=fo[0][:],
                                                                              op0=ALU.mult, op1=ALU.add), r=[fo[0], fo[1], neglam], w=[fd])
                                kk.dve(lambda: nc.vector.tensor_tensor(out=fsq[:], in0=fd[:], in1=fd[:], op=ALU.mult), r=[fd], w=[fsq])

                            def st3(dst=dst):
                                kk.pe(lambda: nc.tensor.matmul(pf[0][:], lhsT=onesS[:], rhs=fsq[:], start=True, stop=True), r=[onesS, fsq], w=[pf[0]])
                                kk.act(lambda: nc.scalar.activation(out=frs[:], in_=pf[0][:], func=AF.Ln, bias=epst[:], scale=1.0), r=[pf[0], epst], w=[frs])
                                kk.act(lambda: nc.scalar.activation(out=frs[:], in_=frs[:], func=AF.Exp, scale=-0.5), r=[frs], w=[frs])
                                kk.dve(lambda: nc.vector.tensor_tensor(out=fd[:], in0=fd[:], in1=frs[:], op=ALU.mult), r=[fd, frs], w=[fd])
                                kk.dve(lambda: nc.vector.tensor_scalar(out=dst, in0=fd[:], scalar1=pvc(l, PV_SUB), scalar2=1.0 - lam_init,
                                                                      op0=ALU.mult, op1=ALU.mult), r=[fd, pv], w=[catT])
                        else:
                            st_sums = None

                            def st23(dst=dst):
                                for m in range(2):
                                    lr = slice(64, 128) if m == 0 else slice(0, 64)
                                    orr = slice(0, 64) if m == 0 else slice(64, 128)
                                    kk.act(lambda m=m, lr=lr: nc.scalar.activation(out=fl[m][lr, :], in_=fo[m][lr, :], func=AF.Ln), r=[fo[m]], w=[fl[m]])
                                    kk.act(lambda m=m, lr=lr: nc.scalar.activation(out=fl[m][lr, :], in_=fl[m][lr, :], func=AF.Exp, scale=-1.0), r=[fl[m]], w=[fl[m]])
                                    kk.pool(lambda m=m, orr=orr: nc.gpsimd.memset(fl[m][orr, :], 0.0), w=[fl[m]])

                            def st3(dst=dst):
                                for m in range(2):
                                    kk.pe(lambda m=m: nc.tensor.matmul(pf[m][:], lhsT=swapF[:], rhs=fl[m][:], start=True, stop=True), r=[swapF, fl[m]], w=[pf[m]])
                                kk.dve(lambda: nc.vector.tensor_tensor(out=dst[0:64, :], in0=fo[0][0:64, :], in1=pf[0][0:64, :], op=ALU.mult),
                                       r=[fo[0], pf[0]], w=[catT])
                                kk.dve(lambda: nc.vector.tensor_tensor(out=dst[64:128, :], in0=fo[1][64:128, :], in1=pf[1][64:128, :], op=ALU.mult),
                                       r=[fo[1], pf[1]], w=[catT])
                        pend["sums"] = st_sums
                        pend["st23"] = st23
                        pend["st3"] = st3
                flush_pending()
                nblk = gn // 128

                def outproj(bi, banks):
                    for jn in range(2):
                        for kc in range(8):
                            kk.pe(lambda jn=jn, kc=kc: nc.tensor.matmul(banks[jn][:], lhsT=catT[:, kc * GTOK + bi * 128: kc * GTOK + (bi + 1) * 128],
                                                                        rhs=WOUT[:, kc * D + jn * 512: kc * D + (jn + 1) * 512],
                                                                        start=(kc == 0), stop=(kc == 7)), r=[catT, WOUT], w=[banks[jn]])

                outproj(0, po)
                for bi in range(nblk):
                    t0 = g0 + bi * 128
                    X = xt[0]
                    banks = po if bi % 2 == 0 else pf
                    if bi + 1 < nblk:
                        outproj(bi + 1, pf if bi % 2 == 0 else po)
                    kk.dma_sp(X[:], x_src[t0:t0 + 128, :], w=[X])
                    for jn in range(2):
                        kk.dve(lambda jn=jn, X=X, banks=banks: nc.vector.tensor_tensor(out=xm[:, jn * 512:(jn + 1) * 512], in0=banks[jn][:],
                                                                                      in1=X[:, jn * 512:(jn + 1) * 512], op=ALU.add),
                               r=[banks[jn], X], w=[xm])
                    kk.dma_pool(xm_d[t0:t0 + 128, :], xm[:], r=[xm])
                    psT = sc[0]
                    psT_bf = psT[:, 0:512].bitcast(BF16)
                    rmsnorm_to_T_c1(kk, nc, xm, hb, hT, psT, psT_bf, junk, ss, lnv, epst, ident_b)
                    kk.dma_pool(h2t_d[:, :, t0:t0 + 128].rearrange("c p t -> p c t"), hT[:].rearrange("p (c t) -> p c t", c=8), r=[hT])
                    if gk == "p" and bi == 0:
                        kk.dma_pool(hhs_d[0:1, :], hb[0:1, :], r=[hb])
                    if gk == "p" and bi == nblk - 1:
                        kk.dma_pool(hhs_d[1:2, :], hb[127:128, :], r=[hb])
            kk.barrier()

        if cfg.use_cc and cfg.stop >= 5:
            kk.emit(kk.POOL, lambda: nc.gpsimd.collective_compute(
                "AllGather", ALU.bypass, replica_groups=RG, ins=[hhs_d.opt()], outs=[hha_d.opt()]), counter=kk.lane("cc"))
            kk.barrier()

        with ExitStack() as st:
          if cfg.stop >= 6:
            WUP = sb(st, "WUP", [128, 8 * 2 * FF], BF16)
            WDN = sb(st, "WDN", [128, 22 * D], BF16)
            h2t = [sb(st, f"h2t{i}", [128, 8 * 512], BF16) for i in range(2)]
            gT = sb(st, "gT", [128, 22 * 512], BF16)
            tvb = [sb(st, f"tv{i}", [128, 512], F32) for i in range(2)]
            tgb = [sb(st, f"tg{i}", [128, 512], F32) for i in range(2)]
            sgb = [sb(st, f"sg{i}", [128, 512], F32) for i in range(2)]
            xmb = [sb(st, f"fxm{i}", [128, D], F32) for i in range(2)]
            xo = [sb(st, f"fxo{i}", [128, D], F32) for i in range(2)]
            hh = sb(st, "hh", [8, D], BF16)
            halo_h = sb(st, "halo_h", [128, 16], BF16)
            psv = [ps(st, f"psv{i}", [128, 512], F32) for i in range(2)]
            psg = [ps(st, f"psg{i}", [128, 512], F32) for i in range(2)]
            psy = [ps(st, f"psy{i}", [128, 512], F32) for i in range(4)]
            for kc in range(8):
                kk.dma_sp(WUP[:, kc * 2 * FF:(kc + 1) * 2 * FF], wb_up[l, kc * 128:(kc + 1) * 128, :], w=[WUP])
            for kc in range(22):
                kk.dma_sp(WDN[:, kc * D:(kc + 1) * D], wb_dn[l, kc * 128:(kc + 1) * 128, :], w=[WDN])
            if cfg.use_cc:
                kk.dma_sp(hh[:], hha_d[:, :], w=[hh])
                for kc in range(8):
                    kk.pe(lambda kc=kc: nc.tensor.matmul(psy[0][:, kc * 2:kc * 2 + 2], lhsT=hh[:, kc * 128:(kc + 1) * 128], rhs=selh[:], start=True, stop=True),
                          r=[hh, selh], w=[psy[0]])
                kk.dve(lambda: nc.vector.tensor_copy(out=halo_h[:], in_=psy[0][:, 0:16]), r=[psy[0]], w=[halo_h])
            else:
                kk.dve(lambda: nc.vector.memset(halo_h[:], 0.0), w=[halo_h])
            halo3 = halo_h[:].rearrange("p (c x) -> p c x", x=2)

            fsegs = [("p", 0, PTOK)] + [("s", PTOK + i * SEG, SEG) for i in range(cfg.ns)]
            tiles = []
            for (gk, g0, gn) in fsegs:
                s0 = 0
                while s0 < gn:
                    n = min(510, gn - s0)
                    tiles.append((gk, g0, gn, s0, n))
                    s0 += n
            pslot = [0]
            yslot = [0]

            def load_tile(i):
                gk, g0, gn, s0, n = tiles[i]
                H = h2t[i % 2]
                H3 = H[:].rearrange("p (c t) -> p c t", c=8)
                lo = s0 - 1
                hi = s0 + n + 1
                clo, chi = max(lo, 0), min(hi, gn)
                kk.dma_sp(H3[:, :, clo - lo: chi - lo], h2t_d[:, :, g0 + clo: g0 + chi].rearrange("c p t -> p c t"), w=[H])
                if lo < 0:
                    if gk == "p":
                        kk.pool(lambda: nc.gpsimd.tensor_copy(out=H3[:, :, 0:1], in_=halo3[:, :, 0:1]), r=[halo_h], w=[H])
                    else:
                        kk.pool(lambda: nc.gpsimd.memset(H3[:, :, 0:1], 0.0), w=[H])
                if hi > gn:
                    if gk == "p":
                        kk.pool(lambda: nc.gpsimd.tensor_copy(out=H3[:, :, n + 1:n + 2], in_=halo3[:, :, 1:2]), r=[halo_h], w=[H])
                    else:
                        kk.pool(lambda: nc.gpsimd.memset(H3[:, :, n + 1:n + 2], 0.0), w=[H])

            load_tile(0)
            for i, (gk, g0, gn, s0, n) in enumerate(tiles):
                if i + 1 < len(tiles):
                    load_tile(i + 1)
                H = h2t[i % 2]
                N = n + 2
                for j in range(22):
                    sl = pslot[0] % 2
                    pslot[0] += 1
                    PV_, PG_ = psv[sl], psg[sl]
                    for (P_, ch) in ((PV_, j), (PG_, 22 + j)):
                        for kc in range(8):
                            kk.pe(lambda P_=P_, ch=ch, kc=kc: nc.tensor.matmul(P_[:, 0:N], lhsT=WUP[:, kc * 2 * FF + ch * 128: kc * 2 * FF + (ch + 1) * 128],
                                                                               rhs=H[:, kc * 512: kc * 512 + N], start=(kc == 0), stop=(kc == 7)),
                                  r=[WUP, H], w=[P_])
                    TV, TG, SG = tvb[sl], tgb[sl], sgb[sl]
                    for (P_, TT, ch) in ((PV_, TV, j), (PG_, TG, 22 + j)):
                        kk.act(lambda P_=P_, TT=TT, ch=ch: nc.scalar.activation(out=TT[:, 0:n], in_=P_[:, 0:n], func=AF.Identity,
                                                                                 scale=pvc(l, PV_FW + ch * 3), bias=pvc(l, PV_FB + ch)), r=[P_, pv], w=[TT])
                        for jj in (1, 2):
                            kk.dve(lambda P_=P_, TT=TT, ch=ch, jj=jj: nc.vector.scalar_tensor_tensor(out=TT[:, 0:n], in0=P_[:, jj:jj + n],
                                                                                                      scalar=pvc(l, PV_FW + ch * 3 + jj), in1=TT[:, 0:n],
                                                                                                      op0=ALU.mult, op1=ALU.add), r=[P_, pv, TT], w=[TT])
                    kk.act(lambda TG=TG, SG=SG: nc.scalar.activation(out=SG[:, 0:n], in_=TG[:, 0:n], func=AF.Silu), r=[TG], w=[SG])
                    kk.pool(lambda TV=TV, SG=SG, j=j: nc.gpsimd.tensor_tensor(out=gT[:, j * 512: j * 512 + n], in0=TV[:, 0:n], in1=SG[:, 0:n], op=ALU.mult),
                            r=[TV, SG], w=[gT])
                b0 = 0
                bi = 0
                while b0 < n:
                    m = min(128, n - b0)
                    tk = g0 + s0 + b0
                    XM, XO = xmb[bi % 2], xo[bi % 2]
                    kk.dma_sp(XM[0:m, :], xm_d[tk:tk + m, :], w=[XM])
                    for jn in range(2):
                        Y = psy[yslot[0] % 4]
                        yslot[0] += 1
                        for j in range(22):
                            kk.pe(lambda Y=Y, j=j, jn=jn, b0=b0, m=m: nc.tensor.matmul(Y[0:m, :], lhsT=gT[:, j * 512 + b0: j * 512 + b0 + m],
                                                                                       rhs=WDN[:, j * D + jn * 512: j * D + (jn + 1) * 512],
                                                                                       start=(j == 0), stop=(j == 21)), r=[gT, WDN], w=[Y])
                        kk.dve(lambda Y=Y, jn=jn, m=m, XM=XM, XO=XO: nc.vector.tensor_tensor(out=XO[0:m, jn * 512:(jn + 1) * 512], in0=Y[0:m, :],
                                                                                            in1=XM[0:m, jn * 512:(jn + 1) * 512], op=ALU.add),
                               r=[Y, XM], w=[XO])
                    kk.dma_pool(x_dst[tk:tk + m, :], XO[0:m, :], r=[XO])
                    b0 += m
                    bi += 1
            kk.barrier()

    kk.final_wait()
    return nc, kk


def rmsnorm_to_T_c1(kk, nc, xsrc, hb, hT, psT, psT_bf, junk, ss, lnv, epst, ident_b):
    kk.dve(lambda: nc.vector.scalar_tensor_tensor(out=junk[:], in0=xsrc[:], scalar=1.0, in1=xsrc[:], op0=ALU.mult, op1=ALU.mult,
                                                  accum_out=ss[:]), r=[xsrc], w=[junk, ss])
    kk.act(lambda: nc.scalar.activation(out=lnv[:], in_=ss[:], func=AF.Ln, bias=epst[:], scale=1.0 / D), r=[ss, epst], w=[lnv])
    kk.act(lambda: nc.scalar.activation(out=lnv[:], in_=lnv[:], func=AF.Exp, scale=-0.5), r=[lnv], w=[lnv])
    kk.act(lambda: nc.scalar.activation(out=hb[:], in_=xsrc[:], func=AF.Copy, scale=lnv[:]), r=[xsrc, lnv], w=[hb])
    for kc in range(8):
        kk.pe(lambda kc=kc: nc.tensor.transpose(out=psT_bf[:, kc * 128:(kc + 1) * 128], in_=hb[:, kc * 128:(kc + 1) * 128], identity=ident_b[:]),
              r=[hb, ident_b], w=[psT])
    kk.dve(lambda: nc.vector.tensor_copy(out=hT[:], in_=psT_bf[:, 0:1024]), r=[psT], w=[hT])


def _perm_w_in():
    a = np.arange(0, 512)
    bq = np.arange(512, 1024)
    bk = np.arange(1024, 1536)
    bv = np.arange(1536, 2048)
    cq = np.arange(2048, 2304).reshape(4, 64)[[0, 2, 1, 3]].reshape(-1)
    ck = np.arange(2304, 2432)
    cv = np.arange(2432, 2560)
    return np.concatenate([a, bq, bk, cq, ck, cv, bv])


def _rot_table(pos):
    pos = pos.astype(np.float32)
    invB = (np.float32(500000.0) ** (-np.arange(0, 16, 2, dtype=np.float32) / np.float32(16))).astype(np.float32)
    invC = (np.float32(10000.0) ** (-np.arange(0, 32, 2, dtype=np.float32) / np.float32(32))).astype(np.float32)
    angB = pos[:, None] * invB[None, :]
    p_i = pos.astype(np.int64)
    row = (p_i // 64).astype(np.float32)
    col = (p_i % 64).astype(np.float32)
    angR = row[:, None] * invC[None, :]
    angC = col[:, None] * invC[None, :]
    out = np.concatenate([np.cos(angB), np.sin(angB), np.cos(angR), np.sin(angR), np.cos(angC), np.sin(angC)], axis=1)
    return np.ascontiguousarray(out.astype(np.float32))


def make_in_maps(cfg, inputs):
    L = cfg.depth
    f = lambda k: np.asarray(inputs[k], dtype=np.float32)
    xp, xs = f("x_prompt"), f("x_sample")
    perm = _perm_w_in()
    w_in = np.ascontiguousarray(f("w_in")[:L][:, :, perm])
    w_out = np.ascontiguousarray(f("w_out")[:L])
    w_up = np.ascontiguousarray(f("w_up")[:L])
    w_dn = np.ascontiguousarray(f("w_down")[:L])
    pv = np.zeros((128, L, NPV), np.float32)
    for l in range(L):
        pv[:, l, PV_G1:PV_G1 + 8] = f("norm1_g")[l].reshape(8, 128).T
        pv[:, l, PV_G2:PV_G2 + 8] = f("norm2_g")[l].reshape(8, 128).T
        caw = f("conv_a_w")[l]
        pv[:, l, PV_CAW:PV_CAW + 62] = caw.reshape(31, 2, 128).transpose(2, 1, 0).reshape(128, 62)
        pv[:, l, PV_CAB:PV_CAB + 2] = f("conv_a_b")[l].reshape(2, 128).T
        pv[:, l, PV_LNG:PV_LNG + 2] = f("ln_a_g")[l].reshape(2, 128).T
        pv[:, l, PV_LNB:PV_LNB + 2] = f("ln_a_b")[l].reshape(2, 128).T
        pv[:, l, PV_SUB] = f("subln_b_g")[l]
        fw = f("conv_f_w")[l]
        pv[:, l, PV_FW:PV_FW + 132] = fw.reshape(3, 44, 128).transpose(2, 1, 0).reshape(128, 132)
        pv[:, l, PV_FB:PV_FB + 44] = f("conv_f_b")[l].reshape(44, 128).T
    pv = np.ascontiguousarray(pv.reshape(128, L * NPV))
    gt = np.zeros((L, 1408), np.float32)
    for l in range(L):
        gt[l] = np.concatenate([np.tile(f("qn_b_g")[l], 8), np.tile(f("kn_b_g")[l], 8), np.tile(f("qn_c_g")[l], 4), np.tile(f("kn_c_g")[l], 2)])
    lam = np.stack([f("lam_q1")[:L], f("lam_k1")[:L], f("lam_q2")[:L], f("lam_k2")[:L]], axis=1)
    lamv = np.ascontiguousarray(np.broadcast_to(lam.reshape(1, L * 4 * 64), (128, L * 4 * 64))).astype(np.float32)
    ident = np.eye(128, dtype=np.float32)
    in_maps = []
    for c in range(NCORES):
        p, r = c // RANKS, c % RANKS
        xpc = xp[p, r * cfg.ptok:(r + 1) * cfg.ptok]
        xsc = xs[c * cfg.ns:(c + 1) * cfg.ns].reshape(cfg.ns * cfg.seg, D)
        x_in = np.ascontiguousarray(np.concatenate([xpc, xsc], axis=0))
        pos = np.concatenate([np.arange(r * cfg.ptok, (r + 1) * cfg.ptok)] + [np.arange(cfg.seg)] * cfg.ns)
        rot = _rot_table(pos)
        sela = np.zeros((120, 30), np.float32)
        selh = np.zeros((8, 2), np.float32)
        if r > 0:
            for j in range(15):
                sela[(r - 1) * 30 + 15 + j, j] = 1.0
            selh[(r - 1) * 2 + 1, 0] = 1.0
        if r < RANKS - 1:
            for j in range(15):
                sela[(r + 1) * 30 + j, 15 + j] = 1.0
            selh[(r + 1) * 2 + 0, 1] = 1.0
        in_maps.append(dict(x_in=x_in, rot=rot, ident=ident, pv=pv, gt=gt, lamv=lamv, sela=sela, selh=selh,
                            w_in=w_in, w_out=w_out, w_up=w_up, w_down=w_dn))
    return in_maps


def assemble(cfg, results, nb_prompt, nb_sample):
    yp = np.zeros((nb_prompt, cfg.sp, D), np.float32)
    ys = np.zeros((nb_sample, cfg.seg, D), np.float32)
    for c in range(NCORES):
        y = np.asarray(results[c]["y_out"], dtype=np.float32).reshape(cfg.T, D)
        p, r = c // RANKS, c % RANKS
        yp[p, r * cfg.ptok:(r + 1) * cfg.ptok] = y[:cfg.ptok]
        ys[c * cfg.ns:(c + 1) * cfg.ns] = y[cfg.ptok:].reshape(cfg.ns, cfg.seg, D)
    return yp, ys


def run(cfg, inputs):
    nc, kk = build_program(cfg)
    in_maps = make_in_maps(cfg, inputs)
    res = run_bass_kernel_spmd(nc, in_maps, core_ids=list(range(NCORES)))
    return assemble(cfg, res.results, 2, NCORES * cfg.ns)


def kernel(**inputs):
    cfg = Cfg()
    return run(cfg, inputs)
```

```python
import math
from contextlib import ExitStack

import numpy as np
import ml_dtypes

import concourse.bass as bass
import concourse.mybir as mybir
from concourse.bass_utils import run_bass_kernel_spmd

F32 = mybir.dt.float32
BF16 = mybir.dt.bfloat16
AF = mybir.ActivationFunctionType
ALU = mybir.AluOpType
AX = mybir.AxisListType

D = 1024
INW = 2560
FF = 2816
EPS = 1e-6
NCORES = 8
RANKS = 4


class Cfg:
    def __init__(self, depth=4, seg=2048, np_=2, ns=4, use_cc=True, stop=99):
        self.stop = stop
        self.depth = depth
        self.seg = seg
        self.np = np_
        self.ns = ns
        self.ptok = np_ * seg
        self.T = (np_ + ns) * seg
        self.sp = RANKS * self.ptok
        self.use_cc = use_cc


EPOCH = 8000


class Counter:
    def __init__(self, kk, name, step):
        self.kk = kk
        self.name = name
        self.step = step
        self.count = 0
        self.sems = []
        self.per = EPOCH // step

    def sem_for(self, c):
        idx = (c - 1) // self.per
        while len(self.sems) <= idx:
            self.sems.append(self.kk.es.enter_context(self.kk.nc.semaphore(f"s_{self.name}{len(self.sems)}")))
        return self.sems[idx], (c - idx * self.per) * self.step


class Issuer:
    def __init__(self, name, h, cnt):
        self.name = name
        self.h = h
        self.cnt = cnt
        self.waited = {}


class Buf:
    def __init__(self, t):
        self.t = t
        self.w = None
        self.r = {}

    def __getitem__(self, idx):
        return self.t[idx]


class K:
    def __init__(self, nc):
        self.nc = nc
        self.es = ExitStack()
        self.counters = []
        mk = lambda n, s: self._mkc(n, s)
        self.PE = Issuer("pe", nc.tensor, mk("pe", 1))
        self.ACT = Issuer("act", nc.scalar, mk("act", 1))
        self.DVE = Issuer("dve", nc.vector, mk("dve", 1))
        self.POOL = Issuer("pool", nc.gpsimd, mk("pool", 1))
        self.SP = Issuer("sp", nc.sync, mk("spc", 1))
        self.issuers = [self.PE, self.ACT, self.DVE, self.POOL, self.SP]
        NL = 16
        self.q_sp = [mk(f"qsp{i}_", 16) for i in range(NL)]
        self.q_pool = [mk(f"qpool{i}_", 16) for i in range(NL)]
        self.q_cc = [mk(f"qcc{i}_", 1) for i in range(4)]
        self.rr = {"sp": 0, "pool": 0, "cc": 0}
        self.ninst = 0

    def lane(self, which):
        lst = {"sp": self.q_sp, "pool": self.q_pool, "cc": self.q_cc}[which]
        c = lst[self.rr[which] % len(lst)]
        self.rr[which] += 1
        return c

    def _mkc(self, n, s):
        c = Counter(self, n, s)
        self.counters.append(c)
        return c

    def _wait(self, iss, c, n):
        if n <= 0:
            return
        if iss.waited.get(c, 0) >= n:
            return
        if c is self.PE.cnt and iss is self.PE:
            return
        sem, val = c.sem_for(n)
        iss.h.wait_ge(sem, val)
        iss.waited[c] = n

    def emit(self, iss, fn, reads=(), writes=(), counter=None):
        c = counter if counter is not None else iss.cnt
        deps = {}

        def add(cn):
            cc, n = cn
            if deps.get(cc, 0) < n:
                deps[cc] = n

        for b in reads:
            if b.w is not None:
                add(b.w)
        for b in writes:
            if b.w is not None:
                add(b.w)
            for cc, n in b.r.items():
                add((cc, n))
        for cc, n in deps.items():
            self._wait(iss, cc, n)
        inst = fn()
        c.count += 1
        n = c.count
        sem, val = c.sem_for(n)
        inst.then_inc(sem, c.step)
        for b in reads:
            if b.r.get(c, 0) < n:
                b.r[c] = n
        for b in writes:
            b.w = (c, n)
            b.r = {}
        self.ninst += 1
        return inst

    def pe(self, fn, r=(), w=()):
        return self.emit(self.PE, fn, r, w)

    def act(self, fn, r=(), w=()):
        return self.emit(self.ACT, fn, r, w)

    def dve(self, fn, r=(), w=()):
        return self.emit(self.DVE, fn, r, w)

    def pool(self, fn, r=(), w=()):
        return self.emit(self.POOL, fn, r, w)

    def dma_sp(self, out, in_, r=(), w=(), **kw):
        return self.emit(self.SP, lambda: self.nc.sync.dma_start(out=out, in_=in_, **kw), r, w, counter=self.lane("sp"))

    def dma_pool(self, out, in_, r=(), w=(), **kw):
        return self.emit(self.POOL, lambda: self.nc.gpsimd.dma_start(out=out, in_=in_, **kw), r, w, counter=self.lane("pool"))

    def barrier(self):
        snap = [(c, c.count) for c in self.counters]
        for iss in self.issuers:
            for c, n in snap:
                self._wait(iss, c, n)

    def final_wait(self):
        snap = [(c, c.count) for c in self.counters]
        for c, n in snap:
            self._wait(self.SP, c, n)


PV_G1 = 0
PV_G2 = 8
PV_CAW = 16
PV_CAB = 78
PV_LNG = 80
PV_LNB = 82
PV_SUB = 84
PV_FW = 85
PV_FB = 217
NPV = 261


def build_program(cfg):
    nc = bass.Bass("TRN2", target_bir_lowering=False)
    kk = K(nc)
    es = kk.es
    L = cfg.depth
    T, SEG, PTOK = cfg.T, cfg.seg, cfg.ptok
    NTB = T // 128

    def din(name, shape, dt=F32):
        return nc.dram_tensor(name, list(shape), dt, kind="ExternalInput").ap()

    def dscr(name, shape, dt):
        return nc.dram_tensor(name, list(shape), dt, kind="Internal").ap()

    def dcc(name, shape, dt):
        return nc.dram_tensor(name, list(shape), dt).ap()

    x_in = din("x_in", [T, D])
    rot_in = din("rot", [T, 80])
    ident_in = din("ident", [128, 128])
    pv_in = din("pv", [128, L * NPV])
    gt_in = din("gt", [L, 1408])
    lam_in = din("lamv", [128, L * 4 * 64])
    sela_in = din("sela", [120, 30])
    selh_in = din("selh", [8, 2])
    w_in_d = din("w_in", [L, D, INW])
    w_out_d = din("w_out", [L, D, D])
    w_up_d = din("w_up", [L, D, 2 * FF])
    w_dn_d = din("w_down", [L, FF, D])
    y_out = nc.dram_tensor("y_out", [T, D], F32, kind="ExternalOutput").ap()

    wb_in = dscr("wb_in", [L, D, INW], BF16)
    wb_out = dscr("wb_out", [L, D, D], BF16)
    wb_up = dscr("wb_up", [L, D, 2 * FF], BF16)
    wb_dn = dscr("wb_dn", [L, FF, D], BF16)
    xm_d = dscr("xm", [T, D], F32)
    xa_d = dscr("xa", [T, D], F32)
    xb_d = dscr("xb", [T, D], F32)
    at_d = dscr("at", [2, 128, T], F32)
    qkt_d = dscr("qkt", [11, 128, T], BF16)
    v_d = dscr("v", [T, 640], BF16)
    cata_d = dscr("cata", [2, 128, T], BF16)
    h2t_d = dscr("h2t", [8, 128, T], BF16)
    kts_l = [dcc(f"kts{u}", [128, PTOK], BF16) for u in range(5)]
    vs_l = [dcc(f"vs{u}", [PTOK, 128], BF16) for u in range(5)]
    ahs_d = dcc("ahs", [30, 256], F32)
    hhs_d = dcc("hhs", [2, D], BF16)
    kta_l = [dcc(f"kta{u}", [RANKS * 128, PTOK], BF16) for u in range(5)]
    va_l = [dcc(f"va{u}", [RANKS * PTOK, 128], BF16) for u in range(5)]
    aha_d = dcc("aha", [RANKS * 30, 256], F32)
    hha_d = dcc("hha", [RANKS * 2, D], BF16)
    RG = [[0, 1, 2, 3], [4, 5, 6, 7]]

    uid = [0]

    def sb(stack, name, shape, dt):
        uid[0] += 1
        return Buf(stack.enter_context(nc.sbuf_tensor(f"sb{uid[0]}_{name}", list(shape), dt)))

    def ps(stack, name, shape, dt):
        uid[0] += 1
        return Buf(stack.enter_context(nc.psum_tensor(f"ps{uid[0]}_{name}", list(shape), dt)))

    ident_f = sb(es, "ident_f", [128, 128], F32)
    ident_b = sb(es, "ident_b", [128, 128], BF16)
    ones_b = sb(es, "ones_b", [128, 128], BF16)
    ones3 = sb(es, "ones3", [128, 192], BF16)
    onesA = sb(es, "onesA", [128, 128], F32)
    onesS = sb(es, "onesS", [128, 128], F32)
    onesF = sb(es, "onesF", [128, 128], F32)
    swapF = sb(es, "swapF", [128, 128], F32)
    epst = sb(es, "epst", [128, 1], F32)
    pv = sb(es, "pv", [128, L * NPV], F32)
    neglam = sb(es, "neglam", [128, L], F32)
    sela = sb(es, "sela", [120, 30], F32)
    selh = sb(es, "selh", [8, 2], BF16)

    def pvc(l, off, n=1):
        return pv[:, l * NPV + off: l * NPV + off + n]

    kk.dma_sp(ident_f[:], ident_in[:, :], w=[ident_f])
    kk.dma_sp(pv[:], pv_in[:, :], w=[pv])
    kk.dma_sp(sela[:], sela_in[:, :], w=[sela])
    kk.dve(lambda: nc.vector.tensor_copy(out=ident_b[:], in_=ident_f[:]), r=[ident_f], w=[ident_b])
    kk.dve(lambda: nc.vector.memset(ones_b[:], 1.0), w=[ones_b])
    kk.dve(lambda: nc.vector.memset(ones3[:], 0.0), w=[ones3])
    kk.dve(lambda: nc.vector.memset(ones3[:, 64:128], 1.0), w=[ones3])
    kk.dve(lambda: nc.vector.memset(onesA[:], 1.0 / 256), w=[onesA])
    kk.dve(lambda: nc.vector.memset(onesS[:], 1.0 / 128), w=[onesS])
    kk.dve(lambda: nc.vector.memset(epst[:], EPS), w=[epst])
    kk.dve(lambda: nc.vector.memset(onesF[:], 1.0), w=[onesF])
    kk.dve(lambda: nc.vector.tensor_copy(out=swapF[:, 0:64], in_=ident_f[:, 64:128]), r=[ident_f], w=[swapF])
    kk.dve(lambda: nc.vector.tensor_copy(out=swapF[:, 64:128], in_=ident_f[:, 0:64]), r=[ident_f], w=[swapF])
    with ExitStack() as st:
        lamv = sb(st, "lamv", [128, L * 4 * 64], F32)
        selh_f = sb(st, "selh_f", [8, 2], F32)
        lp = sb(st, "lp", [128, L * 2 * 64], F32)
        lsum = sb(st, "lsum", [128, L * 2], F32)
        kk.dma_sp(lamv[:], lam_in[:, :], w=[lamv])
        kk.dma_sp(selh_f[:], selh_in[:, :], w=[selh_f])
        kk.dve(lambda: nc.vector.tensor_copy(out=selh[:], in_=selh_f[:]), r=[selh_f], w=[selh])
        lv = lamv[:].rearrange("p (l f d) -> p l f d", l=L, f=4)
        lpv = lp[:].rearrange("p (l f d) -> p l f d", l=L, f=2)
        kk.dve(lambda: nc.vector.tensor_tensor(out=lpv[:, :, 0, :], in0=lv[:, :, 0, :], in1=lv[:, :, 1, :], op=ALU.mult), r=[lamv], w=[lp])
        kk.dve(lambda: nc.vector.tensor_tensor(out=lpv[:, :, 1, :], in0=lv[:, :, 2, :], in1=lv[:, :, 3, :], op=ALU.mult), r=[lamv], w=[lp])
        kk.dve(lambda: nc.vector.tensor_reduce(out=lsum[:], in_=lp[:].rearrange("p (g d) -> p g d", d=64), axis=AX.X, op=ALU.add), r=[lp], w=[lsum])
        kk.act(lambda: nc.scalar.activation(out=lsum[:], in_=lsum[:], func=AF.Exp), r=[lsum], w=[lsum])
        for l in range(L):
            lam_init = 0.8 - 0.6 * math.exp(-0.3 * l)
            kk.dve(lambda l=l, li=lam_init: nc.vector.scalar_tensor_tensor(
                out=neglam[:, l:l + 1], in0=lsum[:, 2 * l + 1:2 * l + 2], scalar=-li, in1=lsum[:, 2 * l:2 * l + 1],
                op0=ALU.add, op1=ALU.subtract), r=[lsum], w=[neglam])
        kk.barrier()

    with ExitStack() as st:
        CW = 2816
        stg = [sb(st, f"wstg{i}", [128, CW], F32) for i in range(3)]
        stb = [sb(st, f"wstb{i}", [128, CW], BF16) for i in range(3)]
        items = []
        for l in range(L):
            for kc in range(8):
                items.append((w_in_d[l, kc * 128:(kc + 1) * 128, :], wb_in[l, kc * 128:(kc + 1) * 128, :], INW, pvc(l, PV_G1 + kc)))
            for kc in range(8):
                items.append((w_out_d[l, kc * 128:(kc + 1) * 128, :], wb_out[l, kc * 128:(kc + 1) * 128, :], D, None))
            for kc in range(8):
                for hh in range(2):
                    items.append((w_up_d[l, kc * 128:(kc + 1) * 128, hh * FF:(hh + 1) * FF],
                                  wb_up[l, kc * 128:(kc + 1) * 128, hh * FF:(hh + 1) * FF], FF, pvc(l, PV_G2 + kc)))
            for kc in range(22):
                items.append((w_dn_d[l, kc * 128:(kc + 1) * 128, :], wb_dn[l, kc * 128:(kc + 1) * 128, :], D, None))
        for i, (src, dst, n, g) in enumerate(items):
            a, b = stg[i % 3], stb[i % 3]
            kk.dma_sp(a[:, 0:n], src, w=[a])
            if i % 2 == 0:
                if g is None:
                    kk.dve(lambda a=a, b=b, n=n: nc.vector.tensor_copy(out=b[:, 0:n], in_=a[:, 0:n]), r=[a], w=[b])
                else:
                    kk.dve(lambda a=a, b=b, n=n, g=g: nc.vector.tensor_scalar(out=b[:, 0:n], in0=a[:, 0:n], scalar1=g, scalar2=None, op0=ALU.mult), r=[a, pv], w=[b])
            else:
                if g is None:
                    kk.act(lambda a=a, b=b, n=n: nc.scalar.copy(out=b[:, 0:n], in_=a[:, 0:n]), r=[a], w=[b])
                else:
                    kk.act(lambda a=a, b=b, n=n, g=g: nc.scalar.activation(out=b[:, 0:n], in_=a[:, 0:n], func=AF.Copy, scale=g), r=[a, pv], w=[b])
            kk.dma_pool(dst, b[:, 0:n], r=[b])
        kk.barrier()

    def rmsnorm_to_T(st_bufs, xsrc, l_unused, hb, hT, psT, junk, ss, lnv):
        kk.dve(lambda: nc.vector.scalar_tensor_tensor(out=junk[:], in0=xsrc[:], scalar=1.0, in1=xsrc[:], op0=ALU.mult, op1=ALU.mult,
                                                      accum_out=ss[:]), r=[xsrc], w=[junk, ss])
        kk.act(lambda: nc.scalar.activation(out=lnv[:], in_=ss[:], func=AF.Ln, bias=epst[:], scale=1.0 / D), r=[ss, epst], w=[lnv])
        kk.act(lambda: nc.scalar.activation(out=lnv[:], in_=lnv[:], func=AF.Exp, scale=-0.5), r=[lnv], w=[lnv])
        kk.act(lambda: nc.scalar.activation(out=hb[:], in_=xsrc[:], func=AF.Copy, scale=lnv[:]), r=[xsrc, lnv], w=[hb])
        for kc in range(8):
            kk.pe(lambda kc=kc: nc.tensor.transpose(out=psT[:, kc * 128:(kc + 1) * 128], in_=hb[:, kc * 128:(kc + 1) * 128], identity=ident_b[:]),
                  r=[hb, ident_b], w=[psT])
        kk.dve(lambda: nc.vector.tensor_copy(out=hT[:], in_=psT[:, 0:1024]), r=[psT], w=[hT])

    prompt_blocks = PTOK // 128

    for l in range(L):
        lam_init = 0.8 - 0.6 * math.exp(-0.3 * l)
        x_src = x_in if l == 0 else (xa_d if l % 2 == 1 else xb_d)
        x_dst = y_out if l == L - 1 else (xa_d if l % 2 == 0 else xb_d)

        with ExitStack() as st:
          if cfg.stop >= 1:
            WIN = sb(st, "WIN", [128, 8 * INW], BF16)
            G = sb(st, "G", [128, 1408], F32)
            xt = [sb(st, f"xt{i}", [128, D], F32) for i in range(3)]
            rot = [sb(st, f"rot{i}", [128, 80], F32) for i in range(3)]
            qk2 = [sb(st, f"qk2{i}", [128, 1408], F32) for i in range(2)]
            vb2 = [sb(st, f"vb2{i}", [128, 640], BF16) for i in range(2)]
            atm2 = [sb(st, f"atm2{i}", [128, 256], F32) for i in range(2)]
            junk = sb(st, "junk", [128, D], BF16)
            ss = sb(st, "ss", [128, 1], F32)
            lnv = sb(st, "lnv", [128, 1], F32)
            hb = sb(st, "hb", [128, D], BF16)
            hT = sb(st, "hT", [128, D], BF16)
            eg = sb(st, "eg", [128, 256], F32)
            a_tm = sb(st, "a_tm", [128, 256], F32)
            aT = sb(st, "aT", [128, 256], F32)
            qk = sb(st, "qk", [128, 1408], F32)
            sq = sb(st, "sq", [128, 1408], F32)
            ssg = sb(st, "ssg", [128, 22], F32)
            rt = [sb(st, f"rt{i}", [128, 192], F32) for i in range(4)]
            qkb = sb(st, "qkb", [128, 1408], BF16)
            qkT = sb(st, "qkT", [128, 1408], BF16)
            vb = sb(st, "vb", [128, 640], BF16)
            psP = [ps(st, f"psP{j}", [128, 512], F32) for j in range(5)]
            psA = ps(st, "psA", [128, 256], F32)
            psQ = ps(st, "psQ", [128, 2048], BF16)
            psT = psQ

            for kc in range(8):
                kk.dma_sp(WIN[:, kc * INW:(kc + 1) * INW], wb_in[l, kc * 128:(kc + 1) * 128, :], w=[WIN])
            kk.dma_sp(G[:], gt_in[l:l + 1, :].to_broadcast([128, 1408]), w=[G])
            kk.dve(lambda: nc.vector.tensor_scalar(out=G[:, 0:512], in0=G[:, 0:512], scalar1=0.125, scalar2=None, op0=ALU.mult), r=[G], w=[G])
            kk.dve(lambda: nc.vector.tensor_scalar(out=G[:, 1024:1280], in0=G[:, 1024:1280], scalar1=0.125, scalar2=None, op0=ALU.mult), r=[G], w=[G])

            def load_blk(tb):
                b = tb % 3
                kk.dma_sp(xt[b][:], x_src[tb * 128:(tb + 1) * 128, :], w=[xt[b]])
                kk.dma_sp(rot[b][:], rot_in[tb * 128:(tb + 1) * 128, :], w=[rot[b]])

            def fh1(tb):
                xsrc = xt[tb % 3]
                kk.dve(lambda: nc.vector.scalar_tensor_tensor(out=junk[:], in0=xsrc[:], scalar=1.0, in1=xsrc[:], op0=ALU.mult, op1=ALU.mult,
                                                              accum_out=ss[:]), r=[xsrc], w=[junk, ss])
                kk.act(lambda: nc.scalar.activation(out=lnv[:], in_=ss[:], func=AF.Ln, bias=epst[:], scale=1.0 / D), r=[ss, epst], w=[lnv])
                kk.act(lambda: nc.scalar.activation(out=lnv[:], in_=lnv[:], func=AF.Exp, scale=-0.5), r=[lnv], w=[lnv])
                kk.act(lambda: nc.scalar.activation(out=hb[:], in_=xsrc[:], func=AF.Copy, scale=lnv[:]), r=[xsrc, lnv], w=[hb])
                for kc in range(8):
                    kk.pe(lambda kc=kc: nc.tensor.transpose(out=psT[:, kc * 128:(kc + 1) * 128], in_=hb[:, kc * 128:(kc + 1) * 128], identity=ident_b[:]),
                          r=[hb, ident_b], w=[psT])

            def fh2(tb):
                kk.dve(lambda: nc.vector.tensor_copy(out=hT[:], in_=psT[:, 0:1024]), r=[psT], w=[hT])
                for j in range(5):
                    for kc in range(8):
                        kk.pe(lambda j=j, kc=kc: nc.tensor.matmul(psP[j][:], lhsT=hT[:, kc * 128:(kc + 1) * 128],
                                                                  rhs=WIN[:, kc * INW + j * 512: kc * INW + (j + 1) * 512],
                                                                  start=(kc == 0), stop=(kc == 7)), r=[hT, WIN], w=[psP[j]])

            def front_tail(tb):
                b = tb % 2
                QK, VBb, ATM = qk2[b], vb2[b], atm2[b]
                kk.act(lambda: nc.scalar.activation(out=eg[:], in_=psP[0][:, 256:512], func=AF.Exp, scale=-1.0), r=[psP[0]], w=[eg])
                kk.act(lambda: nc.scalar.copy(out=QK[:, 0:512], in_=psP[1][:]), r=[psP[1]], w=[QK])
                kk.act(lambda: nc.scalar.copy(out=QK[:, 512:1024], in_=psP[2][:]), r=[psP[2]], w=[QK])
                kk.act(lambda: nc.scalar.copy(out=QK[:, 1024:1408], in_=psP[3][:, 0:384]), r=[psP[3]], w=[QK])
                kk.pool(lambda: nc.gpsimd.tensor_tensor(out=sq[:], in0=QK[:], in1=QK[:], op=ALU.mult), r=[QK], w=[sq])
                kk.act(lambda: nc.scalar.copy(out=VBb[:, 0:512], in_=psP[4][:]), r=[psP[4]], w=[VBb])
                kk.act(lambda: nc.scalar.copy(out=VBb[:, 512:640], in_=psP[3][:, 384:512]), r=[psP[3]], w=[VBb])
                kk.dve(lambda: nc.vector.tensor_scalar(out=eg[:], in0=eg[:], scalar1=1.0, scalar2=None, op0=ALU.add), r=[eg], w=[eg])
                kk.dve(lambda: nc.vector.reciprocal(out=eg[:], in_=eg[:]), r=[eg], w=[eg])
                kk.dve(lambda: nc.vector.tensor_tensor(out=ATM[:], in0=psP[0][:, 0:256], in1=eg[:], op=ALU.mult), r=[psP[0], eg], w=[ATM])

            def back(tb):
                b = tb % 2
                t0 = tb * 128
                R = rot[tb % 3]
                qk, vb, a_tm = qk2[b], vb2[b], atm2[b]
                for c in range(2):
                    kk.pe(lambda c=c: nc.tensor.transpose(out=psA[:, c * 128:(c + 1) * 128], in_=a_tm[:, c * 128:(c + 1) * 128], identity=ident_f[:]),
                          r=[a_tm, ident_f], w=[psA])
                kk.act(lambda: nc.scalar.copy(out=aT[:], in_=psA[:]), r=[psA], w=[aT])
                kk.dma_pool(at_d[:, :, t0:t0 + 128].rearrange("c p t -> p c t"), aT[:].rearrange("p (c t) -> p c t", c=2), r=[aT])
                if tb == 0:
                    kk.dma_pool(ahs_d[0:15, :], a_tm[0:15, :], r=[a_tm])
                if tb == prompt_blocks - 1:
                    kk.dma_pool(ahs_d[15:30, :], a_tm[113:128, :], r=[a_tm])

            def back_qk(tb):
                b = tb % 2
                t0 = tb * 128
                R = rot[tb % 3]
                qk, vb, a_tm = qk2[b], vb2[b], atm2[b]
                kk.dve(lambda: nc.vector.tensor_reduce(out=ssg[:], in_=sq[:].rearrange("p (g d) -> p g d", d=64), axis=AX.X, op=ALU.add), r=[sq], w=[ssg])
                kk.act(lambda: nc.scalar.activation(out=ssg[:], in_=ssg[:], func=AF.Ln, bias=epst[:], scale=1.0 / 64), r=[ssg, epst], w=[ssg])
                kk.act(lambda: nc.scalar.activation(out=ssg[:], in_=ssg[:], func=AF.Exp, scale=-0.5), r=[ssg], w=[ssg])

            def bq2(tb):
                b = tb % 2
                t0 = tb * 128
                R = rot[tb % 3]
                qk, vb, a_tm = qk2[b], vb2[b], atm2[b]
                qk3 = qk[:].rearrange("p (g d) -> p g d", d=64)
                kk.dve(lambda: nc.vector.tensor_tensor(out=qk3, in0=qk3, in1=ssg[:].unsqueeze(2).to_broadcast([128, 22, 64]), op=ALU.mult), r=[qk, ssg], w=[qk])
                kk.dve(lambda: nc.vector.tensor_tensor(out=qk[:], in0=qk[:], in1=G[:], op=ALU.mult), r=[qk, G], w=[qk])
                qB = qk[:, 0:1024].rearrange("p (g d) -> p g d", d=64)
                x1, x2 = qB[:, :, 0:8], qB[:, :, 8:16]
                cB = R[:, 0:8].unsqueeze(1).to_broadcast([128, 16, 8])
                sB = R[:, 8:16].unsqueeze(1).to_broadcast([128, 16, 8])
                tv = [rt[i][:, 0:128].rearrange("p (g d) -> p g d", d=8) for i in range(4)]
                kk.dve(lambda: nc.vector.tensor_tensor(out=tv[0], in0=x1, in1=cB, op=ALU.mult), r=[qk, R], w=[rt[0]])
                kk.dve(lambda: nc.vector.tensor_tensor(out=tv[1], in0=x2, in1=sB, op=ALU.mult), r=[qk, R], w=[rt[1]])
                kk.dve(lambda: nc.vector.tensor_tensor(out=tv[2], in0=x2, in1=cB, op=ALU.mult), r=[qk, R], w=[rt[2]])
                kk.dve(lambda: nc.vector.tensor_tensor(out=tv[3], in0=x1, in1=sB, op=ALU.mult), r=[qk, R], w=[rt[3]])
                kk.dve(lambda: nc.vector.tensor_tensor(out=x1, in0=tv[0], in1=tv[1], op=ALU.subtract), r=[rt[0], rt[1]], w=[qk])
                kk.dve(lambda: nc.vector.tensor_tensor(out=x2, in0=tv[2], in1=tv[3], op=ALU.add), r=[rt[2], rt[3]], w=[qk])
                qC = qk[:, 1024:1408].rearrange("p (g h x d) -> p g h x d", g=6, h=2, x=2)
                y1, y2 = qC[:, :, :, 0, :], qC[:, :, :, 1, :]
                RC = R[:, 16:80].rearrange("p (h x d) -> p h x d", h=2, x=2)
                cC = RC[:, :, 0, :].unsqueeze(1).to_broadcast([128, 6, 2, 16])
                sC = RC[:, :, 1, :].unsqueeze(1).to_broadcast([128, 6, 2, 16])
                tw = [rt[i][:, 0:192].rearrange("p (g h d) -> p g h d", g=6, h=2) for i in range(4)]
                kk.dve(lambda: nc.vector.tensor_tensor(out=tw[0], in0=y1, in1=cC, op=ALU.mult), r=[qk, R], w=[rt[0]])
                kk.dve(lambda: nc.vector.tensor_tensor(out=tw[1], in0=y2, in1=sC, op=ALU.mult), r=[qk, R], w=[rt[1]])
                kk.dve(lambda: nc.vector.tensor_tensor(out=tw[2], in0=y2, in1=cC, op=ALU.mult), r=[qk, R], w=[rt[2]])
                kk.dve(lambda: nc.vector.tensor_tensor(out=tw[3], in0=y1, in1=sC, op=ALU.mult), r=[qk, R], w=[rt[3]])
                kk.dve(lambda: nc.vector.tensor_tensor(out=y1, in0=tw[0], in1=tw[1], op=ALU.subtract), r=[rt[0], rt[1]], w=[qk])
                kk.dve(lambda: nc.vector.tensor_tensor(out=y2, in0=tw[2], in1=tw[3], op=ALU.add), r=[rt[2], rt[3]], w=[qk])
                kk.act(lambda: nc.scalar.copy(out=qkb[:], in_=qk[:]), r=[qk], w=[qkb])
                for c in range(11):
                    kk.pe(lambda c=c: nc.tensor.transpose(out=psQ[:, c * 128:(c + 1) * 128], in_=qkb[:, c * 128:(c + 1) * 128], identity=ident_b[:]),
                          r=[qkb, ident_b], w=[psQ])

            def bq3(tb):
                b = tb % 2
                t0 = tb * 128
                qk, vb, a_tm = qk2[b], vb2[b], atm2[b]
                kk.dve(lambda: nc.vector.tensor_copy(out=qkT[:], in_=psQ[:, 0:1408]), r=[psQ], w=[qkT])
                kk.dma_pool(qkt_d[:, :, t0:t0 + 128].rearrange("c p t -> p c t"), qkT[:].rearrange("p (c t) -> p c t", c=11), r=[qkT])
                kk.dma_pool(v_d[t0:t0 + 128, :], vb[:], r=[vb])
                if tb < prompt_blocks:
                    for u in range(5):
                        kc0 = 512 + u * 128 if u < 4 else 1280
                        kk.dma_pool(kts_l[u][:, t0:t0 + 128], qkT[:, kc0:kc0 + 128], r=[qkT])
                        kk.dma_pool(vs_l[u][t0:t0 + 128, :], vb[:, u * 128:(u + 1) * 128], r=[vb])

            load_blk(0)
            if NTB > 1:
                load_blk(1)
            fh1(0)
            fh2(0)
            front_tail(0)
            for tb in range(NTB):
                if tb + 2 < NTB:
                    load_blk(tb + 2)
                back(tb)
                nxt = tb + 1 < NTB
                if nxt:
                    fh1(tb + 1)
                back_qk(tb)
                if nxt:
                    fh2(tb + 1)
                bq2(tb)
                if nxt:
                    front_tail(tb + 1)
                bq3(tb)
            kk.barrier()

        if cfg.use_cc and cfg.stop >= 2:
            for src, dst in [(ahs_d, aha_d)] + [(kts_l[u], kta_l[u]) for u in range(5)] + [(vs_l[u], va_l[u]) for u in range(5)]:
                kk.emit(kk.POOL, lambda src=src, dst=dst: nc.gpsimd.collective_compute(
                    "AllGather", ALU.bypass, replica_groups=RG, ins=[src.opt()], outs=[dst.opt()]), counter=kk.lane("cc"))

        with ExitStack() as st:
          if cfg.stop >= 3:
            abuf = [sb(st, f"abuf{i}", [128, SEG + 30], F32) for i in range(2)]
            convo = [sb(st, f"convo{i}", [128, SEG], F32) for i in range(2)]
            sqb = [sb(st, f"sqb{i}", [128, 512], F32) for i in range(2)]
            mean_sb = sb(st, "mean_sb", [128, 512], F32)
            m2 = sb(st, "m2", [128, 512], F32)
            rstd = sb(st, "rstd", [128, 512], F32)
            dd = [sb(st, f"dd{i}", [128, 512], F32) for i in range(2)]
            ee = [sb(st, f"ee{i}", [128, 512], F32) for i in range(2)]
            ob = [sb(st, f"ob{i}", [128, 512], BF16) for i in range(2)]
            ahr = sb(st, "ahr", [120, 256], F32)
            ps_mean = ps(st, "ps_mean", [128, 512], F32)
            ps_msq = ps(st, "ps_msq", [128, 512], F32)
            ps_halo = ps(st, "ps_halo", [128, 64], F32)
            nseg = cfg.np + cfg.ns
            for s in list(range(cfg.np, nseg)) + list(range(cfg.np)):
                if s == 0:
                    kk.barrier()
                    if cfg.use_cc:
                        kk.dma_sp(ahr[:], aha_d[:, :], w=[ahr])
                t0 = s * SEG
                is_p = s < cfg.np
                for c in range(2):
                    A = abuf[c]
                    left_local = is_p and s > 0
                    right_local = is_p and s < cfg.np - 1
                    lo = t0 - 15 if left_local else t0
                    hi = t0 + SEG + 15 if right_local else t0 + SEG
                    kk.dma_sp(A[:, 15 + (lo - t0): 15 + (hi - t0)], at_d[c, :, lo:hi], w=[A])
                    if not left_local:
                        kk.pool(lambda A=A: nc.gpsimd.memset(A[:, 0:15], 0.0), w=[A])
                    if not right_local:
                        kk.pool(lambda A=A: nc.gpsimd.memset(A[:, SEG + 15:SEG + 30], 0.0), w=[A])
                    if cfg.use_cc and is_p and (s == 0 or s == cfg.np - 1):
                        kk.pe(lambda c=c: nc.tensor.matmul(ps_halo[:, 0:30], lhsT=ahr[:, c * 128:(c + 1) * 128], rhs=sela[:], start=True, stop=True),
                              r=[ahr, sela], w=[ps_halo])
                        if s == 0:
                            kk.act(lambda A=A: nc.scalar.copy(out=A[:, 0:15], in_=ps_halo[:, 0:15]), r=[ps_halo], w=[A])
                        if s == cfg.np - 1:
                            kk.act(lambda A=A: nc.scalar.copy(out=A[:, SEG + 15:SEG + 30], in_=ps_halo[:, 15:30]), r=[ps_halo], w=[A])
                    CO = convo[c]
                    kk.dve(lambda A=A, CO=CO, c=c: nc.vector.tensor_scalar(out=CO[:], in0=A[:, 0:SEG], scalar1=pvc(l, PV_CAW + c * 31),
                                                                          scalar2=pvc(l, PV_CAB + c), op0=ALU.mult, op1=ALU.add), r=[A, pv], w=[CO])
                    for j in range(1, 31):
                        kk.dve(lambda A=A, CO=CO, c=c, j=j: nc.vector.scalar_tensor_tensor(out=CO[:], in0=A[:, j:j + SEG], scalar=pvc(l, PV_CAW + c * 31 + j),
                                                                                         in1=CO[:], op0=ALU.mult, op1=ALU.add), r=[A, pv, CO], w=[CO])
                for ti in range(SEG // 512):
                    c0 = ti * 512
                    for c in range(2):
                        kk.pool(lambda c=c: nc.gpsimd.tensor_tensor(out=sqb[c][:], in0=convo[c][:, c0:c0 + 512], in1=convo[c][:, c0:c0 + 512], op=ALU.mult),
                                r=[convo[c]], w=[sqb[c]])
                    for c in range(2):
                        kk.pe(lambda c=c: nc.tensor.matmul(ps_mean[:], lhsT=onesA[:], rhs=convo[c][:, c0:c0 + 512], start=(c == 0), stop=(c == 1)),
                              r=[onesA, convo[c]], w=[ps_mean])
                    for c in range(2):
                        kk.pe(lambda c=c: nc.tensor.matmul(ps_msq[:], lhsT=onesA[:], rhs=sqb[c][:], start=(c == 0), stop=(c == 1)),
                              r=[onesA, sqb[c]], w=[ps_msq])
                    kk.act(lambda: nc.scalar.copy(out=mean_sb[:], in_=ps_mean[:]), r=[ps_mean], w=[mean_sb])
                    kk.dve(lambda: nc.vector.tensor_tensor(out=m2[:], in0=mean_sb[:], in1=mean_sb[:], op=ALU.mult), r=[mean_sb], w=[m2])
                    kk.dve(lambda: nc.vector.tensor_tensor(out=m2[:], in0=ps_msq[:], in1=m2[:], op=ALU.subtract), r=[ps_msq, m2], w=[m2])
                    kk.dve(lambda: nc.vector.tensor_scalar(out=m2[:], in0=m2[:], scalar1=0.0, scalar2=None, op0=ALU.max), r=[m2], w=[m2])
                    kk.act(lambda: nc.scalar.activation(out=rstd[:], in_=m2[:], func=AF.Ln, bias=epst[:], scale=1.0), r=[m2, epst], w=[rstd])
                    kk.act(lambda: nc.scalar.activation(out=rstd[:], in_=rstd[:], func=AF.Exp, scale=-0.5), r=[rstd], w=[rstd])
                    for c in range(2):
                        Dd, Ee, Ob = dd[c], ee[c], ob[c]
                        kk.dve(lambda c=c, Dd=Dd: nc.vector.tensor_tensor(out=Dd[:], in0=convo[c][:, c0:c0 + 512], in1=mean_sb[:], op=ALU.subtract),
                               r=[convo[c], mean_sb], w=[Dd])
                        kk.dve(lambda Dd=Dd: nc.vector.tensor_tensor(out=Dd[:], in0=Dd[:], in1=rstd[:], op=ALU.mult), r=[Dd, rstd], w=[Dd])
                        kk.dve(lambda c=c, Dd=Dd: nc.vector.tensor_scalar(out=Dd[:], in0=Dd[:], scalar1=pvc(l, PV_LNG + c), scalar2=pvc(l, PV_LNB + c),
                                                                         op0=ALU.mult, op1=ALU.add), r=[Dd, pv], w=[Dd])
                        kk.act(lambda Dd=Dd, Ee=Ee: nc.scalar.activation(out=Ee[:], in_=Dd[:], func=AF.Exp, scale=-1.0), r=[Dd], w=[Ee])
                        kk.pool(lambda Ee=Ee: nc.gpsimd.tensor_scalar(out=Ee[:], in0=Ee[:], scalar1=1.0, scalar2=1.0, op0=ALU.add, op1=ALU.mult), r=[Ee], w=[Ee])
                        kk.dve(lambda Ee=Ee: nc.vector.reciprocal(out=Ee[:], in_=Ee[:]), r=[Ee], w=[Ee])
                        kk.pool(lambda Dd=Dd, Ee=Ee, Ob=Ob: nc.gpsimd.tensor_tensor(out=Ob[:], in0=Dd[:], in1=Ee[:], op=ALU.mult), r=[Dd, Ee], w=[Ob])
                        kk.dma_pool(cata_d[c, :, t0 + c0:t0 + c0 + 512], Ob[:], r=[Ob])
            kk.barrier()

        with ExitStack() as st:
          if cfg.stop >= 4:
            LKMAX = cfg.sp if cfg.use_cc else max(PTOK, SEG)
            NCKMAX = LKMAX // 128
            GTOK = max(PTOK, SEG)
            KT = sb(st, "KT", [128, LKMAX], BF16)
            VB = sb(st, "VB", [128, NCKMAX * 192], BF16)
            catT = sb(st, "catT", [128, 8 * GTOK], BF16)
            WOUT = sb(st, "WOUT", [128, 8 * D], BF16)
            Qa = [sb(st, f"Qa{i}", [128, 512], BF16) for i in range(2)]
            Qb = [sb(st, f"Qb{i}", [128, 512], BF16) for i in range(2)]
            NPT = 4
            PT = [sb(st, f"PT{i}", [128, 1024], BF16) for i in range(NPT)]
            acc = sb(st, "acc", [128, 1024], F32)
            accp = sb(st, "accp", [128, 1024], F32)
            fo = [sb(st, f"fo{i}", [128, 512], F32) for i in range(2)]
            fl = [sb(st, f"fl{i}", [128, 512], F32) for i in range(2)]
            fd = sb(st, "fd", [128, 512], F32)
            fsq = fl[1]
            frs = fl[0]
            xt = [accp, accp]
            xm = acc
            ss = sb(st, "css", [128, 1], F32)
            lnv = sb(st, "clnv", [128, 1], F32)
            hb = sb(st, "chb", [128, D], BF16)
            hT = sb(st, "chT", [128, D], BF16)
            junk = hb
            sc = [ps(st, f"sc{i}", [128, 1024], F32) for i in range(2)]
            po = [ps(st, f"po{m}", [128, 512], F32) for m in range(2)]
            pf = [ps(st, f"pf{m}", [128, 512], F32) for m in range(2)]

            for kc in range(8):
                kk.dma_sp(WOUT[:, kc * D:(kc + 1) * D], wb_out[l, kc * 128:(kc + 1) * 128, :], w=[WOUT])

            groups = [("p", 0, PTOK)] + [("s", PTOK + i * SEG, SEG) for i in range(cfg.ns)]
            scslot = [0]
            ptslot = [0]
            pend = {}

            def flush_pending():
                if pend.get("sums"):
                    pend["sums"]()
                    pend["sums"] = None
                if pend.get("st23"):
                    pend["st23"]()
                    pend["st23"] = None
                if pend.get("st3"):
                    pend["st3"]()
                    pend["st3"] = None
            for (gk, g0, gn) in groups:
                use_all = (gk == "p" and cfg.use_cc)
                Lk = cfg.sp if use_all else gn
                nck = Lk // 128
                VB3 = VB[:, 0:nck * 192].rearrange("p (c e) -> p c e", e=192)
                for c in range(2):
                    kk.dma_sp(catT[:, c * GTOK: c * GTOK + gn], cata_d[c, :, g0:g0 + gn], w=[catT])
                for u in range(6):
                    isB = u < 4
                    kchunk = u if isB else 4
                    if use_all:
                        for r in range(RANKS):
                            kk.dma_sp(KT[:, r * PTOK:(r + 1) * PTOK], kta_l[kchunk][r * 128:(r + 1) * 128, :], w=[KT])
                    else:
                        kk.dma_sp(KT[:, 0:Lk], qkt_d[4 + u if isB else 10, :, g0:g0 + gn], w=[KT])
                    if isB:
                        vcols = slice(u * 128, (u + 1) * 128)
                        vdst = lambda c0, c1: VB3[:, c0:c1, 0:128]
                    else:
                        g = u - 4
                        vcols = slice(512 + g * 64, 512 + (g + 1) * 64)
                        vdst = lambda c0, c1: VB3[:, c0:c1, 64:128]
                        kk.pool(lambda: nc.gpsimd.memset(VB3[:, :, 0:64], 1.0), w=[VB])
                        kk.pool(lambda: nc.gpsimd.memset(VB3[:, :, 128:192], 1.0), w=[VB])
                    if use_all:
                        vsrc = va_l[kchunk]
                        voff = 0
                        vcols = slice(0, 128) if isB else slice(g * 64, (g + 1) * 64)
                    else:
                        vsrc = v_d
                        voff = g0
                    for c0 in range(0, nck, 16):
                        c1 = min(nck, c0 + 16)
                        kk.dma_sp(vdst(c0, c1), vsrc[voff + c0 * 128: voff + c1 * 128, vcols].rearrange("(c p) e -> p c e", p=128), w=[VB])
                    if isB:
                        qrows = [slice(0, 64), slice(64, 128)]
                    else:
                        qrows = [slice(g * 64, (g + 1) * 64)] * 2
                    if u == 0 or u >= 4:
                        for qi_ in range(2):
                            for m_, QQ in enumerate((Qa[qi_], Qb[qi_])):
                                zr = slice(64, 128) if qrows[m_].start == 0 else slice(0, 64)
                                kk.pool(lambda QQ=QQ, zr=zr: nc.gpsimd.memset(QQ[zr, :], 0.0), w=[QQ])
                    if isB:
                        lhs_v = [lambda ck: VB3[:, ck, 0:128], lambda ck: VB3[:, ck, 0:128]]
                        rows = [slice(0, 64), slice(64, 128)]
                    else:
                        lhs_v = [lambda ck: VB3[:, ck, 64:192], lambda ck: VB3[:, ck, 0:128]]
                        rows = [slice(g * 64, (g + 1) * 64)] * 2
                    for qt in range(gn // 512):
                        q0 = g0 + qt * 512
                        qi = qt % 2
                        if isB:
                            kk.dma_sp(Qa[qi][0:64, :], qkt_d[u, 0:64, q0:q0 + 512], w=[Qa[qi]])
                            kk.dma_sp(Qb[qi][64:128, :], qkt_d[u, 64:128, q0:q0 + 512], w=[Qb[qi]])
                        else:
                            kk.dma_sp(Qa[qi][qrows[0], :], qkt_d[8, qrows[0], q0:q0 + 512], w=[Qa[qi]])
                            kk.dma_sp(Qb[qi][qrows[1], :], qkt_d[9, qrows[1], q0:q0 + 512], w=[Qb[qi]])
                        qbufs = [Qa[qi], Qb[qi]]
                        slots = {}

                        def scores(ck):
                            sl = scslot[0] % 2
                            scslot[0] += 1
                            slots[ck] = sl
                            for m in range(2):
                                kk.pe(lambda m=m, sl=sl, ck=ck: nc.tensor.matmul(sc[sl][:, m * 512:(m + 1) * 512], lhsT=KT[:, ck * 128:(ck + 1) * 128],
                                                                                 rhs=qbufs[m][:, :], start=True, stop=True),
                                      r=[KT, qbufs[m]], w=[sc[sl]])

                        scores(0)
                        for ck in range(nck):
                            if ck + 1 < nck:
                                scores(ck + 1)
                            if ck == min(2, nck - 1) and pend.get("sums"):
                                pend["sums"]()
                                pend["sums"] = None
                            if ck == min(6, nck - 1):
                                if pend.get("sums"):
                                    pend["sums"]()
                                    pend["sums"] = None
                                if pend.get("st23"):
                                    pend["st23"]()
                                    pend["st23"] = None
                            if ck == min(11, nck - 1) and pend.get("st3"):
                                if pend.get("st23"):
                                    pend["st23"]()
                                    pend["st23"] = None
                                pend["st3"]()
                                pend["st3"] = None
                            ssl = slots.pop(ck)
                            sl = ptslot[0] % NPT
                            ptslot[0] += 1
                            kk.act(lambda sl=sl, ssl=ssl: nc.scalar.activation(out=PT[sl][:], in_=sc[ssl][:], func=AF.Exp), r=[sc[ssl]], w=[PT[sl]])
                            def emit_pv(ckk, sll):
                                for m in range(2):
                                    kk.pe(lambda m=m: nc.tensor.matmul(
                                        po[m][:], lhsT=lhs_v[m](ckk), rhs=PT[sll][:, m * 512:(m + 1) * 512], start=(ckk == 0), stop=(ckk == nck - 1)),
                                        r=[VB, PT[sll]], w=[po[m]])
                            if ck >= 1:
                                emit_pv(ck - 1, pv_prev_sl)
                            pv_prev_sl = sl
                            if ck == nck - 1:
                                emit_pv(ck, sl)
                            if isB:
                                tb16 = accp[:].bitcast(BF16)
                                t01, t23 = tb16[:, 0:1024], tb16[:, 1024:2048]
                                if ck % 4 == 0:
                                    prev_sl = sl
                                elif ck % 4 == 1:
                                    kk.dve(lambda a_=prev_sl, b_=sl: nc.vector.tensor_tensor(out=t01, in0=PT[a_][:], in1=PT[b_][:], op=ALU.add),
                                           r=[PT[prev_sl], PT[sl]], w=[accp])
                                elif ck % 4 == 2:
                                    prev_sl = sl
                                else:
                                    kk.dve(lambda a_=prev_sl, b_=sl: nc.vector.tensor_tensor(out=t23, in0=PT[a_][:], in1=PT[b_][:], op=ALU.add),
                                           r=[PT[prev_sl], PT[sl]], w=[accp])
                                    kk.dve(lambda: nc.vector.tensor_tensor(out=t01, in0=t01, in1=t23, op=ALU.add), r=[accp], w=[accp])
                                    if ck == 3:
                                        kk.dve(lambda: nc.vector.tensor_copy(out=acc[:], in_=t01), r=[accp], w=[acc])
                                    else:
                                        kk.dve(lambda: nc.vector.tensor_tensor(out=acc[:], in0=acc[:], in1=t01, op=ALU.add), r=[accp, acc], w=[acc])
                        cchunk = 2 + u if isB else 6 + (u - 4)
                        dst = catT[:, cchunk * GTOK + qt * 512: cchunk * GTOK + (qt + 1) * 512]
                        kk.act(lambda: nc.scalar.copy(out=fo[0][:], in_=po[0][:]), r=[po[0]], w=[fo[0]])
                        kk.dve(lambda: nc.vector.tensor_copy(out=fo[1][:], in_=po[1][:]), r=[po[1]], w=[fo[1]])
                        if isB:
                            def st_sums():
                                for m in range(2):
                                    kk.pe(lambda m=m: nc.tensor.matmul(pf[m][:], lhsT=onesF[:], rhs=acc[:, m * 512:(m + 1) * 512], start=True, stop=True),
                                          r=[onesF, acc], w=[pf[m]])

                            def st23(dst=dst):
                                for m in range(2):
                                    kk.act(lambda m=m: nc.scalar.activation(out=fl[m][:], in_=pf[m][:], func=AF.Ln), r=[pf[m]], w=[fl[m]])
                                    kk.act(lambda m=m: nc.scalar.activation(out=fl[m][:], in_=fl[m][:], func=AF.Exp, scale=-1.0), r=[fl[m]], w=[fl[m]])
                                    kk.dve(lambda m=m: nc.vector.tensor_tensor(out=fo[m][:], in0=fo[m][:], in1=fl[m][:], op=ALU.mult), r=[fo[m], fl[m]], w=[fo[m]])
                                kk.dve(lambda: nc.vector.scalar_tensor_tensor(out=fd[:], in0=fo[1][:], scalar=neglam[:, l:l + 1], in1=fo[0][:],
                                                                              op0=ALU.mult, op1=ALU.add), r=[fo[0], fo[1], neglam], w=[fd])
                                kk.dve(lambda: nc.vector.tensor_tensor(out=fsq[:], in0=fd[:], in1=fd[:], op=ALU.mult), r=[fd], w=[fsq])

                            def st3(dst=dst):
                                kk.pe(lambda: nc.tensor.matmul(pf[0][:], lhsT=onesS[:], rhs=fsq[:], start=True, stop=True), r=[onesS, fsq], w=[pf[0]])
                                kk.act(lambda: nc.scalar.activation(out=frs[:], in_=pf[0][:], func=AF.Ln, bias=epst[:], scale=1.0), r=[pf[0], epst], w=[frs])
                                kk.act(lambda: nc.scalar.activation(out=frs[:], in_=frs[:], func=AF.Exp, scale=-0.5), r=[frs], w=[frs])
                                kk.dve(lambda: nc.vector.tensor_tensor(out=fd[:], in0=fd[:], in1=frs[:], op=ALU.mult), r=[fd, frs], w=[fd])
                                kk.dve(lambda: nc.vector.tensor_scalar(out=dst, in0=fd[:], scalar1=pvc(l, PV_SUB), scalar2=1.0 - lam_init,
                                                                      op0=ALU.mult, op1=ALU.mult), r=[fd, pv], w=[catT])
                        else:
                            st_sums = None

                            def st23(dst=dst):
                                for m in range(2):
                                    lr = slice(64, 128) if m == 0 else slice(0, 64)
                                    orr = slice(0, 64) if m == 0 else slice(64, 128)
                                    kk.act(lambda m=m, lr=lr: nc.scalar.activation(out=fl[m][lr, :], in_=fo[m][lr, :], func=AF.Ln), r=[fo[m]], w=[fl[m]])
                                    kk.act(lambda m=m, lr=lr: nc.scalar.activation(out=fl[m][lr, :], in_=fl[m][lr, :], func=AF.Exp, scale=-1.0), r=[fl[m]], w=[fl[m]])
                                    kk.pool(lambda m=m, orr=orr: nc.gpsimd.memset(fl[m][orr, :], 0.0), w=[fl[m]])

                            def st3(dst=dst):
                                for m in range(2):
                                    kk.pe(lambda m=m: nc.tensor.matmul(pf[m][:], lhsT=swapF[:], rhs=fl[m][:], start=True, stop=True), r=[swapF, fl[m]], w=[pf[m]])
                                kk.dve(lambda: nc.vector.tensor_tensor(out=dst[0:64, :], in0=fo[0][0:64, :], in1=pf[0][0:64, :], op=ALU.mult),
                                       r=[fo[0], pf[0]], w=[catT])
                                kk.dve(lambda: nc.vector.tensor_tensor(out=dst[64:128, :], in0=fo[1][64:128, :], in1=pf[1][64:128, :], op=ALU.mult),
                                       r=[fo[1], pf[1]], w=[catT])
                        pend["sums"] = st_sums
                        pend["st23"] = st23
                        pend["st3"] = st3
                flush_pending()
                nblk = gn // 128

                def outproj(bi, banks):
                    for jn in range(2):
                        for kc in range(8):
                            kk.pe(lambda jn=jn, kc=kc: nc.tensor.matmul(banks[jn][:], lhsT=catT[:, kc * GTOK + bi * 128: kc * GTOK + (bi + 1) * 128],
                                                                        rhs=WOUT[:, kc * D + jn * 512: kc * D + (jn + 1) * 512],
                                                                        start=(kc == 0), stop=(kc == 7)), r=[catT, WOUT], w=[banks[jn]])

                outproj(0, po)
                for bi in range(nblk):
                    t0 = g0 + bi * 128
                    X = xt[0]
                    banks = po if bi % 2 == 0 else pf
                    if bi + 1 < nblk:
                        outproj(bi + 1, pf if bi % 2 == 0 else po)
                    kk.dma_sp(X[:], x_src[t0:t0 + 128, :], w=[X])
                    for jn in range(2):
                        kk.dve(lambda jn=jn, X=X, banks=banks: nc.vector.tensor_tensor(out=xm[:, jn * 512:(jn + 1) * 512], in0=banks[jn][:],
                                                                                      in1=X[:, jn * 512:(jn + 1) * 512], op=ALU.add),
                               r=[banks[jn], X], w=[xm])
                    kk.dma_pool(xm_d[t0:t0 + 128, :], xm[:], r=[xm])
                    psT = sc[0]
                    psT_bf = psT[:, 0:512].bitcast(BF16)
                    rmsnorm_to_T_c1(kk, nc, xm, hb, hT, psT, psT_bf, junk, ss, lnv, epst, ident_b)
                    kk.dma_pool(h2t_d[:, :, t0:t0 + 128].rearrange("c p t -> p c t"), hT[:].rearrange("p (c t) -> p c t", c=8), r=[hT])
                    if gk == "p" and bi == 0:
                        kk.dma_pool(hhs_d[0:1, :], hb[0:1, :], r=[hb])
                    if gk == "p" and bi == nblk - 1:
                        kk.dma_pool(hhs_d[1:2, :], hb[127:128, :], r=[hb])
            kk.barrier()

        if cfg.use_cc and cfg.stop >= 5:
            kk.emit(kk.POOL, lambda: nc.gpsimd.collective_compute(
                "AllGather", ALU.bypass, replica_groups=RG, ins=[hhs_d.opt()], outs=[hha_d.opt()]), counter=kk.lane("cc"))
            kk.barrier()

        with ExitStack() as st:
          if cfg.stop >= 6:
            WUP = sb(st, "WUP", [128, 8 * 2 * FF], BF16)
            WDN = sb(st, "WDN", [128, 22 * D], BF16)
            h2t = [sb(st, f"h2t{i}", [128, 8 * 512], BF16) for i in range(2)]
            gT = sb(st, "gT", [128, 22 * 512], BF16)
            tvb = [sb(st, f"tv{i}", [128, 512], F32) for i in range(2)]
            tgb = [sb(st, f"tg{i}", [128, 512], F32) for i in range(2)]
            sgb = [sb(st, f"sg{i}", [128, 512], F32) for i in range(2)]
            xmb = [sb(st, f"fxm{i}", [128, D], F32) for i in range(2)]
            xo = [sb(st, f"fxo{i}", [128, D], F32) for i in range(2)]
            hh = sb(st, "hh", [8, D], BF16)
            halo_h = sb(st, "halo_h", [128, 16], BF16)
            psv = [ps(st, f"psv{i}", [128, 512], F32) for i in range(2)]
            psg = [ps(st, f"psg{i}", [128, 512], F32) for i in range(2)]
            psy = [ps(st, f"psy{i}", [128, 512], F32) for i in range(4)]
            for kc in range(8):
                kk.dma_sp(WUP[:, kc * 2 * FF:(kc + 1) * 2 * FF], wb_up[l, kc * 128:(kc + 1) * 128, :], w=[WUP])
            for kc in range(22):
                kk.dma_sp(WDN[:, kc * D:(kc + 1) * D], wb_dn[l, kc * 128:(kc + 1) * 128, :], w=[WDN])
            if cfg.use_cc:
                kk.dma_sp(hh[:], hha_d[:, :], w=[hh])
                for kc in range(8):
                    kk.pe(lambda kc=kc: nc.tensor.matmul(psy[0][:, kc * 2:kc * 2 + 2], lhsT=hh[:, kc * 128:(kc + 1) * 128], rhs=selh[:], start=True, stop=True),
                          r=[hh, selh], w=[psy[0]])
                kk.dve(lambda: nc.vector.tensor_copy(out=halo_h[:], in_=psy[0][:, 0:16]), r=[psy[0]], w=[halo_h])
            else:
                kk.dve(lambda: nc.vector.memset(halo_h[:], 0.0), w=[halo_h])
            halo3 = halo_h[:].rearrange("p (c x) -> p c x", x=2)

            fsegs = [("p", 0, PTOK)] + [("s", PTOK + i * SEG, SEG) for i in range(cfg.ns)]
            tiles = []
            for (gk, g0, gn) in fsegs:
                s0 = 0
                while s0 < gn:
                    n = min(510, gn - s0)
                    tiles.append((gk, g0, gn, s0, n))
                    s0 += n
            pslot = [0]
            yslot = [0]

            def load_tile(i):
                gk, g0, gn, s0, n = tiles[i]
                H = h2t[i % 2]
                H3 = H[:].rearrange("p (c t) -> p c t", c=8)
                lo = s0 - 1
                hi = s0 + n + 1
                clo, chi = max(lo, 0), min(hi, gn)
                kk.dma_sp(H3[:, :, clo - lo: chi - lo], h2t_d[:, :, g0 + clo: g0 + chi].rearrange("c p t -> p c t"), w=[H])
                if lo < 0:
                    if gk == "p":
                        kk.pool(lambda: nc.gpsimd.tensor_copy(out=H3[:, :, 0:1], in_=halo3[:, :, 0:1]), r=[halo_h], w=[H])
                    else:
                        kk.pool(lambda: nc.gpsimd.memset(H3[:, :, 0:1], 0.0), w=[H])
                if hi > gn:
                    if gk == "p":
                        kk.pool(lambda: nc.gpsimd.tensor_copy(out=H3[:, :, n + 1:n + 2], in_=halo3[:, :, 1:2]), r=[halo_h], w=[H])
                    else:
                        kk.pool(lambda: nc.gpsimd.memset(H3[:, :, n + 1:n + 2], 0.0), w=[H])

            load_tile(0)
            for i, (gk, g0, gn, s0, n) in enumerate(tiles):
                if i + 1 < len(tiles):
                    load_tile(i + 1)
                H = h2t[i % 2]
                N = n + 2
                for j in range(22):
                    sl = pslot[0] % 2
                    pslot[0] += 1
                    PV_, PG_ = psv[sl], psg[sl]
                    for (P_, ch) in ((PV_, j), (PG_, 22 + j)):
                        for kc in range(8):
                            kk.pe(lambda P_=P_, ch=ch, kc=kc: nc.tensor.matmul(P_[:, 0:N], lhsT=WUP[:, kc * 2 * FF + ch * 128: kc * 2 * FF + (ch + 1) * 128],
                                                                               rhs=H[:, kc * 512: kc * 512 + N], start=(kc == 0), stop=(kc == 7)),
                                  r=[WUP, H], w=[P_])
                    TV, TG, SG = tvb[sl], tgb[sl], sgb[sl]
                    for (P_, TT, ch) in ((PV_, TV, j), (PG_, TG, 22 + j)):
                        kk.act(lambda P_=P_, TT=TT, ch=ch: nc.scalar.activation(out=TT[:, 0:n], in_=P_[:, 0:n], func=AF.Identity,
                                                                                 scale=pvc(l, PV_FW + ch * 3), bias=pvc(l, PV_FB + ch)), r=[P_, pv], w=[TT])
                        for jj in (1, 2):
                            kk.dve(lambda P_=P_, TT=TT, ch=ch, jj=jj: nc.vector.scalar_tensor_tensor(out=TT[:, 0:n], in0=P_[:, jj:jj + n],
                                                                                                      scalar=pvc(l, PV_FW + ch * 3 + jj), in1=TT[:, 0:n],
                                                                                                      op0=ALU.mult, op1=ALU.add), r=[P_, pv, TT], w=[TT])
                    kk.act(lambda TG=TG, SG=SG: nc.scalar.activation(out=SG[:, 0:n], in_=TG[:, 0:n], func=AF.Silu), r=[TG], w=[SG])
                    kk.pool(lambda TV=TV, SG=SG, j=j: nc.gpsimd.tensor_tensor(out=gT[:, j * 512: j * 512 + n], in0=TV[:, 0:n], in1=SG[:, 0:n], op=ALU.mult),
                            r=[TV, SG], w=[gT])
                b0 = 0
                bi = 0
                while b0 < n:
                    m = min(128, n - b0)
                    tk = g0 + s0 + b0
                    XM, XO = xmb[bi % 2], xo[bi % 2]
                    kk.dma_sp(XM[0:m, :], xm_d[tk:tk + m, :], w=[XM])
                    for jn in range(2):
                        Y = psy[yslot[0] % 4]
                        yslot[0] += 1
                        for j in range(22):
                            kk.pe(lambda Y=Y, j=j, jn=jn, b0=b0, m=m: nc.tensor.matmul(Y[0:m, :], lhsT=gT[:, j * 512 + b0: j * 512 + b0 + m],
                                                                                       rhs=WDN[:, j * D + jn * 512: j * D + (jn + 1) * 512],
                                                                                       start=(j == 0), stop=(j == 21)), r=[gT, WDN], w=[Y])
                        kk.dve(lambda Y=Y, jn=jn, m=m, XM=XM, XO=XO: nc.vector.tensor_tensor(out=XO[0:m, jn * 512:(jn + 1) * 512], in0=Y[0:m, :],
                                                                                            in1=XM[0:m, jn * 512:(jn + 1) * 512], op=ALU.add),
                               r=[Y, XM], w=[XO])
                    kk.dma_pool(x_dst[tk:tk + m, :], XO[0:m, :], r=[XO])
                    b0 += m
                    bi += 1
            kk.barrier()

    kk.final_wait()
    return nc, kk


def rmsnorm_to_T_c1(kk, nc, xsrc, hb, hT, psT, psT_bf, junk, ss, lnv, epst, ident_b):
    kk.dve(lambda: nc.vector.scalar_tensor_tensor(out=junk[:], in0=xsrc[:], scalar=1.0, in1=xsrc[:], op0=ALU.mult, op1=ALU.mult,
                                                  accum_out=ss[:]), r=[xsrc], w=[junk, ss])
    kk.act(lambda: nc.scalar.activation(out=lnv[:], in_=ss[:], func=AF.Ln, bias=epst[:], scale=1.0 / D), r=[ss, epst], w=[lnv])
    kk.act(lambda: nc.scalar.activation(out=lnv[:], in_=lnv[:], func=AF.Exp, scale=-0.5), r=[lnv], w=[lnv])
    kk.act(lambda: nc.scalar.activation(out=hb[:], in_=xsrc[:], func=AF.Copy, scale=lnv[:]), r=[xsrc, lnv], w=[hb])
    for kc in range(8):
        kk.pe(lambda kc=kc: nc.tensor.transpose(out=psT_bf[:, kc * 128:(kc + 1) * 128], in_=hb[:, kc * 128:(kc + 1) * 128], identity=ident_b[:]),
              r=[hb, ident_b], w=[psT])
    kk.dve(lambda: nc.vector.tensor_copy(out=hT[:], in_=psT_bf[:, 0:1024]), r=[psT], w=[hT])


def _perm_w_in():
    a = np.arange(0, 512)
    bq = np.arange(512, 1024)
    bk = np.arange(1024, 1536)
    bv = np.arange(1536, 2048)
    cq = np.arange(2048, 2304).reshape(4, 64)[[0, 2, 1, 3]].reshape(-1)
    ck = np.arange(2304, 2432)
    cv = np.arange(2432, 2560)
    return np.concatenate([a, bq, bk, cq, ck, cv, bv])


def _rot_table(pos):
    pos = pos.astype(np.float32)
    invB = (np.float32(500000.0) ** (-np.arange(0, 16, 2, dtype=np.float32) / np.float32(16))).astype(np.float32)
    invC = (np.float32(10000.0) ** (-np.arange(0, 32, 2, dtype=np.float32) / np.float32(32))).astype(np.float32)
    angB = pos[:, None] * invB[None, :]
    p_i = pos.astype(np.int64)
    row = (p_i // 64).astype(np.float32)
    col = (p_i % 64).astype(np.float32)
    angR = row[:, None] * invC[None, :]
    angC = col[:, None] * invC[None, :]
    out = np.concatenate([np.cos(angB), np.sin(angB), np.cos(angR), np.sin(angR), np.cos(angC), np.sin(angC)], axis=1)
    return np.ascontiguousarray(out.astype(np.float32))


def make_in_maps(cfg, inputs):
    L = cfg.depth
    f = lambda k: np.asarray(inputs[k], dtype=np.float32)
    xp, xs = f("x_prompt"), f("x_sample")
    perm = _perm_w_in()
    w_in = np.ascontiguousarray(f("w_in")[:L][:, :, perm])
    w_out = np.ascontiguousarray(f("w_out")[:L])
    w_up = np.ascontiguousarray(f("w_up")[:L])
    w_dn = np.ascontiguousarray(f("w_down")[:L])
    pv = np.zeros((128, L, NPV), np.float32)
    for l in range(L):
        pv[:, l, PV_G1:PV_G1 + 8] = f("norm1_g")[l].reshape(8, 128).T
        pv[:, l, PV_G2:PV_G2 + 8] = f("norm2_g")[l].reshape(8, 128).T
        caw = f("conv_a_w")[l]
        pv[:, l, PV_CAW:PV_CAW + 62] = caw.reshape(31, 2, 128).transpose(2, 1, 0).reshape(128, 62)
        pv[:, l, PV_CAB:PV_CAB + 2] = f("conv_a_b")[l].reshape(2, 128).T
        pv[:, l, PV_LNG:PV_LNG + 2] = f("ln_a_g")[l].reshape(2, 128).T
        pv[:, l, PV_LNB:PV_LNB + 2] = f("ln_a_b")[l].reshape(2, 128).T
        pv[:, l, PV_SUB] = f("subln_b_g")[l]
        fw = f("conv_f_w")[l]
        pv[:, l, PV_FW:PV_FW + 132] = fw.reshape(3, 44, 128).transpose(2, 1, 0).reshape(128, 132)
        pv[:, l, PV_FB:PV_FB + 44] = f("conv_f_b")[l].reshape(44, 128).T
    pv = np.ascontiguousarray(pv.reshape(128, L * NPV))
    gt = np.zeros((L, 1408), np.float32)
    for l in range(L):
        gt[l] = np.concatenate([np.tile(f("qn_b_g")[l], 8), np.tile(f("kn_b_g")[l], 8), np.tile(f("qn_c_g")[l], 4), np.tile(f("kn_c_g")[l], 2)])
    lam = np.stack([f("lam_q1")[:L], f("lam_k1")[:L], f("lam_q2")[:L], f("lam_k2")[:L]], axis=1)
    lamv = np.ascontiguousarray(np.broadcast_to(lam.reshape(1, L * 4 * 64), (128, L * 4 * 64))).astype(np.float32)
    ident = np.eye(128, dtype=np.float32)
    in_maps = []
    for c in range(NCORES):
        p, r = c // RANKS, c % RANKS
        xpc = xp[p, r * cfg.ptok:(r + 1) * cfg.ptok]
        xsc = xs[c * cfg.ns:(c + 1) * cfg.ns].reshape(cfg.ns * cfg.seg, D)
        x_in = np.ascontiguousarray(np.concatenate([xpc, xsc], axis=0))
        pos = np.concatenate([np.arange(r * cfg.ptok, (r + 1) * cfg.ptok)] + [np.arange(cfg.seg)] * cfg.ns)
        rot = _rot_table(pos)
        sela = np.zeros((120, 30), np.float32)
        selh = np.zeros((8, 2), np.float32)
        if r > 0:
            for j in range(15):
                sela[(r - 1) * 30 + 15 + j, j] = 1.0
            selh[(r - 1) * 2 + 1, 0] = 1.0
        if r < RANKS - 1:
            for j in range(15):
                sela[(r + 1) * 30 + j, 15 + j] = 1.0
            selh[(r + 1) * 2 + 0, 1] = 1.0
        in_maps.append(dict(x_in=x_in, rot=rot, ident=ident, pv=pv, gt=gt, lamv=lamv, sela=sela, selh=selh,
                            w_in=w_in, w_out=w_out, w_up=w_up, w_down=w_dn))
    return in_maps


def assemble(cfg, results, nb_prompt, nb_sample):
    yp = np.zeros((nb_prompt, cfg.sp, D), np.float32)
    ys = np.zeros((nb_sample, cfg.seg, D), np.float32)
    for c in range(NCORES):
        y = np.asarray(results[c]["y_out"], dtype=np.float32).reshape(cfg.T, D)
        p, r = c // RANKS, c % RANKS
        yp[p, r * cfg.ptok:(r + 1) * cfg.ptok] = y[:cfg.ptok]
        ys[c * cfg.ns:(c + 1) * cfg.ns] = y[cfg.ptok:].reshape(cfg.ns, cfg.seg, D)
    return yp, ys


def run(cfg, inputs):
    nc, kk = build_program(cfg)
    in_maps = make_in_maps(cfg, inputs)
    res = run_bass_kernel_spmd(nc, in_maps, core_ids=list(range(NCORES)))
    return assemble(cfg, res.results, 2, NCORES * cfg.ns)


def kernel(**inputs):
    cfg = Cfg()
    return run(cfg, inputs)
```

```python
import math
from contextlib import ExitStack

import numpy as np
import ml_dtypes

import concourse.bass as bass
import concourse.mybir as mybir
from concourse.bass_utils import run_bass_kernel_spmd

F32 = mybir.dt.float32
BF16 = mybir.dt.bfloat16
AF = mybir.ActivationFunctionType
ALU = mybir.AluOpType
AX = mybir.AxisListType

D = 1024
INW = 2560
FF = 2816
EPS = 1e-6
NCORES = 8
RANKS = 4


class Cfg:
    def __init__(self, depth=4, seg=2048, np_=2, ns=4, use_cc=True, stop=99):
        self.stop = stop
        self.depth = depth
        self.seg = seg
        self.np = np_
        self.ns = ns
        self.ptok = np_ * seg
        self.T = (np_ + ns) * seg
        self.sp = RANKS * self.ptok
        self.use_cc = use_cc


EPOCH = 8000


class Counter:
    def __init__(self, kk, name, step):
        self.kk = kk
        self.name = name
        self.step = step
        self.count = 0
        self.sems = []
        self.per = EPOCH // step

    def sem_for(self, c):
        idx = (c - 1) // self.per
        while len(self.sems) <= idx:
            self.sems.append(self.kk.es.enter_context(self.kk.nc.semaphore(f"s_{self.name}{len(self.sems)}")))
        return self.sems[idx], (c - idx * self.per) * self.step


class Issuer:
    def __init__(self, name, h, cnt):
        self.name = name
        self.h = h
        self.cnt = cnt
        self.waited = {}


class Buf:
    def __init__(self, t):
        self.t = t
        self.w = None
        self.r = {}

    def __getitem__(self, idx):
        return self.t[idx]


class K:
    def __init__(self, nc):
        self.nc = nc
        self.es = ExitStack()
        self.counters = []
        mk = lambda n, s: self._mkc(n, s)
        self.PE = Issuer("pe", nc.tensor, mk("pe", 1))
        self.ACT = Issuer("act", nc.scalar, mk("act", 1))
        self.DVE = Issuer("dve", nc.vector, mk("dve", 1))
        self.POOL = Issuer("pool", nc.gpsimd, mk("pool", 1))
        self.SP = Issuer("sp", nc.sync, mk("spc", 1))
        self.issuers = [self.PE, self.ACT, self.DVE, self.POOL, self.SP]
        NL = 16
        self.q_sp = [mk(f"qsp{i}_", 16) for i in range(NL)]
        self.q_pool = [mk(f"qpool{i}_", 16) for i in range(NL)]
        self.q_cc = [mk(f"qcc{i}_", 1) for i in range(4)]
        self.rr = {"sp": 0, "pool": 0, "cc": 0}
        self.ninst = 0

    def lane(self, which):
        lst = {"sp": self.q_sp, "pool": self.q_pool, "cc": self.q_cc}[which]
        c = lst[self.rr[which] % len(lst)]
        self.rr[which] += 1
        return c

    def _mkc(self, n, s):
        c = Counter(self, n, s)
        self.counters.append(c)
        return c

    def _wait(self, iss, c, n):
        if n <= 0:
            return
        if iss.waited.get(c, 0) >= n:
            return
        if c is self.PE.cnt and iss is self.PE:
            return
        sem, val = c.sem_for(n)
        iss.h.wait_ge(sem, val)
        iss.waited[c] = n

    def emit(self, iss, fn, reads=(), writes=(), counter=None):
        c = counter if counter is not None else iss.cnt
        deps = {}

        def add(cn):
            cc, n = cn
            if deps.get(cc, 0) < n:
                deps[cc] = n

        for b in reads:
            if b.w is not None:
                add(b.w)
        for b in writes:
            if b.w is not None:
                add(b.w)
            for cc, n in b.r.items():
                add((cc, n))
        for cc, n in deps.items():
            self._wait(iss, cc, n)
        inst = fn()
        c.count += 1
        n = c.count
        sem, val = c.sem_for(n)
        inst.then_inc(sem, c.step)
        for b in reads:
            if b.r.get(c, 0) < n:
                b.r[c] = n
        for b in writes:
            b.w = (c, n)
            b.r = {}
        self.ninst += 1
        return inst

    def pe(self, fn, r=(), w=()):
        return self.emit(self.PE, fn, r, w)

    def act(self, fn, r=(), w=()):
        return self.emit(self.ACT, fn, r, w)

    def dve(self, fn, r=(), w=()):
        return self.emit(self.DVE, fn, r, w)

    def pool(self, fn, r=(), w=()):
        return self.emit(self.POOL, fn, r, w)

    def dma_sp(self, out, in_, r=(), w=(), **kw):
        return self.emit(self.SP, lambda: self.nc.sync.dma_start(out=out, in_=in_, **kw), r, w, counter=self.lane("sp"))

    def dma_pool(self, out, in_, r=(), w=(), **kw):
        return self.emit(self.POOL, lambda: self.nc.gpsimd.dma_start(out=out, in_=in_, **kw), r, w, counter=self.lane("pool"))

    def barrier(self):
        snap = [(c, c.count) for c in self.counters]
        for iss in self.issuers:
            for c, n in snap:
                self._wait(iss, c, n)

    def final_wait(self):
        snap = [(c, c.count) for c in self.counters]
        for c, n in snap:
            self._wait(self.SP, c, n)


PV_G1 = 0
PV_G2 = 8
PV_CAW = 16
PV_CAB = 78
PV_LNG = 80
PV_LNB = 82
PV_SUB = 84
PV_FW = 85
PV_FB = 217
NPV = 261


def build_program(cfg):
    nc = bass.Bass("TRN2", target_bir_lowering=False)
    kk = K(nc)
    es = kk.es
    L = cfg.depth
    T, SEG, PTOK = cfg.T, cfg.seg, cfg.ptok
    NTB = T // 128

    def din(name, shape, dt=F32):
        return nc.dram_tensor(name, list(shape), dt, kind="ExternalInput").ap()

    def dscr(name, shape, dt):
        return nc.dram_tensor(name, list(shape), dt, kind="Internal").ap()

    def dcc(name, shape, dt):
        return nc.dram_tensor(name, list(shape), dt).ap()

    x_in = din("x_in", [T, D])
    rot_in = din("rot", [T, 80])
    ident_in = din("ident", [128, 128])
    pv_in = din("pv", [128, L * NPV])
    gt_in = din("gt", [L, 1408])
    lam_in = din("lamv", [128, L * 4 * 64])
    sela_in = din("sela", [120, 30])
    selh_in = din("selh", [8, 2])
    w_in_d = din("w_in", [L, D, INW])
    w_out_d = din("w_out", [L, D, D])
    w_up_d = din("w_up", [L, D, 2 * FF])
    w_dn_d = din("w_down", [L, FF, D])
    y_out = nc.dram_tensor("y_out", [T, D], F32, kind="ExternalOutput").ap()

    wb_in = dscr("wb_in", [L, D, INW], BF16)
    wb_out = dscr("wb_out", [L, D, D], BF16)
    wb_up = dscr("wb_up", [L, D, 2 * FF], BF16)
    wb_dn = dscr("wb_dn", [L, FF, D], BF16)
    xm_d = dscr("xm", [T, D], F32)
    xa_d = dscr("xa", [T, D], F32)
    xb_d = dscr("xb", [T, D], F32)
    at_d = dscr("at", [2, 128, T], F32)
    qkt_d = dscr("qkt", [11, 128, T], BF16)
    v_d = dscr("v", [T, 640], BF16)
    cata_d = dscr("cata", [2, 128, T], BF16)
    h2t_d = dscr("h2t", [8, 128, T], BF16)
    kts_l = [dcc(f"kts{u}", [128, PTOK], BF16) for u in range(5)]
    vs_l = [dcc(f"vs{u}", [PTOK, 128], BF16) for u in range(5)]
    ahs_d = dcc("ahs", [30, 256], F32)
    hhs_d = dcc("hhs", [2, D], BF16)
    kta_l = [dcc(f"kta{u}", [RANKS * 128, PTOK], BF16) for u in range(5)]
    va_l = [dcc(f"va{u}", [RANKS * PTOK, 128], BF16) for u in range(5)]
    aha_d = dcc("aha", [RANKS * 30, 256], F32)
    hha_d = dcc("hha", [RANKS * 2, D], BF16)
    RG = [[0, 1, 2, 3], [4, 5, 6, 7]]

    uid = [0]

    def sb(stack, name, shape, dt):
        uid[0] += 1
        return Buf(stack.enter_context(nc.sbuf_tensor(f"sb{uid[0]}_{name}", list(shape), dt)))

    def ps(stack, name, shape, dt):
        uid[0] += 1
        return Buf(stack.enter_context(nc.psum_tensor(f"ps{uid[0]}_{name}", list(shape), dt)))

    ident_f = sb(es, "ident_f", [128, 128], F32)
    ident_b = sb(es, "ident_b", [128, 128], BF16)
    ones_b = sb(es, "ones_b", [128, 128], BF16)
    ones3 = sb(es, "ones3", [128, 192], BF16)
    onesA = sb(es, "onesA", [128, 128], F32)
    onesS = sb(es, "onesS", [128, 128], F32)
    onesF = sb(es, "onesF", [128, 128], F32)
    swapF = sb(es, "swapF", [128, 128], F32)
    epst = sb(es, "epst", [128, 1], F32)
    pv = sb(es, "pv", [128, L * NPV], F32)
    neglam = sb(es, "neglam", [128, L], F32)
    sela = sb(es, "sela", [120, 30], F32)
    selh = sb(es, "selh", [8, 2], BF16)

    def pvc(l, off, n=1):
        return pv[:, l * NPV + off: l * NPV + off + n]

    kk.dma_sp(ident_f[:], ident_in[:, :], w=[ident_f])
    kk.dma_sp(pv[:], pv_in[:, :], w=[pv])
    kk.dma_sp(sela[:], sela_in[:, :], w=[sela])
    kk.dve(lambda: nc.vector.tensor_copy(out=ident_b[:], in_=ident_f[:]), r=[ident_f], w=[ident_b])
    kk.dve(lambda: nc.vector.memset(ones_b[:], 1.0), w=[ones_b])
    kk.dve(lambda: nc.vector.memset(ones3[:], 0.0), w=[ones3])
    kk.dve(lambda: nc.vector.memset(ones3[:, 64:128], 1.0), w=[ones3])
    kk.dve(lambda: nc.vector.memset(onesA[:], 1.0 / 256), w=[onesA])
    kk.dve(lambda: nc.vector.memset(onesS[:], 1.0 / 128), w=[onesS])
    kk.dve(lambda: nc.vector.memset(epst[:], EPS), w=[epst])
    kk.dve(lambda: nc.vector.memset(onesF[:], 1.0), w=[onesF])
    kk.dve(lambda: nc.vector.tensor_copy(out=swapF[:, 0:64], in_=ident_f[:, 64:128]), r=[ident_f], w=[swapF])
    kk.dve(lambda: nc.vector.tensor_copy(out=swapF[:, 64:128], in_=ident_f[:, 0:64]), r=[ident_f], w=[swapF])
    with ExitStack() as st:
        lamv = sb(st, "lamv", [128, L * 4 * 64], F32)
        selh_f = sb(st, "selh_f", [8, 2], F32)
        lp = sb(st, "lp", [128, L * 2 * 64], F32)
        lsum = sb(st, "lsum", [128, L * 2], F32)
        kk.dma_sp(lamv[:], lam_in[:, :], w=[lamv])
        kk.dma_sp(selh_f[:], selh_in[:, :], w=[selh_f])
        kk.dve(lambda: nc.vector.tensor_copy(out=selh[:], in_=selh_f[:]), r=[selh_f], w=[selh])
        lv = lamv[:].rearrange("p (l f d) -> p l f d", l=L, f=4)
        lpv = lp[:].rearrange("p (l f d) -> p l f d", l=L, f=2)
        kk.dve(lambda: nc.vector.tensor_tensor(out=lpv[:, :, 0, :], in0=lv[:, :, 0, :], in1=lv[:, :, 1, :], op=ALU.mult), r=[lamv], w=[lp])
        kk.dve(lambda: nc.vector.tensor_tensor(out=lpv[:, :, 1, :], in0=lv[:, :, 2, :], in1=lv[:, :, 3, :], op=ALU.mult), r=[lamv], w=[lp])
        kk.dve(lambda: nc.vector.tensor_reduce(out=lsum[:], in_=lp[:].rearrange("p (g d) -> p g d", d=64), axis=AX.X, op=ALU.add), r=[lp], w=[lsum])
        kk.act(lambda: nc.scalar.activation(out=lsum[:], in_=lsum[:], func=AF.Exp), r=[lsum], w=[lsum])
        for l in range(L):
            lam_init = 0.8 - 0.6 * math.exp(-0.3 * l)
            kk.dve(lambda l=l, li=lam_init: nc.vector.scalar_tensor_tensor(
                out=neglam[:, l:l + 1], in0=lsum[:, 2 * l + 1:2 * l + 2], scalar=-li, in1=lsum[:, 2 * l:2 * l + 1],
                op0=ALU.add, op1=ALU.subtract), r=[lsum], w=[neglam])
        kk.barrier()

    with ExitStack() as st:
        CW = 2816
        stg = [sb(st, f"wstg{i}", [128, CW], F32) for i in range(3)]
        stb = [sb(st, f"wstb{i}", [128, CW], BF16) for i in range(3)]
        items = []
        for l in range(L):
            for kc in range(8):
                items.append((w_in_d[l, kc * 128:(kc + 1) * 128, :], wb_in[l, kc * 128:(kc + 1) * 128, :], INW, pvc(l, PV_G1 + kc)))
            for kc in range(8):
                items.append((w_out_d[l, kc * 128:(kc + 1) * 128, :], wb_out[l, kc * 128:(kc + 1) * 128, :], D, None))
            for kc in range(8):
                for hh in range(2):
                    items.append((w_up_d[l, kc * 128:(kc + 1) * 128, hh * FF:(hh + 1) * FF],
                                  wb_up[l, kc * 128:(kc + 1) * 128, hh * FF:(hh + 1) * FF], FF, pvc(l, PV_G2 + kc)))
            for kc in range(22):
                items.append((w_dn_d[l, kc * 128:(kc + 1) * 128, :], wb_dn[l, kc * 128:(kc + 1) * 128, :], D, None))
        for i, (src, dst, n, g) in enumerate(items):
            a, b = stg[i % 3], stb[i % 3]
            kk.dma_sp(a[:, 0:n], src, w=[a])
            if i % 2 == 0:
                if g is None:
                    kk.dve(lambda a=a, b=b, n=n: nc.vector.tensor_copy(out=b[:, 0:n], in_=a[:, 0:n]), r=[a], w=[b])
                else:
                    kk.dve(lambda a=a, b=b, n=n, g=g: nc.vector.tensor_scalar(out=b[:, 0:n], in0=a[:, 0:n], scalar1=g, scalar2=None, op0=ALU.mult), r=[a, pv], w=[b])
            else:
                if g is None:
                    kk.act(lambda a=a, b=b, n=n: nc.scalar.copy(out=b[:, 0:n], in_=a[:, 0:n]), r=[a], w=[b])
                else:
                    kk.act(lambda a=a, b=b, n=n, g=g: nc.scalar.activation(out=b[:, 0:n], in_=a[:, 0:n], func=AF.Copy, scale=g), r=[a, pv], w=[b])
            kk.dma_pool(dst, b[:, 0:n], r=[b])
        kk.barrier()

    def rmsnorm_to_T(st_bufs, xsrc, l_unused, hb, hT, psT, junk, ss, lnv):
        kk.dve(lambda: nc.vector.scalar_tensor_tensor(out=junk[:], in0=xsrc[:], scalar=1.0, in1=xsrc[:], op0=ALU.mult, op1=ALU.mult,
                                                      accum_out=ss[:]), r=[xsrc], w=[junk, ss])
        kk.act(lambda: nc.scalar.activation(out=lnv[:], in_=ss[:], func=AF.Ln, bias=epst[:], scale=1.0 / D), r=[ss, epst], w=[lnv])
        kk.act(lambda: nc.scalar.activation(out=lnv[:], in_=lnv[:], func=AF.Exp, scale=-0.5), r=[lnv], w=[lnv])
        kk.act(lambda: nc.scalar.activation(out=hb[:], in_=xsrc[:], func=AF.Copy, scale=lnv[:]), r=[xsrc, lnv], w=[hb])
        for kc in range(8):
            kk.pe(lambda kc=kc: nc.tensor.transpose(out=psT[:, kc * 128:(kc + 1) * 128], in_=hb[:, kc * 128:(kc + 1) * 128], identity=ident_b[:]),
                  r=[hb, ident_b], w=[psT])
        kk.dve(lambda: nc.vector.tensor_copy(out=hT[:], in_=psT[:, 0:1024]), r=[psT], w=[hT])

    prompt_blocks = PTOK // 128

    for l in range(L):
        lam_init = 0.8 - 0.6 * math.exp(-0.3 * l)
        x_src = x_in if l == 0 else (xa_d if l % 2 == 1 else xb_d)
        x_dst = y_out if l == L - 1 else (xa_d if l % 2 == 0 else xb_d)

        with ExitStack() as st:
          if cfg.stop >= 1:
            WIN = sb(st, "WIN", [128, 8 * INW], BF16)
            G = sb(st, "G", [128, 1408], F32)
            xt = [sb(st, f"xt{i}", [128, D], F32) for i in range(3)]
            rot = [sb(st, f"rot{i}", [128, 80], F32) for i in range(3)]
            qk2 = [sb(st, f"qk2{i}", [128, 1408], F32) for i in range(2)]
            vb2 = [sb(st, f"vb2{i}", [128, 640], BF16) for i in range(2)]
            atm2 = [sb(st, f"atm2{i}", [128, 256], F32) for i in range(2)]
            junk = sb(st, "junk", [128, D], BF16)
            ss = sb(st, "ss", [128, 1], F32)
            lnv = sb(st, "lnv", [128, 1], F32)
            hb = sb(st, "hb", [128, D], BF16)
            hT = sb(st, "hT", [128, D], BF16)
            eg = sb(st, "eg", [128, 256], F32)
            a_tm = sb(st, "a_tm", [128, 256], F32)
            aT = sb(st, "aT", [128, 256], F32)
            qk = sb(st, "qk", [128, 1408], F32)
            sq = sb(st, "sq", [128, 1408], F32)
            ssg = sb(st, "ssg", [128, 22], F32)
            rt = [sb(st, f"rt{i}", [128, 192], F32) for i in range(4)]
            qkb = sb(st, "qkb", [128, 1408], BF16)
            qkT = sb(st, "qkT", [128, 1408], BF16)
            vb = sb(st, "vb", [128, 640], BF16)
            psP = [ps(st, f"psP{j}", [128, 512], F32) for j in range(5)]
            psA = ps(st, "psA", [128, 256], F32)
            psQ = ps(st, "psQ", [128, 2048], BF16)
            psT = psQ

            for kc in range(8):
                kk.dma_sp(WIN[:, kc * INW:(kc + 1) * INW], wb_in[l, kc * 128:(kc + 1) * 128, :], w=[WIN])
            kk.dma_sp(G[:], gt_in[l:l + 1, :].to_broadcast([128, 1408]), w=[G])
            kk.dve(lambda: nc.vector.tensor_scalar(out=G[:, 0:512], in0=G[:, 0:512], scalar1=0.125, scalar2=None, op0=ALU.mult), r=[G], w=[G])
            kk.dve(lambda: nc.vector.tensor_scalar(out=G[:, 1024:1280], in0=G[:, 1024:1280], scalar1=0.125, scalar2=None, op0=ALU.mult), r=[G], w=[G])

            def load_blk(tb):
                b = tb % 3
                kk.dma_sp(xt[b][:], x_src[tb * 128:(tb + 1) * 128, :], w=[xt[b]])
                kk.dma_sp(rot[b][:], rot_in[tb * 128:(tb + 1) * 128, :], w=[rot[b]])

            def fh1(tb):
                xsrc = xt[tb % 3]
                kk.dve(lambda: nc.vector.scalar_tensor_tensor(out=junk[:], in0=xsrc[:], scalar=1.0, in1=xsrc[:], op0=ALU.mult, op1=ALU.mult,
                                                              accum_out=ss[:]), r=[xsrc], w=[junk, ss])
                kk.act(lambda: nc.scalar.activation(out=lnv[:], in_=ss[:], func=AF.Ln, bias=epst[:], scale=1.0 / D), r=[ss, epst], w=[lnv])
                kk.act(lambda: nc.scalar.activation(out=lnv[:], in_=lnv[:], func=AF.Exp, scale=-0.5), r=[lnv], w=[lnv])
                kk.act(lambda: nc.scalar.activation(out=hb[:], in_=xsrc[:], func=AF.Copy, scale=lnv[:]), r=[xsrc, lnv], w=[hb])
                for kc in range(8):
                    kk.pe(lambda kc=kc: nc.tensor.transpose(out=psT[:, kc * 128:(kc + 1) * 128], in_=hb[:, kc * 128:(kc + 1) * 128], identity=ident_b[:]),
                          r=[hb, ident_b], w=[psT])

            def fh2(tb):
                kk.dve(lambda: nc.vector.tensor_copy(out=hT[:], in_=psT[:, 0:1024]), r=[psT], w=[hT])
                for j in range(5):
                    for kc in range(8):
                        kk.pe(lambda j=j, kc=kc: nc.tensor.matmul(psP[j][:], lhsT=hT[:, kc * 128:(kc + 1) * 128],
                                                                  rhs=WIN[:, kc * INW + j * 512: kc * INW + (j + 1) * 512],
                                                                  start=(kc == 0), stop=(kc == 7)), r=[hT, WIN], w=[psP[j]])

            def front_tail(tb):
                b = tb % 2
                QK, VBb, ATM = qk2[b], vb2[b], atm2[b]
                kk.act(lambda: nc.scalar.activation(out=eg[:], in_=psP[0][:, 256:512], func=AF.Exp, scale=-1.0), r=[psP[0]], w=[eg])
                kk.act(lambda: nc.scalar.copy(out=QK[:, 0:512], in_=psP[1][:]), r=[psP[1]], w=[QK])
                kk.act(lambda: nc.scalar.copy(out=QK[:, 512:1024], in_=psP[2][:]), r=[psP[2]], w=[QK])
                kk.act(lambda: nc.scalar.copy(out=QK[:, 1024:1408], in_=psP[3][:, 0:384]), r=[psP[3]], w=[QK])
                kk.pool(lambda: nc.gpsimd.tensor_tensor(out=sq[:], in0=QK[:], in1=QK[:], op=ALU.mult), r=[QK], w=[sq])
                kk.act(lambda: nc.scalar.copy(out=VBb[:, 0:512], in_=psP[4][:]), r=[psP[4]], w=[VBb])
                kk.act(lambda: nc.scalar.copy(out=VBb[:, 512:640], in_=psP[3][:, 384:512]), r=[psP[3]], w=[VBb])
                kk.dve(lambda: nc.vector.tensor_scalar(out=eg[:], in0=eg[:], scalar1=1.0, scalar2=None, op0=ALU.add), r=[eg], w=[eg])
                kk.dve(lambda: nc.vector.reciprocal(out=eg[:], in_=eg[:]), r=[eg], w=[eg])
                kk.dve(lambda: nc.vector.tensor_tensor(out=ATM[:], in0=psP[0][:, 0:256], in1=eg[:], op=ALU.mult), r=[psP[0], eg], w=[ATM])

            def back(tb):
                b = tb % 2
                t0 = tb * 128
                R = rot[tb % 3]
                qk, vb, a_tm = qk2[b], vb2[b], atm2[b]
                for c in range(2):
                    kk.pe(lambda c=c: nc.tensor.transpose(out=psA[:, c * 128:(c + 1) * 128], in_=a_tm[:, c * 128:(c + 1) * 128], identity=ident_f[:]),
                          r=[a_tm, ident_f], w=[psA])
                kk.act(lambda: nc.scalar.copy(out=aT[:], in_=psA[:]), r=[psA], w=[aT])
                kk.dma_pool(at_d[:, :, t0:t0 + 128].rearrange("c p t -> p c t"), aT[:].rearrange("p (c t) -> p c t", c=2), r=[aT])
                if tb == 0:
                    kk.dma_pool(ahs_d[0:15, :], a_tm[0:15, :], r=[a_tm])
                if tb == prompt_blocks - 1:
                    kk.dma_pool(ahs_d[15:30, :], a_tm[113:128, :], r=[a_tm])

            def back_qk(tb):
                b = tb % 2
                t0 = tb * 128
                R = rot[tb % 3]
                qk, vb, a_tm = qk2[b], vb2[b], atm2[b]
                kk.dve(lambda: nc.vector.tensor_reduce(out=ssg[:], in_=sq[:].rearrange("p (g d) -> p g d", d=64), axis=AX.X, op=ALU.add), r=[sq], w=[ssg])
                kk.act(lambda: nc.scalar.activation(out=ssg[:], in_=ssg[:], func=AF.Ln, bias=epst[:], scale=1.0 / 64), r=[ssg, epst], w=[ssg])
                kk.act(lambda: nc.scalar.activation(out=ssg[:], in_=ssg[:], func=AF.Exp, scale=-0.5), r=[ssg], w=[ssg])

            def bq2(tb):
                b = tb % 2
                t0 = tb * 128
                R = rot[tb % 3]
                qk, vb, a_tm = qk2[b], vb2[b], atm2[b]
                qk3 = qk[:].rearrange("p (g d) -> p g d", d=64)
                kk.dve(lambda: nc.vector.tensor_tensor(out=qk3, in0=qk3, in1=ssg[:].unsqueeze(2).to_broadcast([128, 22, 64]), op=ALU.mult), r=[qk, ssg], w=[qk])
                kk.dve(lambda: nc.vector.tensor_tensor(out=qk[:], in0=qk[:], in1=G[:], op=ALU.mult), r=[qk, G], w=[qk])
                qB = qk[:, 0:1024].rearrange("p (g d) -> p g d", d=64)
                x1, x2 = qB[:, :, 0:8], qB[:, :, 8:16]
                cB = R[:, 0:8].unsqueeze(1).to_broadcast([128, 16, 8])
                sB = R[:, 8:16].unsqueeze(1).to_broadcast([128, 16, 8])
                tv = [rt[i][:, 0:128].rearrange("p (g d) -> p g d", d=8) for i in range(4)]
                kk.dve(lambda: nc.vector.tensor_tensor(out=tv[0], in0=x1, in1=cB, op=ALU.mult), r=[qk, R], w=[rt[0]])
                kk.dve(lambda: nc.vector.tensor_tensor(out=tv[1], in0=x2, in1=sB, op=ALU.mult), r=[qk, R], w=[rt[1]])
                kk.dve(lambda: nc.vector.tensor_tensor(out=tv[2], in0=x2, in1=cB, op=ALU.mult), r=[qk, R], w=[rt[2]])
                kk.dve(lambda: nc.vector.tensor_tensor(out=tv[3], in0=x1, in1=sB, op=ALU.mult), r=[qk, R], w=[rt[3]])
                kk.dve(lambda: nc.vector.tensor_tensor(out=x1, in0=tv[0], in1=tv[1], op=ALU.subtract), r=[rt[0], rt[1]], w=[qk])
                kk.dve(lambda: nc.vector.tensor_tensor(out=x2, in0=tv[2], in1=tv[3], op=ALU.add), r=[rt[2], rt[3]], w=[qk])
                qC = qk[:, 1024:1408].rearrange("p (g h x d) -> p g h x d", g=6, h=2, x=2)
                y1, y2 = qC[:, :, :, 0, :], qC[:, :, :, 1, :]
                RC = R[:, 16:80].rearrange("p (h x d) -> p h x d", h=2, x=2)
                cC = RC[:, :, 0, :].unsqueeze(1).to_broadcast([128, 6, 2, 16])
                sC = RC[:, :, 1, :].unsqueeze(1).to_broadcast([128, 6, 2, 16])
                tw = [rt[i][:, 0:192].rearrange("p (g h d) -> p g h d", g=6, h=2) for i in range(4)]
                kk.dve(lambda: nc.vector.tensor_tensor(out=tw[0], in0=y1, in1=cC, op=ALU.mult), r=[qk, R], w=[rt[0]])
                kk.dve(lambda: nc.vector.tensor_tensor(out=tw[1], in0=y2, in1=sC, op=ALU.mult), r=[qk, R], w=[rt[1]])
                kk.dve(lambda: nc.vector.tensor_tensor(out=tw[2], in0=y2, in1=cC, op=ALU.mult), r=[qk, R], w=[rt[2]])
                kk.dve(lambda: nc.vector.tensor_tensor(out=tw[3], in0=y1, in1=sC, op=ALU.mult), r=[qk, R], w=[rt[3]])
                kk.dve(lambda: nc.vector.tensor_tensor(out=y1, in0=tw[0], in1=tw[1], op=ALU.subtract), r=[rt[0], rt[1]], w=[qk])
                kk.dve(lambda: nc.vector.tensor_tensor(out=y2, in0=tw[2], in1=tw[3], op=ALU.add), r=[rt[2], rt[3]], w=[qk])
                kk.act(lambda: nc.scalar.copy(out=qkb[:], in_=qk[:]), r=[qk], w=[qkb])
                for c in range(11):
                    kk.pe(lambda c=c: nc.tensor.transpose(out=psQ[:, c * 128:(c + 1) * 128], in_=qkb[:, c * 128:(c + 1) * 128], identity=ident_b[:]),
                          r=[qkb, ident_b], w=[psQ])

            def bq3(tb):
                b = tb % 2
                t0 = tb * 128
                qk, vb, a_tm = qk2[b], vb2[b], atm2[b]
                kk.dve(lambda: nc.vector.tensor_copy(out=qkT[:], in_=psQ[:, 0:1408]), r=[psQ], w=[qkT])
                kk.dma_pool(qkt_d[:, :, t0:t0 + 128].rearrange("c p t -> p c t"), qkT[:].rearrange("p (c t) -> p c t", c=11), r=[qkT])
                kk.dma_pool(v_d[t0:t0 + 128, :], vb[:], r=[vb])
                if tb < prompt_blocks:
                    for u in range(5):
                        kc0 = 512 + u * 128 if u < 4 else 1280
                        kk.dma_pool(kts_l[u][:, t0:t0 + 128], qkT[:, kc0:kc0 + 128], r=[qkT])
                        kk.dma_pool(vs_l[u][t0:t0 + 128, :], vb[:, u * 128:(u + 1) * 128], r=[vb])

            load_blk(0)
            if NTB > 1:
                load_blk(1)
            fh1(0)
            fh2(0)
            front_tail(0)
            for tb in range(NTB):
                if tb + 2 < NTB:
                    load_blk(tb + 2)
                back(tb)
                nxt = tb + 1 < NTB
                if nxt:
                    fh1(tb + 1)
                back_qk(tb)
                if nxt:
                    fh2(tb + 1)
                bq2(tb)
                if nxt:
                    front_tail(tb + 1)
                bq3(tb)
            kk.barrier()

        if cfg.use_cc and cfg.stop >= 2:
            for src, dst in [(ahs_d, aha_d)] + [(kts_l[u], kta_l[u]) for u in range(5)] + [(vs_l[u], va_l[u]) for u in range(5)]:
                kk.emit(kk.POOL, lambda src=src, dst=dst: nc.gpsimd.collective_compute(
                    "AllGather", ALU.bypass, replica_groups=RG, ins=[src.opt()], outs=[dst.opt()]), counter=kk.lane("cc"))

        with ExitStack() as st:
          if cfg.stop >= 3:
            abuf = [sb(st, f"abuf{i}", [128, SEG + 30], F32) for i in range(2)]
            convo = [sb(st, f"convo{i}", [128, SEG], F32) for i in range(2)]
            sqb = [sb(st, f"sqb{i}", [128, 512], F32) for i in range(2)]
            mean_sb = sb(st, "mean_sb", [128, 512], F32)
            m2 = sb(st, "m2", [128, 512], F32)
            rstd = sb(st, "rstd", [128, 512], F32)
            dd = [sb(st, f"dd{i}", [128, 512], F32) for i in range(2)]
            ee = [sb(st, f"ee{i}", [128, 512], F32) for i in range(2)]
            ob = [sb(st, f"ob{i}", [128, 512], BF16) for i in range(2)]
            ahr = sb(st, "ahr", [120, 256], F32)
            ps_mean = ps(st, "ps_mean", [128, 512], F32)
            ps_msq = ps(st, "ps_msq", [128, 512], F32)
            ps_halo = ps(st, "ps_halo", [128, 64], F32)
            nseg = cfg.np + cfg.ns
            for s in list(range(cfg.np, nseg)) + list(range(cfg.np)):
                if s == 0:
                    kk.barrier()
                    if cfg.use_cc:
                        kk.dma_sp(ahr[:], aha_d[:, :], w=[ahr])
                t0 = s * SEG
                is_p = s < cfg.np
                for c in range(2):
                    A = abuf[c]
                    left_local = is_p and s > 0
                    right_local = is_p and s < cfg.np - 1
                    lo = t0 - 15 if left_local else t0
                    hi = t0 + SEG + 15 if right_local else t0 + SEG
                    kk.dma_sp(A[:, 15 + (lo - t0): 15 + (hi - t0)], at_d[c, :, lo:hi], w=[A])
                    if not left_local:
                        kk.pool(lambda A=A: nc.gpsimd.memset(A[:, 0:15], 0.0), w=[A])
                    if not right_local:
                        kk.pool(lambda A=A: nc.gpsimd.memset(A[:, SEG + 15:SEG + 30], 0.0), w=[A])
                    if cfg.use_cc and is_p and (s == 0 or s == cfg.np - 1):
                        kk.pe(lambda c=c: nc.tensor.matmul(ps_halo[:, 0:30], lhsT=ahr[:, c * 128:(c + 1) * 128], rhs=sela[:], start=True, stop=True),
                              r=[ahr, sela], w=[ps_halo])
                        if s == 0:
                            kk.act(lambda A=A: nc.scalar.copy(out=A[:, 0:15], in_=ps_halo[:, 0:15]), r=[ps_halo], w=[A])
                        if s == cfg.np - 1:
                            kk.act(lambda A=A: nc.scalar.copy(out=A[:, SEG + 15:SEG + 30], in_=ps_halo[:, 15:30]), r=[ps_halo], w=[A])
                    CO = convo[c]
                    kk.dve(lambda A=A, CO=CO, c=c: nc.vector.tensor_scalar(out=CO[:], in0=A[:, 0:SEG], scalar1=pvc(l, PV_CAW + c * 31),
                                                                          scalar2=pvc(l, PV_CAB + c), op0=ALU.mult, op1=ALU.add), r=[A, pv], w=[CO])
                    for j in range(1, 31):
                        kk.dve(lambda A=A, CO=CO, c=c, j=j: nc.vector.scalar_tensor_tensor(out=CO[:], in0=A[:, j:j + SEG], scalar=pvc(l, PV_CAW + c * 31 + j),
                                                                                         in1=CO[:], op0=ALU.mult, op1=ALU.add), r=[A, pv, CO], w=[CO])
                for ti in range(SEG // 512):
                    c0 = ti * 512
                    for c in range(2):
                        kk.pool(lambda c=c: nc.gpsimd.tensor_tensor(out=sqb[c][:], in0=convo[c][:, c0:c0 + 512], in1=convo[c][:, c0:c0 + 512], op=ALU.mult),
                                r=[convo[c]], w=[sqb[c]])
                    for c in range(2):
                        kk.pe(lambda c=c: nc.tensor.matmul(ps_mean[:], lhsT=onesA[:], rhs=convo[c][:, c0:c0 + 512], start=(c == 0), stop=(c == 1)),
                              r=[onesA, convo[c]], w=[ps_mean])
                    for c in range(2):
                        kk.pe(lambda c=c: nc.tensor.matmul(ps_msq[:], lhsT=onesA[:], rhs=sqb[c][:], start=(c == 0), stop=(c == 1)),
                              r=[onesA, sqb[c]], w=[ps_msq])
                    kk.act(lambda: nc.scalar.copy(out=mean_sb[:], in_=ps_mean[:]), r=[ps_mean], w=[mean_sb])
                    kk.dve(lambda: nc.vector.tensor_tensor(out=m2[:], in0=mean_sb[:], in1=mean_sb[:], op=ALU.mult), r=[mean_sb], w=[m2])
                    kk.dve(lambda: nc.vector.tensor_tensor(out=m2[:], in0=ps_msq[:], in1=m2[:], op=ALU.subtract), r=[ps_msq, m2], w=[m2])
                    kk.dve(lambda: nc.vector.tensor_scalar(out=m2[:], in0=m2[:], scalar1=0.0, scalar2=None, op0=ALU.max), r=[m2], w=[m2])
                    kk.act(lambda: nc.scalar.activation(out=rstd[:], in_=m2[:], func=AF.Ln, bias=epst[:], scale=1.0), r=[m2, epst], w=[rstd])
                    kk.act(lambda: nc.scalar.activation(out=rstd[:], in_=rstd[:], func=AF.Exp, scale=-0.5), r=[rstd], w=[rstd])
                    for c in range(2):
                        Dd, Ee, Ob = dd[c], ee[c], ob[c]
                        kk.dve(lambda c=c, Dd=Dd: nc.vector.tensor_tensor(out=Dd[:], in0=convo[c][:, c0:c0 + 512], in1=mean_sb[:], op=ALU.subtract),
                               r=[convo[c], mean_sb], w=[Dd])
                        kk.dve(lambda Dd=Dd: nc.vector.tensor_tensor(out=Dd[:], in0=Dd[:], in1=rstd[:], op=ALU.mult), r=[Dd, rstd], w=[Dd])
                        kk.dve(lambda c=c, Dd=Dd: nc.vector.tensor_scalar(out=Dd[:], in0=Dd[:], scalar1=pvc(l, PV_LNG + c), scalar2=pvc(l, PV_LNB + c),
                                                                         op0=ALU.mult, op1=ALU.add), r=[Dd, pv], w=[Dd])
                        kk.act(lambda Dd=Dd, Ee=Ee: nc.scalar.activation(out=Ee[:], in_=Dd[:], func=AF.Exp, scale=-1.0), r=[Dd], w=[Ee])
                        kk.pool(lambda Ee=Ee: nc.gpsimd.tensor_scalar(out=Ee[:], in0=Ee[:], scalar1=1.0, scalar2=1.0, op0=ALU.add, op1=ALU.mult), r=[Ee], w=[Ee])
                        kk.dve(lambda Ee=Ee: nc.vector.reciprocal(out=Ee[:], in_=Ee[:]), r=[Ee], w=[Ee])
                        kk.pool(lambda Dd=Dd, Ee=Ee, Ob=Ob: nc.gpsimd.tensor_tensor(out=Ob[:], in0=Dd[:], in1=Ee[:], op=ALU.mult), r=[Dd, Ee], w=[Ob])
                        kk.dma_pool(cata_d[c, :, t0 + c0:t0 + c0 + 512], Ob[:], r=[Ob])
            kk.barrier()

        with ExitStack() as st:
          if cfg.stop >= 4:
            LKMAX = cfg.sp if cfg.use_cc else max(PTOK, SEG)
            NCKMAX = LKMAX // 128
            GTOK = max(PTOK, SEG)
            KT = sb(st, "KT", [128, LKMAX], BF16)
            VB = sb(st, "VB", [128, NCKMAX * 192], BF16)
            catT = sb(st, "catT", [128, 8 * GTOK], BF16)
            WOUT = sb(st, "WOUT", [128, 8 * D], BF16)
            Qa = [sb(st, f"Qa{i}", [128, 512], BF16) for i in range(2)]
            Qb = [sb(st, f"Qb{i}", [128, 512], BF16) for i in range(2)]
            NPT = 4
            PT = [sb(st, f"PT{i}", [128, 1024], BF16) for i in range(NPT)]
            acc = sb(st, "acc", [128, 1024], F32)
            accp = sb(st, "accp", [128, 1024], F32)
            fo = [sb(st, f"fo{i}", [128, 512], F32) for i in range(2)]
            fl = [sb(st, f"fl{i}", [128, 512], F32) for i in range(2)]
            fd = sb(st, "fd", [128, 512], F32)
            fsq = fl[1]
            frs = fl[0]
            xt = [accp, accp]
            xm = acc
            ss = sb(st, "css", [128, 1], F32)
            lnv = sb(st, "clnv", [128, 1], F32)
            hb = sb(st, "chb", [128, D], BF16)
            hT = sb(st, "chT", [128, D], BF16)
            junk = hb
            sc = [ps(st, f"sc{i}", [128, 1024], F32) for i in range(2)]
            po = [ps(st, f"po{m}", [128, 512], F32) for m in range(2)]
            pf = [ps(st, f"pf{m}", [128, 512], F32) for m in range(2)]

            for kc in range(8):
                kk.dma_sp(WOUT[:, kc * D:(kc + 1) * D], wb_out[l, kc * 128:(kc + 1) * 128, :], w=[WOUT])

            groups = [("p", 0, PTOK)] + [("s", PTOK + i * SEG, SEG) for i in range(cfg.ns)]
            scslot = [0]
            ptslot = [0]
            pend = {}

            def flush_pending():
                if pend.get("sums"):
                    pend["sums"]()
                    pend["sums"] = None
                if pend.get("st23"):
                    pend["st23"]()
                    pend["st23"] = None
                if pend.get("st3"):
                    pend["st3"]()
                    pend["st3"] = None
            for (gk, g0, gn) in groups:
                use_all = (gk == "p" and cfg.use_cc)
                Lk = cfg.sp if use_all else gn
                nck = Lk // 128
                VB3 = VB[:, 0:nck * 192].rearrange("p (c e) -> p c e", e=192)
                for c in range(2):
                    kk.dma_sp(catT[:, c * GTOK: c * GTOK + gn], cata_d[c, :, g0:g0 + gn], w=[catT])
                for u in range(6):
                    isB = u < 4
                    kchunk = u if isB else 4
                    if use_all:
                        for r in range(RANKS):
                            kk.dma_sp(KT[:, r * PTOK:(r + 1) * PTOK], kta_l[kchunk][r * 128:(r + 1) * 128, :], w=[KT])
                    else:
                        kk.dma_sp(KT[:, 0:Lk], qkt_d[4 + u if isB else 10, :, g0:g0 + gn], w=[KT])
                    if isB:
                        vcols = slice(u * 128, (u + 1) * 128)
                        vdst = lambda c0, c1: VB3[:, c0:c1, 0:128]
                    else:
                        g = u - 4
                        vcols = slice(512 + g * 64, 512 + (g + 1) * 64)
                        vdst = lambda c0, c1: VB3[:, c0:c1, 64:128]
                        kk.pool(lambda: nc.gpsimd.memset(VB3[:, :, 0:64], 1.0), w=[VB])
                        kk.pool(lambda: nc.gpsimd.memset(VB3[:, :, 128:192], 1.0), w=[VB])
                    if use_all:
                        vsrc = va_l[kchunk]
                        voff = 0
                        vcols = slice(0, 128) if isB else slice(g * 64, (g + 1) * 64)
                    else:
                        vsrc = v_d
                        voff = g0
                    for c0 in range(0, nck, 16):
                        c1 = min(nck, c0 + 16)
                        kk.dma_sp(vdst(c0, c1), vsrc[voff + c0 * 128: voff + c1 * 128, vcols].rearrange("(c p) e -> p c e", p=128), w=[VB])
                    if isB:
                        qrows = [slice(0, 64), slice(64, 128)]
                    else:
                        qrows = [slice(g * 64, (g + 1) * 64)] * 2
                    if u == 0 or u >= 4:
                        for qi_ in range(2):
                            for m_, QQ in enumerate((Qa[qi_], Qb[qi_])):
                                zr = slice(64, 128) if qrows[m_].start == 0 else slice(0, 64)
                                kk.pool(lambda QQ=QQ, zr=zr: nc.gpsimd.memset(QQ[zr, :], 0.0), w=[QQ])
                    if isB:
                        lhs_v = [lambda ck: VB3[:, ck, 0:128], lambda ck: VB3[:, ck, 0:128]]
                        rows = [slice(0, 64), slice(64, 128)]
                    else:
                        lhs_v = [lambda ck: VB3[:, ck, 64:192], lambda ck: VB3[:, ck, 0:128]]
                        rows = [slice(g * 64, (g + 1) * 64)] * 2
                    for qt in range(gn // 512):
                        q0 = g0 + qt * 512
                        qi = qt % 2
                        if isB:
                            kk.dma_sp(Qa[qi][0:64, :], qkt_d[u, 0:64, q0:q0 + 512], w=[Qa[qi]])
                            kk.dma_sp(Qb[qi][64:128, :], qkt_d[u, 64:128, q0:q0 + 512], w=[Qb[qi]])
                        else:
                            kk.dma_sp(Qa[qi][qrows[0], :], qkt_d[8, qrows[0], q0:q0 + 512], w=[Qa[qi]])
                            kk.dma_sp(Qb[qi][qrows[1], :], qkt_d[9, qrows[1], q0:q0 + 512], w=[Qb[qi]])
                        qbufs = [Qa[qi], Qb[qi]]
                        slots = {}

                        def scores(ck):
                            sl = scslot[0] % 2
                            scslot[0] += 1
                            slots[ck] = sl
                            for m in range(2):
                                kk.pe(lambda m=m, sl=sl, ck=ck: nc.tensor.matmul(sc[sl][:, m * 512:(m + 1) * 512], lhsT=KT[:, ck * 128:(ck + 1) * 128],
                                                                                 rhs=qbufs[m][:, :], start=True, stop=True),
                                      r=[KT, qbufs[m]], w=[sc[sl]])

                        scores(0)
                        for ck in range(nck):
                            if ck + 1 < nck:
                                scores(ck + 1)
                            if ck == min(2, nck - 1) and pend.get("sums"):
                                pend["sums"]()
                                pend["sums"] = None
                            if ck == min(6, nck - 1):
                                if pend.get("sums"):
                                    pend["sums"]()
                                    pend["sums"] = None
                                if pend.get("st23"):
                                    pend["st23"]()
                                    pend["st23"] = None
                            if ck == min(11, nck - 1) and pend.get("st3"):
                                if pend.get("st23"):
                                    pend["st23"]()
                                    pend["st23"] = None
                                pend["st3"]()
                                pend["st3"] = None
                            ssl = slots.pop(ck)
                            sl = ptslot[0] % NPT
                            ptslot[0] += 1
                            kk.act(lambda sl=sl, ssl=ssl: nc.scalar.activation(out=PT[sl][:], in_=sc[ssl][:], func=AF.Exp), r=[sc[ssl]], w=[PT[sl]])
                            def emit_pv(ckk, sll):
                                for m in range(2):
                                    kk.pe(lambda m=m: nc.tensor.matmul(
                                        po[m][:], lhsT=lhs_v[m](ckk), rhs=PT[sll][:, m * 512:(m + 1) * 512], start=(ckk == 0), stop=(ckk == nck - 1)),
                                        r=[VB, PT[sll]], w=[po[m]])
                            if ck >= 1:
                                emit_pv(ck - 1, pv_prev_sl)
                            pv_prev_sl = sl
                            if ck == nck - 1:
                                emit_pv(ck, sl)
                            if isB:
                                tb16 = accp[:].bitcast(BF16)
                                t01, t23 = tb16[:, 0:1024], tb16[:, 1024:2048]
                                if ck % 4 == 0:
                                    prev_sl = sl
                                elif ck % 4 == 1:
                                    kk.dve(lambda a_=prev_sl, b_=sl: nc.vector.tensor_tensor(out=t01, in0=PT[a_][:], in1=PT[b_][:], op=ALU.add),
                                           r=[PT[prev_sl], PT[sl]], w=[accp])
                                elif ck % 4 == 2:
                                    prev_sl = sl
                                else:
                                    kk.dve(lambda a_=prev_sl, b_=sl: nc.vector.tensor_tensor(out=t23, in0=PT[a_][:], in1=PT[b_][:], op=ALU.add),
                                           r=[PT[prev_sl], PT[sl]], w=[accp])
                                    kk.dve(lambda: nc.vector.tensor_tensor(out=t01, in0=t01, in1=t23, op=ALU.add), r=[accp], w=[accp])
                                    if ck == 3:
                                        kk.dve(lambda: nc.vector.tensor_copy(out=acc[:], in_=t01), r=[accp], w=[acc])
                                    else:
                                        kk.dve(lambda: nc.vector.tensor_tensor(out=acc[:], in0=acc[:], in1=t01, op=ALU.add), r=[accp, acc], w=[acc])
                        cchunk = 2 + u if isB else 6 + (u - 4)
                        dst = catT[:, cchunk * GTOK + qt * 512: cchunk * GTOK + (qt + 1) * 512]
                        kk.act(lambda: nc.scalar.copy(out=fo[0][:], in_=po[0][:]), r=[po[0]], w=[fo[0]])
                        kk.dve(lambda: nc.vector.tensor_copy(out=fo[1][:], in_=po[1][:]), r=[po[1]], w=[fo[1]])
                        if isB:
                            def st_sums():
                                for m in range(2):
                                    kk.pe(lambda m=m: nc.tensor.matmul(pf[m][:], lhsT=onesF[:], rhs=acc[:, m * 512:(m + 1) * 512], start=True, stop=True),
                                          r=[onesF, acc], w=[pf[m]])

                            def st23(dst=dst):
                                for m in range(2):
                                    kk.act(lambda m=m: nc.scalar.activation(out=fl[m][:], in_=pf[m][:], func=AF.Ln), r=[pf[m]], w=[fl[m]])
                                    kk.act(lambda m=m: nc.scalar.activation(out=fl[m][:], in_=fl[m][:], func=AF.Exp, scale=-1.0), r=[fl[m]], w=[fl[m]])
                                    kk.dve(lambda m=m: nc.vector.tensor_tensor(out=fo[m][:], in0=fo[m][:], in1=fl[m][:], op=ALU.mult), r=[fo[m], fl[m]], w=[fo[m]])
                                kk.dve(lambda: nc.vector.scalar_tensor_tensor(out=fd[:], in0=fo[1][:], scalar=neglam[:, l:l + 1], in1=fo[0][:],
                                                                              op0=ALU.mult, op1=ALU.add), r=[fo[0], fo[1], neglam], w=[fd])
                                kk.dve(lambda: nc.vector.tensor_tensor(out=fsq[:], in0=fd[:], in1=fd[:], op=ALU.mult), r=[fd], w=[fsq])

                            def st3(dst=dst):
                                kk.pe(lambda: nc.tensor.matmul(pf[0][:], lhsT=onesS[:], rhs=fsq[:], start=True, stop=True), r=[onesS, fsq], w=[pf[0]])
                                kk.act(lambda: nc.scalar.activation(out=frs[:], in_=pf[0][:], func=AF.Ln, bias=epst[:], scale=1.0), r=[pf[0], epst], w=[frs])
                                kk.act(lambda: nc.scalar.activation(out=frs[:], in_=frs[:], func=AF.Exp, scale=-0.5), r=[frs], w=[frs])
                                kk.dve(lambda: nc.vector.tensor_tensor(out=fd[:], in0=fd[:], in1=frs[:], op=ALU.mult), r=[fd, frs], w=[fd])
                                kk.dve(lambda: nc.vector.tensor_scalar(out=dst, in0=fd[:], scalar1=pvc(l, PV_SUB), scalar2=1.0 - lam_init,
                                                                      op0=ALU.mult, op1=ALU.mult), r=[fd, pv], w=[catT])
                        else:
                            st_sums = None

                            def st23(dst=dst):
                                for m in range(2):
                                    lr = slice(64, 128) if m == 0 else slice(0, 64)
                                    orr = slice(0, 64) if m == 0 else slice(64, 128)
                                    kk.act(lambda m=m, lr=lr: nc.scalar.activation(out=fl[m][lr, :], in_=fo[m][lr, :], func=AF.Ln), r=[fo[m]], w=[fl[m]])
                                    kk.act(lambda m=m, lr=lr: nc.scalar.activation(out=fl[m][lr, :], in_=fl[m][lr, :], func=AF.Exp, scale=-1.0), r=[fl[m]], w=[fl[m]])
                                    kk.pool(lambda m=m, orr=orr: nc.gpsimd.memset(fl[m][orr, :], 0.0), w=[fl[m]])

                            def st3(dst=dst):
                                for m in range(2):
                                    kk.pe(lambda m=m: nc.tensor.matmul(pf[m][:], lhsT=swapF[:], rhs=fl[m][:], start=True, stop=True), r=[swapF, fl[m]], w=[pf[m]])
                                kk.dve(lambda: nc.vector.tensor_tensor(out=dst[0:64, :], in0=fo[0][0:64, :], in1=pf[0][0:64, :], op=ALU.mult),
                                       r=[fo[0], pf[0]], w=[catT])
                                kk.dve(lambda: nc.vector.tensor_tensor(out=dst[64:128, :], in0=fo[1][64:128, :], in1=pf[1][64:128, :], op=ALU.mult),
                                       r=[fo[1], pf[1]], w=[catT])
                        pend["sums"] = st_sums
                        pend["st23"] = st23
                        pend["st3"] = st3
                flush_pending()
                nblk = gn // 128

                def outproj(bi, banks):
                    for jn in range(2):
                        for kc in range(8):
                            kk.pe(lambda jn=jn, kc=kc: nc.tensor.matmul(banks[jn][:], lhsT=catT[:, kc * GTOK + bi * 128: kc * GTOK + (bi + 1) * 128],
                                                                        rhs=WOUT[:, kc * D + jn * 512: kc * D + (jn + 1) * 512],
                                                                        start=(kc == 0), stop=(kc == 7)), r=[catT, WOUT], w=[banks[jn]])

                outproj(0, po)
                for bi in range(nblk):
                    t0 = g0 + bi * 128
                    X = xt[0]
                    banks = po if bi % 2 == 0 else pf
                    if bi + 1 < nblk:
                        outproj(bi + 1, pf if bi % 2 == 0 else po)
                    kk.dma_sp(X[:], x_src[t0:t0 + 128, :], w=[X])
                    for jn in range(2):
                        kk.dve(lambda jn=jn, X=X, banks=banks: nc.vector.tensor_tensor(out=xm[:, jn * 512:(jn + 1) * 512], in0=banks[jn][:],
                                                                                      in1=X[:, jn * 512:(jn + 1) * 512], op=ALU.add),
                               r=[banks[jn], X], w=[xm])
                    kk.dma_pool(xm_d[t0:t0 + 128, :], xm[:], r=[xm])
                    psT = sc[0]
                    psT_bf = psT[:, 0:512].bitcast(BF16)
                    rmsnorm_to_T_c1(kk, nc, xm, hb, hT, psT, psT_bf, junk, ss, lnv, epst, ident_b)
                    kk.dma_pool(h2t_d[:, :, t0:t0 + 128].rearrange("c p t -> p c t"), hT[:].rearrange("p (c t) -> p c t", c=8), r=[hT])
                    if gk == "p" and bi == 0:
                        kk.dma_pool(hhs_d[0:1, :], hb[0:1, :], r=[hb])
                    if gk == "p" and bi == nblk - 1:
                        kk.dma_pool(hhs_d[1:2, :], hb[127:128, :], r=[hb])
            kk.barrier()

        if cfg.use_cc and cfg.stop >= 5:
            cc2_lane = kk.lane("cc")
            kk.emit(kk.POOL, lambda: nc.gpsimd.collective_compute(
                "AllGather", ALU.bypass, replica_groups=RG, ins=[hhs_d.opt()], outs=[hha_d.opt()]), counter=cc2_lane)
            cc2_dep = (cc2_lane, cc2_lane.count)

        with ExitStack() as st:
          if cfg.stop >= 6:
            WUP = sb(st, "WUP", [128, 8 * 2 * FF], BF16)
            WDN = sb(st, "WDN", [128, 22 * D], BF16)
            h2t = [sb(st, f"h2t{i}", [128, 8 * 512], BF16) for i in range(2)]
            gT = sb(st, "gT", [128, 22 * 512], BF16)
            tvb = [sb(st, f"tv{i}", [128, 512], F32) for i in range(2)]
            tgb = [sb(st, f"tg{i}", [128, 512], F32) for i in range(2)]
            sgb = [sb(st, f"sg{i}", [128, 512], F32) for i in range(2)]
            xmb = [sb(st, f"fxm{i}", [128, D], F32) for i in range(2)]
            xo = [sb(st, f"fxo{i}", [128, D], F32) for i in range(2)]
            hh = sb(st, "hh", [8, D], BF16)
            halo_h = sb(st, "halo_h", [128, 16], BF16)
            psv = [ps(st, f"psv{i}", [128, 512], F32) for i in range(2)]
            psg = [ps(st, f"psg{i}", [128, 512], F32) for i in range(2)]
            psy = [ps(st, f"psy{i}", [128, 512], F32) for i in range(4)]
            for kc in range(8):
                kk.dma_sp(WUP[:, kc * 2 * FF:(kc + 1) * 2 * FF], wb_up[l, kc * 128:(kc + 1) * 128, :], w=[WUP])
            for kc in range(22):
                kk.dma_sp(WDN[:, kc * D:(kc + 1) * D], wb_dn[l, kc * 128:(kc + 1) * 128, :], w=[WDN])
            halo_done = [False]

            def prep_halo():
                if halo_done[0]:
                    return
                halo_done[0] = True
                if cfg.use_cc:
                    kk._wait(kk.SP, cc2_dep[0], cc2_dep[1])
                    kk.dma_sp(hh[:], hha_d[:, :], w=[hh])
                    for kc in range(8):
                        kk.pe(lambda kc=kc: nc.tensor.matmul(psy[0][:, kc * 2:kc * 2 + 2], lhsT=hh[:, kc * 128:(kc + 1) * 128], rhs=selh[:], start=True, stop=True),
                              r=[hh, selh], w=[psy[0]])
                    kk.dve(lambda: nc.vector.tensor_copy(out=halo_h[:], in_=psy[0][:, 0:16]), r=[psy[0]], w=[halo_h])
                else:
                    kk.dve(lambda: nc.vector.memset(halo_h[:], 0.0), w=[halo_h])
            halo3 = halo_h[:].rearrange("p (c x) -> p c x", x=2)

            fsegs = [("s", PTOK + i * SEG, SEG) for i in range(cfg.ns)] + [("p", 0, PTOK)]
            tiles = []
            for (gk, g0, gn) in fsegs:
                s0 = 0
                while s0 < gn:
                    n = min(510, gn - s0)
                    tiles.append((gk, g0, gn, s0, n))
                    s0 += n
            pslot = [0]
            yslot = [0]

            def load_tile(i):
                gk, g0, gn, s0, n = tiles[i]
                if gk == "p":
                    prep_halo()
                H = h2t[i % 2]
                H3 = H[:].rearrange("p (c t) -> p c t", c=8)
                lo = s0 - 1
                hi = s0 + n + 1
                clo, chi = max(lo, 0), min(hi, gn)
                kk.dma_sp(H3[:, :, clo - lo: chi - lo], h2t_d[:, :, g0 + clo: g0 + chi].rearrange("c p t -> p c t"), w=[H])
                if lo < 0:
                    if gk == "p":
                        kk.pool(lambda: nc.gpsimd.tensor_copy(out=H3[:, :, 0:1], in_=halo3[:, :, 0:1]), r=[halo_h], w=[H])
                    else:
                        kk.pool(lambda: nc.gpsimd.memset(H3[:, :, 0:1], 0.0), w=[H])
                if hi > gn:
                    if gk == "p":
                        kk.pool(lambda: nc.gpsimd.tensor_copy(out=H3[:, :, n + 1:n + 2], in_=halo3[:, :, 1:2]), r=[halo_h], w=[H])
                    else:
                        kk.pool(lambda: nc.gpsimd.memset(H3[:, :, n + 1:n + 2], 0.0), w=[H])

            load_tile(0)
            for i, (gk, g0, gn, s0, n) in enumerate(tiles):
                if i + 1 < len(tiles):
                    load_tile(i + 1)
                H = h2t[i % 2]
                N = n + 2
                for j in range(22):
                    sl = pslot[0] % 2
                    pslot[0] += 1
                    PV_, PG_ = psv[sl], psg[sl]
                    for (P_, ch) in ((PV_, j), (PG_, 22 + j)):
                        for kc in range(8):
                            kk.pe(lambda P_=P_, ch=ch, kc=kc: nc.tensor.matmul(P_[:, 0:N], lhsT=WUP[:, kc * 2 * FF + ch * 128: kc * 2 * FF + (ch + 1) * 128],
                                                                               rhs=H[:, kc * 512: kc * 512 + N], start=(kc == 0), stop=(kc == 7)),
                                  r=[WUP, H], w=[P_])
                    TV, TG, SG = tvb[sl], tgb[sl], sgb[sl]
                    for (P_, TT, ch) in ((PV_, TV, j), (PG_, TG, 22 + j)):
                        kk.act(lambda P_=P_, TT=TT, ch=ch: nc.scalar.activation(out=TT[:, 0:n], in_=P_[:, 0:n], func=AF.Identity,
                                                                                 scale=pvc(l, PV_FW + ch * 3), bias=pvc(l, PV_FB + ch)), r=[P_, pv], w=[TT])
                        for jj in (1, 2):
                            kk.dve(lambda P_=P_, TT=TT, ch=ch, jj=jj: nc.vector.scalar_tensor_tensor(out=TT[:, 0:n], in0=P_[:, jj:jj + n],
                                                                                                      scalar=pvc(l, PV_FW + ch * 3 + jj), in1=TT[:, 0:n],
                                                                                                      op0=ALU.mult, op1=ALU.add), r=[P_, pv, TT], w=[TT])
                    kk.act(lambda TG=TG, SG=SG: nc.scalar.activation(out=SG[:, 0:n], in_=TG[:, 0:n], func=AF.Silu), r=[TG], w=[SG])
                    kk.pool(lambda TV=TV, SG=SG, j=j: nc.gpsimd.tensor_tensor(out=gT[:, j * 512: j * 512 + n], in0=TV[:, 0:n], in1=SG[:, 0:n], op=ALU.mult),
                            r=[TV, SG], w=[gT])
                b0 = 0
                bi = 0
                while b0 < n:
                    m = min(128, n - b0)
                    tk = g0 + s0 + b0
                    XM, XO = xmb[bi % 2], xo[bi % 2]
                    kk.dma_sp(XM[0:m, :], xm_d[tk:tk + m, :], w=[XM])
                    for jn in range(2):
                        Y = psy[yslot[0] % 4]
                        yslot[0] += 1
                        for j in range(22):
                            kk.pe(lambda Y=Y, j=j, jn=jn, b0=b0, m=m: nc.tensor.matmul(Y[0:m, :], lhsT=gT[:, j * 512 + b0: j * 512 + b0 + m],
                                                                                       rhs=WDN[:, j * D + jn * 512: j * D + (jn + 1) * 512],
                                                                                       start=(j == 0), stop=(j == 21)), r=[gT, WDN], w=[Y])
                        kk.dve(lambda Y=Y, jn=jn, m=m, XM=XM, XO=XO: nc.vector.tensor_tensor(out=XO[0:m, jn * 512:(jn + 1) * 512], in0=Y[0:m, :],
                                                                                            in1=XM[0:m, jn * 512:(jn + 1) * 512], op=ALU.add),
                               r=[Y, XM], w=[XO])
                    kk.dma_pool(x_dst[tk:tk + m, :], XO[0:m, :], r=[XO])
                    b0 += m
                    bi += 1
            kk.barrier()

    kk.final_wait()
    return nc, kk


def rmsnorm_to_T_c1(kk, nc, xsrc, hb, hT, psT, psT_bf, junk, ss, lnv, epst, ident_b):
    kk.dve(lambda: nc.vector.scalar_tensor_tensor(out=junk[:], in0=xsrc[:], scalar=1.0, in1=xsrc[:], op0=ALU.mult, op1=ALU.mult,
                                                  accum_out=ss[:]), r=[xsrc], w=[junk, ss])
    kk.act(lambda: nc.scalar.activation(out=lnv[:], in_=ss[:], func=AF.Ln, bias=epst[:], scale=1.0 / D), r=[ss, epst], w=[lnv])
    kk.act(lambda: nc.scalar.activation(out=lnv[:], in_=lnv[:], func=AF.Exp, scale=-0.5), r=[lnv], w=[lnv])
    kk.act(lambda: nc.scalar.activation(out=hb[:], in_=xsrc[:], func=AF.Copy, scale=lnv[:]), r=[xsrc, lnv], w=[hb])
    for kc in range(8):
        kk.pe(lambda kc=kc: nc.tensor.transpose(out=psT_bf[:, kc * 128:(kc + 1) * 128], in_=hb[:, kc * 128:(kc + 1) * 128], identity=ident_b[:]),
              r=[hb, ident_b], w=[psT])
    kk.dve(lambda: nc.vector.tensor_copy(out=hT[:], in_=psT_bf[:, 0:1024]), r=[psT], w=[hT])


def _perm_w_in():
    a = np.arange(0, 512)
    bq = np.arange(512, 1024)
    bk = np.arange(1024, 1536)
    bv = np.arange(1536, 2048)
    cq = np.arange(2048, 2304).reshape(4, 64)[[0, 2, 1, 3]].reshape(-1)
    ck = np.arange(2304, 2432)
    cv = np.arange(2432, 2560)
    return np.concatenate([a, bq, bk, cq, ck, cv, bv])


def _rot_table(pos):
    pos = pos.astype(np.float32)
    invB = (np.float32(500000.0) ** (-np.arange(0, 16, 2, dtype=np.float32) / np.float32(16))).astype(np.float32)
    invC = (np.float32(10000.0) ** (-np.arange(0, 32, 2, dtype=np.float32) / np.float32(32))).astype(np.float32)
    angB = pos[:, None] * invB[None, :]
    p_i = pos.astype(np.int64)
    row = (p_i // 64).astype(np.float32)
    col = (p_i % 64).astype(np.float32)
    angR = row[:, None] * invC[None, :]
    angC = col[:, None] * invC[None, :]
    out = np.concatenate([np.cos(angB), np.sin(angB), np.cos(angR), np.sin(angR), np.cos(angC), np.sin(angC)], axis=1)
    return np.ascontiguousarray(out.astype(np.float32))


def make_in_maps(cfg, inputs):
    L = cfg.depth
    f = lambda k: np.asarray(inputs[k], dtype=np.float32)
    xp, xs = f("x_prompt"), f("x_sample")
    perm = _perm_w_in()
    w_in = np.ascontiguousarray(f("w_in")[:L][:, :, perm])
    w_out = np.ascontiguousarray(f("w_out")[:L])
    w_up = np.ascontiguousarray(f("w_up")[:L])
    w_dn = np.ascontiguousarray(f("w_down")[:L])
    pv = np.zeros((128, L, NPV), np.float32)
    for l in range(L):
        pv[:, l, PV_G1:PV_G1 + 8] = f("norm1_g")[l].reshape(8, 128).T
        pv[:, l, PV_G2:PV_G2 + 8] = f("norm2_g")[l].reshape(8, 128).T
        caw = f("conv_a_w")[l]
        pv[:, l, PV_CAW:PV_CAW + 62] = caw.reshape(31, 2, 128).transpose(2, 1, 0).reshape(128, 62)
        pv[:, l, PV_CAB:PV_CAB + 2] = f("conv_a_b")[l].reshape(2, 128).T
        pv[:, l, PV_LNG:PV_LNG + 2] = f("ln_a_g")[l].reshape(2, 128).T
        pv[:, l, PV_LNB:PV_LNB + 2] = f("ln_a_b")[l].reshape(2, 128).T
        pv[:, l, PV_SUB] = f("subln_b_g")[l]
        fw = f("conv_f_w")[l]
        pv[:, l, PV_FW:PV_FW + 132] = fw.reshape(3, 44, 128).transpose(2, 1, 0).reshape(128, 132)
        pv[:, l, PV_FB:PV_FB + 44] = f("conv_f_b")[l].reshape(44, 128).T
    pv = np.ascontiguousarray(pv.reshape(128, L * NPV))
    gt = np.zeros((L, 1408), np.float32)
    for l in range(L):
        gt[l] = np.concatenate([np.tile(f("qn_b_g")[l], 8), np.tile(f("kn_b_g")[l], 8), np.tile(f("qn_c_g")[l], 4), np.tile(f("kn_c_g")[l], 2)])
    lam = np.stack([f("lam_q1")[:L], f("lam_k1")[:L], f("lam_q2")[:L], f("lam_k2")[:L]], axis=1)
    lamv = np.ascontiguousarray(np.broadcast_to(lam.reshape(1, L * 4 * 64), (128, L * 4 * 64))).astype(np.float32)
    ident = np.eye(128, dtype=np.float32)
    in_maps = []
    for c in range(NCORES):
        p, r = c // RANKS, c % RANKS
        xpc = xp[p, r * cfg.ptok:(r + 1) * cfg.ptok]
        xsc = xs[c * cfg.ns:(c + 1) * cfg.ns].reshape(cfg.ns * cfg.seg, D)
        x_in = np.ascontiguousarray(np.concatenate([xpc, xsc], axis=0))
        pos = np.concatenate([np.arange(r * cfg.ptok, (r + 1) * cfg.ptok)] + [np.arange(cfg.seg)] * cfg.ns)
        rot = _rot_table(pos)
        sela = np.zeros((120, 30), np.float32)
        selh = np.zeros((8, 2), np.float32)
        if r > 0:
            for j in range(15):
                sela[(r - 1) * 30 + 15 + j, j] = 1.0
            selh[(r - 1) * 2 + 1, 0] = 1.0
        if r < RANKS - 1:
            for j in range(15):
                sela[(r + 1) * 30 + j, 15 + j] = 1.0
            selh[(r + 1) * 2 + 0, 1] = 1.0
        in_maps.append(dict(x_in=x_in, rot=rot, ident=ident, pv=pv, gt=gt, lamv=lamv, sela=sela, selh=selh,
                            w_in=w_in, w_out=w_out, w_up=w_up, w_down=w_dn))
    return in_maps


def assemble(cfg, results, nb_prompt, nb_sample):
    yp = np.zeros((nb_prompt, cfg.sp, D), np.float32)
    ys = np.zeros((nb_sample, cfg.seg, D), np.float32)
    for c in range(NCORES):
        y = np.asarray(results[c]["y_out"], dtype=np.float32).reshape(cfg.T, D)
        p, r = c // RANKS, c % RANKS
        yp[p, r * cfg.ptok:(r + 1) * cfg.ptok] = y[:cfg.ptok]
        ys[c * cfg.ns:(c + 1) * cfg.ns] = y[cfg.ptok:].reshape(cfg.ns, cfg.seg, D)
    return yp, ys


def run(cfg, inputs):
    nc, kk = build_program(cfg)
    in_maps = make_in_maps(cfg, inputs)
    res = run_bass_kernel_spmd(nc, in_maps, core_ids=list(range(NCORES)))
    return assemble(cfg, res.results, 2, NCORES * cfg.ns)


def kernel(**inputs):
    cfg = Cfg()
    return run(cfg, inputs)
```

```python
import math
from contextlib import ExitStack

import numpy as np
import ml_dtypes

import concourse.bass as bass
import concourse.mybir as mybir
from concourse.bass_utils import run_bass_kernel_spmd

F32 = mybir.dt.float32
BF16 = mybir.dt.bfloat16
AF = mybir.ActivationFunctionType
ALU = mybir.AluOpType
AX = mybir.AxisListType

D = 1024
INW = 2560
FF = 2816
EPS = 1e-6
NCORES = 8
RANKS = 4


class Cfg:
    def __init__(self, depth=4, seg=2048, np_=2, ns=4, use_cc=True, stop=99):
        self.stop = stop
        self.depth = depth
        self.seg = seg
        self.np = np_
        self.ns = ns
        self.ptok = np_ * seg
        self.T = (np_ + ns) * seg
        self.sp = RANKS * self.ptok
        self.use_cc = use_cc


EPOCH = 8000


class Counter:
    def __init__(self, kk, name, step):
        self.kk = kk
        self.name = name
        self.step = step
        self.count = 0
        self.sems = []
        self.per = EPOCH // step

    def sem_for(self, c):
        idx = (c - 1) // self.per
        while len(self.sems) <= idx:
            self.sems.append(self.kk.es.enter_context(self.kk.nc.semaphore(f"s_{self.name}{len(self.sems)}")))
        return self.sems[idx], (c - idx * self.per) * self.step


class Issuer:
    def __init__(self, name, h, cnt):
        self.name = name
        self.h = h
        self.cnt = cnt
        self.waited = {}


class Buf:
    def __init__(self, t):
        self.t = t
        self.w = None
        self.r = {}

    def __getitem__(self, idx):
        return self.t[idx]


class K:
    def __init__(self, nc):
        self.nc = nc
        self.es = ExitStack()
        self.counters = []
        mk = lambda n, s: self._mkc(n, s)
        self.PE = Issuer("pe", nc.tensor, mk("pe", 1))
        self.ACT = Issuer("act", nc.scalar, mk("act", 1))
        self.DVE = Issuer("dve", nc.vector, mk("dve", 1))
        self.POOL = Issuer("pool", nc.gpsimd, mk("pool", 1))
        self.SP = Issuer("sp", nc.sync, mk("spc", 1))
        self.issuers = [self.PE, self.ACT, self.DVE, self.POOL, self.SP]
        NL = 16
        self.q_sp = [mk(f"qsp{i}_", 16) for i in range(NL)]
        self.q_pool = [mk(f"qpool{i}_", 16) for i in range(NL)]
        self.q_cc = [mk(f"qcc{i}_", 1) for i in range(4)]
        self.rr = {"sp": 0, "pool": 0, "cc": 0}
        self.ninst = 0

    def lane(self, which):
        lst = {"sp": self.q_sp, "pool": self.q_pool, "cc": self.q_cc}[which]
        c = lst[self.rr[which] % len(lst)]
        self.rr[which] += 1
        return c

    def _mkc(self, n, s):
        c = Counter(self, n, s)
        self.counters.append(c)
        return c

    def _wait(self, iss, c, n):
        if n <= 0:
            return
        if iss.waited.get(c, 0) >= n:
            return
        if c is self.PE.cnt and iss is self.PE:
            return
        sem, val = c.sem_for(n)
        iss.h.wait_ge(sem, val)
        iss.waited[c] = n

    def emit(self, iss, fn, reads=(), writes=(), counter=None):
        c = counter if counter is not None else iss.cnt
        deps = {}

        def add(cn):
            cc, n = cn
            if deps.get(cc, 0) < n:
                deps[cc] = n

        for b in reads:
            if b.w is not None:
                add(b.w)
        for b in writes:
            if b.w is not None:
                add(b.w)
            for cc, n in b.r.items():
                add((cc, n))
        for cc, n in deps.items():
            self._wait(iss, cc, n)
        inst = fn()
        c.count += 1
        n = c.count
        sem, val = c.sem_for(n)
        inst.then_inc(sem, c.step)
        for b in reads:
            if b.r.get(c, 0) < n:
                b.r[c] = n
        for b in writes:
            b.w = (c, n)
            b.r = {}
        self.ninst += 1
        return inst

    def pe(self, fn, r=(), w=()):
        return self.emit(self.PE, fn, r, w)

    def act(self, fn, r=(), w=()):
        return self.emit(self.ACT, fn, r, w)

    def dve(self, fn, r=(), w=()):
        return self.emit(self.DVE, fn, r, w)

    def pool(self, fn, r=(), w=()):
        return self.emit(self.POOL, fn, r, w)

    def dma_sp(self, out, in_, r=(), w=(), **kw):
        return self.emit(self.SP, lambda: self.nc.sync.dma_start(out=out, in_=in_, **kw), r, w, counter=self.lane("sp"))

    def dma_pool(self, out, in_, r=(), w=(), **kw):
        return self.emit(self.POOL, lambda: self.nc.gpsimd.dma_start(out=out, in_=in_, **kw), r, w, counter=self.lane("pool"))

    def barrier(self):
        snap = [(c, c.count) for c in self.counters]
        for iss in self.issuers:
            for c, n in snap:
                self._wait(iss, c, n)

    def final_wait(self):
        snap = [(c, c.count) for c in self.counters]
        for c, n in snap:
            self._wait(self.SP, c, n)


PV_G1 = 0
PV_G2 = 8
PV_CAW = 16
PV_CAB = 78
PV_LNG = 80
PV_LNB = 82
PV_SUB = 84
PV_FW = 85
PV_FB = 217
NPV = 261


def build_program(cfg):
    nc = bass.Bass("TRN2", target_bir_lowering=False)
    kk = K(nc)
    es = kk.es
    L = cfg.depth
    T, SEG, PTOK = cfg.T, cfg.seg, cfg.ptok
    NTB = T // 128

    def din(name, shape, dt=F32):
        return nc.dram_tensor(name, list(shape), dt, kind="ExternalInput").ap()

    def dscr(name, shape, dt):
        return nc.dram_tensor(name, list(shape), dt, kind="Internal").ap()

    def dcc(name, shape, dt):
        return nc.dram_tensor(name, list(shape), dt).ap()

    x_in = din("x_in", [T, D])
    rot_in = din("rot", [T, 80])
    ident_in = din("ident", [128, 128])
    pv_in = din("pv", [128, L * NPV])
    gt_in = din("gt", [L, 1408])
    lam_in = din("lamv", [128, L * 4 * 64])
    sela_in = din("sela", [120, 30])
    selh_in = din("selh", [8, 2])
    w_in_d = din("w_in", [L, D, INW])
    w_out_d = din("w_out", [L, D, D])
    w_up_d = din("w_up", [L, D, 2 * FF])
    w_dn_d = din("w_down", [L, FF, D])
    y_out = nc.dram_tensor("y_out", [T, D], F32, kind="ExternalOutput").ap()

    wb_in = dscr("wb_in", [L, D, INW], BF16)
    wb_out = dscr("wb_out", [L, D, D], BF16)
    wb_up = dscr("wb_up", [L, D, 2 * FF], BF16)
    wb_dn = dscr("wb_dn", [L, FF, D], BF16)
    xm_d = dscr("xm", [T, D], F32)
    xa_d = dscr("xa", [T, D], F32)
    xb_d = dscr("xb", [T, D], F32)
    at_d = dscr("at", [2, 128, T], F32)
    qkt_d = dscr("qkt", [11, 128, T], BF16)
    v_d = dscr("v", [T, 640], BF16)
    cata_d = dscr("cata", [2, 128, T], BF16)
    h2t_d = dscr("h2t", [8, 128, T], BF16)
    kts_l = [dcc(f"kts{u}", [128, PTOK], BF16) for u in range(5)]
    vs_l = [dcc(f"vs{u}", [PTOK, 128], BF16) for u in range(5)]
    ahs_d = dcc("ahs", [30, 256], F32)
    hhs_d = dcc("hhs", [2, D], BF16)
    kta_l = [dcc(f"kta{u}", [RANKS * 128, PTOK], BF16) for u in range(5)]
    va_l = [dcc(f"va{u}", [RANKS * PTOK, 128], BF16) for u in range(5)]
    aha_d = dcc("aha", [RANKS * 30, 256], F32)
    hha_d = dcc("hha", [RANKS * 2, D], BF16)
    RG = [[0, 1, 2, 3], [4, 5, 6, 7]]

    uid = [0]

    def sb(stack, name, shape, dt):
        uid[0] += 1
        return Buf(stack.enter_context(nc.sbuf_tensor(f"sb{uid[0]}_{name}", list(shape), dt)))

    def ps(stack, name, shape, dt):
        uid[0] += 1
        return Buf(stack.enter_context(nc.psum_tensor(f"ps{uid[0]}_{name}", list(shape), dt)))

    ident_f = sb(es, "ident_f", [128, 128], F32)
    ident_b = sb(es, "ident_b", [128, 128], BF16)
    ones_b = sb(es, "ones_b", [128, 128], BF16)
    ones3 = sb(es, "ones3", [128, 192], BF16)
    onesA = sb(es, "onesA", [128, 128], F32)
    onesS = sb(es, "onesS", [128, 128], F32)
    onesF = sb(es, "onesF", [128, 128], F32)
    swapF = sb(es, "swapF", [128, 128], F32)
    epst = sb(es, "epst", [128, 1], F32)
    pv = sb(es, "pv", [128, L * NPV], F32)
    neglam = sb(es, "neglam", [128, L], F32)
    sela = sb(es, "sela", [120, 30], F32)
    selh = sb(es, "selh", [8, 2], BF16)

    def pvc(l, off, n=1):
        return pv[:, l * NPV + off: l * NPV + off + n]

    kk.dma_sp(ident_f[:], ident_in[:, :], w=[ident_f])
    kk.dma_sp(pv[:], pv_in[:, :], w=[pv])
    kk.dma_sp(sela[:], sela_in[:, :], w=[sela])
    kk.dve(lambda: nc.vector.tensor_copy(out=ident_b[:], in_=ident_f[:]), r=[ident_f], w=[ident_b])
    kk.dve(lambda: nc.vector.memset(ones_b[:], 1.0), w=[ones_b])
    kk.dve(lambda: nc.vector.memset(ones3[:], 0.0), w=[ones3])
    kk.dve(lambda: nc.vector.memset(ones3[:, 64:128], 1.0), w=[ones3])
    kk.dve(lambda: nc.vector.memset(onesA[:], 1.0 / 256), w=[onesA])
    kk.dve(lambda: nc.vector.memset(onesS[:], 1.0 / 128), w=[onesS])
    kk.dve(lambda: nc.vector.memset(epst[:], EPS), w=[epst])
    kk.dve(lambda: nc.vector.memset(onesF[:], 1.0), w=[onesF])
    kk.dve(lambda: nc.vector.tensor_copy(out=swapF[:, 0:64], in_=ident_f[:, 64:128]), r=[ident_f], w=[swapF])
    kk.dve(lambda: nc.vector.tensor_copy(out=swapF[:, 64:128], in_=ident_f[:, 0:64]), r=[ident_f], w=[swapF])
    with ExitStack() as st:
        lamv = sb(st, "lamv", [128, L * 4 * 64], F32)
        selh_f = sb(st, "selh_f", [8, 2], F32)
        lp = sb(st, "lp", [128, L * 2 * 64], F32)
        lsum = sb(st, "lsum", [128, L * 2], F32)
        kk.dma_sp(lamv[:], lam_in[:, :], w=[lamv])
        kk.dma_sp(selh_f[:], selh_in[:, :], w=[selh_f])
        kk.dve(lambda: nc.vector.tensor_copy(out=selh[:], in_=selh_f[:]), r=[selh_f], w=[selh])
        lv = lamv[:].rearrange("p (l f d) -> p l f d", l=L, f=4)
        lpv = lp[:].rearrange("p (l f d) -> p l f d", l=L, f=2)
        kk.dve(lambda: nc.vector.tensor_tensor(out=lpv[:, :, 0, :], in0=lv[:, :, 0, :], in1=lv[:, :, 1, :], op=ALU.mult), r=[lamv], w=[lp])
        kk.dve(lambda: nc.vector.tensor_tensor(out=lpv[:, :, 1, :], in0=lv[:, :, 2, :], in1=lv[:, :, 3, :], op=ALU.mult), r=[lamv], w=[lp])
        kk.dve(lambda: nc.vector.tensor_reduce(out=lsum[:], in_=lp[:].rearrange("p (g d) -> p g d", d=64), axis=AX.X, op=ALU.add), r=[lp], w=[lsum])
        kk.act(lambda: nc.scalar.activation(out=lsum[:], in_=lsum[:], func=AF.Exp), r=[lsum], w=[lsum])
        for l in range(L):
            lam_init = 0.8 - 0.6 * math.exp(-0.3 * l)
            kk.dve(lambda l=l, li=lam_init: nc.vector.scalar_tensor_tensor(
                out=neglam[:, l:l + 1], in0=lsum[:, 2 * l + 1:2 * l + 2], scalar=-li, in1=lsum[:, 2 * l:2 * l + 1],
                op0=ALU.add, op1=ALU.subtract), r=[lsum], w=[neglam])
        kk.barrier()

    with ExitStack() as st:
        CW = 2816
        stg = [sb(st, f"wstg{i}", [128, CW], F32) for i in range(3)]
        stb = [sb(st, f"wstb{i}", [128, CW], BF16) for i in range(3)]
        items = []
        for l in range(L):
            for kc in range(8):
                items.append((w_in_d[l, kc * 128:(kc + 1) * 128, :], wb_in[l, kc * 128:(kc + 1) * 128, :], INW, pvc(l, PV_G1 + kc)))
            for kc in range(8):
                items.append((w_out_d[l, kc * 128:(kc + 1) * 128, :], wb_out[l, kc * 128:(kc + 1) * 128, :], D, None))
            for kc in range(8):
                for hh in range(2):
                    items.append((w_up_d[l, kc * 128:(kc + 1) * 128, hh * FF:(hh + 1) * FF],
                                  wb_up[l, kc * 128:(kc + 1) * 128, hh * FF:(hh + 1) * FF], FF, pvc(l, PV_G2 + kc)))
            for kc in range(22):
                items.append((w_dn_d[l, kc * 128:(kc + 1) * 128, :], wb_dn[l, kc * 128:(kc + 1) * 128, :], D, None))
        for i, (src, dst, n, g) in enumerate(items):
            a, b = stg[i % 3], stb[i % 3]
            kk.dma_sp(a[:, 0:n], src, w=[a])
            if i % 2 == 0:
                if g is None:
                    kk.dve(lambda a=a, b=b, n=n: nc.vector.tensor_copy(out=b[:, 0:n], in_=a[:, 0:n]), r=[a], w=[b])
                else:
                    kk.dve(lambda a=a, b=b, n=n, g=g: nc.vector.tensor_scalar(out=b[:, 0:n], in0=a[:, 0:n], scalar1=g, scalar2=None, op0=ALU.mult), r=[a, pv], w=[b])
            else:
                if g is None:
                    kk.act(lambda a=a, b=b, n=n: nc.scalar.copy(out=b[:, 0:n], in_=a[:, 0:n]), r=[a], w=[b])
                else:
                    kk.act(lambda a=a, b=b, n=n, g=g: nc.scalar.activation(out=b[:, 0:n], in_=a[:, 0:n], func=AF.Copy, scale=g), r=[a, pv], w=[b])
            kk.dma_pool(dst, b[:, 0:n], r=[b])
        kk.barrier()

    def rmsnorm_to_T(st_bufs, xsrc, l_unused, hb, hT, psT, junk, ss, lnv):
        kk.dve(lambda: nc.vector.scalar_tensor_tensor(out=junk[:], in0=xsrc[:], scalar=1.0, in1=xsrc[:], op0=ALU.mult, op1=ALU.mult,
                                                      accum_out=ss[:]), r=[xsrc], w=[junk, ss])
        kk.act(lambda: nc.scalar.activation(out=lnv[:], in_=ss[:], func=AF.Ln, bias=epst[:], scale=1.0 / D), r=[ss, epst], w=[lnv])
        kk.act(lambda: nc.scalar.activation(out=lnv[:], in_=lnv[:], func=AF.Exp, scale=-0.5), r=[lnv], w=[lnv])
        kk.act(lambda: nc.scalar.activation(out=hb[:], in_=xsrc[:], func=AF.Copy, scale=lnv[:]), r=[xsrc, lnv], w=[hb])
        for kc in range(8):
            kk.pe(lambda kc=kc: nc.tensor.transpose(out=psT[:, kc * 128:(kc + 1) * 128], in_=hb[:, kc * 128:(kc + 1) * 128], identity=ident_b[:]),
                  r=[hb, ident_b], w=[psT])
        kk.dve(lambda: nc.vector.tensor_copy(out=hT[:], in_=psT[:, 0:1024]), r=[psT], w=[hT])

    prompt_blocks = PTOK // 128

    for l in range(L):
        lam_init = 0.8 - 0.6 * math.exp(-0.3 * l)
        x_src = x_in if l == 0 else (xa_d if l % 2 == 1 else xb_d)
        x_dst = y_out if l == L - 1 else (xa_d if l % 2 == 0 else xb_d)

        with ExitStack() as st:
          if cfg.stop >= 1:
            WIN = sb(st, "WIN", [128, 8 * INW], BF16)
            G = sb(st, "G", [128, 1408], F32)
            xt = [sb(st, f"xt{i}", [128, D], F32) for i in range(3)]
            rot = [sb(st, f"rot{i}", [128, 80], F32) for i in range(3)]
            qk2 = [sb(st, f"qk2{i}", [128, 1408], F32) for i in range(2)]
            vb2 = [sb(st, f"vb2{i}", [128, 640], BF16) for i in range(2)]
            atm2 = [sb(st, f"atm2{i}", [128, 256], F32) for i in range(2)]
            junk = sb(st, "junk", [128, D], BF16)
            ss = sb(st, "ss", [128, 1], F32)
            lnv = sb(st, "lnv", [128, 1], F32)
            hb = sb(st, "hb", [128, D], BF16)
            hT = sb(st, "hT", [128, D], BF16)
            eg = sb(st, "eg", [128, 256], F32)
            a_tm = sb(st, "a_tm", [128, 256], F32)
            aT = sb(st, "aT", [128, 256], F32)
            qk = sb(st, "qk", [128, 1408], F32)
            sq = sb(st, "sq", [128, 1408], F32)
            ssg = sb(st, "ssg", [128, 22], F32)
            rt = [sb(st, f"rt{i}", [128, 192], F32) for i in range(4)]
            qkb = sb(st, "qkb", [128, 1408], BF16)
            qkT = sb(st, "qkT", [128, 1408], BF16)
            vb = sb(st, "vb", [128, 640], BF16)
            psP = [ps(st, f"psP{j}", [128, 512], F32) for j in range(5)]
            psA = ps(st, "psA", [128, 256], F32)
            psQ = ps(st, "psQ", [128, 2048], BF16)
            psT = psQ

            for kc in range(8):
                kk.dma_sp(WIN[:, kc * INW:(kc + 1) * INW], wb_in[l, kc * 128:(kc + 1) * 128, :], w=[WIN])
            kk.dma_sp(G[:], gt_in[l:l + 1, :].to_broadcast([128, 1408]), w=[G])
            kk.dve(lambda: nc.vector.tensor_scalar(out=G[:, 0:512], in0=G[:, 0:512], scalar1=0.125, scalar2=None, op0=ALU.mult), r=[G], w=[G])
            kk.dve(lambda: nc.vector.tensor_scalar(out=G[:, 1024:1280], in0=G[:, 1024:1280], scalar1=0.125, scalar2=None, op0=ALU.mult), r=[G], w=[G])

            def load_blk(tb):
                b = tb % 3
                kk.dma_sp(xt[b][:], x_src[tb * 128:(tb + 1) * 128, :], w=[xt[b]])
                kk.dma_sp(rot[b][:], rot_in[tb * 128:(tb + 1) * 128, :], w=[rot[b]])

            def fh1(tb):
                xsrc = xt[tb % 3]
                kk.dve(lambda: nc.vector.scalar_tensor_tensor(out=junk[:], in0=xsrc[:], scalar=1.0, in1=xsrc[:], op0=ALU.mult, op1=ALU.mult,
                                                              accum_out=ss[:]), r=[xsrc], w=[junk, ss])
                kk.act(lambda: nc.scalar.activation(out=lnv[:], in_=ss[:], func=AF.Ln, bias=epst[:], scale=1.0 / D), r=[ss, epst], w=[lnv])
                kk.act(lambda: nc.scalar.activation(out=lnv[:], in_=lnv[:], func=AF.Exp, scale=-0.5), r=[lnv], w=[lnv])
                kk.act(lambda: nc.scalar.activation(out=hb[:], in_=xsrc[:], func=AF.Copy, scale=lnv[:]), r=[xsrc, lnv], w=[hb])
                for kc in range(8):
                    kk.pe(lambda kc=kc: nc.tensor.transpose(out=psT[:, kc * 128:(kc + 1) * 128], in_=hb[:, kc * 128:(kc + 1) * 128], identity=ident_b[:]),
                          r=[hb, ident_b], w=[psT])

            def fh2(tb):
                kk.dve(lambda: nc.vector.tensor_copy(out=hT[:], in_=psT[:, 0:1024]), r=[psT], w=[hT])
                for j in range(5):
                    for kc in range(8):
                        kk.pe(lambda j=j, kc=kc: nc.tensor.matmul(psP[j][:], lhsT=hT[:, kc * 128:(kc + 1) * 128],
                                                                  rhs=WIN[:, kc * INW + j * 512: kc * INW + (j + 1) * 512],
                                                                  start=(kc == 0), stop=(kc == 7)), r=[hT, WIN], w=[psP[j]])

            def front_tail(tb):
                b = tb % 2
                QK, VBb, ATM = qk2[b], vb2[b], atm2[b]
                kk.act(lambda: nc.scalar.activation(out=eg[:], in_=psP[0][:, 256:512], func=AF.Exp, scale=-1.0), r=[psP[0]], w=[eg])
                kk.act(lambda: nc.scalar.copy(out=QK[:, 0:512], in_=psP[1][:]), r=[psP[1]], w=[QK])
                kk.act(lambda: nc.scalar.copy(out=QK[:, 512:1024], in_=psP[2][:]), r=[psP[2]], w=[QK])
                kk.act(lambda: nc.scalar.copy(out=QK[:, 1024:1408], in_=psP[3][:, 0:384]), r=[psP[3]], w=[QK])
                kk.pool(lambda: nc.gpsimd.tensor_tensor(out=sq[:], in0=QK[:], in1=QK[:], op=ALU.mult), r=[QK], w=[sq])
                kk.act(lambda: nc.scalar.copy(out=VBb[:, 0:512], in_=psP[4][:]), r=[psP[4]], w=[VBb])
                kk.act(lambda: nc.scalar.copy(out=VBb[:, 512:640], in_=psP[3][:, 384:512]), r=[psP[3]], w=[VBb])
                kk.dve(lambda: nc.vector.tensor_scalar(out=eg[:], in0=eg[:], scalar1=1.0, scalar2=None, op0=ALU.add), r=[eg], w=[eg])
                kk.dve(lambda: nc.vector.reciprocal(out=eg[:], in_=eg[:]), r=[eg], w=[eg])
                kk.dve(lambda: nc.vector.tensor_tensor(out=ATM[:], in0=psP[0][:, 0:256], in1=eg[:], op=ALU.mult), r=[psP[0], eg], w=[ATM])

            def back(tb):
                b = tb % 2
                t0 = tb * 128
                R = rot[tb % 3]
                qk, vb, a_tm = qk2[b], vb2[b], atm2[b]
                for c in range(2):
                    kk.pe(lambda c=c: nc.tensor.transpose(out=psA[:, c * 128:(c + 1) * 128], in_=a_tm[:, c * 128:(c + 1) * 128], identity=ident_f[:]),
                          r=[a_tm, ident_f], w=[psA])
                kk.act(lambda: nc.scalar.copy(out=aT[:], in_=psA[:]), r=[psA], w=[aT])
                kk.dma_pool(at_d[:, :, t0:t0 + 128].rearrange("c p t -> p c t"), aT[:].rearrange("p (c t) -> p c t", c=2), r=[aT])
                if tb == 0:
                    kk.dma_pool(ahs_d[0:15, :], a_tm[0:15, :], r=[a_tm])
                if tb == prompt_blocks - 1:
                    kk.dma_pool(ahs_d[15:30, :], a_tm[113:128, :], r=[a_tm])

            def back_qk(tb):
                b = tb % 2
                t0 = tb * 128
                R = rot[tb % 3]
                qk, vb, a_tm = qk2[b], vb2[b], atm2[b]
                kk.dve(lambda: nc.vector.tensor_reduce(out=ssg[:], in_=sq[:].rearrange("p (g d) -> p g d", d=64), axis=AX.X, op=ALU.add), r=[sq], w=[ssg])
                kk.act(lambda: nc.scalar.activation(out=ssg[:], in_=ssg[:], func=AF.Ln, bias=epst[:], scale=1.0 / 64), r=[ssg, epst], w=[ssg])
                kk.act(lambda: nc.scalar.activation(out=ssg[:], in_=ssg[:], func=AF.Exp, scale=-0.5), r=[ssg], w=[ssg])

            def bq2(tb):
                b = tb % 2
                t0 = tb * 128
                R = rot[tb % 3]
                qk, vb, a_tm = qk2[b], vb2[b], atm2[b]
                qk3 = qk[:].rearrange("p (g d) -> p g d", d=64)
                kk.dve(lambda: nc.vector.tensor_tensor(out=qk3, in0=qk3, in1=ssg[:].unsqueeze(2).to_broadcast([128, 22, 64]), op=ALU.mult), r=[qk, ssg], w=[qk])
                kk.dve(lambda: nc.vector.tensor_tensor(out=qk[:], in0=qk[:], in1=G[:], op=ALU.mult), r=[qk, G], w=[qk])
                qB = qk[:, 0:1024].rearrange("p (g d) -> p g d", d=64)
                x1, x2 = qB[:, :, 0:8], qB[:, :, 8:16]
                cB = R[:, 0:8].unsqueeze(1).to_broadcast([128, 16, 8])
                sB = R[:, 8:16].unsqueeze(1).to_broadcast([128, 16, 8])
                tv = [rt[i][:, 0:128].rearrange("p (g d) -> p g d", d=8) for i in range(4)]
                kk.dve(lambda: nc.vector.tensor_tensor(out=tv[0], in0=x1, in1=cB, op=ALU.mult), r=[qk, R], w=[rt[0]])
                kk.dve(lambda: nc.vector.tensor_tensor(out=tv[1], in0=x2, in1=sB, op=ALU.mult), r=[qk, R], w=[rt[1]])
                kk.dve(lambda: nc.vector.tensor_tensor(out=tv[2], in0=x2, in1=cB, op=ALU.mult), r=[qk, R], w=[rt[2]])
                kk.dve(lambda: nc.vector.tensor_tensor(out=tv[3], in0=x1, in1=sB, op=ALU.mult), r=[qk, R], w=[rt[3]])
                kk.dve(lambda: nc.vector.tensor_tensor(out=x1, in0=tv[0], in1=tv[1], op=ALU.subtract), r=[rt[0], rt[1]], w=[qk])
                kk.dve(lambda: nc.vector.tensor_tensor(out=x2, in0=tv[2], in1=tv[3], op=ALU.add), r=[rt[2], rt[3]], w=[qk])
                qC = qk[:, 1024:1408].rearrange("p (g h x d) -> p g h x d", g=6, h=2, x=2)
                y1, y2 = qC[:, :, :, 0, :], qC[:, :, :, 1, :]
                RC = R[:, 16:80].rearrange("p (h x d) -> p h x d", h=2, x=2)
                cC = RC[:, :, 0, :].unsqueeze(1).to_broadcast([128, 6, 2, 16])
                sC = RC[:, :, 1, :].unsqueeze(1).to_broadcast([128, 6, 2, 16])
                tw = [rt[i][:, 0:192].rearrange("p (g h d) -> p g h d", g=6, h=2) for i in range(4)]
                kk.dve(lambda: nc.vector.tensor_tensor(out=tw[0], in0=y1, in1=cC, op=ALU.mult), r=[qk, R], w=[rt[0]])
                kk.dve(lambda: nc.vector.tensor_tensor(out=tw[1], in0=y2, in1=sC, op=ALU.mult), r=[qk, R], w=[rt[1]])
                kk.dve(lambda: nc.vector.tensor_tensor(out=tw[2], in0=y2, in1=cC, op=ALU.mult), r=[qk, R], w=[rt[2]])
                kk.dve(lambda: nc.vector.tensor_tensor(out=tw[3], in0=y1, in1=sC, op=ALU.mult), r=[qk, R], w=[rt[3]])
                kk.dve(lambda: nc.vector.tensor_tensor(out=y1, in0=tw[0], in1=tw[1], op=ALU.subtract), r=[rt[0], rt[1]], w=[qk])
                kk.dve(lambda: nc.vector.tensor_tensor(out=y2, in0=tw[2], in1=tw[3], op=ALU.add), r=[rt[2], rt[3]], w=[qk])
                kk.act(lambda: nc.scalar.copy(out=qkb[:], in_=qk[:]), r=[qk], w=[qkb])
                for c in range(11):
                    kk.pe(lambda c=c: nc.tensor.transpose(out=psQ[:, c * 128:(c + 1) * 128], in_=qkb[:, c * 128:(c + 1) * 128], identity=ident_b[:]),
                          r=[qkb, ident_b], w=[psQ])

            def bq3(tb):
                b = tb % 2
                t0 = tb * 128
                qk, vb, a_tm = qk2[b], vb2[b], atm2[b]
                kk.dve(lambda: nc.vector.tensor_copy(out=qkT[:], in_=psQ[:, 0:1408]), r=[psQ], w=[qkT])
                kk.dma_pool(qkt_d[:, :, t0:t0 + 128].rearrange("c p t -> p c t"), qkT[:].rearrange("p (c t) -> p c t", c=11), r=[qkT])
                kk.dma_pool(v_d[t0:t0 + 128, :], vb[:], r=[vb])
                if tb < prompt_blocks:
                    for u in range(5):
                        kc0 = 512 + u * 128 if u < 4 else 1280
                        kk.dma_pool(kts_l[u][:, t0:t0 + 128], qkT[:, kc0:kc0 + 128], r=[qkT])
                        kk.dma_pool(vs_l[u][t0:t0 + 128, :], vb[:, u * 128:(u + 1) * 128], r=[vb])

            load_blk(0)
            if NTB > 1:
                load_blk(1)
            fh1(0)
            fh2(0)
            front_tail(0)
            for tb in range(NTB):
                if tb + 2 < NTB:
                    load_blk(tb + 2)
                back(tb)
                nxt = tb + 1 < NTB
                if nxt:
                    fh1(tb + 1)
                back_qk(tb)
                if nxt:
                    fh2(tb + 1)
                bq2(tb)
                if nxt:
                    front_tail(tb + 1)
                bq3(tb)
            kk.barrier()

        if cfg.use_cc and cfg.stop >= 2:
            for src, dst in [(ahs_d, aha_d)] + [(kts_l[u], kta_l[u]) for u in range(5)] + [(vs_l[u], va_l[u]) for u in range(5)]:
                kk.emit(kk.POOL, lambda src=src, dst=dst: nc.gpsimd.collective_compute(
                    "AllGather", ALU.bypass, replica_groups=RG, ins=[src.opt()], outs=[dst.opt()]), counter=kk.lane("cc"))

        with ExitStack() as st:
          if cfg.stop >= 3:
            abuf = [sb(st, f"abuf{i}", [128, SEG + 30], F32) for i in range(2)]
            convo = [sb(st, f"convo{i}", [128, SEG], F32) for i in range(2)]
            sqb = [sb(st, f"sqb{i}", [128, 512], F32) for i in range(2)]
            mean_sb = sb(st, "mean_sb", [128, 512], F32)
            m2 = sb(st, "m2", [128, 512], F32)
            rstd = sb(st, "rstd", [128, 512], F32)
            dd = [sb(st, f"dd{i}", [128, 512], F32) for i in range(2)]
            ee = [sb(st, f"ee{i}", [128, 512], F32) for i in range(2)]
            ob = [sb(st, f"ob{i}", [128, 512], BF16) for i in range(2)]
            ahr = sb(st, "ahr", [120, 256], F32)
            ps_mean = ps(st, "ps_mean", [128, 512], F32)
            ps_msq = ps(st, "ps_msq", [128, 512], F32)
            ps_halo = ps(st, "ps_halo", [128, 64], F32)
            nseg = cfg.np + cfg.ns
            for s in list(range(cfg.np, nseg)) + list(range(cfg.np)):
                if s == 0:
                    kk.barrier()
                    if cfg.use_cc:
                        kk.dma_sp(ahr[:], aha_d[:, :], w=[ahr])
                t0 = s * SEG
                is_p = s < cfg.np
                for c in range(2):
                    A = abuf[c]
                    left_local = is_p and s > 0
                    right_local = is_p and s < cfg.np - 1
                    lo = t0 - 15 if left_local else t0
                    hi = t0 + SEG + 15 if right_local else t0 + SEG
                    kk.dma_sp(A[:, 15 + (lo - t0): 15 + (hi - t0)], at_d[c, :, lo:hi], w=[A])
                    if not left_local:
                        kk.pool(lambda A=A: nc.gpsimd.memset(A[:, 0:15], 0.0), w=[A])
                    if not right_local:
                        kk.pool(lambda A=A: nc.gpsimd.memset(A[:, SEG + 15:SEG + 30], 0.0), w=[A])
                    if cfg.use_cc and is_p and (s == 0 or s == cfg.np - 1):
                        kk.pe(lambda c=c: nc.tensor.matmul(ps_halo[:, 0:30], lhsT=ahr[:, c * 128:(c + 1) * 128], rhs=sela[:], start=True, stop=True),
                              r=[ahr, sela], w=[ps_halo])
                        if s == 0:
                            kk.act(lambda A=A: nc.scalar.copy(out=A[:, 0:15], in_=ps_halo[:, 0:15]), r=[ps_halo], w=[A])
                        if s == cfg.np - 1:
                            kk.act(lambda A=A: nc.scalar.copy(out=A[:, SEG + 15:SEG + 30], in_=ps_halo[:, 15:30]), r=[ps_halo], w=[A])
                    CO = convo[c]
                    kk.dve(lambda A=A, CO=CO, c=c: nc.vector.tensor_scalar(out=CO[:], in0=A[:, 0:SEG], scalar1=pvc(l, PV_CAW + c * 31),
                                                                          scalar2=pvc(l, PV_CAB + c), op0=ALU.mult, op1=ALU.add), r=[A, pv], w=[CO])
                    for j in range(1, 31):
                        kk.dve(lambda A=A, CO=CO, c=c, j=j: nc.vector.scalar_tensor_tensor(out=CO[:], in0=A[:, j:j + SEG], scalar=pvc(l, PV_CAW + c * 31 + j),
                                                                                         in1=CO[:], op0=ALU.mult, op1=ALU.add), r=[A, pv, CO], w=[CO])
                for ti in range(SEG // 512):
                    c0 = ti * 512
                    for c in range(2):
                        kk.pool(lambda c=c: nc.gpsimd.tensor_tensor(out=sqb[c][:], in0=convo[c][:, c0:c0 + 512], in1=convo[c][:, c0:c0 + 512], op=ALU.mult),
                                r=[convo[c]], w=[sqb[c]])
                    for c in range(2):
                        kk.pe(lambda c=c: nc.tensor.matmul(ps_mean[:], lhsT=onesA[:], rhs=convo[c][:, c0:c0 + 512], start=(c == 0), stop=(c == 1)),
                              r=[onesA, convo[c]], w=[ps_mean])
                    for c in range(2):
                        kk.pe(lambda c=c: nc.tensor.matmul(ps_msq[:], lhsT=onesA[:], rhs=sqb[c][:], start=(c == 0), stop=(c == 1)),
                              r=[onesA, sqb[c]], w=[ps_msq])
                    kk.act(lambda: nc.scalar.copy(out=mean_sb[:], in_=ps_mean[:]), r=[ps_mean], w=[mean_sb])
                    kk.dve(lambda: nc.vector.tensor_tensor(out=m2[:], in0=mean_sb[:], in1=mean_sb[:], op=ALU.mult), r=[mean_sb], w=[m2])
                    kk.dve(lambda: nc.vector.tensor_tensor(out=m2[:], in0=ps_msq[:], in1=m2[:], op=ALU.subtract), r=[ps_msq, m2], w=[m2])
                    kk.dve(lambda: nc.vector.tensor_scalar(out=m2[:], in0=m2[:], scalar1=0.0, scalar2=None, op0=ALU.max), r=[m2], w=[m2])
                    kk.act(lambda: nc.scalar.activation(out=rstd[:], in_=m2[:], func=AF.Ln, bias=epst[:], scale=1.0), r=[m2, epst], w=[rstd])
                    kk.act(lambda: nc.scalar.activation(out=rstd[:], in_=rstd[:], func=AF.Exp, scale=-0.5), r=[rstd], w=[rstd])
                    for c in range(2):
                        Dd, Ee, Ob = dd[c], ee[c], ob[c]
                        kk.dve(lambda c=c, Dd=Dd: nc.vector.tensor_tensor(out=Dd[:], in0=convo[c][:, c0:c0 + 512], in1=mean_sb[:], op=ALU.subtract),
                               r=[convo[c], mean_sb], w=[Dd])
                        kk.dve(lambda Dd=Dd: nc.vector.tensor_tensor(out=Dd[:], in0=Dd[:], in1=rstd[:], op=ALU.mult), r=[Dd, rstd], w=[Dd])
                        kk.dve(lambda c=c, Dd=Dd: nc.vector.tensor_scalar(out=Dd[:], in0=Dd[:], scalar1=pvc(l, PV_LNG + c), scalar2=pvc(l, PV_LNB + c),
                                                                         op0=ALU.mult, op1=ALU.add), r=[Dd, pv], w=[Dd])
                        kk.act(lambda Dd=Dd, Ee=Ee: nc.scalar.activation(out=Ee[:], in_=Dd[:], func=AF.Exp, scale=-1.0), r=[Dd], w=[Ee])
                        kk.pool(lambda Ee=Ee: nc.gpsimd.tensor_scalar(out=Ee[:], in0=Ee[:], scalar1=1.0, scalar2=1.0, op0=ALU.add, op1=ALU.mult), r=[Ee], w=[Ee])
                        kk.dve(lambda Ee=Ee: nc.vector.reciprocal(out=Ee[:], in_=Ee[:]), r=[Ee], w=[Ee])
                        kk.pool(lambda Dd=Dd, Ee=Ee, Ob=Ob: nc.gpsimd.tensor_tensor(out=Ob[:], in0=Dd[:], in1=Ee[:], op=ALU.mult), r=[Dd, Ee], w=[Ob])
                        kk.dma_pool(cata_d[c, :, t0 + c0:t0 + c0 + 512], Ob[:], r=[Ob])
            kk.barrier()

        with ExitStack() as st:
          if cfg.stop >= 4:
            LKMAX = cfg.sp if cfg.use_cc else max(PTOK, SEG)
            NCKMAX = LKMAX // 128
            GTOK = max(PTOK, SEG)
            KT = sb(st, "KT", [128, LKMAX], BF16)
            VB = sb(st, "VB", [128, NCKMAX * 192], BF16)
            catT = sb(st, "catT", [128, 8 * GTOK], BF16)
            WOUT = sb(st, "WOUT", [128, 8 * D], BF16)
            Qa = [sb(st, f"Qa{i}", [128, 512], BF16) for i in range(2)]
            Qb = [sb(st, f"Qb{i}", [128, 512], BF16) for i in range(2)]
            NPT = 4
            PT = [sb(st, f"PT{i}", [128, 1024], BF16) for i in range(NPT)]
            acc = sb(st, "acc", [128, 1024], F32)
            accp = sb(st, "accp", [128, 1024], F32)
            fo = [sb(st, f"fo{i}", [128, 512], F32) for i in range(2)]
            fl = [sb(st, f"fl{i}", [128, 512], F32) for i in range(2)]
            fd = sb(st, "fd", [128, 512], F32)
            fsq = fl[1]
            frs = fl[0]
            xt = [accp, accp]
            xm = acc
            ss = sb(st, "css", [128, 1], F32)
            lnv = sb(st, "clnv", [128, 1], F32)
            hb = sb(st, "chb", [128, D], BF16)
            hT = sb(st, "chT", [128, D], BF16)
            junk = hb
            sc = [ps(st, f"sc{i}", [128, 1024], F32) for i in range(2)]
            po = [ps(st, f"po{m}", [128, 512], F32) for m in range(2)]
            pf = [ps(st, f"pf{m}", [128, 512], F32) for m in range(2)]

            for kc in range(8):
                kk.dma_sp(WOUT[:, kc * D:(kc + 1) * D], wb_out[l, kc * 128:(kc + 1) * 128, :], w=[WOUT])

            groups = [("p", 0, PTOK)] + [("s", PTOK + i * SEG, SEG) for i in range(cfg.ns)]
            scslot = [0]
            ptslot = [0]
            KTr = [Buf(KT.t), Buf(KT.t)]
            VBr = [Buf(VB.t), Buf(VB.t)]
            ucount = [0]
            synced = [False]
            pend = {}

            def flush_pending():
                if pend.get("sums"):
                    pend["sums"]()
                    pend["sums"] = None
                if pend.get("st23"):
                    pend["st23"]()
                    pend["st23"] = None
                if pend.get("st3"):
                    pend["st3"]()
                    pend["st3"] = None
            for (gk, g0, gn) in groups:
                use_all = (gk == "p" and cfg.use_cc)
                Lk = cfg.sp if use_all else gn
                nck = Lk // 128
                use_reg = (gk == "s") and (2 * Lk <= LKMAX)
                if use_reg and not synced[0]:
                    synced[0] = True
                    for rb in KTr:
                        rb.w = KT.w
                        rb.r = dict(KT.r)
                    for rb in VBr:
                        rb.w = VB.w
                        rb.r = dict(VB.r)
                for c in range(2):
                    kk.dma_sp(catT[:, c * GTOK: c * GTOK + gn], cata_d[c, :, g0:g0 + gn], w=[catT])
                for u in range(6):
                    isB = u < 4
                    if use_reg:
                        reg = ucount[0] % 2
                        ucount[0] += 1
                        KTb, VBb, kt0 = KTr[reg], VBr[reg], reg * Lk
                        VB3 = VB[:, reg * nck * 192:(reg + 1) * nck * 192].rearrange("p (c e) -> p c e", e=192)
                    else:
                        KTb, VBb, kt0 = KT, VB, 0
                        VB3 = VB[:, 0:nck * 192].rearrange("p (c e) -> p c e", e=192)
                    kchunk = u if isB else 4
                    if use_all:
                        for r in range(RANKS):
                            kk.dma_sp(KT[:, r * PTOK:(r + 1) * PTOK], kta_l[kchunk][r * 128:(r + 1) * 128, :], w=[KT])
                    else:
                        kk.dma_sp(KTb[:, kt0:kt0 + Lk], qkt_d[4 + u if isB else 10, :, g0:g0 + gn], w=[KTb])
                    if isB:
                        vcols = slice(u * 128, (u + 1) * 128)
                        vdst = lambda c0, c1, VB3=VB3: VB3[:, c0:c1, 0:128]
                    else:
                        g = u - 4
                        vcols = slice(512 + g * 64, 512 + (g + 1) * 64)
                        vdst = lambda c0, c1, VB3=VB3: VB3[:, c0:c1, 64:128]
                        kk.pool(lambda VB3=VB3: nc.gpsimd.memset(VB3[:, :, 0:64], 1.0), w=[VBb])
                        kk.pool(lambda VB3=VB3: nc.gpsimd.memset(VB3[:, :, 128:192], 1.0), w=[VBb])
                    if use_all:
                        vsrc = va_l[kchunk]
                        voff = 0
                        vcols = slice(0, 128) if isB else slice(g * 64, (g + 1) * 64)
                    else:
                        vsrc = v_d
                        voff = g0
                    for c0 in range(0, nck, 16):
                        c1 = min(nck, c0 + 16)
                        kk.dma_sp(vdst(c0, c1), vsrc[voff + c0 * 128: voff + c1 * 128, vcols].rearrange("(c p) e -> p c e", p=128), w=[VBb])
                    if isB:
                        qrows = [slice(0, 64), slice(64, 128)]
                    else:
                        qrows = [slice(g * 64, (g + 1) * 64)] * 2
                    if u == 0 or u >= 4:
                        for qi_ in range(2):
                            for m_, QQ in enumerate((Qa[qi_], Qb[qi_])):
                                zr = slice(64, 128) if qrows[m_].start == 0 else slice(0, 64)
                                kk.pool(lambda QQ=QQ, zr=zr: nc.gpsimd.memset(QQ[zr, :], 0.0), w=[QQ])
                    if isB:
                        lhs_v = [lambda ck, VB3=VB3: VB3[:, ck, 0:128], lambda ck, VB3=VB3: VB3[:, ck, 0:128]]
                        rows = [slice(0, 64), slice(64, 128)]
                    else:
                        lhs_v = [lambda ck, VB3=VB3: VB3[:, ck, 64:192], lambda ck, VB3=VB3: VB3[:, ck, 0:128]]
                        rows = [slice(g * 64, (g + 1) * 64)] * 2
                    for qt in range(gn // 512):
                        q0 = g0 + qt * 512
                        qi = qt % 2
                        if isB:
                            kk.dma_sp(Qa[qi][0:64, :], qkt_d[u, 0:64, q0:q0 + 512], w=[Qa[qi]])
                            kk.dma_sp(Qb[qi][64:128, :], qkt_d[u, 64:128, q0:q0 + 512], w=[Qb[qi]])
                        else:
                            kk.dma_sp(Qa[qi][qrows[0], :], qkt_d[8, qrows[0], q0:q0 + 512], w=[Qa[qi]])
                            kk.dma_sp(Qb[qi][qrows[1], :], qkt_d[9, qrows[1], q0:q0 + 512], w=[Qb[qi]])
                        qbufs = [Qa[qi], Qb[qi]]
                        slots = {}

                        def scores(ck):
                            sl = scslot[0] % 2
                            scslot[0] += 1
                            slots[ck] = sl
                            for m in range(2):
                                kk.pe(lambda m=m, sl=sl, ck=ck: nc.tensor.matmul(sc[sl][:, m * 512:(m + 1) * 512], lhsT=KTb[:, kt0 + ck * 128:kt0 + (ck + 1) * 128],
                                                                                 rhs=qbufs[m][:, :], start=True, stop=True),
                                      r=[KTb, qbufs[m]], w=[sc[sl]])

                        scores(0)
                        for ck in range(nck):
                            if ck + 1 < nck:
                                scores(ck + 1)
                            if ck == min(2, nck - 1) and pend.get("sums"):
                                pend["sums"]()
                                pend["sums"] = None
                            if ck == min(6, nck - 1):
                                if pend.get("sums"):
                                    pend["sums"]()
                                    pend["sums"] = None
                                if pend.get("st23"):
                                    pend["st23"]()
                                    pend["st23"] = None
                            if ck == min(11, nck - 1) and pend.get("st3"):
                                if pend.get("st23"):
                                    pend["st23"]()
                                    pend["st23"] = None
                                pend["st3"]()
                                pend["st3"] = None
                            ssl = slots.pop(ck)
                            sl = ptslot[0] % NPT
                            ptslot[0] += 1
                            kk.act(lambda sl=sl, ssl=ssl: nc.scalar.activation(out=PT[sl][:], in_=sc[ssl][:], func=AF.Exp), r=[sc[ssl]], w=[PT[sl]])
                            def emit_pv(ckk, sll):
                                for m in range(2):
                                    kk.pe(lambda m=m: nc.tensor.matmul(
                                        po[m][:], lhsT=lhs_v[m](ckk), rhs=PT[sll][:, m * 512:(m + 1) * 512], start=(ckk == 0), stop=(ckk == nck - 1)),
                                        r=[VBb, PT[sll]], w=[po[m]])
                            if ck >= 1:
                                emit_pv(ck - 1, pv_prev_sl)
                            pv_prev_sl = sl
                            if ck == nck - 1:
                                emit_pv(ck, sl)
                            if isB:
                                tb16 = accp[:].bitcast(BF16)
                                t01, t23 = tb16[:, 0:1024], tb16[:, 1024:2048]
                                if ck % 4 == 0:
                                    prev_sl = sl
                                elif ck % 4 == 1:
                                    kk.dve(lambda a_=prev_sl, b_=sl: nc.vector.tensor_tensor(out=t01, in0=PT[a_][:], in1=PT[b_][:], op=ALU.add),
                                           r=[PT[prev_sl], PT[sl]], w=[accp])
                                elif ck % 4 == 2:
                                    prev_sl = sl
                                else:
                                    kk.dve(lambda a_=prev_sl, b_=sl: nc.vector.tensor_tensor(out=t23, in0=PT[a_][:], in1=PT[b_][:], op=ALU.add),
                                           r=[PT[prev_sl], PT[sl]], w=[accp])
                                    kk.dve(lambda: nc.vector.tensor_tensor(out=t01, in0=t01, in1=t23, op=ALU.add), r=[accp], w=[accp])
                                    if ck == 3:
                                        kk.dve(lambda: nc.vector.tensor_copy(out=acc[:], in_=t01), r=[accp], w=[acc])
                                    else:
                                        kk.dve(lambda: nc.vector.tensor_tensor(out=acc[:], in0=acc[:], in1=t01, op=ALU.add), r=[accp, acc], w=[acc])
                        cchunk = 2 + u if isB else 6 + (u - 4)
                        dst = catT[:, cchunk * GTOK + qt * 512: cchunk * GTOK + (qt + 1) * 512]
                        kk.act(lambda: nc.scalar.copy(out=fo[0][:], in_=po[0][:]), r=[po[0]], w=[fo[0]])
                        kk.dve(lambda: nc.vector.tensor_copy(out=fo[1][:], in_=po[1][:]), r=[po[1]], w=[fo[1]])
                        if isB:
                            def st_sums():
                                for m in range(2):
                                    kk.pe(lambda m=m: nc.tensor.matmul(pf[m][:], lhsT=onesF[:], rhs=acc[:, m * 512:(m + 1) * 512], start=True, stop=True),
                                          r=[onesF, acc], w=[pf[m]])

                            def st23(dst=dst):
                                for m in range(2):
                                    kk.act(lambda m=m: nc.scalar.activation(out=fl[m][:], in_=pf[m][:], func=AF.Ln), r=[pf[m]], w=[fl[m]])
                                    kk.act(lambda m=m: nc.scalar.activation(out=fl[m][:], in_=fl[m][:], func=AF.Exp, scale=-1.0), r=[fl[m]], w=[fl[m]])
                                    kk.dve(lambda m=m: nc.vector.tensor_tensor(out=fo[m][:], in0=fo[m][:], in1=fl[m][:], op=ALU.mult), r=[fo[m], fl[m]], w=[fo[m]])
                                kk.dve(lambda: nc.vector.scalar_tensor_tensor(out=fd[:], in0=fo[1][:], scalar=neglam[:, l:l + 1], in1=fo[0][:],
                                                                              op0=ALU.mult, op1=ALU.add), r=[fo[0], fo[1], neglam], w=[fd])
                                kk.dve(lambda: nc.vector.tensor_tensor(out=fsq[:], in0=fd[:], in1=fd[:], op=ALU.mult), r=[fd], w=[fsq])

                            def st3(dst=dst):
                                kk.pe(lambda: nc.tensor.matmul(pf[0][:], lhsT=onesS[:], rhs=fsq[:], start=True, stop=True), r=[onesS, fsq], w=[pf[0]])
                                kk.act(lambda: nc.scalar.activation(out=frs[:], in_=pf[0][:], func=AF.Ln, bias=epst[:], scale=1.0), r=[pf[0], epst], w=[frs])
                                kk.act(lambda: nc.scalar.activation(out=frs[:], in_=frs[:], func=AF.Exp, scale=-0.5), r=[frs], w=[frs])
                                kk.dve(lambda: nc.vector.tensor_tensor(out=fd[:], in0=fd[:], in1=frs[:], op=ALU.mult), r=[fd, frs], w=[fd])
                                kk.dve(lambda: nc.vector.tensor_scalar(out=dst, in0=fd[:], scalar1=pvc(l, PV_SUB), scalar2=1.0 - lam_init,
                                                                      op0=ALU.mult, op1=ALU.mult), r=[fd, pv], w=[catT])
                        else:
                            st_sums = None

                            def st23(dst=dst):
                                for m in range(2):
                                    lr = slice(64, 128) if m == 0 else slice(0, 64)
                                    orr = slice(0, 64) if m == 0 else slice(64, 128)
                                    kk.act(lambda m=m, lr=lr: nc.scalar.activation(out=fl[m][lr, :], in_=fo[m][lr, :], func=AF.Ln), r=[fo[m]], w=[fl[m]])
                                    kk.act(lambda m=m, lr=lr: nc.scalar.activation(out=fl[m][lr, :], in_=fl[m][lr, :], func=AF.Exp, scale=-1.0), r=[fl[m]], w=[fl[m]])
                                    kk.pool(lambda m=m, orr=orr: nc.gpsimd.memset(fl[m][orr, :], 0.0), w=[fl[m]])

                            def st3(dst=dst):
                                for m in range(2):
                                    kk.pe(lambda m=m: nc.tensor.matmul(pf[m][:], lhsT=swapF[:], rhs=fl[m][:], start=True, stop=True), r=[swapF, fl[m]], w=[pf[m]])
                                kk.dve(lambda: nc.vector.tensor_tensor(out=dst[0:64, :], in0=fo[0][0:64, :], in1=pf[0][0:64, :], op=ALU.mult),
                                       r=[fo[0], pf[0]], w=[catT])
                                kk.dve(lambda: nc.vector.tensor_tensor(out=dst[64:128, :], in0=fo[1][64:128, :], in1=pf[1][64:128, :], op=ALU.mult),
                                       r=[fo[1], pf[1]], w=[catT])
                        pend["sums"] = st_sums
                        pend["st23"] = st23
                        pend["st3"] = st3
                flush_pending()
                nblk = gn // 128

                def outproj(bi, banks):
                    for jn in range(2):
                        for kc in range(8):
                            kk.pe(lambda jn=jn, kc=kc: nc.tensor.matmul(banks[jn][:], lhsT=catT[:, kc * GTOK + bi * 128: kc * GTOK + (bi + 1) * 128],
                                                                        rhs=WOUT[:, kc * D + jn * 512: kc * D + (jn + 1) * 512],
                                                                        start=(kc == 0), stop=(kc == 7)), r=[catT, WOUT], w=[banks[jn]])

                outproj(0, po)
                for bi in range(nblk):
                    t0 = g0 + bi * 128
                    X = xt[0]
                    banks = po if bi % 2 == 0 else pf
                    if bi + 1 < nblk:
                        outproj(bi + 1, pf if bi % 2 == 0 else po)
                    kk.dma_sp(X[:], x_src[t0:t0 + 128, :], w=[X])
                    for jn in range(2):
                        kk.dve(lambda jn=jn, X=X, banks=banks: nc.vector.tensor_tensor(out=xm[:, jn * 512:(jn + 1) * 512], in0=banks[jn][:],
                                                                                      in1=X[:, jn * 512:(jn + 1) * 512], op=ALU.add),
                               r=[banks[jn], X], w=[xm])
                    kk.dma_pool(xm_d[t0:t0 + 128, :], xm[:], r=[xm])
                    psT = sc[0]
                    psT_bf = psT[:, 0:512].bitcast(BF16)
                    rmsnorm_to_T_c1(kk, nc, xm, hb, hT, psT, psT_bf, junk, ss, lnv, epst, ident_b)
                    kk.dma_pool(h2t_d[:, :, t0:t0 + 128].rearrange("c p t -> p c t"), hT[:].rearrange("p (c t) -> p c t", c=8), r=[hT])
                    if gk == "p" and bi == 0:
                        kk.dma_pool(hhs_d[0:1, :], hb[0:1, :], r=[hb])
                    if gk == "p" and bi == nblk - 1:
                        kk.dma_pool(hhs_d[1:2, :], hb[127:128, :], r=[hb])
            kk.barrier()

        if cfg.use_cc and cfg.stop >= 5:
            cc2_lane = kk.lane("cc")
            kk.emit(kk.POOL, lambda: nc.gpsimd.collective_compute(
                "AllGather", ALU.bypass, replica_groups=RG, ins=[hhs_d.opt()], outs=[hha_d.opt()]), counter=cc2_lane)
            cc2_dep = (cc2_lane, cc2_lane.count)

        with ExitStack() as st:
          if cfg.stop >= 6:
            WUP = sb(st, "WUP", [128, 8 * 2 * FF], BF16)
            WDN = sb(st, "WDN", [128, 22 * D], BF16)
            h2t = [sb(st, f"h2t{i}", [128, 8 * 512], BF16) for i in range(2)]
            gT = sb(st, "gT", [128, 22 * 512], BF16)
            tvb = [sb(st, f"tv{i}", [128, 512], F32) for i in range(2)]
            tgb = [sb(st, f"tg{i}", [128, 512], F32) for i in range(2)]
            sgb = [sb(st, f"sg{i}", [128, 512], F32) for i in range(2)]
            xmb = [sb(st, f"fxm{i}", [128, D], F32) for i in range(2)]
            xo = [sb(st, f"fxo{i}", [128, D], F32) for i in range(2)]
            hh = sb(st, "hh", [8, D], BF16)
            halo_h = sb(st, "halo_h", [128, 16], BF16)
            psv = [ps(st, f"psv{i}", [128, 512], F32) for i in range(2)]
            psg = [ps(st, f"psg{i}", [128, 512], F32) for i in range(2)]
            psy = [ps(st, f"psy{i}", [128, 512], F32) for i in range(4)]
            for kc in range(8):
                kk.dma_sp(WUP[:, kc * 2 * FF:(kc + 1) * 2 * FF], wb_up[l, kc * 128:(kc + 1) * 128, :], w=[WUP])
            for kc in range(22):
                kk.dma_sp(WDN[:, kc * D:(kc + 1) * D], wb_dn[l, kc * 128:(kc + 1) * 128, :], w=[WDN])
            halo_done = [False]

            def prep_halo():
                if halo_done[0]:
                    return
                halo_done[0] = True
                if cfg.use_cc:
                    kk._wait(kk.SP, cc2_dep[0], cc2_dep[1])
                    kk.dma_sp(hh[:], hha_d[:, :], w=[hh])
                    for kc in range(8):
                        kk.pe(lambda kc=kc: nc.tensor.matmul(psy[0][:, kc * 2:kc * 2 + 2], lhsT=hh[:, kc * 128:(kc + 1) * 128], rhs=selh[:], start=True, stop=True),
                              r=[hh, selh], w=[psy[0]])
                    kk.dve(lambda: nc.vector.tensor_copy(out=halo_h[:], in_=psy[0][:, 0:16]), r=[psy[0]], w=[halo_h])
                else:
                    kk.dve(lambda: nc.vector.memset(halo_h[:], 0.0), w=[halo_h])
            halo3 = halo_h[:].rearrange("p (c x) -> p c x", x=2)

            fsegs = [("s", PTOK + i * SEG, SEG) for i in range(cfg.ns)] + [("p", 0, PTOK)]
            tiles = []
            for (gk, g0, gn) in fsegs:
                s0 = 0
                while s0 < gn:
                    n = min(510, gn - s0)
                    tiles.append((gk, g0, gn, s0, n))
                    s0 += n
            pslot = [0]
            yslot = [0]

            def load_tile(i):
                gk, g0, gn, s0, n = tiles[i]
                if gk == "p":
                    prep_halo()
                H = h2t[i % 2]
                H3 = H[:].rearrange("p (c t) -> p c t", c=8)
                lo = s0 - 1
                hi = s0 + n + 1
                clo, chi = max(lo, 0), min(hi, gn)
                kk.dma_sp(H3[:, :, clo - lo: chi - lo], h2t_d[:, :, g0 + clo: g0 + chi].rearrange("c p t -> p c t"), w=[H])
                if lo < 0:
                    if gk == "p":
                        kk.pool(lambda: nc.gpsimd.tensor_copy(out=H3[:, :, 0:1], in_=halo3[:, :, 0:1]), r=[halo_h], w=[H])
                    else:
                        kk.pool(lambda: nc.gpsimd.memset(H3[:, :, 0:1], 0.0), w=[H])
                if hi > gn:
                    if gk == "p":
                        kk.pool(lambda: nc.gpsimd.tensor_copy(out=H3[:, :, n + 1:n + 2], in_=halo3[:, :, 1:2]), r=[halo_h], w=[H])
                    else:
                        kk.pool(lambda: nc.gpsimd.memset(H3[:, :, n + 1:n + 2], 0.0), w=[H])

            load_tile(0)
            for i, (gk, g0, gn, s0, n) in enumerate(tiles):
                if i + 1 < len(tiles):
                    load_tile(i + 1)
                H = h2t[i % 2]
                N = n + 2
                for j in range(22):
                    sl = pslot[0] % 2
                    pslot[0] += 1
                    PV_, PG_ = psv[sl], psg[sl]
                    for (P_, ch) in ((PV_, j), (PG_, 22 + j)):
                        for kc in range(8):
                            kk.pe(lambda P_=P_, ch=ch, kc=kc: nc.tensor.matmul(P_[:, 0:N], lhsT=WUP[:, kc * 2 * FF + ch * 128: kc * 2 * FF + (ch + 1) * 128],
                                                                               rhs=H[:, kc * 512: kc * 512 + N], start=(kc == 0), stop=(kc == 7)),
                                  r=[WUP, H], w=[P_])
                    TV, TG, SG = tvb[sl], tgb[sl], sgb[sl]
                    for (P_, TT, ch) in ((PV_, TV, j), (PG_, TG, 22 + j)):
                        kk.act(lambda P_=P_, TT=TT, ch=ch: nc.scalar.activation(out=TT[:, 0:n], in_=P_[:, 0:n], func=AF.Identity,
                                                                                 scale=pvc(l, PV_FW + ch * 3), bias=pvc(l, PV_FB + ch)), r=[P_, pv], w=[TT])
                        for jj in (1, 2):
                            kk.dve(lambda P_=P_, TT=TT, ch=ch, jj=jj: nc.vector.scalar_tensor_tensor(out=TT[:, 0:n], in0=P_[:, jj:jj + n],
                                                                                                      scalar=pvc(l, PV_FW + ch * 3 + jj), in1=TT[:, 0:n],
                                                                                                      op0=ALU.mult, op1=ALU.add), r=[P_, pv, TT], w=[TT])
                    kk.act(lambda TG=TG, SG=SG: nc.scalar.activation(out=SG[:, 0:n], in_=TG[:, 0:n], func=AF.Silu), r=[TG], w=[SG])
                    kk.pool(lambda TV=TV, SG=SG, j=j: nc.gpsimd.tensor_tensor(out=gT[:, j * 512: j * 512 + n], in0=TV[:, 0:n], in1=SG[:, 0:n], op=ALU.mult),
                            r=[TV, SG], w=[gT])
                b0 = 0
                bi = 0
                while b0 < n:
                    m = min(128, n - b0)
                    tk = g0 + s0 + b0
                    XM, XO = xmb[bi % 2], xo[bi % 2]
                    kk.dma_sp(XM[0:m, :], xm_d[tk:tk + m, :], w=[XM])
                    for jn in range(2):
                        Y = psy[yslot[0] % 4]
                        yslot[0] += 1
                        for j in range(22):
                            kk.pe(lambda Y=Y, j=j, jn=jn, b0=b0, m=m: nc.tensor.matmul(Y[0:m, :], lhsT=gT[:, j * 512 + b0: j * 512 + b0 + m],
                                                                                       rhs=WDN[:, j * D + jn * 512: j * D + (jn + 1) * 512],
                                                                                       start=(j == 0), stop=(j == 21)), r=[gT, WDN], w=[Y])
                        kk.dve(lambda Y=Y, jn=jn, m=m, XM=XM, XO=XO: nc.vector.tensor_tensor(out=XO[0:m, jn * 512:(jn + 1) * 512], in0=Y[0:m, :],
                                                                                            in1=XM[0:m, jn * 512:(jn + 1) * 512], op=ALU.add),
                               r=[Y, XM], w=[XO])
                    kk.dma_pool(x_dst[tk:tk + m, :], XO[0:m, :], r=[XO])
                    b0 += m
                    bi += 1
            kk.barrier()

    kk.final_wait()
    return nc, kk


def rmsnorm_to_T_c1(kk, nc, xsrc, hb, hT, psT, psT_bf, junk, ss, lnv, epst, ident_b):
    kk.dve(lambda: nc.vector.scalar_tensor_tensor(out=junk[:], in0=xsrc[:], scalar=1.0, in1=xsrc[:], op0=ALU.mult, op1=ALU.mult,
                                                  accum_out=ss[:]), r=[xsrc], w=[junk, ss])
    kk.act(lambda: nc.scalar.activation(out=lnv[:], in_=ss[:], func=AF.Ln, bias=epst[:], scale=1.0 / D), r=[ss, epst], w=[lnv])
    kk.act(lambda: nc.scalar.activation(out=lnv[:], in_=lnv[:], func=AF.Exp, scale=-0.5), r=[lnv], w=[lnv])
    kk.act(lambda: nc.scalar.activation(out=hb[:], in_=xsrc[:], func=AF.Copy, scale=lnv[:]), r=[xsrc, lnv], w=[hb])
    for kc in range(8):
        kk.pe(lambda kc=kc: nc.tensor.transpose(out=psT_bf[:, kc * 128:(kc + 1) * 128], in_=hb[:, kc * 128:(kc + 1) * 128], identity=ident_b[:]),
              r=[hb, ident_b], w=[psT])
    kk.dve(lambda: nc.vector.tensor_copy(out=hT[:], in_=psT_bf[:, 0:1024]), r=[psT], w=[hT])


def _perm_w_in():
    a = np.arange(0, 512)
    bq = np.arange(512, 1024)
    bk = np.arange(1024, 1536)
    bv = np.arange(1536, 2048)
    cq = np.arange(2048, 2304).reshape(4, 64)[[0, 2, 1, 3]].reshape(-1)
    ck = np.arange(2304, 2432)
    cv = np.arange(2432, 2560)
    return np.concatenate([a, bq, bk, cq, ck, cv, bv])


def _rot_table(pos):
    pos = pos.astype(np.float32)
    invB = (np.float32(500000.0) ** (-np.arange(0, 16, 2, dtype=np.float32) / np.float32(16))).astype(np.float32)
    invC = (np.float32(10000.0) ** (-np.arange(0, 32, 2, dtype=np.float32) / np.float32(32))).astype(np.float32)
    angB = pos[:, None] * invB[None, :]
    p_i = pos.astype(np.int64)
    row = (p_i // 64).astype(np.float32)
    col = (p_i % 64).astype(np.float32)
    angR = row[:, None] * invC[None, :]
    angC = col[:, None] * invC[None, :]
    out = np.concatenate([np.cos(angB), np.sin(angB), np.cos(angR), np.sin(angR), np.cos(angC), np.sin(angC)], axis=1)
    return np.ascontiguousarray(out.astype(np.float32))


def make_in_maps(cfg, inputs):
    L = cfg.depth
    f = lambda k: np.asarray(inputs[k], dtype=np.float32)
    xp, xs = f("x_prompt"), f("x_sample")
    perm = _perm_w_in()
    w_in = np.ascontiguousarray(f("w_in")[:L][:, :, perm])
    w_out = np.ascontiguousarray(f("w_out")[:L])
    w_up = np.ascontiguousarray(f("w_up")[:L])
    w_dn = np.ascontiguousarray(f("w_down")[:L])
    pv = np.zeros((128, L, NPV), np.float32)
    for l in range(L):
        pv[:, l, PV_G1:PV_G1 + 8] = f("norm1_g")[l].reshape(8, 128).T
        pv[:, l, PV_G2:PV_G2 + 8] = f("norm2_g")[l].reshape(8, 128).T
        caw = f("conv_a_w")[l]
        pv[:, l, PV_CAW:PV_CAW + 62] = caw.reshape(31, 2, 128).transpose(2, 1, 0).reshape(128, 62)
        pv[:, l, PV_CAB:PV_CAB + 2] = f("conv_a_b")[l].reshape(2, 128).T
        pv[:, l, PV_LNG:PV_LNG + 2] = f("ln_a_g")[l].reshape(2, 128).T
        pv[:, l, PV_LNB:PV_LNB + 2] = f("ln_a_b")[l].reshape(2, 128).T
        pv[:, l, PV_SUB] = f("subln_b_g")[l]
        fw = f("conv_f_w")[l]
        pv[:, l, PV_FW:PV_FW + 132] = fw.reshape(3, 44, 128).transpose(2, 1, 0).reshape(128, 132)
        pv[:, l, PV_FB:PV_FB + 44] = f("conv_f_b")[l].reshape(44, 128).T
    pv = np.ascontiguousarray(pv.reshape(128, L * NPV))
    gt = np.zeros((L, 1408), np.float32)
    for l in range(L):
        gt[l] = np.concatenate([np.tile(f("qn_b_g")[l], 8), np.tile(f("kn_b_g")[l], 8), np.tile(f("qn_c_g")[l], 4), np.tile(f("kn_c_g")[l], 2)])
    lam = np.stack([f("lam_q1")[:L], f("lam_k1")[:L], f("lam_q2")[:L], f("lam_k2")[:L]], axis=1)
    lamv = np.ascontiguousarray(np.broadcast_to(lam.reshape(1, L * 4 * 64), (128, L * 4 * 64))).astype(np.float32)
    ident = np.eye(128, dtype=np.float32)
    in_maps = []
    for c in range(NCORES):
        p, r = c // RANKS, c % RANKS
        xpc = xp[p, r * cfg.ptok:(r + 1) * cfg.ptok]
        xsc = xs[c * cfg.ns:(c + 1) * cfg.ns].reshape(cfg.ns * cfg.seg, D)
        x_in = np.ascontiguousarray(np.concatenate([xpc, xsc], axis=0))
        pos = np.concatenate([np.arange(r * cfg.ptok, (r + 1) * cfg.ptok)] + [np.arange(cfg.seg)] * cfg.ns)
        rot = _rot_table(pos)
        sela = np.zeros((120, 30), np.float32)
        selh = np.zeros((8, 2), np.float32)
        if r > 0:
            for j in range(15):
                sela[(r - 1) * 30 + 15 + j, j] = 1.0
            selh[(r - 1) * 2 + 1, 0] = 1.0
        if r < RANKS - 1:
            for j in range(15):
                sela[(r + 1) * 30 + j, 15 + j] = 1.0
            selh[(r + 1) * 2 + 0, 1] = 1.0
        in_maps.append(dict(x_in=x_in, rot=rot, ident=ident, pv=pv, gt=gt, lamv=lamv, sela=sela, selh=selh,
                            w_in=w_in, w_out=w_out, w_up=w_up, w_down=w_dn))
    return in_maps


def assemble(cfg, results, nb_prompt, nb_sample):
    yp = np.zeros((nb_prompt, cfg.sp, D), np.float32)
    ys = np.zeros((nb_sample, cfg.seg, D), np.float32)
    for c in range(NCORES):
        y = np.asarray(results[c]["y_out"], dtype=np.float32).reshape(cfg.T, D)
        p, r = c // RANKS, c % RANKS
        yp[p, r * cfg.ptok:(r + 1) * cfg.ptok] = y[:cfg.ptok]
        ys[c * cfg.ns:(c + 1) * cfg.ns] = y[cfg.ptok:].reshape(cfg.ns, cfg.seg, D)
    return yp, ys


def run(cfg, inputs):
    nc, kk = build_program(cfg)
    in_maps = make_in_maps(cfg, inputs)
    res = run_bass_kernel_spmd(nc, in_maps, core_ids=list(range(NCORES)))
    return assemble(cfg, res.results, 2, NCORES * cfg.ns)


def kernel(**inputs):
    cfg = Cfg()
    return run(cfg, inputs)
```

```python
import math
from contextlib import ExitStack

import numpy as np
import ml_dtypes

import concourse.bass as bass
import concourse.mybir as mybir
from concourse.bass_utils import run_bass_kernel_spmd

F32 = mybir.dt.float32
BF16 = mybir.dt.bfloat16
AF = mybir.ActivationFunctionType
ALU = mybir.AluOpType
AX = mybir.AxisListType

D = 1024
INW = 2560
FF = 2816
EPS = 1e-6
NCORES = 8
RANKS = 4


class Cfg:
    def __init__(self, depth=4, seg=2048, np_=2, ns=4, use_cc=True, stop=99):
        self.stop = stop
        self.depth = depth
        self.seg = seg
        self.np = np_
        self.ns = ns
        self.ptok = np_ * seg
        self.T = (np_ + ns) * seg
        self.sp = RANKS * self.ptok
        self.use_cc = use_cc


EPOCH = 8000


class Counter:
    def __init__(self, kk, name, step):
        self.kk = kk
        self.name = name
        self.step = step
        self.count = 0
        self.sems = []
        self.per = EPOCH // step

    def sem_for(self, c):
        idx = (c - 1) // self.per
        while len(self.sems) <= idx:
            self.sems.append(self.kk.es.enter_context(self.kk.nc.semaphore(f"s_{self.name}{len(self.sems)}")))
        return self.sems[idx], (c - idx * self.per) * self.step


class Issuer:
    def __init__(self, name, h, cnt):
        self.name = name
        self.h = h
        self.cnt = cnt
        self.waited = {}


class Buf:
    def __init__(self, t):
        self.t = t
        self.w = None
        self.r = {}

    def __getitem__(self, idx):
        return self.t[idx]


class K:
    def __init__(self, nc):
        self.nc = nc
        self.es = ExitStack()
        self.counters = []
        mk = lambda n, s: self._mkc(n, s)
        self.PE = Issuer("pe", nc.tensor, mk("pe", 1))
        self.ACT = Issuer("act", nc.scalar, mk("act", 1))
        self.DVE = Issuer("dve", nc.vector, mk("dve", 1))
        self.POOL = Issuer("pool", nc.gpsimd, mk("pool", 1))
        self.SP = Issuer("sp", nc.sync, mk("spc", 1))
        self.issuers = [self.PE, self.ACT, self.DVE, self.POOL, self.SP]
        NL = 16
        self.q_sp = [mk(f"qsp{i}_", 16) for i in range(NL)]
        self.q_pool = [mk(f"qpool{i}_", 16) for i in range(NL)]
        self.q_cc = [mk(f"qcc{i}_", 1) for i in range(4)]
        self.rr = {"sp": 0, "pool": 0, "cc": 0}
        self.ninst = 0

    def lane(self, which):
        lst = {"sp": self.q_sp, "pool": self.q_pool, "cc": self.q_cc}[which]
        c = lst[self.rr[which] % len(lst)]
        self.rr[which] += 1
        return c

    def _mkc(self, n, s):
        c = Counter(self, n, s)
        self.counters.append(c)
        return c

    def _wait(self, iss, c, n):
        if n <= 0:
            return
        if iss.waited.get(c, 0) >= n:
            return
        if c is self.PE.cnt and iss is self.PE:
            return
        sem, val = c.sem_for(n)
        iss.h.wait_ge(sem, val)
        iss.waited[c] = n

    def emit(self, iss, fn, reads=(), writes=(), counter=None):
        c = counter if counter is not None else iss.cnt
        deps = {}

        def add(cn):
            cc, n = cn
            if deps.get(cc, 0) < n:
                deps[cc] = n

        for b in reads:
            if b.w is not None:
                add(b.w)
        for b in writes:
            if b.w is not None:
                add(b.w)
            for cc, n in b.r.items():
                add((cc, n))
        for cc, n in deps.items():
            self._wait(iss, cc, n)
        inst = fn()
        c.count += 1
        n = c.count
        sem, val = c.sem_for(n)
        inst.then_inc(sem, c.step)
        for b in reads:
            if b.r.get(c, 0) < n:
                b.r[c] = n
        for b in writes:
            b.w = (c, n)
            b.r = {}
        self.ninst += 1
        return inst

    def pe(self, fn, r=(), w=()):
        return self.emit(self.PE, fn, r, w)

    def act(self, fn, r=(), w=()):
        return self.emit(self.ACT, fn, r, w)

    def dve(self, fn, r=(), w=()):
        return self.emit(self.DVE, fn, r, w)

    def pool(self, fn, r=(), w=()):
        return self.emit(self.POOL, fn, r, w)

    def dma_sp(self, out, in_, r=(), w=(), **kw):
        return self.emit(self.SP, lambda: self.nc.sync.dma_start(out=out, in_=in_, **kw), r, w, counter=self.lane("sp"))

    def dma_pool(self, out, in_, r=(), w=(), **kw):
        return self.emit(self.POOL, lambda: self.nc.gpsimd.dma_start(out=out, in_=in_, **kw), r, w, counter=self.lane("pool"))

    def barrier(self):
        snap = [(c, c.count) for c in self.counters]
        for iss in self.issuers:
            for c, n in snap:
                self._wait(iss, c, n)

    def final_wait(self):
        snap = [(c, c.count) for c in self.counters]
        for c, n in snap:
            self._wait(self.SP, c, n)


PV_G1 = 0
PV_G2 = 8
PV_CAW = 16
PV_CAB = 78
PV_LNG = 80
PV_LNB = 82
PV_SUB = 84
PV_FW = 85
PV_FB = 217
NPV = 261


def build_program(cfg):
    nc = bass.Bass("TRN2", target_bir_lowering=False)
    kk = K(nc)
    es = kk.es
    L = cfg.depth
    T, SEG, PTOK = cfg.T, cfg.seg, cfg.ptok
    NTB = T // 128

    def din(name, shape, dt=F32):
        return nc.dram_tensor(name, list(shape), dt, kind="ExternalInput").ap()

    def dscr(name, shape, dt):
        return nc.dram_tensor(name, list(shape), dt, kind="Internal").ap()

    def dcc(name, shape, dt):
        return nc.dram_tensor(name, list(shape), dt).ap()

    x_in = din("x_in", [T, D])
    rot_in = din("rot", [T, 80])
    ident_in = din("ident", [128, 128])
    pv_in = din("pv", [128, L * NPV])
    gt_in = din("gt", [L, 1408])
    lam_in = din("lamv", [128, L * 4 * 64])
    sela_in = din("sela", [120, 30])
    selh_in = din("selh", [8, 2])
    w_in_d = din("w_in", [L, D, INW])
    w_out_d = din("w_out", [L, D, D])
    w_up_d = din("w_up", [L, D, 2 * FF])
    w_dn_d = din("w_down", [L, FF, D])
    y_out = nc.dram_tensor("y_out", [T, D], F32, kind="ExternalOutput").ap()

    wb_in = dscr("wb_in", [L, D, INW], BF16)
    wb_out = dscr("wb_out", [L, D, D], BF16)
    wb_up = dscr("wb_up", [L, D, 2 * FF], BF16)
    wb_dn = dscr("wb_dn", [L, FF, D], BF16)
    xm_d = dscr("xm", [T, D], F32)
    xa_d = dscr("xa", [T, D], F32)
    xb_d = dscr("xb", [T, D], F32)
    at_d = dscr("at", [2, 128, T], F32)
    qkt_d = dscr("qkt", [11, 128, T], BF16)
    v_d = dscr("v", [T, 640], BF16)
    cata_d = dscr("cata", [2, 128, T], BF16)
    h2t_d = dscr("h2t", [8, 128, T], BF16)
    kts_l = [dcc(f"kts{u}", [128, PTOK], BF16) for u in range(5)]
    vs_l = [dcc(f"vs{u}", [PTOK, 128], BF16) for u in range(5)]
    ahs_d = dcc("ahs", [30, 256], F32)
    hhs_d = dcc("hhs", [2, D], BF16)
    kta_l = [dcc(f"kta{u}", [RANKS * 128, PTOK], BF16) for u in range(5)]
    va_l = [dcc(f"va{u}", [RANKS * PTOK, 128], BF16) for u in range(5)]
    aha_d = dcc("aha", [RANKS * 30, 256], F32)
    hha_d = dcc("hha", [RANKS * 2, D], BF16)
    RG = [[0, 1, 2, 3], [4, 5, 6, 7]]

    uid = [0]

    def sb(stack, name, shape, dt):
        uid[0] += 1
        return Buf(stack.enter_context(nc.sbuf_tensor(f"sb{uid[0]}_{name}", list(shape), dt)))

    def ps(stack, name, shape, dt):
        uid[0] += 1
        return Buf(stack.enter_context(nc.psum_tensor(f"ps{uid[0]}_{name}", list(shape), dt)))

    ident_f = sb(es, "ident_f", [128, 128], F32)
    ident_b = sb(es, "ident_b", [128, 128], BF16)
    ones_b = sb(es, "ones_b", [128, 128], BF16)
    ones3 = sb(es, "ones3", [128, 192], BF16)
    onesA = sb(es, "onesA", [128, 128], F32)
    onesS = sb(es, "onesS", [128, 128], F32)
    onesF = sb(es, "onesF", [128, 128], F32)
    swapF = sb(es, "swapF", [128, 128], F32)
    epst = sb(es, "epst", [128, 1], F32)
    pv = sb(es, "pv", [128, L * NPV], F32)
    neglam = sb(es, "neglam", [128, L], F32)
    sela = sb(es, "sela", [120, 30], F32)
    selh = sb(es, "selh", [8, 2], BF16)

    def pvc(l, off, n=1):
        return pv[:, l * NPV + off: l * NPV + off + n]

    kk.dma_sp(ident_f[:], ident_in[:, :], w=[ident_f])
    kk.dma_sp(pv[:], pv_in[:, :], w=[pv])
    kk.dma_sp(sela[:], sela_in[:, :], w=[sela])
    kk.dve(lambda: nc.vector.tensor_copy(out=ident_b[:], in_=ident_f[:]), r=[ident_f], w=[ident_b])
    kk.dve(lambda: nc.vector.memset(ones_b[:], 1.0), w=[ones_b])
    kk.dve(lambda: nc.vector.memset(ones3[:], 0.0), w=[ones3])
    kk.dve(lambda: nc.vector.memset(ones3[:, 64:128], 1.0), w=[ones3])
    kk.dve(lambda: nc.vector.memset(onesA[:], 1.0 / 256), w=[onesA])
    kk.dve(lambda: nc.vector.memset(onesS[:], 1.0 / 128), w=[onesS])
    kk.dve(lambda: nc.vector.memset(epst[:], EPS), w=[epst])
    kk.dve(lambda: nc.vector.memset(onesF[:], 1.0), w=[onesF])
    kk.dve(lambda: nc.vector.tensor_copy(out=swapF[:, 0:64], in_=ident_f[:, 64:128]), r=[ident_f], w=[swapF])
    kk.dve(lambda: nc.vector.tensor_copy(out=swapF[:, 64:128], in_=ident_f[:, 0:64]), r=[ident_f], w=[swapF])
    with ExitStack() as st:
        lamv = sb(st, "lamv", [128, L * 4 * 64], F32)
        selh_f = sb(st, "selh_f", [8, 2], F32)
        lp = sb(st, "lp", [128, L * 2 * 64], F32)
        lsum = sb(st, "lsum", [128, L * 2], F32)
        kk.dma_sp(lamv[:], lam_in[:, :], w=[lamv])
        kk.dma_sp(selh_f[:], selh_in[:, :], w=[selh_f])
        kk.dve(lambda: nc.vector.tensor_copy(out=selh[:], in_=selh_f[:]), r=[selh_f], w=[selh])
        lv = lamv[:].rearrange("p (l f d) -> p l f d", l=L, f=4)
        lpv = lp[:].rearrange("p (l f d) -> p l f d", l=L, f=2)
        kk.dve(lambda: nc.vector.tensor_tensor(out=lpv[:, :, 0, :], in0=lv[:, :, 0, :], in1=lv[:, :, 1, :], op=ALU.mult), r=[lamv], w=[lp])
        kk.dve(lambda: nc.vector.tensor_tensor(out=lpv[:, :, 1, :], in0=lv[:, :, 2, :], in1=lv[:, :, 3, :], op=ALU.mult), r=[lamv], w=[lp])
        kk.dve(lambda: nc.vector.tensor_reduce(out=lsum[:], in_=lp[:].rearrange("p (g d) -> p g d", d=64), axis=AX.X, op=ALU.add), r=[lp], w=[lsum])
        kk.act(lambda: nc.scalar.activation(out=lsum[:], in_=lsum[:], func=AF.Exp), r=[lsum], w=[lsum])
        for l in range(L):
            lam_init = 0.8 - 0.6 * math.exp(-0.3 * l)
            kk.dve(lambda l=l, li=lam_init: nc.vector.scalar_tensor_tensor(
                out=neglam[:, l:l + 1], in0=lsum[:, 2 * l + 1:2 * l + 2], scalar=-li, in1=lsum[:, 2 * l:2 * l + 1],
                op0=ALU.add, op1=ALU.subtract), r=[lsum], w=[neglam])
        kk.barrier()

    with ExitStack() as st:
        CW = 2816
        stg = [sb(st, f"wstg{i}", [128, CW], F32) for i in range(3)]
        stb = [sb(st, f"wstb{i}", [128, CW], BF16) for i in range(3)]
        items = []
        for l in range(L):
            for kc in range(8):
                items.append((w_in_d[l, kc * 128:(kc + 1) * 128, :], wb_in[l, kc * 128:(kc + 1) * 128, :], INW, pvc(l, PV_G1 + kc)))
            for kc in range(8):
                items.append((w_out_d[l, kc * 128:(kc + 1) * 128, :], wb_out[l, kc * 128:(kc + 1) * 128, :], D, None))
            for kc in range(8):
                for hh in range(2):
                    items.append((w_up_d[l, kc * 128:(kc + 1) * 128, hh * FF:(hh + 1) * FF],
                                  wb_up[l, kc * 128:(kc + 1) * 128, hh * FF:(hh + 1) * FF], FF, pvc(l, PV_G2 + kc)))
            for kc in range(22):
                items.append((w_dn_d[l, kc * 128:(kc + 1) * 128, :], wb_dn[l, kc * 128:(kc + 1) * 128, :], D, None))
        for i, (src, dst, n, g) in enumerate(items):
            a, b = stg[i % 3], stb[i % 3]
            kk.dma_sp(a[:, 0:n], src, w=[a])
            if i % 2 == 0:
                if g is None:
                    kk.dve(lambda a=a, b=b, n=n: nc.vector.tensor_copy(out=b[:, 0:n], in_=a[:, 0:n]), r=[a], w=[b])
                else:
                    kk.dve(lambda a=a, b=b, n=n, g=g: nc.vector.tensor_scalar(out=b[:, 0:n], in0=a[:, 0:n], scalar1=g, scalar2=None, op0=ALU.mult), r=[a, pv], w=[b])
            else:
                if g is None:
                    kk.act(lambda a=a, b=b, n=n: nc.scalar.copy(out=b[:, 0:n], in_=a[:, 0:n]), r=[a], w=[b])
                else:
                    kk.act(lambda a=a, b=b, n=n, g=g: nc.scalar.activation(out=b[:, 0:n], in_=a[:, 0:n], func=AF.Copy, scale=g), r=[a, pv], w=[b])
            kk.dma_pool(dst, b[:, 0:n], r=[b])
        kk.barrier()

    def rmsnorm_to_T(st_bufs, xsrc, l_unused, hb, hT, psT, junk, ss, lnv):
        kk.dve(lambda: nc.vector.scalar_tensor_tensor(out=junk[:], in0=xsrc[:], scalar=1.0, in1=xsrc[:], op0=ALU.mult, op1=ALU.mult,
                                                      accum_out=ss[:]), r=[xsrc], w=[junk, ss])
        kk.act(lambda: nc.scalar.activation(out=lnv[:], in_=ss[:], func=AF.Ln, bias=epst[:], scale=1.0 / D), r=[ss, epst], w=[lnv])
        kk.act(lambda: nc.scalar.activation(out=lnv[:], in_=lnv[:], func=AF.Exp, scale=-0.5), r=[lnv], w=[lnv])
        kk.act(lambda: nc.scalar.activation(out=hb[:], in_=xsrc[:], func=AF.Copy, scale=lnv[:]), r=[xsrc, lnv], w=[hb])
        for kc in range(8):
            kk.pe(lambda kc=kc: nc.tensor.transpose(out=psT[:, kc * 128:(kc + 1) * 128], in_=hb[:, kc * 128:(kc + 1) * 128], identity=ident_b[:]),
                  r=[hb, ident_b], w=[psT])
        kk.dve(lambda: nc.vector.tensor_copy(out=hT[:], in_=psT[:, 0:1024]), r=[psT], w=[hT])

    prompt_blocks = PTOK // 128

    for l in range(L):
        lam_init = 0.8 - 0.6 * math.exp(-0.3 * l)
        x_src = x_in if l == 0 else (xa_d if l % 2 == 1 else xb_d)
        x_dst = y_out if l == L - 1 else (xa_d if l % 2 == 0 else xb_d)

        with ExitStack() as st:
          if cfg.stop >= 1:
            WIN = sb(st, "WIN", [128, 8 * INW], BF16)
            G = sb(st, "G", [128, 1408], F32)
            xt = [sb(st, f"xt{i}", [128, D], F32) for i in range(3)]
            rot = [sb(st, f"rot{i}", [128, 80], F32) for i in range(3)]
            qk2 = [sb(st, f"qk2{i}", [128, 1408], F32) for i in range(2)]
            vb2 = [sb(st, f"vb2{i}", [128, 640], BF16) for i in range(2)]
            atm2 = [sb(st, f"atm2{i}", [128, 256], F32) for i in range(2)]
            junk = sb(st, "junk", [128, D], BF16)
            ss = sb(st, "ss", [128, 1], F32)
            lnv = sb(st, "lnv", [128, 1], F32)
            hb = sb(st, "hb", [128, D], BF16)
            hT = sb(st, "hT", [128, D], BF16)
            eg = sb(st, "eg", [128, 256], F32)
            a_tm = sb(st, "a_tm", [128, 256], F32)
            aT = sb(st, "aT", [128, 256], F32)
            qk = sb(st, "qk", [128, 1408], F32)
            sq = sb(st, "sq", [128, 1408], F32)
            ssg = sb(st, "ssg", [128, 22], F32)
            rt = [sb(st, f"rt{i}", [128, 192], F32) for i in range(4)]
            qkb = sb(st, "qkb", [128, 1408], BF16)
            qkT = sb(st, "qkT", [128, 1408], BF16)
            vb = sb(st, "vb", [128, 640], BF16)
            psP = [ps(st, f"psP{j}", [128, 512], F32) for j in range(5)]
            psA = ps(st, "psA", [128, 256], F32)
            psQ = ps(st, "psQ", [128, 2048], BF16)
            psT = psQ

            for kc in range(8):
                kk.dma_sp(WIN[:, kc * INW:(kc + 1) * INW], wb_in[l, kc * 128:(kc + 1) * 128, :], w=[WIN])
            kk.dma_sp(G[:], gt_in[l:l + 1, :].to_broadcast([128, 1408]), w=[G])
            kk.dve(lambda: nc.vector.tensor_scalar(out=G[:, 0:512], in0=G[:, 0:512], scalar1=0.125, scalar2=None, op0=ALU.mult), r=[G], w=[G])
            kk.dve(lambda: nc.vector.tensor_scalar(out=G[:, 1024:1280], in0=G[:, 1024:1280], scalar1=0.125, scalar2=None, op0=ALU.mult), r=[G], w=[G])

            def load_blk(tb):
                b = tb % 3
                kk.dma_sp(xt[b][:], x_src[tb * 128:(tb + 1) * 128, :], w=[xt[b]])
                kk.dma_sp(rot[b][:], rot_in[tb * 128:(tb + 1) * 128, :], w=[rot[b]])

            def fh1(tb):
                xsrc = xt[tb % 3]
                kk.dve(lambda: nc.vector.scalar_tensor_tensor(out=junk[:], in0=xsrc[:], scalar=1.0, in1=xsrc[:], op0=ALU.mult, op1=ALU.mult,
                                                              accum_out=ss[:]), r=[xsrc], w=[junk, ss])
                kk.act(lambda: nc.scalar.activation(out=lnv[:], in_=ss[:], func=AF.Ln, bias=epst[:], scale=1.0 / D), r=[ss, epst], w=[lnv])
                kk.act(lambda: nc.scalar.activation(out=lnv[:], in_=lnv[:], func=AF.Exp, scale=-0.5), r=[lnv], w=[lnv])
                kk.act(lambda: nc.scalar.activation(out=hb[:], in_=xsrc[:], func=AF.Copy, scale=lnv[:]), r=[xsrc, lnv], w=[hb])
                for kc in range(8):
                    kk.pe(lambda kc=kc: nc.tensor.transpose(out=psT[:, kc * 128:(kc + 1) * 128], in_=hb[:, kc * 128:(kc + 1) * 128], identity=ident_b[:]),
                          r=[hb, ident_b], w=[psT])

            def fh2(tb):
                kk.dve(lambda: nc.vector.tensor_copy(out=hT[:], in_=psT[:, 0:1024]), r=[psT], w=[hT])
                for j in range(5):
                    for kc in range(8):
                        kk.pe(lambda j=j, kc=kc: nc.tensor.matmul(psP[j][:], lhsT=hT[:, kc * 128:(kc + 1) * 128],
                                                                  rhs=WIN[:, kc * INW + j * 512: kc * INW + (j + 1) * 512],
                                                                  start=(kc == 0), stop=(kc == 7)), r=[hT, WIN], w=[psP[j]])

            def front_tail(tb):
                b = tb % 2
                QK, VBb, ATM = qk2[b], vb2[b], atm2[b]
                kk.act(lambda: nc.scalar.activation(out=eg[:], in_=psP[0][:, 256:512], func=AF.Exp, scale=-1.0), r=[psP[0]], w=[eg])
                kk.act(lambda: nc.scalar.copy(out=QK[:, 0:512], in_=psP[1][:]), r=[psP[1]], w=[QK])
                kk.act(lambda: nc.scalar.copy(out=QK[:, 512:1024], in_=psP[2][:]), r=[psP[2]], w=[QK])
                kk.act(lambda: nc.scalar.copy(out=QK[:, 1024:1408], in_=psP[3][:, 0:384]), r=[psP[3]], w=[QK])
                kk.pool(lambda: nc.gpsimd.tensor_tensor(out=sq[:], in0=QK[:], in1=QK[:], op=ALU.mult), r=[QK], w=[sq])
                kk.act(lambda: nc.scalar.copy(out=VBb[:, 0:512], in_=psP[4][:]), r=[psP[4]], w=[VBb])
                kk.act(lambda: nc.scalar.copy(out=VBb[:, 512:640], in_=psP[3][:, 384:512]), r=[psP[3]], w=[VBb])
                kk.dve(lambda: nc.vector.tensor_scalar(out=eg[:], in0=eg[:], scalar1=1.0, scalar2=None, op0=ALU.add), r=[eg], w=[eg])
                kk.dve(lambda: nc.vector.reciprocal(out=eg[:], in_=eg[:]), r=[eg], w=[eg])
                kk.dve(lambda: nc.vector.tensor_tensor(out=ATM[:], in0=psP[0][:, 0:256], in1=eg[:], op=ALU.mult), r=[psP[0], eg], w=[ATM])

            def back(tb):
                b = tb % 2
                t0 = tb * 128
                R = rot[tb % 3]
                qk, vb, a_tm = qk2[b], vb2[b], atm2[b]
                for c in range(2):
                    kk.pe(lambda c=c: nc.tensor.transpose(out=psA[:, c * 128:(c + 1) * 128], in_=a_tm[:, c * 128:(c + 1) * 128], identity=ident_f[:]),
                          r=[a_tm, ident_f], w=[psA])
                kk.act(lambda: nc.scalar.copy(out=aT[:], in_=psA[:]), r=[psA], w=[aT])
                kk.dma_pool(at_d[:, :, t0:t0 + 128].rearrange("c p t -> p c t"), aT[:].rearrange("p (c t) -> p c t", c=2), r=[aT])
                if tb == 0:
                    kk.dma_pool(ahs_d[0:15, :], a_tm[0:15, :], r=[a_tm])
                if tb == prompt_blocks - 1:
                    kk.dma_pool(ahs_d[15:30, :], a_tm[113:128, :], r=[a_tm])

            def back_qk(tb):
                b = tb % 2
                t0 = tb * 128
                R = rot[tb % 3]
                qk, vb, a_tm = qk2[b], vb2[b], atm2[b]
                kk.dve(lambda: nc.vector.tensor_reduce(out=ssg[:], in_=sq[:].rearrange("p (g d) -> p g d", d=64), axis=AX.X, op=ALU.add), r=[sq], w=[ssg])
                kk.act(lambda: nc.scalar.activation(out=ssg[:], in_=ssg[:], func=AF.Ln, bias=epst[:], scale=1.0 / 64), r=[ssg, epst], w=[ssg])
                kk.act(lambda: nc.scalar.activation(out=ssg[:], in_=ssg[:], func=AF.Exp, scale=-0.5), r=[ssg], w=[ssg])

            def bq2(tb):
                b = tb % 2
                t0 = tb * 128
                R = rot[tb % 3]
                qk, vb, a_tm = qk2[b], vb2[b], atm2[b]
                qk3 = qk[:].rearrange("p (g d) -> p g d", d=64)
                kk.dve(lambda: nc.vector.tensor_tensor(out=qk3, in0=qk3, in1=ssg[:].unsqueeze(2).to_broadcast([128, 22, 64]), op=ALU.mult), r=[qk, ssg], w=[qk])
                kk.dve(lambda: nc.vector.tensor_tensor(out=qk[:], in0=qk[:], in1=G[:], op=ALU.mult), r=[qk, G], w=[qk])
                qB = qk[:, 0:1024].rearrange("p (g d) -> p g d", d=64)
                x1, x2 = qB[:, :, 0:8], qB[:, :, 8:16]
                cB = R[:, 0:8].unsqueeze(1).to_broadcast([128, 16, 8])
                sB = R[:, 8:16].unsqueeze(1).to_broadcast([128, 16, 8])
                tv = [rt[i][:, 0:128].rearrange("p (g d) -> p g d", d=8) for i in range(4)]
                kk.dve(lambda: nc.vector.tensor_tensor(out=tv[0], in0=x1, in1=cB, op=ALU.mult), r=[qk, R], w=[rt[0]])
                kk.dve(lambda: nc.vector.tensor_tensor(out=tv[1], in0=x2, in1=sB, op=ALU.mult), r=[qk, R], w=[rt[1]])
                kk.dve(lambda: nc.vector.tensor_tensor(out=tv[2], in0=x2, in1=cB, op=ALU.mult), r=[qk, R], w=[rt[2]])
                kk.dve(lambda: nc.vector.tensor_tensor(out=tv[3], in0=x1, in1=sB, op=ALU.mult), r=[qk, R], w=[rt[3]])
                kk.dve(lambda: nc.vector.tensor_tensor(out=x1, in0=tv[0], in1=tv[1], op=ALU.subtract), r=[rt[0], rt[1]], w=[qk])
                kk.dve(lambda: nc.vector.tensor_tensor(out=x2, in0=tv[2], in1=tv[3], op=ALU.add), r=[rt[2], rt[3]], w=[qk])
                qC = qk[:, 1024:1408].rearrange("p (g h x d) -> p g h x d", g=6, h=2, x=2)
                y1, y2 = qC[:, :, :, 0, :], qC[:, :, :, 1, :]
                RC = R[:, 16:80].rearrange("p (h x d) -> p h x d", h=2, x=2)
                cC = RC[:, :, 0, :].unsqueeze(1).to_broadcast([128, 6, 2, 16])
                sC = RC[:, :, 1, :].unsqueeze(1).to_broadcast([128, 6, 2, 16])
                tw = [rt[i][:, 0:192].rearrange("p (g h d) -> p g h d", g=6, h=2) for i in range(4)]
                kk.dve(lambda: nc.vector.tensor_tensor(out=tw[0], in0=y1, in1=cC, op=ALU.mult), r=[qk, R], w=[rt[0]])
                kk.dve(lambda: nc.vector.tensor_tensor(out=tw[1], in0=y2, in1=sC, op=ALU.mult), r=[qk, R], w=[rt[1]])
                kk.dve(lambda: nc.vector.tensor_tensor(out=tw[2], in0=y2, in1=cC, op=ALU.mult), r=[qk, R], w=[rt[2]])
                kk.dve(lambda: nc.vector.tensor_tensor(out=tw[3], in0=y1, in1=sC, op=ALU.mult), r=[qk, R], w=[rt[3]])
                kk.dve(lambda: nc.vector.tensor_tensor(out=y1, in0=tw[0], in1=tw[1], op=ALU.subtract), r=[rt[0], rt[1]], w=[qk])
                kk.dve(lambda: nc.vector.tensor_tensor(out=y2, in0=tw[2], in1=tw[3], op=ALU.add), r=[rt[2], rt[3]], w=[qk])
                kk.act(lambda: nc.scalar.copy(out=qkb[:], in_=qk[:]), r=[qk], w=[qkb])
                for c in range(11):
                    kk.pe(lambda c=c: nc.tensor.transpose(out=psQ[:, c * 128:(c + 1) * 128], in_=qkb[:, c * 128:(c + 1) * 128], identity=ident_b[:]),
                          r=[qkb, ident_b], w=[psQ])

            def bq3(tb):
                b = tb % 2
                t0 = tb * 128
                qk, vb, a_tm = qk2[b], vb2[b], atm2[b]
                kk.dve(lambda: nc.vector.tensor_copy(out=qkT[:], in_=psQ[:, 0:1408]), r=[psQ], w=[qkT])
                kk.dma_pool(qkt_d[:, :, t0:t0 + 128].rearrange("c p t -> p c t"), qkT[:].rearrange("p (c t) -> p c t", c=11), r=[qkT])
                kk.dma_pool(v_d[t0:t0 + 128, :], vb[:], r=[vb])
                if tb < prompt_blocks:
                    for u in range(5):
                        kc0 = 512 + u * 128 if u < 4 else 1280
                        kk.dma_pool(kts_l[u][:, t0:t0 + 128], qkT[:, kc0:kc0 + 128], r=[qkT])
                        kk.dma_pool(vs_l[u][t0:t0 + 128, :], vb[:, u * 128:(u + 1) * 128], r=[vb])

            load_blk(0)
            if NTB > 1:
                load_blk(1)
            fh1(0)
            fh2(0)
            front_tail(0)
            for tb in range(NTB):
                if tb + 2 < NTB:
                    load_blk(tb + 2)
                back(tb)
                nxt = tb + 1 < NTB
                if nxt:
                    fh1(tb + 1)
                back_qk(tb)
                if nxt:
                    fh2(tb + 1)
                bq2(tb)
                if nxt:
                    front_tail(tb + 1)
                bq3(tb)
            kk.barrier()

        if cfg.use_cc and cfg.stop >= 2:
            for src, dst in [(ahs_d, aha_d)] + [(kts_l[u], kta_l[u]) for u in range(5)] + [(vs_l[u], va_l[u]) for u in range(5)]:
                kk.emit(kk.POOL, lambda src=src, dst=dst: nc.gpsimd.collective_compute(
                    "AllGather", ALU.bypass, replica_groups=RG, ins=[src.opt()], outs=[dst.opt()]), counter=kk.lane("cc"))

        with ExitStack() as st:
          if cfg.stop >= 3:
            abuf = [sb(st, f"abuf{i}", [128, SEG + 30], F32) for i in range(2)]
            convo = [sb(st, f"convo{i}", [128, SEG], F32) for i in range(2)]
            sqb = [sb(st, f"sqb{i}", [128, 512], F32) for i in range(2)]
            mean_sb = sb(st, "mean_sb", [128, 512], F32)
            m2 = sb(st, "m2", [128, 512], F32)
            rstd = sb(st, "rstd", [128, 512], F32)
            dd = [sb(st, f"dd{i}", [128, 512], F32) for i in range(2)]
            ee = [sb(st, f"ee{i}", [128, 512], F32) for i in range(2)]
            ob = [sb(st, f"ob{i}", [128, 512], BF16) for i in range(2)]
            ahr = sb(st, "ahr", [120, 256], F32)
            ps_mean = ps(st, "ps_mean", [128, 512], F32)
            ps_msq = ps(st, "ps_msq", [128, 512], F32)
            ps_halo = ps(st, "ps_halo", [128, 64], F32)
            nseg = cfg.np + cfg.ns
            for s in list(range(cfg.np, nseg)) + list(range(cfg.np)):
                if s == 0:
                    kk.barrier()
                    if cfg.use_cc:
                        kk.dma_sp(ahr[:], aha_d[:, :], w=[ahr])
                t0 = s * SEG
                is_p = s < cfg.np
                for c in range(2):
                    A = abuf[c]
                    left_local = is_p and s > 0
                    right_local = is_p and s < cfg.np - 1
                    lo = t0 - 15 if left_local else t0
                    hi = t0 + SEG + 15 if right_local else t0 + SEG
                    kk.dma_sp(A[:, 15 + (lo - t0): 15 + (hi - t0)], at_d[c, :, lo:hi], w=[A])
                    if not left_local:
                        kk.pool(lambda A=A: nc.gpsimd.memset(A[:, 0:15], 0.0), w=[A])
                    if not right_local:
                        kk.pool(lambda A=A: nc.gpsimd.memset(A[:, SEG + 15:SEG + 30], 0.0), w=[A])
                    if cfg.use_cc and is_p and (s == 0 or s == cfg.np - 1):
                        kk.pe(lambda c=c: nc.tensor.matmul(ps_halo[:, 0:30], lhsT=ahr[:, c * 128:(c + 1) * 128], rhs=sela[:], start=True, stop=True),
                              r=[ahr, sela], w=[ps_halo])
                        if s == 0:
                            kk.act(lambda A=A: nc.scalar.copy(out=A[:, 0:15], in_=ps_halo[:, 0:15]), r=[ps_halo], w=[A])
                        if s == cfg.np - 1:
                            kk.act(lambda A=A: nc.scalar.copy(out=A[:, SEG + 15:SEG + 30], in_=ps_halo[:, 15:30]), r=[ps_halo], w=[A])
                    CO = convo[c]
                    kk.dve(lambda A=A, CO=CO, c=c: nc.vector.tensor_scalar(out=CO[:], in0=A[:, 0:SEG], scalar1=pvc(l, PV_CAW + c * 31),
                                                                          scalar2=pvc(l, PV_CAB + c), op0=ALU.mult, op1=ALU.add), r=[A, pv], w=[CO])
                    for j in range(1, 31):
                        kk.dve(lambda A=A, CO=CO, c=c, j=j: nc.vector.scalar_tensor_tensor(out=CO[:], in0=A[:, j:j + SEG], scalar=pvc(l, PV_CAW + c * 31 + j),
                                                                                         in1=CO[:], op0=ALU.mult, op1=ALU.add), r=[A, pv, CO], w=[CO])
                for ti in range(SEG // 512):
                    c0 = ti * 512
                    for c in range(2):
                        kk.pool(lambda c=c: nc.gpsimd.tensor_tensor(out=sqb[c][:], in0=convo[c][:, c0:c0 + 512], in1=convo[c][:, c0:c0 + 512], op=ALU.mult),
                                r=[convo[c]], w=[sqb[c]])
                    for c in range(2):
                        kk.pe(lambda c=c: nc.tensor.matmul(ps_mean[:], lhsT=onesA[:], rhs=convo[c][:, c0:c0 + 512], start=(c == 0), stop=(c == 1)),
                              r=[onesA, convo[c]], w=[ps_mean])
                    for c in range(2):
                        kk.pe(lambda c=c: nc.tensor.matmul(ps_msq[:], lhsT=onesA[:], rhs=sqb[c][:], start=(c == 0), stop=(c == 1)),
                              r=[onesA, sqb[c]], w=[ps_msq])
                    kk.act(lambda: nc.scalar.copy(out=mean_sb[:], in_=ps_mean[:]), r=[ps_mean], w=[mean_sb])
                    kk.dve(lambda: nc.vector.tensor_tensor(out=m2[:], in0=mean_sb[:], in1=mean_sb[:], op=ALU.mult), r=[mean_sb], w=[m2])
                    kk.dve(lambda: nc.vector.tensor_tensor(out=m2[:], in0=ps_msq[:], in1=m2[:], op=ALU.subtract), r=[ps_msq, m2], w=[m2])
                    kk.dve(lambda: nc.vector.tensor_scalar(out=m2[:], in0=m2[:], scalar1=0.0, scalar2=None, op0=ALU.max), r=[m2], w=[m2])
                    kk.act(lambda: nc.scalar.activation(out=rstd[:], in_=m2[:], func=AF.Ln, bias=epst[:], scale=1.0), r=[m2, epst], w=[rstd])
                    kk.act(lambda: nc.scalar.activation(out=rstd[:], in_=rstd[:], func=AF.Exp, scale=-0.5), r=[rstd], w=[rstd])
                    for c in range(2):
                        Dd, Ee, Ob = dd[c], ee[c], ob[c]
                        kk.dve(lambda c=c, Dd=Dd: nc.vector.tensor_tensor(out=Dd[:], in0=convo[c][:, c0:c0 + 512], in1=mean_sb[:], op=ALU.subtract),
                               r=[convo[c], mean_sb], w=[Dd])
                        kk.dve(lambda Dd=Dd: nc.vector.tensor_tensor(out=Dd[:], in0=Dd[:], in1=rstd[:], op=ALU.mult), r=[Dd, rstd], w=[Dd])
                        kk.dve(lambda c=c, Dd=Dd: nc.vector.tensor_scalar(out=Dd[:], in0=Dd[:], scalar1=pvc(l, PV_LNG + c), scalar2=pvc(l, PV_LNB + c),
                                                                         op0=ALU.mult, op1=ALU.add), r=[Dd, pv], w=[Dd])
                        kk.act(lambda Dd=Dd, Ee=Ee: nc.scalar.activation(out=Ee[:], in_=Dd[:], func=AF.Exp, scale=-1.0), r=[Dd], w=[Ee])
                        kk.pool(lambda Ee=Ee: nc.gpsimd.tensor_scalar(out=Ee[:], in0=Ee[:], scalar1=1.0, scalar2=1.0, op0=ALU.add, op1=ALU.mult), r=[Ee], w=[Ee])
                        kk.dve(lambda Ee=Ee: nc.vector.reciprocal(out=Ee[:], in_=Ee[:]), r=[Ee], w=[Ee])
                        kk.pool(lambda Dd=Dd, Ee=Ee, Ob=Ob: nc.gpsimd.tensor_tensor(out=Ob[:], in0=Dd[:], in1=Ee[:], op=ALU.mult), r=[Dd, Ee], w=[Ob])
                        kk.dma_pool(cata_d[c, :, t0 + c0:t0 + c0 + 512], Ob[:], r=[Ob])
            kk.barrier()

        with ExitStack() as st:
          if cfg.stop >= 4:
            LKMAX = cfg.sp if cfg.use_cc else max(PTOK, SEG)
            NCKMAX = LKMAX // 128
            GTOK = max(PTOK, SEG)
            KT = sb(st, "KT", [128, LKMAX], BF16)
            VB = sb(st, "VB", [128, NCKMAX * 192], BF16)
            catT = sb(st, "catT", [128, 8 * GTOK], BF16)
            WOUT = sb(st, "WOUT", [128, 8 * D], BF16)
            Qa = [sb(st, f"Qa{i}", [128, 512], BF16) for i in range(2)]
            Qb = [sb(st, f"Qb{i}", [128, 512], BF16) for i in range(2)]
            NPT = 4
            PT = [sb(st, f"PT{i}", [128, 1024], BF16) for i in range(NPT)]
            acc = sb(st, "acc", [128, 1024], F32)
            accp = sb(st, "accp", [128, 1024], F32)
            fo = [sb(st, f"fo{i}", [128, 512], F32) for i in range(2)]
            fl = [sb(st, f"fl{i}", [128, 512], F32) for i in range(2)]
            fd = sb(st, "fd", [128, 512], F32)
            fsq = fl[1]
            frs = fl[0]
            xt = [accp, accp]
            xm = acc
            ss = sb(st, "css", [128, 1], F32)
            lnv = sb(st, "clnv", [128, 1], F32)
            hb = sb(st, "chb", [128, D], BF16)
            hT = sb(st, "chT", [128, D], BF16)
            junk = hb
            sc = [ps(st, f"sc{i}", [128, 1024], F32) for i in range(2)]
            po = [ps(st, f"po{m}", [128, 512], F32) for m in range(2)]
            pf = [ps(st, f"pf{m}", [128, 512], F32) for m in range(2)]

            for kc in range(8):
                kk.dma_sp(WOUT[:, kc * D:(kc + 1) * D], wb_out[l, kc * 128:(kc + 1) * 128, :], w=[WOUT])

            groups = [("p", 0, PTOK)] + [("s", PTOK + i * SEG, SEG) for i in range(cfg.ns)]
            scslot = [0]
            ptslot = [0]
            KTr = [Buf(KT.t), Buf(KT.t)]
            VBr = [Buf(VB.t), Buf(VB.t)]
            ucount = [0]
            synced = [False]
            pend = {}

            def flush_pending():
                if pend.get("sums"):
                    pend["sums"]()
                    pend["sums"] = None
                if pend.get("st23"):
                    pend["st23"]()
                    pend["st23"] = None
                if pend.get("st3"):
                    pend["st3"]()
                    pend["st3"] = None
            for (gk, g0, gn) in groups:
                use_all = (gk == "p" and cfg.use_cc)
                Lk = cfg.sp if use_all else gn
                nck = Lk // 128
                use_reg = (gk == "s") and (2 * Lk <= LKMAX)
                if use_reg and not synced[0]:
                    synced[0] = True
                    for rb in KTr:
                        rb.w = KT.w
                        rb.r = dict(KT.r)
                    for rb in VBr:
                        rb.w = VB.w
                        rb.r = dict(VB.r)
                for c in range(2):
                    kk.dma_sp(catT[:, c * GTOK: c * GTOK + gn], cata_d[c, :, g0:g0 + gn], w=[catT])
                for u in range(6):
                    isB = u < 4
                    if use_reg:
                        reg = ucount[0] % 2
                        ucount[0] += 1
                        KTb, VBb, kt0 = KTr[reg], VBr[reg], reg * Lk
                        VB3 = VB[:, reg * nck * 192:(reg + 1) * nck * 192].rearrange("p (c e) -> p c e", e=192)
                    else:
                        KTb, VBb, kt0 = KT, VB, 0
                        VB3 = VB[:, 0:nck * 192].rearrange("p (c e) -> p c e", e=192)
                    kchunk = u if isB else 4
                    if use_all:
                        for r in range(RANKS):
                            kk.dma_sp(KT[:, r * PTOK:(r + 1) * PTOK], kta_l[kchunk][r * 128:(r + 1) * 128, :], w=[KT])
                    else:
                        kk.dma_sp(KTb[:, kt0:kt0 + Lk], qkt_d[4 + u if isB else 10, :, g0:g0 + gn], w=[KTb])
                    if isB:
                        vcols = slice(u * 128, (u + 1) * 128)
                        vdst = lambda c0, c1, VB3=VB3: VB3[:, c0:c1, 0:128]
                    else:
                        g = u - 4
                        vcols = slice(512 + g * 64, 512 + (g + 1) * 64)
                        vdst = lambda c0, c1, VB3=VB3: VB3[:, c0:c1, 64:128]
                        kk.pool(lambda VB3=VB3: nc.gpsimd.memset(VB3[:, :, 0:64], 1.0), w=[VBb])
                        kk.pool(lambda VB3=VB3: nc.gpsimd.memset(VB3[:, :, 128:192], 1.0), w=[VBb])
                    if use_all:
                        vsrc = va_l[kchunk]
                        voff = 0
                        vcols = slice(0, 128) if isB else slice(g * 64, (g + 1) * 64)
                    else:
                        vsrc = v_d
                        voff = g0
                    for c0 in range(0, nck, 16):
                        c1 = min(nck, c0 + 16)
                        kk.dma_sp(vdst(c0, c1), vsrc[voff + c0 * 128: voff + c1 * 128, vcols].rearrange("(c p) e -> p c e", p=128), w=[VBb])
                    if isB:
                        qrows = [slice(0, 64), slice(64, 128)]
                    else:
                        qrows = [slice(g * 64, (g + 1) * 64)] * 2
                    if u == 0 or u >= 4:
                        for qi_ in range(2):
                            for m_, QQ in enumerate((Qa[qi_], Qb[qi_])):
                                zr = slice(64, 128) if qrows[m_].start == 0 else slice(0, 64)
                                kk.pool(lambda QQ=QQ, zr=zr: nc.gpsimd.memset(QQ[zr, :], 0.0), w=[QQ])
                    if isB:
                        lhs_v = [lambda ck, VB3=VB3: VB3[:, ck, 0:128], lambda ck, VB3=VB3: VB3[:, ck, 0:128]]
                        rows = [slice(0, 64), slice(64, 128)]
                    else:
                        lhs_v = [lambda ck, VB3=VB3: VB3[:, ck, 64:192], lambda ck, VB3=VB3: VB3[:, ck, 0:128]]
                        rows = [slice(g * 64, (g + 1) * 64)] * 2
                    for qt in range(gn // 512):
                        q0 = g0 + qt * 512
                        qi = qt % 2
                        if isB:
                            kk.dma_sp(Qa[qi][0:64, :], qkt_d[u, 0:64, q0:q0 + 512], w=[Qa[qi]])
                            kk.dma_sp(Qb[qi][64:128, :], qkt_d[u, 64:128, q0:q0 + 512], w=[Qb[qi]])
                        else:
                            kk.dma_sp(Qa[qi][qrows[0], :], qkt_d[8, qrows[0], q0:q0 + 512], w=[Qa[qi]])
                            kk.dma_sp(Qb[qi][qrows[1], :], qkt_d[9, qrows[1], q0:q0 + 512], w=[Qb[qi]])
                        qbufs = [Qa[qi], Qb[qi]]
                        slots = {}

                        def scores(ck):
                            sl = scslot[0] % 2
                            scslot[0] += 1
                            slots[ck] = sl
                            for m in range(2):
                                kk.pe(lambda m=m, sl=sl, ck=ck: nc.tensor.matmul(sc[sl][:, m * 512:(m + 1) * 512], lhsT=KTb[:, kt0 + ck * 128:kt0 + (ck + 1) * 128],
                                                                                 rhs=qbufs[m][:, :], start=True, stop=True),
                                      r=[KTb, qbufs[m]], w=[sc[sl]])

                        scores(0)
                        for ck in range(nck):
                            if ck + 1 < nck:
                                scores(ck + 1)
                            if ck == min(2, nck - 1) and pend.get("sums"):
                                pend["sums"]()
                                pend["sums"] = None
                            if ck == min(6, nck - 1):
                                if pend.get("sums"):
                                    pend["sums"]()
                                    pend["sums"] = None
                                if pend.get("st23"):
                                    pend["st23"]()
                                    pend["st23"] = None
                            if ck == min(11, nck - 1) and pend.get("st3"):
                                if pend.get("st23"):
                                    pend["st23"]()
                                    pend["st23"] = None
                                pend["st3"]()
                                pend["st3"] = None
                            ssl = slots.pop(ck)
                            sl = ptslot[0] % NPT
                            ptslot[0] += 1
                            kk.act(lambda sl=sl, ssl=ssl: nc.scalar.activation(out=PT[sl][:], in_=sc[ssl][:], func=AF.Exp), r=[sc[ssl]], w=[PT[sl]])
                            def emit_pv(ckk, sll):
                                for m in range(2):
                                    kk.pe(lambda m=m: nc.tensor.matmul(
                                        po[m][:], lhsT=lhs_v[m](ckk), rhs=PT[sll][:, m * 512:(m + 1) * 512], start=(ckk == 0), stop=(ckk == nck - 1)),
                                        r=[VBb, PT[sll]], w=[po[m]])
                            if ck >= 1:
                                emit_pv(ck - 1, pv_prev_sl)
                            pv_prev_sl = sl
                            if ck == nck - 1:
                                emit_pv(ck, sl)
                            if isB:
                                tb16 = accp[:].bitcast(BF16)
                                t01, t23 = tb16[:, 0:1024], tb16[:, 1024:2048]
                                if ck % 4 == 0:
                                    prev_sl = sl
                                elif ck % 4 == 1:
                                    kk.dve(lambda a_=prev_sl, b_=sl: nc.vector.tensor_tensor(out=t01, in0=PT[a_][:], in1=PT[b_][:], op=ALU.add),
                                           r=[PT[prev_sl], PT[sl]], w=[accp])
                                elif ck % 4 == 2:
                                    prev_sl = sl
                                else:
                                    kk.dve(lambda a_=prev_sl, b_=sl: nc.vector.tensor_tensor(out=t23, in0=PT[a_][:], in1=PT[b_][:], op=ALU.add),
                                           r=[PT[prev_sl], PT[sl]], w=[accp])
                                    kk.dve(lambda: nc.vector.tensor_tensor(out=t01, in0=t01, in1=t23, op=ALU.add), r=[accp], w=[accp])
                                    if ck == 3:
                                        kk.dve(lambda: nc.vector.tensor_copy(out=acc[:], in_=t01), r=[accp], w=[acc])
                                    else:
                                        kk.dve(lambda: nc.vector.tensor_tensor(out=acc[:], in0=acc[:], in1=t01, op=ALU.add), r=[accp, acc], w=[acc])
                        cchunk = 2 + u if isB else 6 + (u - 4)
                        dst = catT[:, cchunk * GTOK + qt * 512: cchunk * GTOK + (qt + 1) * 512]
                        kk.act(lambda: nc.scalar.copy(out=fo[0][:], in_=po[0][:]), r=[po[0]], w=[fo[0]])
                        kk.dve(lambda: nc.vector.tensor_copy(out=fo[1][:], in_=po[1][:]), r=[po[1]], w=[fo[1]])
                        if isB:
                            def st_sums():
                                for m in range(2):
                                    kk.pe(lambda m=m: nc.tensor.matmul(pf[m][:], lhsT=onesF[:], rhs=acc[:, m * 512:(m + 1) * 512], start=True, stop=True),
                                          r=[onesF, acc], w=[pf[m]])

                            def st23(dst=dst):
                                for m in range(2):
                                    kk.act(lambda m=m: nc.scalar.activation(out=fl[m][:], in_=pf[m][:], func=AF.Ln), r=[pf[m]], w=[fl[m]])
                                    kk.act(lambda m=m: nc.scalar.activation(out=fl[m][:], in_=fl[m][:], func=AF.Exp, scale=-1.0), r=[fl[m]], w=[fl[m]])
                                    kk.dve(lambda m=m: nc.vector.tensor_tensor(out=fo[m][:], in0=fo[m][:], in1=fl[m][:], op=ALU.mult), r=[fo[m], fl[m]], w=[fo[m]])
                                kk.dve(lambda: nc.vector.scalar_tensor_tensor(out=fd[:], in0=fo[1][:], scalar=neglam[:, l:l + 1], in1=fo[0][:],
                                                                              op0=ALU.mult, op1=ALU.add), r=[fo[0], fo[1], neglam], w=[fd])
                                kk.dve(lambda: nc.vector.tensor_tensor(out=fsq[:], in0=fd[:], in1=fd[:], op=ALU.mult), r=[fd], w=[fsq])

                            def st3(dst=dst):
                                kk.pe(lambda: nc.tensor.matmul(pf[0][:], lhsT=onesS[:], rhs=fsq[:], start=True, stop=True), r=[onesS, fsq], w=[pf[0]])
                                kk.act(lambda: nc.scalar.activation(out=frs[:], in_=pf[0][:], func=AF.Ln, bias=epst[:], scale=1.0), r=[pf[0], epst], w=[frs])
                                kk.act(lambda: nc.scalar.activation(out=frs[:], in_=frs[:], func=AF.Exp, scale=-0.5), r=[frs], w=[frs])
                                kk.dve(lambda: nc.vector.tensor_tensor(out=fd[:], in0=fd[:], in1=frs[:], op=ALU.mult), r=[fd, frs], w=[fd])
                                kk.dve(lambda: nc.vector.tensor_scalar(out=dst, in0=fd[:], scalar1=pvc(l, PV_SUB), scalar2=1.0 - lam_init,
                                                                      op0=ALU.mult, op1=ALU.mult), r=[fd, pv], w=[catT])
                        else:
                            st_sums = None

                            def st23(dst=dst):
                                for m in range(2):
                                    lr = slice(64, 128) if m == 0 else slice(0, 64)
                                    orr = slice(0, 64) if m == 0 else slice(64, 128)
                                    kk.act(lambda m=m, lr=lr: nc.scalar.activation(out=fl[m][lr, :], in_=fo[m][lr, :], func=AF.Ln), r=[fo[m]], w=[fl[m]])
                                    kk.act(lambda m=m, lr=lr: nc.scalar.activation(out=fl[m][lr, :], in_=fl[m][lr, :], func=AF.Exp, scale=-1.0), r=[fl[m]], w=[fl[m]])
                                    kk.pool(lambda m=m, orr=orr: nc.gpsimd.memset(fl[m][orr, :], 0.0), w=[fl[m]])

                            def st3(dst=dst):
                                for m in range(2):
                                    kk.pe(lambda m=m: nc.tensor.matmul(pf[m][:], lhsT=swapF[:], rhs=fl[m][:], start=True, stop=True), r=[swapF, fl[m]], w=[pf[m]])
                                kk.dve(lambda: nc.vector.tensor_tensor(out=dst[0:64, :], in0=fo[0][0:64, :], in1=pf[0][0:64, :], op=ALU.mult),
                                       r=[fo[0], pf[0]], w=[catT])
                                kk.dve(lambda: nc.vector.tensor_tensor(out=dst[64:128, :], in0=fo[1][64:128, :], in1=pf[1][64:128, :], op=ALU.mult),
                                       r=[fo[1], pf[1]], w=[catT])
                        pend["sums"] = st_sums
                        pend["st23"] = st23
                        pend["st3"] = st3
                flush_pending()
                nblk = gn // 128

                def outproj(bi, banks):
                    for jn in range(2):
                        for kc in range(8):
                            kk.pe(lambda jn=jn, kc=kc: nc.tensor.matmul(banks[jn][:], lhsT=catT[:, kc * GTOK + bi * 128: kc * GTOK + (bi + 1) * 128],
                                                                        rhs=WOUT[:, kc * D + jn * 512: kc * D + (jn + 1) * 512],
                                                                        start=(kc == 0), stop=(kc == 7)), r=[catT, WOUT], w=[banks[jn]])

                outproj(0, po)
                for bi in range(nblk):
                    t0 = g0 + bi * 128
                    X = xt[0]
                    banks = po if bi % 2 == 0 else pf
                    if bi + 1 < nblk:
                        outproj(bi + 1, pf if bi % 2 == 0 else po)
                    kk.dma_sp(X[:], x_src[t0:t0 + 128, :], w=[X])
                    for jn in range(2):
                        kk.dve(lambda jn=jn, X=X, banks=banks: nc.vector.tensor_tensor(out=xm[:, jn * 512:(jn + 1) * 512], in0=banks[jn][:],
                                                                                      in1=X[:, jn * 512:(jn + 1) * 512], op=ALU.add),
                               r=[banks[jn], X], w=[xm])
                    kk.dma_pool(xm_d[t0:t0 + 128, :], xm[:], r=[xm])
                    psT = sc[0]
                    psT_bf = psT[:, 0:512].bitcast(BF16)
                    rmsnorm_to_T_c1(kk, nc, xm, hb, hT, psT, psT_bf, junk, ss, lnv, epst, ident_b)
                    kk.dma_pool(h2t_d[:, :, t0:t0 + 128].rearrange("c p t -> p c t"), hT[:].rearrange("p (c t) -> p c t", c=8), r=[hT])
                    if gk == "p" and bi == 0:
                        kk.dma_pool(hhs_d[0:1, :], hb[0:1, :], r=[hb])
                    if gk == "p" and bi == nblk - 1:
                        kk.dma_pool(hhs_d[1:2, :], hb[127:128, :], r=[hb])
            kk.barrier()

        if cfg.use_cc and cfg.stop >= 5:
            cc2_lane = kk.lane("cc")
            kk.emit(kk.POOL, lambda: nc.gpsimd.collective_compute(
                "AllGather", ALU.bypass, replica_groups=RG, ins=[hhs_d.opt()], outs=[hha_d.opt()]), counter=cc2_lane)
            cc2_dep = (cc2_lane, cc2_lane.count)

        with ExitStack() as st:
          if cfg.stop >= 6:
            WUP = sb(st, "WUP", [128, 8 * 2 * FF], BF16)
            WDN = sb(st, "WDN", [128, 22 * D], BF16)
            h2t = [sb(st, f"h2t{i}", [128, 8 * 512], BF16) for i in range(2)]
            gT = sb(st, "gT", [128, 22 * 512], BF16)
            tvb = [sb(st, f"tv{i}", [128, 512], F32) for i in range(2)]
            tgb = [sb(st, f"tg{i}", [128, 512], F32) for i in range(2)]
            sgb = [sb(st, f"sg{i}", [128, 512], F32) for i in range(2)]
            xmb = [sb(st, f"fxm{i}", [128, D], F32) for i in range(2)]
            xo = [sb(st, f"fxo{i}", [128, D], F32) for i in range(2)]
            hh = sb(st, "hh", [8, D], BF16)
            halo_h = sb(st, "halo_h", [128, 16], BF16)
            psv = [ps(st, f"psv{i}", [128, 512], F32) for i in range(2)]
            psg = [ps(st, f"psg{i}", [128, 512], F32) for i in range(2)]
            psy = [ps(st, f"psy{i}", [128, 512], F32) for i in range(4)]
            for kc in range(8):
                kk.dma_sp(WUP[:, kc * 2 * FF:(kc + 1) * 2 * FF], wb_up[l, kc * 128:(kc + 1) * 128, :], w=[WUP])
            for kc in range(22):
                kk.dma_sp(WDN[:, kc * D:(kc + 1) * D], wb_dn[l, kc * 128:(kc + 1) * 128, :], w=[WDN])
            halo_done = [False]

            def prep_halo():
                if halo_done[0]:
                    return
                halo_done[0] = True
                if cfg.use_cc:
                    kk._wait(kk.SP, cc2_dep[0], cc2_dep[1])
                    kk.dma_sp(hh[:], hha_d[:, :], w=[hh])
                    for kc in range(8):
                        kk.pe(lambda kc=kc: nc.tensor.matmul(psy[0][:, kc * 2:kc * 2 + 2], lhsT=hh[:, kc * 128:(kc + 1) * 128], rhs=selh[:], start=True, stop=True),
                              r=[hh, selh], w=[psy[0]])
                    kk.dve(lambda: nc.vector.tensor_copy(out=halo_h[:], in_=psy[0][:, 0:16]), r=[psy[0]], w=[halo_h])
                else:
                    kk.dve(lambda: nc.vector.memset(halo_h[:], 0.0), w=[halo_h])
            halo3 = halo_h[:].rearrange("p (c x) -> p c x", x=2)

            fsegs = [("s", PTOK + i * SEG, SEG) for i in range(cfg.ns)] + [("p", 0, PTOK)]
            tiles = []
            for (gk, g0, gn) in fsegs:
                kfull, rem = gn // 510, gn % 510
                sizes = [510] * kfull + ([rem] if rem else [])
                if 0 < rem < 256 and kfull >= 1:
                    tot = rem + 510
                    sizes = [510] * (kfull - 1) + [tot // 2, tot - tot // 2]
                s0 = 0
                for n in sizes:
                    tiles.append((gk, g0, gn, s0, n))
                    s0 += n
                assert s0 == gn
            pslot = [0]
            yslot = [0]

            def load_tile(i):
                gk, g0, gn, s0, n = tiles[i]
                if gk == "p":
                    prep_halo()
                H = h2t[i % 2]
                H3 = H[:].rearrange("p (c t) -> p c t", c=8)
                lo = s0 - 1
                hi = s0 + n + 1
                clo, chi = max(lo, 0), min(hi, gn)
                kk.dma_sp(H3[:, :, clo - lo: chi - lo], h2t_d[:, :, g0 + clo: g0 + chi].rearrange("c p t -> p c t"), w=[H])
                if lo < 0:
                    if gk == "p":
                        kk.pool(lambda: nc.gpsimd.tensor_copy(out=H3[:, :, 0:1], in_=halo3[:, :, 0:1]), r=[halo_h], w=[H])
                    else:
                        kk.pool(lambda: nc.gpsimd.memset(H3[:, :, 0:1], 0.0), w=[H])
                if hi > gn:
                    if gk == "p":
                        kk.pool(lambda: nc.gpsimd.tensor_copy(out=H3[:, :, n + 1:n + 2], in_=halo3[:, :, 1:2]), r=[halo_h], w=[H])
                    else:
                        kk.pool(lambda: nc.gpsimd.memset(H3[:, :, n + 1:n + 2], 0.0), w=[H])

            load_tile(0)
            for i, (gk, g0, gn, s0, n) in enumerate(tiles):
                if i + 1 < len(tiles):
                    load_tile(i + 1)
                H = h2t[i % 2]
                N = n + 2
                for j in range(22):
                    sl = pslot[0] % 2
                    pslot[0] += 1
                    PV_, PG_ = psv[sl], psg[sl]
                    for (P_, ch) in ((PV_, j), (PG_, 22 + j)):
                        for kc in range(8):
                            kk.pe(lambda P_=P_, ch=ch, kc=kc: nc.tensor.matmul(P_[:, 0:N], lhsT=WUP[:, kc * 2 * FF + ch * 128: kc * 2 * FF + (ch + 1) * 128],
                                                                               rhs=H[:, kc * 512: kc * 512 + N], start=(kc == 0), stop=(kc == 7)),
                                  r=[WUP, H], w=[P_])
                    TV, TG, SG = tvb[sl], tgb[sl], sgb[sl]
                    for (P_, TT, ch) in ((PV_, TV, j), (PG_, TG, 22 + j)):
                        kk.act(lambda P_=P_, TT=TT, ch=ch: nc.scalar.activation(out=TT[:, 0:n], in_=P_[:, 0:n], func=AF.Identity,
                                                                                 scale=pvc(l, PV_FW + ch * 3), bias=pvc(l, PV_FB + ch)), r=[P_, pv], w=[TT])
                        for jj in (1, 2):
                            kk.dve(lambda P_=P_, TT=TT, ch=ch, jj=jj: nc.vector.scalar_tensor_tensor(out=TT[:, 0:n], in0=P_[:, jj:jj + n],
                                                                                                      scalar=pvc(l, PV_FW + ch * 3 + jj), in1=TT[:, 0:n],
                                                                                                      op0=ALU.mult, op1=ALU.add), r=[P_, pv, TT], w=[TT])
                    kk.act(lambda TG=TG, SG=SG: nc.scalar.activation(out=SG[:, 0:n], in_=TG[:, 0:n], func=AF.Silu), r=[TG], w=[SG])
                    kk.pool(lambda TV=TV, SG=SG, j=j: nc.gpsimd.tensor_tensor(out=gT[:, j * 512: j * 512 + n], in0=TV[:, 0:n], in1=SG[:, 0:n], op=ALU.mult),
                            r=[TV, SG], w=[gT])
                b0 = 0
                bi = 0
                while b0 < n:
                    m = min(128, n - b0)
                    tk = g0 + s0 + b0
                    XM, XO = xmb[bi % 2], xo[bi % 2]
                    kk.dma_sp(XM[0:m, :], xm_d[tk:tk + m, :], w=[XM])
                    for jn in range(2):
                        Y = psy[yslot[0] % 4]
                        yslot[0] += 1
                        for j in range(22):
                            kk.pe(lambda Y=Y, j=j, jn=jn, b0=b0, m=m: nc.tensor.matmul(Y[0:m, :], lhsT=gT[:, j * 512 + b0: j * 512 + b0 + m],
                                                                                       rhs=WDN[:, j * D + jn * 512: j * D + (jn + 1) * 512],
                                                                                       start=(j == 0), stop=(j == 21)), r=[gT, WDN], w=[Y])
                        kk.dve(lambda Y=Y, jn=jn, m=m, XM=XM, XO=XO: nc.vector.tensor_tensor(out=XO[0:m, jn * 512:(jn + 1) * 512], in0=Y[0:m, :],
                                                                                            in1=XM[0:m, jn * 512:(jn + 1) * 512], op=ALU.add),
                               r=[Y, XM], w=[XO])
                    kk.dma_pool(x_dst[tk:tk + m, :], XO[0:m, :], r=[XO])
                    b0 += m
                    bi += 1
            kk.barrier()

    kk.final_wait()
    return nc, kk


def rmsnorm_to_T_c1(kk, nc, xsrc, hb, hT, psT, psT_bf, junk, ss, lnv, epst, ident_b):
    kk.dve(lambda: nc.vector.scalar_tensor_tensor(out=junk[:], in0=xsrc[:], scalar=1.0, in1=xsrc[:], op0=ALU.mult, op1=ALU.mult,
                                                  accum_out=ss[:]), r=[xsrc], w=[junk, ss])
    kk.act(lambda: nc.scalar.activation(out=lnv[:], in_=ss[:], func=AF.Ln, bias=epst[:], scale=1.0 / D), r=[ss, epst], w=[lnv])
    kk.act(lambda: nc.scalar.activation(out=lnv[:], in_=lnv[:], func=AF.Exp, scale=-0.5), r=[lnv], w=[lnv])
    kk.act(lambda: nc.scalar.activation(out=hb[:], in_=xsrc[:], func=AF.Copy, scale=lnv[:]), r=[xsrc, lnv], w=[hb])
    for kc in range(8):
        kk.pe(lambda kc=kc: nc.tensor.transpose(out=psT_bf[:, kc * 128:(kc + 1) * 128], in_=hb[:, kc * 128:(kc + 1) * 128], identity=ident_b[:]),
              r=[hb, ident_b], w=[psT])
    kk.dve(lambda: nc.vector.tensor_copy(out=hT[:], in_=psT_bf[:, 0:1024]), r=[psT], w=[hT])


def _perm_w_in():
    a = np.arange(0, 512)
    bq = np.arange(512, 1024)
    bk = np.arange(1024, 1536)
    bv = np.arange(1536, 2048)
    cq = np.arange(2048, 2304).reshape(4, 64)[[0, 2, 1, 3]].reshape(-1)
    ck = np.arange(2304, 2432)
    cv = np.arange(2432, 2560)
    return np.concatenate([a, bq, bk, cq, ck, cv, bv])


def _rot_table(pos):
    pos = pos.astype(np.float32)
    invB = (np.float32(500000.0) ** (-np.arange(0, 16, 2, dtype=np.float32) / np.float32(16))).astype(np.float32)
    invC = (np.float32(10000.0) ** (-np.arange(0, 32, 2, dtype=np.float32) / np.float32(32))).astype(np.float32)
    angB = pos[:, None] * invB[None, :]
    p_i = pos.astype(np.int64)
    row = (p_i // 64).astype(np.float32)
    col = (p_i % 64).astype(np.float32)
    angR = row[:, None] * invC[None, :]
    angC = col[:, None] * invC[None, :]
    out = np.concatenate([np.cos(angB), np.sin(angB), np.cos(angR), np.sin(angR), np.cos(angC), np.sin(angC)], axis=1)
    return np.ascontiguousarray(out.astype(np.float32))


def make_in_maps(cfg, inputs):
    L = cfg.depth
    f = lambda k: np.asarray(inputs[k], dtype=np.float32)
    xp, xs = f("x_prompt"), f("x_sample")
    perm = _perm_w_in()
    w_in = np.ascontiguousarray(f("w_in")[:L][:, :, perm])
    w_out = np.ascontiguousarray(f("w_out")[:L])
    w_up = np.ascontiguousarray(f("w_up")[:L])
    w_dn = np.ascontiguousarray(f("w_down")[:L])
    pv = np.zeros((128, L, NPV), np.float32)
    for l in range(L):
        pv[:, l, PV_G1:PV_G1 + 8] = f("norm1_g")[l].reshape(8, 128).T
        pv[:, l, PV_G2:PV_G2 + 8] = f("norm2_g")[l].reshape(8, 128).T
        caw = f("conv_a_w")[l]
        pv[:, l, PV_CAW:PV_CAW + 62] = caw.reshape(31, 2, 128).transpose(2, 1, 0).reshape(128, 62)
        pv[:, l, PV_CAB:PV_CAB + 2] = f("conv_a_b")[l].reshape(2, 128).T
        pv[:, l, PV_LNG:PV_LNG + 2] = f("ln_a_g")[l].reshape(2, 128).T
        pv[:, l, PV_LNB:PV_LNB + 2] = f("ln_a_b")[l].reshape(2, 128).T
        pv[:, l, PV_SUB] = f("subln_b_g")[l]
        fw = f("conv_f_w")[l]
        pv[:, l, PV_FW:PV_FW + 132] = fw.reshape(3, 44, 128).transpose(2, 1, 0).reshape(128, 132)
        pv[:, l, PV_FB:PV_FB + 44] = f("conv_f_b")[l].reshape(44, 128).T
    pv = np.ascontiguousarray(pv.reshape(128, L * NPV))
    gt = np.zeros((L, 1408), np.float32)
    for l in range(L):
        gt[l] = np.concatenate([np.tile(f("qn_b_g")[l], 8), np.tile(f("kn_b_g")[l], 8), np.tile(f("qn_c_g")[l], 4), np.tile(f("kn_c_g")[l], 2)])
    lam = np.stack([f("lam_q1")[:L], f("lam_k1")[:L], f("lam_q2")[:L], f("lam_k2")[:L]], axis=1)
    lamv = np.ascontiguousarray(np.broadcast_to(lam.reshape(1, L * 4 * 64), (128, L * 4 * 64))).astype(np.float32)
    ident = np.eye(128, dtype=np.float32)
    in_maps = []
    for c in range(NCORES):
        p, r = c // RANKS, c % RANKS
        xpc = xp[p, r * cfg.ptok:(r + 1) * cfg.ptok]
        xsc = xs[c * cfg.ns:(c + 1) * cfg.ns].reshape(cfg.ns * cfg.seg, D)
        x_in = np.ascontiguousarray(np.concatenate([xpc, xsc], axis=0))
        pos = np.concatenate([np.arange(r * cfg.ptok, (r + 1) * cfg.ptok)] + [np.arange(cfg.seg)] * cfg.ns)
        rot = _rot_table(pos)
        sela = np.zeros((120, 30), np.float32)
        selh = np.zeros((8, 2), np.float32)
        if r > 0:
            for j in range(15):
                sela[(r - 1) * 30 + 15 + j, j] = 1.0
            selh[(r - 1) * 2 + 1, 0] = 1.0
        if r < RANKS - 1:
            for j in range(15):
                sela[(r + 1) * 30 + j, 15 + j] = 1.0
            selh[(r + 1) * 2 + 0, 1] = 1.0
        in_maps.append(dict(x_in=x_in, rot=rot, ident=ident, pv=pv, gt=gt, lamv=lamv, sela=sela, selh=selh,
                            w_in=w_in, w_out=w_out, w_up=w_up, w_down=w_dn))
    return in_maps


def assemble(cfg, results, nb_prompt, nb_sample):
    yp = np.zeros((nb_prompt, cfg.sp, D), np.float32)
    ys = np.zeros((nb_sample, cfg.seg, D), np.float32)
    for c in range(NCORES):
        y = np.asarray(results[c]["y_out"], dtype=np.float32).reshape(cfg.T, D)
        p, r = c // RANKS, c % RANKS
        yp[p, r * cfg.ptok:(r + 1) * cfg.ptok] = y[:cfg.ptok]
        ys[c * cfg.ns:(c + 1) * cfg.ns] = y[cfg.ptok:].reshape(cfg.ns, cfg.seg, D)
    return yp, ys


def run(cfg, inputs):
    nc, kk = build_program(cfg)
    in_maps = make_in_maps(cfg, inputs)
    res = run_bass_kernel_spmd(nc, in_maps, core_ids=list(range(NCORES)))
    return assemble(cfg, res.results, 2, NCORES * cfg.ns)


def kernel(**inputs):
    cfg = Cfg()
    return run(cfg, inputs)
```
